# Optimizing a Trainium2 kernel written in Bass

```python
import math
import jax
import jax.numpy as jnp
from jax import lax
import numpy as np

D_MODEL = 1024
BATCH = 8
SEQ = 4096
DEPTH = 4

N_EVEN = (DEPTH + 1) // 2
N_ODD = DEPTH // 2
EPS = 1e-6

GLA_HEADS = 4
GLA_DK = D_MODEL // 2
GLA_DV = D_MODEL
GLA_HK = GLA_DK // GLA_HEADS
GLA_HV = GLA_DV // GLA_HEADS
GLA_RANK = 16
GLA_TAU = 16.0
GLA_CHUNK = 64

CONV_WIDTH = D_MODEL
CONV_K = 31

S5_WIDTH = D_MODEL // 2
S5_GROUP = 16
S5_GROUPS = S5_WIDTH // S5_GROUP
S5_STATE = 64

SG_WIDTH = D_MODEL
SG_HEADS = 8
SG_HD = SG_WIDTH // SG_HEADS
SG_CHUNK = 128

EVEN_IN = 2 * GLA_DK + 2 * GLA_DV + GLA_RANK + 3 * CONV_WIDTH
EVEN_MIX = GLA_DV + CONV_WIDTH
ODD_IN = 2 * S5_WIDTH + 3 * SG_WIDTH
ODD_MIX = S5_WIDTH + SG_WIDTH

kernel_name = "hybrid_gla_conv_s5_gmlp_trunk"


def rmsnorm(x, g):
    xf = x.astype(jnp.float32)
    y = xf * lax.rsqrt(jnp.mean(xf * xf, axis=-1, keepdims=True) + EPS) * g.astype(jnp.float32)
    return y.astype(x.dtype)


def layernorm(x, g, b):
    xf = x.astype(jnp.float32)
    mu = jnp.mean(xf, axis=-1, keepdims=True)
    var = jnp.mean(jnp.square(xf - mu), axis=-1, keepdims=True)
    y = (xf - mu) * lax.rsqrt(var + EPS) * g.astype(jnp.float32) + b.astype(jnp.float32)
    return y.astype(x.dtype)


def gla_chunked(q, k, v, log_a):
    bsz, seq, nh, dk = q.shape
    dv = v.shape[-1]
    c = GLA_CHUNK
    n = seq // c

    def chunk(t):
        return t.reshape(bsz, n, c, nh, t.shape[-1]).transpose(0, 3, 1, 2, 4).astype(jnp.float32)

    q, k, v, g = chunk(q), chunk(k), chunk(v), chunk(log_a)
    b = jnp.cumsum(g, axis=3)
    b_last = b[:, :, :, -1:, :]
    qf = q * jnp.exp(b) * (dk ** -0.5)
    k_intra = k * jnp.exp(-b)
    k_state = k * jnp.exp(b_last - b)
    causal = jnp.tril(jnp.ones((c, c), dtype=bool))
    att = jnp.where(causal, jnp.einsum('bhnik,bhnjk->bhnij', qf, k_intra), 0.0)
    o_intra = jnp.einsum('bhnij,bhnjv->bhniv', att, v)
    chunk_kv = jnp.einsum('bhnjk,bhnjv->nbhkv', k_state, v)
    chunk_decay = jnp.exp(b_last[:, :, :, 0, :]).transpose(2, 0, 1, 3)

    def step(s, inp):
        kv, dec = inp
        return s * dec[..., None] + kv, s

    s0 = jnp.zeros((bsz, nh, dk, dv), jnp.float32)
    _, s_in = lax.scan(step, s0, (chunk_kv, chunk_decay))
    o = o_intra + jnp.einsum('bhnik,nbhkv->bhniv', qf, s_in)
    return o.transpose(0, 2, 3, 1, 4).reshape(bsz, seq, nh, dv)


def even_mixer(h, w_in, w_a2, b_a, gla_g, conv_w, conv_b, cln_g, cln_b, w_out):
    bsz, seq, _ = h.shape
    p = h @ w_in
    widths = [GLA_DK, GLA_DK, GLA_DV, GLA_DV, GLA_RANK, CONV_WIDTH, CONV_WIDTH, CONV_WIDTH]
    idx = np.cumsum(widths)[:-1].tolist()
    q, k, v, z_gla, a_low, c_val, c_gate, z_conv = jnp.split(p, idx, axis=-1)

    log_a = jax.nn.log_sigmoid((a_low @ w_a2 + b_a).astype(jnp.float32)) / GLA_TAU
    hd = lambda t, d: t.reshape(bsz, seq, GLA_HEADS, d)
    o = gla_chunked(hd(q, GLA_HK), hd(k, GLA_HK), hd(v, GLA_HV), hd(log_a, GLA_HK))
    o = o * lax.rsqrt(jnp.mean(o * o, axis=-1, keepdims=True) + EPS) * gla_g
    y_gla = o.reshape(bsz, seq, GLA_DV).astype(h.dtype) * jax.nn.silu(z_gla)

    u = c_val * jax.nn.sigmoid(c_gate)
    u = lax.conv_general_dilated(
        u, conv_w[:, None, :].astype(u.dtype), window_strides=(1,),
        padding=[(CONV_K - 1, 0)], dimension_numbers=('NWC', 'WIO', 'NWC'),
        feature_group_count=CONV_WIDTH) + conv_b
    u = jax.nn.silu(layernorm(u, cln_g, cln_b))
    y_conv = u * jax.nn.silu(z_conv)

    return jnp.concatenate([y_gla, y_conv], axis=-1) @ w_out


def s5_ssm(u, lam_re, lam_im, log_dt, b_re, b_im, c_re, c_im, d_skip):
    bsz, seq, _ = u.shape
    uf = u.astype(jnp.float32).reshape(bsz, seq, S5_GROUPS, S5_GROUP)
    dt = jnp.exp(log_dt.astype(jnp.float32))[:, None]
    mag = jnp.exp(lam_re * dt)
    abar_re = mag * jnp.cos(lam_im * dt)
    abar_im = mag * jnp.sin(lam_im * dt)
    den = lam_re * lam_re + lam_im * lam_im
    nr, ni = abar_re - 1.0, abar_im
    coef_re = (nr * lam_re + ni * lam_im) / den
    coef_im = (ni * lam_re - nr * lam_im) / den
    bbar_re = coef_re[..., None] * b_re - coef_im[..., None] * b_im
    bbar_im = coef_re[..., None] * b_im + coef_im[..., None] * b_re
    bu_re = jnp.einsum('blgh,gph->blgp', uf, bbar_re)
    bu_im = jnp.einsum('blgh,gph->blgp', uf, bbar_im)
    a_re = jnp.broadcast_to(abar_re, (1, seq, S5_GROUPS, S5_STATE))
    a_im = jnp.broadcast_to(abar_im, (1, seq, S5_GROUPS, S5_STATE))

    def combine(e1, e2):
        a1r, a1i, b1r, b1i = e1
        a2r, a2i, b2r, b2i = e2
        return (a1r * a2r - a1i * a2i,
                a1r * a2i + a1i * a2r,
                a2r * b1r - a2i * b1i + b2r,
                a2r * b1i + a2i * b1r + b2i)

    _, _, x_re, x_im = lax.associative_scan(combine, (a_re, a_im, bu_re, bu_im), axis=1)
    y = (jnp.einsum('blgp,ghp->blgh', x_re, c_re) - jnp.einsum('blgp,ghp->blgh', x_im, c_im)
         + d_skip.reshape(S5_GROUPS, S5_GROUP) * uf)
    return y.reshape(bsz, seq, S5_WIDTH)


def odd_mixer(h, w_in, lam_re, lam_im, log_dt, b_re, b_im, c_re, c_im, d_skip,
              w_glu, b_glu, sg_ln_g, sg_ln_b, w_s, b_s, w_out):
    bsz, seq, _ = h.shape
    p = h @ w_in
    widths = [S5_WIDTH, S5_WIDTH, SG_WIDTH, SG_WIDTH, SG_WIDTH]
    idx = np.cumsum(widths)[:-1].tolist()
    s5_u, s5_z, sg_u, sg_v, sg_z = jnp.split(p, idx, axis=-1)

    y = jax.nn.gelu(s5_ssm(s5_u, lam_re, lam_im, log_dt, b_re, b_im, c_re, c_im, d_skip))
    y = y * jax.nn.sigmoid(y @ w_glu + b_glu)
    y_s5 = y.astype(h.dtype) * jax.nn.silu(s5_z)

    n = seq // SG_CHUNK
    v = layernorm(sg_v, sg_ln_g, sg_ln_b).reshape(bsz, n, SG_CHUNK, SG_HEADS, SG_HD)
    causal = jnp.tril(jnp.ones((SG_CHUNK, SG_CHUNK), dtype=bool))
    ws = jnp.where(causal, w_s, 0.0)
    sv = jnp.einsum('hts,bnshc->bnthc', ws, v) + b_s.T[None, None, :, :, None]
    y_sg = sg_u * sv.reshape(bsz, seq, SG_WIDTH) * jax.nn.silu(sg_z)

    return jnp.concatenate([y_s5, y_sg.astype(h.dtype)], axis=-1) @ w_out


def setup_inputs(seed: int = 0) -> dict:
    key = jax.random.key(seed)
    ks = jax.random.split(key, 32)
    f32 = jnp.float32

    def nrm(k, shape, s):
        return jax.random.normal(k, shape, f32) * s

    n_idx = jnp.arange(S5_STATE, dtype=f32)
    return {
        "x": nrm(ks[0], (BATCH, SEQ, D_MODEL), 1.0),
        "norm_g": 1.0 + nrm(ks[1], (DEPTH, D_MODEL), 0.01),
        "final_g": 1.0 + nrm(ks[2], (D_MODEL,), 0.01),
        "e_w_in": nrm(ks[3], (N_EVEN, D_MODEL, EVEN_IN), D_MODEL ** -0.5),
        "e_w_a2": nrm(ks[4], (N_EVEN, GLA_RANK, GLA_DK), GLA_RANK ** -0.5),
        "e_b_a": nrm(ks[5], (N_EVEN, GLA_DK), 0.1),
        "e_gla_g": 1.0 + nrm(ks[6], (N_EVEN, GLA_HEADS, GLA_HV), 0.01),
        "e_conv_w": nrm(ks[7], (N_EVEN, CONV_K, CONV_WIDTH), CONV_K ** -0.5),
        "e_conv_b": nrm(ks[8], (N_EVEN, CONV_WIDTH), 0.01),
        "e_cln_g": 1.0 + nrm(ks[9], (N_EVEN, CONV_WIDTH), 0.01),
        "e_cln_b": nrm(ks[10], (N_EVEN, CONV_WIDTH), 0.01),
        "e_w_out": nrm(ks[11], (N_EVEN, EVEN_MIX, D_MODEL), EVEN_MIX ** -0.5),
        "o_w_in": nrm(ks[12], (N_ODD, D_MODEL, ODD_IN), D_MODEL ** -0.5),
        "o_lam_re": -0.5 + nrm(ks[13], (N_ODD, S5_GROUPS, S5_STATE), 0.01),
        "o_lam_im": math.pi * n_idx + nrm(ks[14], (N_ODD, S5_GROUPS, S5_STATE), 0.01),
        "o_log_dt": jax.random.uniform(ks[15], (N_ODD, S5_GROUPS), f32, math.log(1e-3), math.log(1e-1)),
        "o_b_re": nrm(ks[16], (N_ODD, S5_GROUPS, S5_STATE, S5_GROUP), (2 * S5_GROUP) ** -0.5),
        "o_b_im": nrm(ks[17], (N_ODD, S5_GROUPS, S5_STATE, S5_GROUP), (2 * S5_GROUP) ** -0.5),
        "o_c_re": nrm(ks[18], (N_ODD, S5_GROUPS, S5_GROUP, S5_STATE), S5_STATE ** -0.5),
        "o_c_im": nrm(ks[19], (N_ODD, S5_GROUPS, S5_GROUP, S5_STATE), S5_STATE ** -0.5),
        "o_d": nrm(ks[20], (N_ODD, S5_WIDTH), 1.0),
        "o_w_glu": nrm(ks[21], (N_ODD, S5_WIDTH, S5_WIDTH), S5_WIDTH ** -0.5),
        "o_b_glu": nrm(ks[22], (N_ODD, S5_WIDTH), 0.01),
        "o_sg_ln_g": 1.0 + nrm(ks[23], (N_ODD, SG_WIDTH), 0.01),
        "o_sg_ln_b": nrm(ks[24], (N_ODD, SG_WIDTH), 0.01),
        "o_w_s": nrm(ks[25], (N_ODD, SG_HEADS, SG_CHUNK, SG_CHUNK), 0.02),
        "o_b_s": 1.0 + nrm(ks[26], (N_ODD, SG_HEADS, SG_CHUNK), 0.01),
        "o_w_out": nrm(ks[27], (N_ODD, ODD_MIX, D_MODEL), ODD_MIX ** -0.5),
    }


def reference(x, norm_g, final_g, e_w_in, e_w_a2, e_b_a, e_gla_g, e_conv_w, e_conv_b,
              e_cln_g, e_cln_b, e_w_out, o_w_in, o_lam_re, o_lam_im, o_log_dt, o_b_re,
              o_b_im, o_c_re, o_c_im, o_d, o_w_glu, o_b_glu, o_sg_ln_g, o_sg_ln_b,
              o_w_s, o_b_s, o_w_out):
    h = x
    for layer in range(DEPTH):
        i = layer // 2
        hn = rmsnorm(h, norm_g[layer])
        if layer % 2 == 0:
            out = even_mixer(hn, e_w_in[i], e_w_a2[i], e_b_a[i], e_gla_g[i], e_conv_w[i],
                             e_conv_b[i], e_cln_g[i], e_cln_b[i], e_w_out[i])
        else:
            out = odd_mixer(hn, o_w_in[i], o_lam_re[i], o_lam_im[i], o_log_dt[i], o_b_re[i],
                            o_b_im[i], o_c_re[i], o_c_im[i], o_d[i], o_w_glu[i], o_b_glu[i],
                            o_sg_ln_g[i], o_sg_ln_b[i], o_w_s[i], o_b_s[i], o_w_out[i])
        h = h + out.astype(h.dtype)
    return rmsnorm(h, final_g)
```

```python
import numpy as np
import concourse.bass as bass
import concourse.mybir as mybir
from concourse.bass_utils import run_bass_kernel_spmd
from contextlib import ExitStack

F32 = mybir.dt.float32
BF16 = mybir.dt.bfloat16
AF = mybir.ActivationFunctionType
ALU = mybir.AluOpType

P = 128
SEQ = 4096
D = 1024
MT = 256
NMT = SEQ // MT
NS = MT // 128
EPS = 1e-6
EVEN_IN = 6160
ODD_IN = 4096
WA_N = 33792
OA_STAGE = 3
PREP_CUT = 99


class _Op:
    __slots__ = ("eng", "fn", "deps", "edges", "signal", "dma_sem", "dma_val", "cnt", "cost", "seg", "idx", "fin", "placed")

    def __init__(self, eng, fn):
        self.eng = eng
        self.fn = fn
        self.deps = []
        self.edges = []
        self.signal = False
        self.dma_sem = None
        self.dma_val = 0
        self.cnt = 0
        self.cost = 200.0
        self.seg = 0
        self.idx = 0
        self.fin = 0.0
        self.placed = False


class Sched:
    ENGS = ("pe", "act", "dve", "pool", "sp")
    WINDOW = {"pe": 256, "act": 64, "dve": 64, "pool": 32, "sp": 32}

    def __init__(self, nc, stack):
        self.nc = nc
        self.stack = stack
        self.all_ops = []
        self.last_w = {}
        self.readers = {}
        self.dma_keys = {}
        self.last_dma = {}
        self.eng_sem = {}
        for e in ("pe", "act", "dve", "pool"):
            self.eng_sem[e] = stack.enter_context(nc.semaphore("es_" + e))
        self.seg = 0
        self.reorder = True

    def barrier(self):
        self.seg += 1

    def add(self, eng, fn, reads=(), writes=(), dma_key=None, cost=None):
        op = _Op(eng, fn)
        op.seg = self.seg
        op.idx = len(self.all_ops)
        is_dma = dma_key is not None
        if cost is not None:
            op.cost = float(cost)
        my_sem = None
        if is_dma:
            ent = self.dma_keys.get(dma_key)
            if ent is None:
                sem = self.stack.enter_context(self.nc.semaphore("ds_%d" % len(self.dma_keys)))
                ent = [sem, 0]
                self.dma_keys[dma_key] = ent
            my_sem = ent[0]
            prev = self.last_dma.get(dma_key)
            if prev is not None:
                op.edges.append(prev)
        cand = []
        for k in reads:
            w = self.last_w.get(k)
            if w is not None:
                cand.append(w)
        for k in writes:
            w = self.last_w.get(k)
            if w is not None:
                cand.append(w)
            for r in self.readers.get(k, ()):
                cand.append(r)
        seen = set(id(x) for x in op.edges)
        for d in cand:
            if id(d) in seen or d is op:
                continue
            seen.add(id(d))
            op.edges.append(d)
            d_is_dma = d.dma_sem is not None
            if d_is_dma:
                if is_dma and d.dma_sem is my_sem:
                    continue
            elif d.eng == eng and not is_dma and eng == "pe":
                continue
            op.deps.append(d)
        if is_dma:
            ent[1] += 16
            op.dma_sem = ent[0]
            op.dma_val = ent[1]
            self.last_dma[dma_key] = op
        for k in reads:
            self.readers.setdefault(k, []).append(op)
        for k in writes:
            self.last_w[k] = op
            self.readers[k] = []
        self.all_ops.append(op)
        return op

    def _schedule(self):
        order = {e: [] for e in self.ENGS}
        segs = {}
        for op in self.all_ops:
            segs.setdefault(op.seg, []).append(op)
        bar_points = {e: [] for e in self.ENGS}
        t_base = 0.0
        LAT = 120.0
        for sg in sorted(segs):
            ops = segs[sg]
            for e in self.ENGS:
                bar_points[e].append(len(order[e]))
            if not self.reorder:
                for op in ops:
                    order[op.eng].append(op)
                    op.placed = True
                continue
            queues = {e: [op for op in ops if op.eng == e] for e in self.ENGS}
            heads = {e: 0 for e in self.ENGS}
            free = {e: t_base for e in self.ENGS}
            remaining = len(ops)
            tmax = t_base
            while remaining:
                best = None
                best_key = None
                for e in self.ENGS:
                    q = queues[e]
                    h = heads[e]
                    while h < len(q) and q[h].placed:
                        h += 1
                    heads[e] = h
                    lim = min(len(q), h + self.WINDOW[e])
                    k = h
                    found = 0
                    while k < lim:
                        op = q[k]
                        k += 1
                        if op.placed:
                            continue
                        ok = True
                        rdy = free[e]
                        for d in op.edges:
                            if d.seg != sg:
                                continue
                            if not d.placed:
                                ok = False
                                break
                            f = d.fin + (LAT if d.eng != e or d.dma_sem is not None else 30.0)
                            if f > rdy:
                                rdy = f
                        if not ok:
                            continue
                        key = (rdy, op.idx)
                        if best_key is None or key < best_key:
                            best_key = key
                            best = op
                        found += 1
                        if found >= 6:
                            break
                op = best
                assert op is not None, "scheduler stuck (cyclic deps?)"
                st = best_key[0]
                op.placed = True
                if op.dma_sem is not None:
                    op.fin = st + op.cost
                    free[op.eng] = st + 60.0
                else:
                    op.fin = st + op.cost
                    free[op.eng] = op.fin
                if op.fin > tmax:
                    tmax = op.fin
                order[op.eng].append(op)
                remaining -= 1
            t_base = tmax
        self.est_ns = t_base
        return order, bar_points

    def emit(self):
        nc = self.nc
        order, bar_points = self._schedule()
        dma_by_seg = {}
        for op in self.all_ops:
            if op.dma_sem is not None:
                dma_by_seg.setdefault(op.seg, {})[id(op.dma_sem)] = op
        nseg = self.seg + 1
        for si in range(1, nseg):
            lasts = []
            for e in self.ENGS:
                pos = bar_points[e][si]
                for k in range(pos - 1, -1, -1):
                    if order[e][k].dma_sem is None:
                        lasts.append(order[e][k])
                        break
            dl = {}
            for sj in range(si):
                dl.update(dma_by_seg.get(sj, {}))
            lasts += list(dl.values())
            for e in self.ENGS:
                pos = bar_points[e][si]
                if pos < len(order[e]):
                    op = order[e][pos]
                    have = set(id(x) for x in op.deps)
                    for d in lasts:
                        if id(d) in have:
                            continue
                        if d.dma_sem is None and d.eng == e and op.dma_sem is None:
                            continue
                        op.deps.append(d)
        for e in self.ENGS:
            for op in order[e]:
                for d in op.deps:
                    if d.dma_sem is None:
                        d.signal = True
        for e in ("pe", "act", "dve", "pool"):
            for op in reversed(order[e]):
                if op.dma_sem is None:
                    op.signal = True
                    break
        for e in self.ENGS:
            c = 0
            for op in order[e]:
                if op.dma_sem is None and op.signal:
                    c += 1
                op.cnt = c
        sched = self

        def replay(ename, eng):
            waited = {}
            for op in order[ename]:
                for d in op.deps:
                    if d.dma_sem is not None:
                        sem, val = d.dma_sem, d.dma_val
                    else:
                        sem, val = sched.eng_sem[d.eng], d.cnt
                    key = id(sem)
                    if waited.get(key, 0) >= val:
                        continue
                    waited[key] = val
                    eng.wait_ge(sem, val)
                ins = op.fn(eng)
                if op.dma_sem is not None:
                    ins.then_inc(op.dma_sem, 16)
                elif op.signal:
                    ins.then_inc(sched.eng_sem[ename], 1)
            if ename == "sp":
                for e in ("pe", "act", "dve", "pool"):
                    last = None
                    for op in order[e]:
                        if op.dma_sem is None:
                            last = op
                    if last is not None:
                        eng.wait_ge(sched.eng_sem[e], last.cnt)
                for k, (sem, cnt) in sched.dma_keys.items():
                    eng.wait_ge(sem, cnt)

        with nc.Block() as block:
            @block.tensor
            def _(e):
                replay("pe", e)

            @block.scalar
            def _(e):
                replay("act", e)

            @block.vector
            def _(e):
                replay("dve", e)

            @block.gpsimd
            def _(e):
                replay("pool", e)

            @block.sync
            def _(e):
                replay("sp", e)


class Arena:
    def __init__(self, tile_ap, n, name):
        self.t = tile_ap
        self.n = n
        self.off = 0
        self.name = name
        self.gen = 0

    def reset(self):
        self.off = 0
        self.gen += 1

    def f32(self, n):
        a = self.t[:, self.off:self.off + n]
        self.off += n
        assert self.off <= self.n, (self.name, self.off, self.n)
        return a

    def bf16(self, n):
        w = (n + 1) // 2
        a = self.t[:, self.off:self.off + w].bitcast(BF16)
        self.off += w
        assert self.off <= self.n, (self.name, self.off, self.n)
        return a


class Builder:
    def __init__(self, nc, st, n_layers=4, final_norm=True, max_passes=None):
        self.max_passes = max_passes
        self.debug = max_passes == 1
        self.nc = nc
        self.st = st
        self.S = Sched(nc, st)
        self.n_layers = n_layers
        self.final_norm = final_norm
        self.uid = 0
        S = self.S
        dt_in = lambda name, shape: nc.dram_tensor(name, shape, F32, kind="ExternalInput").ap()
        self.x = dt_in("x", [SEQ, D])
        self.norm_g = dt_in("norm_g", [4, D])
        self.final_g = dt_in("final_g", [1, D])
        self.e_w_in = dt_in("e_w_in", [2, D, EVEN_IN])
        self.e_w_a2 = dt_in("e_w_a2", [2, 16, 512])
        self.e_b_a = dt_in("e_b_a", [2, 1, 512])
        self.e_gla_g = dt_in("e_gla_g", [2, 1, 1024])
        self.e_cw = dt_in("e_cw", [2, P, 8, 31])
        self.e_cb = dt_in("e_cb", [2, P, 8])
        self.e_lg = dt_in("e_lg", [2, P, 8])
        self.e_lb = dt_in("e_lb", [2, P, 8])
        self.e_w_out = dt_in("e_w_out", [2, 2048, D])
        self.o_w_in = dt_in("o_w_in", [2, D, ODD_IN])
        self.o_lamre = dt_in("o_lamre", [2, P, 16])
        self.o_lamim = dt_in("o_lamim", [2, P, 16])
        self.o_logdt = dt_in("o_logdt", [2, P, 16])
        self.o_bre = dt_in("o_bre", [2, P, 256])
        self.o_bim = dt_in("o_bim", [2, P, 256])
        self.o_cre = dt_in("o_cre", [2, P, 256])
        self.o_cim = dt_in("o_cim", [2, P, 256])
        self.o_dcol = dt_in("o_dcol", [2, P, 32])
        self.o_w_glu = dt_in("o_w_glu", [2, 512, 512])
        self.o_bglu = dt_in("o_bglu", [2, P, 4])
        self.o_lng = dt_in("o_lng", [2, 1, 1024])
        self.o_lnb = dt_in("o_lnb", [2, 1, 1024])
        self.o_wsT = dt_in("o_wsT", [2, P, 1024])
        self.o_bs = dt_in("o_bs", [2, 1, 1024])
        self.o_w_out = dt_in("o_w_out", [2, 1536, D])
        self.out = nc.dram_tensor("out", [SEQ, D], F32, kind="ExternalOutput").ap()
        self.hbuf = [nc.dram_tensor("hbuf%d" % i, [SEQ, D], F32, kind="Internal").ap() for i in range(2)]
        self.hnscr = nc.dram_tensor("hnscr", [NMT, P, 8 * MT], BF16, kind="Internal").ap()

        sb = lambda name, shape, dt: st.enter_context(nc.sbuf_tensor(name, shape, dt))
        self.wa = [sb("wa%d" % i, [P, WA_N], BF16) for i in range(2)]
        self.cst = sb("cst", [P, 4 * 128], BF16)
        self.ident = self.cst[:, 0:128]
        self.triInc = self.cst[:, 128:256]
        self.triRev = self.cst[:, 256:384]
        self.cmask = self.cst[:, 384:512]
        self.pst = sb("pst", [P, 1024], F32)
        rem = int(nc.sbuf_bytes_remaining) - 256
        self.act_n = rem // 4
        self.act_t = sb("actarena", [P, self.act_n], F32)
        self.A = Arena(self.act_t, self.act_n, "act")
        self.pb = [st.enter_context(nc.psum_tensor("pb%d" % i, [P, 512], F32)) for i in range(8)]
        self.ring = list(range(8))
        self.ring_i = 0
        self.slot = 0
        self._consts()

    def key(self, name):
        self.uid += 1
        return "%s#%d" % (name, self.uid)

    def bank(self):
        i = self.ring[self.ring_i % len(self.ring)]
        self.ring_i += 1
        return self.pb[i], ("pb", i)

    def set_ring(self, banks):
        self.ring = list(banks)
        self.ring_i = 0

    def add(self, *a, **k):
        return self.S.add(*a, **k)

    @staticmethod
    def _n(ap):
        n = 1
        for d in ap.shape[1:]:
            n *= int(d)
        return n

    def mm(self, out, lhsT, rhs, start, stop, reads, writes):
        c = max(self._n(out), 64) / 2.0 + 10.0
        if lhsT.dtype == F32:
            c *= 4.0
        self.S.add("pe", lambda e: e.matmul(out, lhsT=lhsT, rhs=rhs, start=start, stop=stop), reads=reads, writes=writes, cost=c)

    def tp(self, out, in_, ident, reads, writes):
        self.S.add("pe", lambda e: e.transpose(out=out, in_=in_, identity=ident), reads=reads, writes=writes, cost=80.0)

    def actf(self, out, in_, func, reads, writes, bias=None, scale=None, accum=None):
        kw = {}
        if bias is not None:
            kw["bias"] = bias
        if scale is not None:
            kw["scale"] = scale
        if accum is not None:
            kw["accum_out"] = accum
        c = self._n(out) / 1.4 + 230.0 + (100.0 if accum is not None else 0.0)
        self.S.add("act", lambda e: e.activation(out=out, in_=in_, func=func, **kw), reads=reads, writes=writes, cost=c)

    def tt(self, out, in0, in1, op, reads, writes, eng="dve"):
        c = self._n(out) / (0.96 if eng == "dve" else 0.45) + (130.0 if eng == "dve" else 300.0)
        self.S.add(eng, lambda e: e.tensor_tensor(out=out, in0=in0, in1=in1, op=op), reads=reads, writes=writes, cost=c)

    def ts(self, out, in0, s1, op0, reads, writes, s2=None, op1=None, eng="dve"):
        c = self._n(out) / (0.96 if eng == "dve" else 0.45) + (130.0 if eng == "dve" else 300.0)
        if op1 is None:
            self.S.add(eng, lambda e: e.tensor_scalar(out=out, in0=in0, scalar1=s1, scalar2=None, op0=op0), reads=reads, writes=writes, cost=c)
        else:
            self.S.add(eng, lambda e: e.tensor_scalar(out=out, in0=in0, scalar1=s1, scalar2=s2, op0=op0, op1=op1), reads=reads, writes=writes, cost=c)

    def stt(self, out, in0, scalar, in1, op0, op1, reads, writes):
        c = self._n(out) / 0.96 + 130.0
        self.S.add("dve", lambda e: e.scalar_tensor_tensor(out=out, in0=in0, scalar=scalar, in1=in1, op0=op0, op1=op1), reads=reads, writes=writes, cost=c)

    def cp(self, out, in_, reads, writes, eng="act"):
        n = self._n(out)
        if eng == "act":
            self.S.add("act", lambda e: e.copy(out=out, in_=in_), reads=reads, writes=writes, cost=n / 1.4 + 230.0)
        else:
            c = n / (0.96 if eng == "dve" else 0.45) + (130.0 if eng == "dve" else 300.0)
            self.S.add(eng, lambda e: e.tensor_copy(out=out, in_=in_), reads=reads, writes=writes, cost=c)

    def memset(self, ap, val, writes, eng="pool"):
        self.S.add(eng, lambda e: e.memset(ap, val), writes=writes, cost=self._n(ap) / 0.9 + 150.0)

    def dma(self, out, in_, reads, writes, key, eng="sp"):
        nbytes = self._n(out) * int(out.shape[0]) * 4
        self.S.add(eng, lambda e: e.dma_start(out=out, in_=in_), reads=reads, writes=writes, dma_key=key, cost=2500.0 + nbytes / 60.0)

    def rsqrt_small(self, out, in_, scale, reads, writes, tmpkey=None):
        self.ts(out, in_, scale, ALU.mult, reads, writes, s2=EPS, op1=ALU.add)
        self.actf(out, out, AF.Sqrt, writes, writes)
        self.S.add("dve", lambda e: e.reciprocal(out=out, in_=out), reads=writes, writes=writes)

    def dump(self, name, ap, reads, row0, col0=0):
        if not getattr(self, "debug", False):
            return
        n = ap.shape[-1] if len(ap.shape) == 2 else None
        self.dma(self.out[row0:row0 + ap.shape[0], col0:col0 + n], ap, reads, [("dbg", name)], "dbg", eng="pool")

    def _consts(self):
        A = self.A
        ones = A.bf16(128)
        k1 = "c_ones"
        self.memset(ones, 1.0, [k1])
        self.memset(self.cst[:, :], 0.0, ["cst"])
        self.S.add("pool", lambda e: e.affine_select(out=self.ident, in_=ones, pattern=[[-1, 128]], compare_op=ALU.is_equal,
                                                    fill=0.0, base=0, channel_multiplier=1), reads=[k1, "cst"], writes=["cst"])
        sc = A.bf16(128)
        tmpm = A.bf16(128)
        self.memset(sc, -1.0 / 16.0, ["c_sc"])

        def tri(dst, src, srckey, upper):
            if upper:
                self.S.add("pool", lambda e: e.affine_select(out=tmpm, in_=src, pattern=[[1, 128]], compare_op=ALU.is_ge,
                                                            fill=0.0, base=0, channel_multiplier=-1), reads=[srckey, "tmpm"], writes=["tmpm"])
                self.cp(dst[:, 0:64], tmpm[:, 0:64], ["tmpm"], ["cst"], eng="pool")
                self.S.add("pool", lambda e: e.affine_select(out=dst[:, 64:128], in_=tmpm[:, 64:128], pattern=[[0, 64]], compare_op=ALU.is_ge,
                                                            fill=0.0, base=-64, channel_multiplier=1), reads=["tmpm", "cst"], writes=["cst"])
            else:
                self.S.add("pool", lambda e: e.affine_select(out=tmpm, in_=src, pattern=[[-1, 128]], compare_op=ALU.is_ge,
                                                            fill=0.0, base=-1, channel_multiplier=1), reads=[srckey, "tmpm"], writes=["tmpm"])
                self.cp(dst[:, 64:128], tmpm[:, 64:128], ["tmpm"], ["cst"], eng="pool")
                self.S.add("pool", lambda e: e.affine_select(out=dst[:, 0:64], in_=tmpm[:, 0:64], pattern=[[0, 64]], compare_op=ALU.is_ge,
                                                            fill=0.0, base=63, channel_multiplier=-1), reads=["tmpm", "cst"], writes=["cst"])
        tri(self.triInc, sc, "c_sc", True)
        tri(self.triRev, sc, "c_sc", False)
        tri(self.cmask, ones, k1, True)

    def load_w_rows(self, slot, off, src, r0, nrows_k, c0, ncols, key):
        view = self.wa[slot][:, off:off + nrows_k * ncols].rearrange("p (k n) -> p k n", n=ncols)
        s = src[r0:r0 + 128 * nrows_k, c0:c0 + ncols].rearrange("(k p) n -> p k n", p=128)
        for k in range(nrows_k):
            self.dma(view[:, k, :], s[:, k, :], [], [key], key, eng="pool")
        return view

    def norm_tile(self, m, hsrc, hsrc_key, bufs, store_scr):
        hA, hnb, gtile, hnT, ss, gkey = bufs["hA"], bufs["hnb"], bufs["gn"], bufs["hnT"], bufs["ss"], bufs["gkey"]
        hk = bufs.get("hk", "hnT")
        for j in range(NS):
            t0 = m * MT + j * 128
            hb = hA[j]
            kh = "hA%d" % j
            self.dma(hb, hsrc[t0:t0 + 128, :], [(hsrc_key, m, j)], [kh], "ld_hA%d" % j)
            self.actf(hnb.bitcast(F32) if False else bufs["junk"], hb, AF.Square, [kh], ["junk", "ss"], accum=ss[:, 0:1])
            self.rsqrt_small(ss[:, 1:2], ss[:, 0:1], 1.0 / D, ["ss"], ["rstd"], "ss_t")
            self.stt(hnb, hb, ss[:, 1:2], gtile, ALU.mult, ALU.mult, [kh, "rstd", gkey], ["hnb"])
            if m == 0 and j == 0:
                self.dump("hnb", hnb, ["hnb"], 0)
            pbk, pk = self.bank()
            pv = pbk[:, :].bitcast(BF16)
            for dk in range(8):
                self.tp(pv[:, dk * 128:(dk + 1) * 128], hnb[:, dk * 128:(dk + 1) * 128], self.ident, ["hnb", "cst"], [pk])
            self.cp(hnT[:, :, j * 128:(j + 1) * 128], pv[:, 0:1024].rearrange("p (k t) -> p k t", t=128), [pk], [hk])
        if store_scr:
            self.dma(self.hnscr[m].rearrange("p (k t) -> p k t", t=MT), hnT, [hk], [("hnscr", m)], "st_" + hk)

    def load_hnT(self, m, hnT, hk="hnT"):
        self.dma(hnT, self.hnscr[m].rearrange("p (k t) -> p k t", t=MT), [("hnscr", m)], [hk], "ld_" + hk)

    def outproj_residual(self, m, yT, nk, Wo, wkey, hres, hres_key, hdst, hdst_key, hB, final_g=None, ss=None, junk=None, hbk="hB", yk="yT"):
        for j in range(NS):
            t0 = m * MT + j * 128
            hb = hB[j]
            kh = "%s%d" % (hbk, j)
            self.dma(hb, hres[t0:t0 + 128, :], [(hres_key, m, j)], [kh], "ld_%s%d" % (hbk, j))
            for n2 in range(2):
                pbk, pk = self.bank()
                for mk in range(nk):
                    self.mm(pbk[:, :], yT[:, mk, j * 128:(j + 1) * 128], Wo[:, mk, n2 * 512:(n2 + 1) * 512],
                            mk == 0, mk == nk - 1, [yk, wkey], [pk])
                self.tt(hb[:, n2 * 512:(n2 + 1) * 512], hb[:, n2 * 512:(n2 + 1) * 512], pbk[:, :], ALU.add, [kh, pk], [kh])
            if final_g is not None:
                self.actf(junk, hb, AF.Square, [kh], ["junk", "fss"], accum=ss[:, 2:3])
                self.rsqrt_small(ss[:, 3:4], ss[:, 2:3], 1.0 / D, ["fss"], ["frstd"], "fss_t")
                self.stt(hb, hb, ss[:, 3:4], final_g, ALU.mult, ALU.mult, [kh, "frstd", "gfin"], [kh])
            self.dma(hdst[t0:t0 + 128, :], hb, [kh], [(hdst_key, m, j)], "st_h%d" % j)

    def w_e1(self, L, slot):
        i = L // 2
        wk = ("w", slot)
        Win = self.load_w_rows(slot, 0, self.e_w_in[i], 0, 8, 0, 3088, wk)
        Wo = self.load_w_rows(slot, 8 * 3088, self.e_w_out[i], 0, 8, 0, 1024, wk)
        return (Win, Wo, wk)

    def pass_e1(self, L, W, hin, hin_key, hmid, hmid_key):
        i = L // 2
        A = self.A
        self.set_ring(range(6))
        Win, Wo, wk = W
        gn = A.f32(1024)
        gg = A.f32(1024)
        pk_ = "par_e1"
        self.dma(gn, self.norm_g[L:L + 1, :].partition_broadcast(128), [], [pk_], pk_)
        self.dma(gg, self.e_gla_g[i].partition_broadcast(128), [], [pk_], pk_)
        wa2f = A.f32(512)
        wa2 = A.bf16(512)
        self.memset(wa2f[0:32, :], 0.0, ["wa2f"])
        self.dma(wa2f[0:16, :], self.e_w_a2[i], ["wa2f"], [pk_], pk_)
        self.dma(wa2f[16:17, :], self.e_b_a[i], ["wa2f"], [pk_], pk_)
        self.cp(wa2[0:32, :], wa2f[0:32, :], [pk_, "wa2f"], ["wa2"], eng="dve")
        hA = [A.f32(1024) for _ in range(NS)]
        hnb = A.bf16(1024)
        junk = A.bf16(1024)
        ss = A.f32(8)
        hnT2 = [A.bf16(8 * MT).rearrange("p (k t) -> p k t", t=MT) for _ in range(2)]
        nb = dict(hA=hA, hnb=hnb, gn=gn, hnT=None, ss=ss, gkey=pk_, junk=junk)
        alow = A.bf16(MT)
        self.memset(alow[0:32, :], 1.0, ["alow"])
        tmpf = [A.f32(512) for _ in range(2)]
        spT = A.bf16(NS * 512).rearrange("p (j n) -> p j n", n=512)
        Eb = A.f32(MT)
        Enb = A.f32(MT)
        dec = A.f32(16).rearrange("p (h c) -> p h c", c=4)
        qf = A.bf16(4 * MT).rearrange("p (h t) -> p h t", t=MT)
        kin = A.bf16(4 * MT).rearrange("p (h t) -> p h t", t=MT)
        kst = A.bf16(NS * 512).rearrange("p (j n) -> p j n", n=512)
        v = A.bf16(NS * 1024).rearrange("p (j n) -> p j n", n=1024)
        S32 = A.f32(1024).rearrange("p (h n) -> p h n", n=256)
        Sbf = [A.bf16(1024).rearrange("p (h n) -> p h n", n=256) for _ in range(2)]
        attT = A.bf16(512).rearrange("p (h n) -> p h n", n=128)
        sz = A.f32(1024)
        ss4 = A.f32(8)
        ygla = A.bf16(1024)
        yT = A.bf16(8 * MT).rearrange("p (k t) -> p k t", t=MT)
        self.memset(S32, 0.0, ["S32_%d" % h for h in range(4)])
        self.memset(Sbf[0], 0.0, ["Sbf0_%d" % h for h in range(4)])
        chunk_ctr = 0
        QSC = 128.0 ** -0.5
        for m in range(NMT):
            hnT = hnT2[m % 2]
            hk = "hnT%d" % (m % 2)
            nb["hnT"] = hnT
            nb["hk"] = hk
            self.norm_tile(m, hin, hin_key, nb, store_scr=True)
            pbk, pk = self.bank()
            for dk in range(8):
                self.mm(pbk[0:16, 0:MT], Win[:, dk, 3072:3088], hnT[:, dk, :], dk == 0, dk == 7, [hk, wk], [pk])
            self.cp(alow[0:16, :], pbk[0:16, 0:MT], [pk], ["alow"])
            for j in range(NS):
                pbk, pk = self.bank()
                self.mm(pbk[:, :], alow[0:32, j * 128:(j + 1) * 128], wa2[0:32, :], True, True, ["alow", "wa2"], [pk])
                self.actf(tmpf[j], pbk[:, :], AF.Exp, [pk], ["tmpf%d" % j], scale=-1.0)
            for j in range(NS):
                self.actf(spT[:, j, :], tmpf[j], AF.Ln, ["tmpf%d" % j], ["spT"], bias=1.0)
            if m == 0:
                self.dump("spT", spT[:, 0, :], ["spT"], 128)
                self.dump("hnT0", hnT[:, 0, :], ["hnT"], 128, 512)
                self.dump("hnT1", hnT[:, 1, :], ["hnT"], 128, 768)
            for hd in range(4):
                pbk, pk = self.bank()
                for j in range(NS):
                    self.mm(pbk[:, j * 128:(j + 1) * 128], spT[:, j, hd * 128:(hd + 1) * 128], self.triInc, True, True, ["spT", "cst"], [pk])
                self.actf(Eb, pbk[:, 0:MT], AF.Exp, [pk], ["Eb"])
                self.actf(Enb, pbk[:, 0:MT], AF.Exp, [pk], ["Enb"], scale=-1.0)
                self.cp(dec[:, hd, :], Eb.rearrange("p (c t) -> p c t", t=64)[:, :, 63], ["Eb"], ["dec%d" % hd], eng="dve")
                pq, pqk = self.bank()
                for dk in range(8):
                    self.mm(pq[:, 0:MT], Win[:, dk, hd * 128:(hd + 1) * 128], hnT[:, dk, :], dk == 0, dk == 7, [hk, wk], [pqk])
                self.stt(qf[:, hd, :], pq[:, 0:MT], QSC, Eb, ALU.mult, ALU.mult, [pqk, "Eb"], ["qf%d" % hd])
                pkk, pkkk = self.bank()
                for dk in range(8):
                    self.mm(pkk[:, 0:MT], Win[:, dk, 512 + hd * 128:512 + (hd + 1) * 128], hnT[:, dk, :], dk == 0, dk == 7, [hk, wk], [pkkk])
                self.tt(kin[:, hd, :], pkk[:, 0:MT], Enb, ALU.mult, [pkkk, "Enb"], ["kin%d" % hd])
                if m == 0 and hd == 0:
                    self.dump("Eb", Eb, ["Eb"], 256)
                    self.dump("qf", qf[:, 0, :], ["qf0"], 256, 256)
                    self.dump("kin", kin[:, 0, :], ["kin0"], 256, 512)
            for j in range(NS):
                pbk, pk = self.bank()
                self.mm(pbk[:, :], self.triRev, spT[:, j, :], True, True, ["spT", "cst"], [pk])
                self.actf(tmpf[j], pbk[:, :], AF.Exp, [pk], ["tmpf%d" % j])
                pk2, pk2k = self.bank()
                for dk in range(8):
                    self.mm(pk2[:, :], hnT[:, dk, j * 128:(j + 1) * 128], Win[:, dk, 512:1024], dk == 0, dk == 7, [hk, wk], [pk2k])
                self.tt(kst[:, j, :], pk2[:, :], tmpf[j], ALU.mult, [pk2k, "tmpf%d" % j], ["kst%d" % j])
                for n2 in range(2):
                    pv_, pvk = self.bank()
                    for dk in range(8):
                        self.mm(pv_[:, :], hnT[:, dk, j * 128:(j + 1) * 128], Win[:, dk, 1024 + n2 * 512:1024 + (n2 + 1) * 512],
                                dk == 0, dk == 7, [hk, wk], [pvk])
                    self.cp(v[:, j, n2 * 512:(n2 + 1) * 512], pv_[:, :], [pvk], ["v%d" % j])
            for j in range(NS):
                pat, patk = self.bank()
                for hd in range(4):
                    self.mm(pat[:, hd * 128:(hd + 1) * 128], kin[:, hd, j * 128:(j + 1) * 128], qf[:, hd, j * 128:(j + 1) * 128],
                            True, True, ["kin%d" % hd, "qf%d" % hd], [patk])
                self.tt(attT, pat[:, :].rearrange("p (h n) -> p h n", n=128), self.cmask.unsqueeze(1).to_broadcast([P, 4, 128]),
                        ALU.mult, [patk, "cst"], ["attT"])
                po = [(self.pb[6], ("pb", 6)), (self.pb[7], ("pb", 7))]
                for c2 in range(2):
                    cs = slice(64 * c2, 64 * c2 + 64)
                    cur = chunk_ctr % 2
                    nxt = 1 - cur
                    cidx = 2 * j + c2
                    for hd in range(4):
                        ob, obk = po[hd // 2]
                        osl = ob[cs, (hd % 2) * 256:(hd % 2) * 256 + 256]
                        okey = obk
                        self.mm(osl, attT[cs, hd, 64 * c2:64 * c2 + 64], v[cs, j, hd * 256:(hd + 1) * 256], True, False,
                                ["attT", "v%d" % j], [okey])
                        self.mm(osl, qf[:, hd, j * 128 + 64 * c2:j * 128 + 64 * c2 + 64], Sbf[cur][:, hd, :], False, True,
                                ["qf%d" % hd, "Sbf%d_%d" % (cur, hd)], [okey])
                    for hd in range(4):
                        pkv, pkvk = self.bank()
                        self.mm(pkv[:, 0:256], kst[cs, j, hd * 128:(hd + 1) * 128], v[cs, j, hd * 256:(hd + 1) * 256], True, True,
                                ["kst%d" % j, "v%d" % j], [pkvk])
                        self.stt(S32[:, hd, :], S32[:, hd, :], dec[:, hd, cidx:cidx + 1], pkv[:, 0:256], ALU.mult, ALU.add,
                                 ["S32_%d" % hd, "dec%d" % hd, pkvk], ["S32_%d" % hd])
                        self.cp(Sbf[nxt][:, hd, :], S32[:, hd, :], ["S32_%d" % hd], ["Sbf%d_%d" % (nxt, hd)])
                    chunk_ctr += 1
                for n2 in range(2):
                    pz, pzk = self.bank()
                    for dk in range(8):
                        self.mm(pz[:, :], hnT[:, dk, j * 128:(j + 1) * 128], Win[:, dk, 2048 + n2 * 512:2048 + (n2 + 1) * 512],
                                dk == 0, dk == 7, [hk, wk], [pzk])
                    self.actf(sz[:, n2 * 512:(n2 + 1) * 512], pz[:, :], AF.Silu, [pzk], ["sz%d" % n2])
                    self.tt(sz[:, n2 * 512:(n2 + 1) * 512], sz[:, n2 * 512:(n2 + 1) * 512], gg[:, n2 * 512:(n2 + 1) * 512], ALU.mult,
                            ["sz%d" % n2, pk_], ["sz%d" % n2], eng="pool")
                for hd in range(4):
                    ob, obk = po[hd // 2]
                    osl = ob[:, (hd % 2) * 256:(hd % 2) * 256 + 256]
                    okeys = [obk]
                    self.actf(junk[:, 0:256], osl, AF.Square, okeys, ["junk", "ss4_%d" % hd], accum=ss4[:, hd:hd + 1])
                self.rsqrt_small(ss4[:, 4:8], ss4[:, 0:4], 1.0 / 256.0, ["ss4_%d" % h for h in range(4)], ["rstd4"], "ss4_t")
                for hd in range(4):
                    ob, obk = po[hd // 2]
                    osl = ob[:, (hd % 2) * 256:(hd % 2) * 256 + 256]
                    okeys = [obk]
                    self.stt(ygla[:, hd * 256:(hd + 1) * 256], osl, ss4[:, 4 + hd:5 + hd], sz[:, hd * 256:(hd + 1) * 256],
                             ALU.mult, ALU.mult, okeys + ["rstd4", "sz%d" % (hd // 2)], ["ygla"])
                if m == 0 and j == 0:
                    self.dump("ygla", ygla, ["ygla"], 384)
                    self.dump("v", v[:, 0, :], ["v0"], 512)
                    self.dump("kst", kst[:, 0, :], ["kst0"], 640, 0)
                    self.dump("attT", attT[:, 0, :], ["attT"], 640, 512)
                    self.dump("sz", sz, ["sz0", "sz1"], 768)
                pbk, pk = self.bank()
                pvw = pbk[:, :].bitcast(BF16)
                for mk in range(8):
                    self.tp(pvw[:, mk * 128:(mk + 1) * 128], ygla[:, mk * 128:(mk + 1) * 128], self.ident, ["ygla", "cst"], [pk])
                self.cp(yT[:, :, j * 128:(j + 1) * 128], pvw[:, 0:1024].rearrange("p (k t) -> p k t", t=128), [pk], ["yT"])
            self.outproj_residual(m, yT, 8, Wo, wk, hin, hin_key, hmid, hmid_key, hA, hbk="hA")

    def w_e2(self, L, slot):
        i = L // 2
        wk = ("w", slot)
        Win = self.load_w_rows(slot, 0, self.e_w_in[i], 0, 8, 3088, 3072, wk)
        Wo = self.load_w_rows(slot, 8 * 3072, self.e_w_out[i], 1024, 8, 0, 1024, wk)
        return (Win, Wo, wk)

    def pass_e2(self, L, W, hmid, hmid_key):
        i = L // 2
        A = self.A
        self.set_ring(range(2, 8))
        Win, Wo, wk = W
        NPE = 29
        oslot = wk[1] ^ 1
        dg = self.wa[oslot][:, 4096:4096 + 8 * NPE * 128].rearrange("p (c t n) -> p c t n", c=8, t=NPE)
        pk_ = "par_e2"
        cw = A.f32(8 * 31).rearrange("p (c k) -> p c k", k=31)
        cb = A.f32(8)
        lg = A.f32(8)
        lb = A.f32(8)
        self.dma(cw, self.e_cw[i], [], [pk_], pk_)
        self.dma(cb, self.e_cb[i], [], [pk_], pk_)
        self.dma(lg, self.e_lg[i], [], [pk_], pk_)
        self.dma(lb, self.e_lb[i], [], [pk_], pk_)
        for ct in range(8):
            self.tt(dg[:, ct, :, :], self.ident.unsqueeze(1).to_broadcast([P, NPE, 128]),
                    cw[:, ct, 0:NPE].unsqueeze(2).to_broadcast([P, NPE, 128]), ALU.mult, ["cst", pk_], [("dg", ct)],
                    eng=("dve" if ct % 2 == 0 else "pool"))
        ones32 = A.f32(128)
        self.memset(ones32, 1.0, ["ones32"])
        hnT2 = [A.bf16(8 * MT).rearrange("p (k t) -> p k t", t=MT) for _ in range(2)]
        u = [A.bf16(MT + 32) for _ in range(2)]
        halo = A.bf16(8 * 32).rearrange("p (c k) -> p c k", k=32)
        self.memset(halo, 0.0, [("halo", c) for c in range(8)])
        sg = [A.f32(MT) for _ in range(2)]
        acc = [A.f32(MT) for _ in range(2)]
        xc = A.f32(8 * MT).rearrange("p (c t) -> p c t", t=MT)
        sq = [A.f32(MT) for _ in range(2)]
        mean = A.f32(MT)
        rstd = A.f32(MT)
        nmr = A.f32(MT)
        tn = [A.f32(MT) for _ in range(2)]
        sl = [A.f32(MT) for _ in range(2)]
        szc = [A.f32(MT) for _ in range(2)]
        yT2 = [A.bf16(8 * MT).rearrange("p (k t) -> p k t", t=MT) for _ in range(2)]
        hB = [A.f32(1024) for _ in range(NS)]
        s1b, s1k = self.pb[0], ("pb", 0)
        s2b, s2k = self.pb[1], ("pb", 1)
        for m in range(NMT):
            hnT = hnT2[m % 2]
            hk = "hnT%d" % (m % 2)
            yT = yT2[m % 2]
            yk = "yT%d" % (m % 2)
            self.load_hnT(m, hnT, hk)
            for ct in range(8):
                b = ct % 2
                ub = u[b]
                uk = "u%d" % b
                pval, pvk = self.bank()
                for dk in range(8):
                    self.mm(pval[:, 0:MT], Win[:, dk, ct * 128:(ct + 1) * 128], hnT[:, dk, :], dk == 0, dk == 7, [hk, wk], [pvk])
                pg, pgk = self.bank()
                for dk in range(8):
                    self.mm(pg[:, 0:MT], Win[:, dk, 1024 + ct * 128:1024 + (ct + 1) * 128], hnT[:, dk, :], dk == 0, dk == 7, [hk, wk], [pgk])
                self.actf(sg[b], pg[:, 0:MT], AF.Sigmoid, [pgk], ["sg%d" % b])
                self.cp(ub[:, 0:30], halo[:, ct, 0:30], [("halo", ct)], [uk + "h"], eng="pool")
                self.tt(ub[:, 30:30 + MT], pval[:, 0:MT], sg[b], ALU.mult, [pvk, "sg%d" % b], [uk])
                self.cp(halo[:, ct, 0:30], ub[:, MT:MT + 30], [uk], [("halo", ct)], eng="pool")
                pc, pck = self.bank()
                for t in range(NPE):
                    self.mm(pc[:, 0:MT], dg[:, ct, t, :], ub[:, t:t + MT], t == 0, t == NPE - 1, [uk, uk + "h", ("dg", ct)], [pck])
                a_ = acc[b]
                ka = "acc%d" % b
                self.ts(a_, ub[:, NPE:NPE + MT], cw[:, ct, NPE:NPE + 1], ALU.mult, [uk, uk + "h", pk_], [ka])
                for t in range(NPE + 1, 31):
                    self.stt(a_, ub[:, t:t + MT], cw[:, ct, t:t + 1], a_, ALU.mult, ALU.add, [uk, uk + "h", ka], [ka])
                self.stt(xc[:, ct, :], pc[:, 0:MT], cb[:, ct:ct + 1], a_, ALU.add, ALU.add, [pck, ka, pk_], [("xc", ct)])
                self.actf(sq[b], xc[:, ct, :], AF.Square, [("xc", ct)], ["sq%d" % b])
                self.mm(s1b[:, 0:MT], ones32, xc[:, ct, :], ct == 0, ct == 7, [("xc", ct), "ones32"], [s1k])
                self.mm(s2b[:, 0:MT], ones32, sq[b], ct == 0, ct == 7, ["sq%d" % b, "ones32"], [s2k])
            self.ts(mean, s1b[:, 0:MT], 1.0 / 1024.0, ALU.mult, [s1k], ["mean"])
            self.tt(nmr, mean, mean, ALU.mult, ["mean"], ["nmr"])
            self.stt(rstd, s2b[:, 0:MT], 1.0 / 1024.0, nmr, ALU.mult, ALU.subtract, [s2k, "nmr"], ["rstd"])
            self.ts(rstd, rstd, EPS, ALU.add, ["rstd"], ["rstd"])
            self.actf(rstd, rstd, AF.Sqrt, ["rstd"], ["rstd"])
            self.S.add("dve", lambda e: e.reciprocal(out=rstd, in_=rstd), reads=["rstd"], writes=["rstd"])
            self.stt(nmr, mean, -1.0, rstd, ALU.mult, ALU.mult, ["mean", "rstd"], ["nmr"])
            for ct in range(8):
                b = ct % 2
                self.tt(tn[b], xc[:, ct, :], rstd, ALU.mult, [("xc", ct), "rstd"], ["tn%d" % b])
                self.tt(tn[b], tn[b], nmr, ALU.add, ["tn%d" % b, "nmr"], ["tn%d" % b])
                self.actf(sl[b], tn[b], AF.Silu, ["tn%d" % b, pk_], ["sl%d" % b], bias=lb[:, ct:ct + 1], scale=lg[:, ct:ct + 1])
                pz, pzk = self.bank()
                for dk in range(8):
                    self.mm(pz[:, 0:MT], Win[:, dk, 2048 + ct * 128:2048 + (ct + 1) * 128], hnT[:, dk, :], dk == 0, dk == 7, [hk, wk], [pzk])
                self.actf(szc[b], pz[:, 0:MT], AF.Silu, [pzk], ["szc%d" % b])
                self.tt(yT[:, ct, :], sl[b], szc[b], ALU.mult, ["sl%d" % b, "szc%d" % b], [yk])
            self.outproj_residual(m, yT, 8, Wo, wk, hmid, hmid_key, hmid, hmid_key, hB, yk=yk)

    def w_oa(self, L, slot):
        i = L // 2
        wk = ("w", slot)
        Wu = self.load_w_rows(slot, 0, self.o_w_in[i], 0, 8, 0, 512, wk)
        w = self.wa[slot]
        V = dict(Wu=Wu, wk=wk, slot=slot)
        V["Kblk"] = w[:, 4096:8192].rearrange("p (g n) -> p g n", n=128)
        V["Wa_re"] = w[:, 8192:10240].rearrange("p (g n) -> p g n", n=128)
        V["Wa_im"] = w[:, 10240:12288].rearrange("p (g n) -> p g n", n=128)
        V["CAre"] = w[:, 12288:14336].rearrange("p (g n) -> p g n", n=128)
        V["nCAim"] = w[:, 14336:16384].rearrange("p (g n) -> p g n", n=128)
        V["usm"] = w[:, 16384:32768].rearrange("p (c s n) -> p c s n", c=4, s=8)
        return V

    def cmul(self, o_re, o_im, a_re, a_im, b_re, b_im, tmp, reads, wkeys, neg_im=False):
        kre, kim, kt = wkeys
        self.tt(o_re, a_re, b_re, ALU.mult, reads, [kre])
        self.tt(tmp, a_im, b_im, ALU.mult, reads, [kt])
        self.tt(o_re, o_re, tmp, ALU.subtract, [kre, kt], [kre])
        self.tt(o_im, a_re, b_im, ALU.mult, reads, [kim])
        self.tt(tmp, a_im, b_re, ALU.mult, reads + [kre], [kt])
        if neg_im:
            self.stt(o_im, o_im, -1.0, tmp, ALU.mult, ALU.subtract, [kim, kt], [kim])
        else:
            self.tt(o_im, o_im, tmp, ALU.add, [kim, kt], [kim])

    def reduce_angle(self, x, t, key):
        TWO_PI = float(2.0 * np.pi)
        PI = float(np.pi)
        for _ in range(8):
            self.ts(t, x, PI, ALU.is_gt, [key], ["ra_t"], s2=TWO_PI, op1=ALU.mult)
            self.tt(x, x, t, ALU.subtract, [key, "ra_t"], [key])
        for _ in range(2):
            self.ts(t, x, -PI, ALU.is_lt, [key], ["ra_t"], s2=TWO_PI, op1=ALU.mult)
            self.tt(x, x, t, ALU.add, [key, "ra_t"], [key])

    def s5_prep(self, L, V):
        i = L // 2
        A = self.A
        pk_ = "par_s5"
        T = self.pst
        sm = lambda: A.f32(16)
        lr, li, ldt, dtt, mag, ang, sarg, carg, t1, are, aim, den, nr, cfr, cfi, u1, u2 = [sm() for _ in range(17)]
        self.dma(lr, self.o_lamre[i], [], [pk_], pk_)
        self.dma(li, self.o_lamim[i], [], [pk_], pk_)
        self.dma(ldt, self.o_logdt[i], [], [pk_], pk_)
        b3 = lambda: A.f32(256).rearrange("p (g h) -> p g h", h=16)
        bre, bim, cre, cim, Bre, Bim, tb = [b3() for _ in range(7)]
        self.dma(bre, self.o_bre[i].rearrange("p (g h) -> p g h", h=16), [], [pk_], pk_)
        self.dma(bim, self.o_bim[i].rearrange("p (g h) -> p g h", h=16), [], [pk_], pk_)
        self.dma(cre, self.o_cre[i].rearrange("p (g h) -> p g h", h=16), [], [pk_], pk_)
        self.dma(cim, self.o_cim[i].rearrange("p (g h) -> p g h", h=16), [], [pk_], pk_)
        dcol = A.f32(32)
        self.dma(dcol, self.o_dcol[i], [], [pk_], pk_)
        R = [pk_]
        self.actf(dtt, ldt, AF.Exp, R, ["dtt"])
        self.tt(u1, lr, dtt, ALU.mult, R + ["dtt"], ["u1"])
        self.actf(mag, u1, AF.Exp, ["u1"], ["mag"])
        self.tt(ang, li, dtt, ALU.mult, R + ["dtt"], ["ang"])
        self.cp(sarg, ang, ["ang"], ["sarg"], eng="dve")
        self.ts(carg, ang, float(np.pi / 2), ALU.add, ["ang"], ["carg"])
        self.reduce_angle(sarg, t1, "sarg")
        self.reduce_angle(carg, t1, "carg")
        self.actf(sarg, sarg, AF.Sin, ["sarg"], ["sarg"])
        self.actf(carg, carg, AF.Sin, ["carg"], ["carg"])
        self.tt(are, mag, carg, ALU.mult, ["mag", "carg"], ["are"])
        self.tt(aim, mag, sarg, ALU.mult, ["mag", "sarg"], ["aim"])
        self.tt(den, lr, lr, ALU.mult, R, ["den"])
        self.tt(u1, li, li, ALU.mult, R, ["u1"])
        self.tt(den, den, u1, ALU.add, ["den", "u1"], ["den"])
        self.S.add("dve", lambda e: e.reciprocal(out=den, in_=den), reads=["den"], writes=["den"])
        self.ts(nr, are, -1.0, ALU.add, ["are"], ["nr"])
        self.tt(cfr, nr, lr, ALU.mult, ["nr"] + R, ["cfr"])
        self.tt(u1, aim, li, ALU.mult, ["aim"] + R, ["u1"])
        self.tt(cfr, cfr, u1, ALU.add, ["cfr", "u1"], ["cfr"])
        self.tt(cfr, cfr, den, ALU.mult, ["cfr", "den"], ["cfr"])
        self.tt(cfi, aim, lr, ALU.mult, ["aim"] + R, ["cfi"])
        self.tt(u2, nr, li, ALU.mult, ["nr"] + R, ["u2"])
        self.tt(cfi, cfi, u2, ALU.subtract, ["cfi", "u2"], ["cfi"])
        self.tt(cfi, cfi, den, ALU.mult, ["cfi", "den"], ["cfi"])
        bc3 = lambda a: a.unsqueeze(2).to_broadcast([P, 16, 16])
        self.cmul(Bre, Bim, bc3(cfr), bc3(cfi), bre, bim, tb, ["cfr", "cfi"] + R, ["Bre", "Bim", "tb"])
        if PREP_CUT <= 1:
            return
        Pre = A.f32(144).rearrange("p (g j) -> p g j", j=9)
        Pim = A.f32(144).rearrange("p (g j) -> p g j", j=9)
        Qre = A.f32(128).rearrange("p (g j) -> p g j", j=8)
        Qim = A.f32(128).rearrange("p (g j) -> p g j", j=8)
        Vre = A.f32(128).rearrange("p (g j) -> p g j", j=8)
        Vim = A.f32(128).rearrange("p (g j) -> p g j", j=8)
        self.memset(Pre[:, :, 0], 1.0, ["Pre"], eng="dve")
        self.memset(Pim[:, :, 0], 0.0, ["Pim"], eng="dve")
        self.memset(Qre[:, :, 0], 1.0, ["Qre"], eng="dve")
        self.memset(Qim[:, :, 0], 0.0, ["Qim"], eng="dve")
        for j in range(1, 9):
            self.cmul(Pre[:, :, j], Pim[:, :, j], Pre[:, :, j - 1], Pim[:, :, j - 1], are, aim, u1, ["Pre", "Pim", "are", "aim"], ["Pre", "Pim", "u1"])
        ire, iim = sm(), sm()
        self.tt(u2, mag, mag, ALU.mult, ["mag"], ["u2"])
        self.S.add("dve", lambda e: e.reciprocal(out=u2, in_=u2), reads=["u2"], writes=["u2"])
        self.tt(ire, are, u2, ALU.mult, ["are", "u2"], ["ire"])
        self.stt(iim, aim, -1.0, u2, ALU.mult, ALU.mult, ["aim", "u2"], ["iim"])
        for j in range(1, 8):
            self.cmul(Qre[:, :, j], Qim[:, :, j], Qre[:, :, j - 1], Qim[:, :, j - 1], ire, iim, u1, ["Qre", "Qim", "ire", "iim"], ["Qre", "Qim", "u1"])
        for s_ in range(8):
            self.cp(Vre[:, :, s_], Pre[:, :, 7 - s_], ["Pre"], ["Vre"], eng="dve")
            self.cp(Vim[:, :, s_], Pim[:, :, 7 - s_], ["Pim"], ["Vim"], eng="dve")
        c8 = [T[:, k * 128:(k + 1) * 128].rearrange("p (g j) -> p g j", j=8) for k in range(3)]
        c64 = [T[:, 384 + k * 128:384 + (k + 1) * 128].rearrange("p (g j) -> p g j", j=8) for k in range(3)]
        c512 = [T[:, 768 + k * 16:768 + (k + 1) * 16] for k in range(3)]
        self.cp(c8[0][:, :, 0], Pre[:, :, 8], ["Pre"], ["c8"], eng="dve")
        self.cp(c8[1][:, :, 0], Pim[:, :, 8], ["Pim"], ["c8"], eng="dve")
        for j in range(1, 8):
            self.cmul(c8[0][:, :, j], c8[1][:, :, j], c8[0][:, :, j - 1], c8[1][:, :, j - 1], c8[0][:, :, 0], c8[1][:, :, 0], u1, ["c8"], ["c8", "c8", "u1"])
        self.cp(c64[0][:, :, 0], c8[0][:, :, 7], ["c8"], ["c64"], eng="dve")
        self.cp(c64[1][:, :, 0], c8[1][:, :, 7], ["c8"], ["c64"], eng="dve")
        for j in range(1, 8):
            self.cmul(c64[0][:, :, j], c64[1][:, :, j], c64[0][:, :, j - 1], c64[1][:, :, j - 1], c64[0][:, :, 0], c64[1][:, :, 0], u1, ["c64"], ["c64", "c64", "u1"])
        self.cp(c512[0], c64[0][:, :, 7], ["c64"], ["c512"], eng="dve")
        self.cp(c512[1], c64[1][:, :, 7], ["c64"], ["c512"], eng="dve")
        self.ts(c8[2], c8[1], -1.0, ALU.mult, ["c8"], ["c8n"])
        self.ts(c64[2], c64[1], -1.0, ALU.mult, ["c64"], ["c64n"])
        self.ts(c512[2], c512[1], -1.0, ALU.mult, ["c512"], ["c512n"])
        if PREP_CUT <= 2:
            return
        big = lambda: A.f32(2048).rearrange("p (g s h) -> p g s h", s=8, h=16)
        Lre, Lim, Rre, nRim, tbig = [big() for _ in range(5)]
        bp = lambda a: a.unsqueeze(3).to_broadcast([P, 16, 8, 16])
        bb = lambda a: a.unsqueeze(2).to_broadcast([P, 16, 8, 16])
        self.cmul(Lre, Lim, bp(Qre), bp(Qim), bb(Bre), bb(Bim), tbig, ["Qre", "Qim", "Bre", "Bim"], ["Lre", "Lim", "tbig"])
        self.cmul(Rre, nRim, bp(Pre[:, :, 0:8]), bp(Pim[:, :, 0:8]), bb(cre), bb(cim), tbig, ["Pre", "Pim"] + R, ["Rre", "nRim", "tbig"], neg_im=True)
        if PREP_CUT <= 3:
            return
        ident32 = A.f32(128)
        ones32 = A.f32(128)
        maskST = A.f32(128)
        tK = [A.f32(128) for _ in range(2)]
        self.memset(ones32, 1.0, ["ones32"])
        self.S.add("pool", lambda e: e.affine_select(out=ident32, in_=ones32, pattern=[[-1, 128]], compare_op=ALU.is_equal,
                                                    fill=0.0, base=0, channel_multiplier=1), reads=["ones32"], writes=["ident32"])
        self.S.add("pool", lambda e: e.affine_select(out=maskST.rearrange("p (t h) -> p t h", h=16), in_=ones32.rearrange("p (t h) -> p t h", h=16),
                                                    pattern=[[16, 8], [0, 16]], compare_op=ALU.is_ge, fill=0.0, base=15, channel_multiplier=-1),
                   reads=["ones32"], writes=["maskST"])
        bigb = lambda: A.bf16(2048).rearrange("p (g n) -> p g n", n=128)
        Lre_b, Lim_b, Rre_b, nRim_b = [bigb() for _ in range(4)]
        f3 = lambda a: a.rearrange("p g s h -> p g (s h)")
        self.cp(Lre_b, f3(Lre), ["Lre"], ["Lre_b"], eng="dve")
        self.cp(Lim_b, f3(Lim), ["Lim"], ["Lim_b"], eng="act")
        self.cp(Rre_b, f3(Rre), ["Rre"], ["Rre_b"], eng="dve")
        self.cp(nRim_b, f3(nRim), ["nRim"], ["nRim_b"], eng="act")
        Kblk = V["Kblk"]
        if PREP_CUT <= 3.2:
            return
        for g0 in range(0, 32, 8):
            bks = [self.bank(), self.bank()]
            for q in range(8):
                g = g0 + q
                gp, g2 = g // 2, g % 2
                rs = slice(64 * g2, 64 * g2 + 64)
                pbk, pk = bks[g2]
                c0 = (q // 2) * 128
                self.mm(pbk[:, c0:c0 + 128], Lre_b[rs, gp, :], Rre_b[rs, gp, :], True, False, ["Lre_b", "Rre_b"], [pk])
                self.mm(pbk[:, c0:c0 + 128], Lim_b[rs, gp, :], nRim_b[rs, gp, :], False, True, ["Lim_b", "nRim_b"], [pk])
            for q in range(8):
                g = g0 + q
                g2 = g % 2
                pbk, pk = bks[g2]
                c0 = (q // 2) * 128
                tk = tK[q % 2]
                self.tt(tk, pbk[:, c0:c0 + 128], maskST, ALU.mult, [pk, "maskST"], ["tK%d" % (q % 2)])
                self.stt(Kblk[:, g, :], ident32, dcol[:, g:g + 1], tk, ALU.mult, ALU.add, ["ident32", "tK%d" % (q % 2)] + R, ["Kblk"])
        if PREP_CUT <= 4:
            return
        Wre, Wim = Rre, nRim
        self.cmul(Wre, Wim, bp(Vre), bp(Vim), bb(Bre), bb(Bim), tbig, ["Vre", "Vim", "Bre", "Bim"], ["Rre", "nRim", "tbig"])
        self.cp(Rre_b, f3(Wre), ["Rre"], ["Rre_b"], eng="dve")
        self.cp(nRim_b, f3(Wim), ["nRim"], ["nRim_b"], eng="act")
        for src, dst, sk in ((Rre_b, V["Wa_re"], "Rre_b"), (nRim_b, V["Wa_im"], "nRim_b")):
            for g0 in range(0, 16, 4):
                pbk, pk = self.bank()
                pvw = pbk[:, :].bitcast(BF16)
                for q in range(4):
                    self.tp(pvw[:, q * 128:(q + 1) * 128], src[:, g0 + q, :], self.ident, [sk, "cst"], [pk])
                self.cp(dst[:, g0:g0 + 4, :], pvw[:, 0:512].rearrange("p (g n) -> p g n", n=128), [pk], ["Wa"])
        if PREP_CUT <= 5:
            return
        Cre32, Cim32 = Lre, Lim
        self.cmul(Cre32, Cim32, bp(Pre[:, :, 1:9]), bp(Pim[:, :, 1:9]), bb(cre), bb(cim), tbig, ["Pre", "Pim"] + R, ["Lre", "Lim", "tbig"], neg_im=True)
        self.cp(V["CAre"], Cre32.rearrange("p g s h -> p g (s h)"), ["Lre"], ["CA"], eng="dve")
        self.cp(V["nCAim"], Cim32.rearrange("p g s h -> p g (s h)"), ["Lim"], ["CA"], eng="dve")
        V["c8"], V["c64"], V["c512"] = c8, c64, c512

    def s5_core(self, V):
        A = self.A
        usm = V["usm"]
        c8, c64, c512 = V["c8"], V["c64"], V["c512"]
        Zs = A.bf16(8 * 240).rearrange("p (g x) -> p g x", x=240)
        onesb = A.bf16(128)
        self.memset(onesb, 1.0, ["onesb"])
        self.memset(Zs, 0.0, ["Zs"])
        self.S.add("pool", lambda e: e.affine_select(out=Zs[:, :, 112:128], in_=onesb.rearrange("p (a b) -> p a b", b=16),
                                                    pattern=[[-16, 8], [-1, 16]], compare_op=ALU.is_equal, fill=0.0, base=0, channel_multiplier=1),
                   reads=["onesb", "Zs"], writes=["Zs"])
        U8 = A.bf16(8 * 512).rearrange("p (g c) -> p g c", c=512)
        X = [[A.f32(512) for _ in range(2)] for _ in range(4)]
        Sp = [[A.bf16(512) for _ in range(2)] for _ in range(4)]
        for pp in range(4):
            for r in range(2):
                self.memset(Sp[pp][r][:, 0:1], 0.0, [("Sp", pp, r)])
        for ct in range(4):
            uk = [("usm", ct, s_) for s_ in range(8)]
            for gq in range(8):
                pbk, pk = self.bank()
                for s_ in range(8):
                    self.mm(pbk[:, :], Zs[:, gq, 112 - 16 * s_:240 - 16 * s_], usm[:, ct, s_, :], s_ == 0, s_ == 7, ["Zs", uk[s_]], [pk])
                self.cp(U8[:, gq, :], pbk[:, :], [pk], [("U8", gq)], eng=("act" if gq % 2 == 0 else "dve"))
            for pp in range(4):
                gp = 4 * ct + pp
                for r, Wn in ((0, "Wa_re"), (1, "Wa_im")):
                    pbk, pk = self.bank()
                    self.mm(pbk[0:64, :], V[Wn][:, gp, 0:64], U8[:, 2 * pp, :], True, True, ["Wa", ("U8", 2 * pp)], [pk])
                    self.mm(pbk[64:128, :], V[Wn][:, gp, 64:128], U8[:, 2 * pp + 1, :], True, True, ["Wa", ("U8", 2 * pp + 1)], [pk])
                    self.cp(X[pp][r], pbk[:, :], [pk], [("X", pp, r)])
            steps = []
            for pp in range(4):
                gp = 4 * ct + pp
                xr, xi = X[pp]
                kr, ki = ("X", pp, 0), ("X", pp, 1)
                lst = []

                def cmac(o_r, o_i, s_r, s_i, cr, ci, cni, lst=lst, kr=kr, ki=ki):
                    lst.append((o_r, s_r, cr, o_r, [kr, "c8", "c64", "c512"], [kr]))
                    lst.append((o_r, s_i, cni, o_r, [kr, ki, "c8n", "c64n", "c512n"], [kr]))
                    lst.append((o_i, s_i, cr, o_i, [ki, "c8", "c64", "c512"], [ki]))
                    lst.append((o_i, s_r, ci, o_i, [ki, kr, "c8", "c64", "c512"], [ki]))
                v3 = lambda a: a.rearrange("p (m j) -> p m j", j=8)
                vz = lambda a: a.rearrange("p (q j r) -> p q j r", j=8, r=8)[:, :, :, 7]
                vw = lambda a: a.rearrange("p (q r) -> p q r", r=64)[:, :, 63]
                co = lambda c, j: (c[0][:, gp, j:j + 1], c[1][:, gp, j:j + 1], c[2][:, gp, j:j + 1])
                for j in range(1, 8):
                    cmac(v3(xr)[:, :, j], v3(xi)[:, :, j], v3(xr)[:, :, j - 1], v3(xi)[:, :, j - 1], *co(c8, 0))
                for j in range(1, 8):
                    cmac(vz(xr)[:, :, j], vz(xi)[:, :, j], vz(xr)[:, :, j - 1], vz(xi)[:, :, j - 1], *co(c64, 0))
                c5 = (c512[0][:, gp:gp + 1], c512[1][:, gp:gp + 1], c512[2][:, gp:gp + 1])
                for q in range(1, 8):
                    cmac(vw(xr)[:, q:q + 1], vw(xi)[:, q:q + 1], vw(xr)[:, q - 1:q], vw(xi)[:, q - 1:q], *c5)
                for j in range(0, 7):
                    cmac(vz(xr)[:, 1:8, j], vz(xi)[:, 1:8, j], vw(xr)[:, 0:7], vw(xi)[:, 0:7], *co(c64, j))
                for j in range(0, 7):
                    cmac(v3(xr)[:, 1:64, j], v3(xi)[:, 1:64, j], v3(xr)[:, 0:63, 7], v3(xi)[:, 0:63, 7], *co(c8, j))
                steps.append(lst)
            for k in range(len(steps[0])):
                for pp in range(4):
                    o_, s_in, c_, a_, rd, wr = steps[pp][k]
                    self.stt(o_, s_in, c_, a_, ALU.mult, ALU.add, rd, wr)
            for pp in range(4):
                for r in range(2):
                    self.cp(Sp[pp][r][:, 1:512], X[pp][r][:, 0:511], [("X", pp, r)], [("Sp", pp, r)], eng=("act" if r == 0 else "pool"))
            for gq in range(8):
                g = 8 * ct + gq
                gp, g2 = g // 2, g % 2
                pp = gq // 2
                rs = slice(64 * g2, 64 * g2 + 64)
                pbk, pk = self.bank()
                self.mm(pbk[:, :], V["Kblk"][:, g, :], U8[:, gq, :], True, False, ["Kblk", ("U8", gq)], [pk])
                self.mm(pbk[:, :], V["CAre"][rs, gp, :], Sp[pp][0][rs, :], False, False, ["CA", ("Sp", pp, 0)], [pk])
                self.mm(pbk[:, :], V["nCAim"][rs, gp, :], Sp[pp][1][rs, :], False, True, ["CA", ("Sp", pp, 1)], [pk])
                self.cp(U8[:, gq, :], pbk[:, :], [pk], [("U8", gq)], eng=("act" if gq % 2 == 0 else "dve"))
            for t_ in range(8):
                pbk, pk = self.bank()
                for gq in range(8):
                    self.mm(pbk[:, :], Zs[:, t_, 112 - 16 * gq:240 - 16 * gq], U8[:, gq, :], gq == 0, gq == 7, ["Zs", ("U8", gq)], [pk])
                self.cp(usm[:, ct, t_, :], pbk[:, :], [pk], [("usm", ct, t_)], eng=("act" if t_ % 2 == 0 else "dve"))

    def pass_oa(self, L, V, hin, hin_key):
        A = self.A
        self.set_ring(range(8))
        self.s5_prep(L, V)
        if OA_STAGE < 2:
            return
        self.S.barrier()
        A.reset()
        Wu, wk, usm = V["Wu"], V["wk"], V["usm"]
        gn = A.f32(1024)
        pk_ = "par_oa"
        self.dma(gn, self.norm_g[L:L + 1, :].partition_broadcast(128), [], [pk_], pk_)
        hA = [A.f32(1024) for _ in range(NS)]
        hnb = A.bf16(1024)
        junk = A.bf16(1024)
        ss = A.f32(8)
        hnT2 = [A.bf16(8 * MT).rearrange("p (k t) -> p k t", t=MT) for _ in range(2)]
        nb = dict(hA=hA, hnb=hnb, gn=gn, hnT=None, ss=ss, gkey=pk_, junk=junk)
        for m in range(NMT):
            hnT = hnT2[m % 2]
            hk = "hnT%d" % (m % 2)
            nb["hnT"] = hnT
            nb["hk"] = hk
            self.norm_tile(m, hin, hin_key, nb, store_scr=True)
            for ct in range(4):
                pbk, pk = self.bank()
                for dk in range(8):
                    self.mm(pbk[:, 0:MT], Wu[:, dk, ct * 128:(ct + 1) * 128], hnT[:, dk, :], dk == 0, dk == 7, [hk, wk], [pk])
                self.cp(usm[:, ct, :, m * 32:(m + 1) * 32], pbk[:, 0:MT].rearrange("p (c s) -> p s c", s=8), [pk],
                        [("usm", ct, s_) for s_ in range(8)], eng=("act" if ct % 2 == 0 else "dve"))
        if OA_STAGE < 3:
            return
        self.S.barrier()
        A.reset()
        self.s5_core(V)

    def w_ob1(self, L, slot):
        i = L // 2
        wk = ("w", slot)
        Win = self.load_w_rows(slot, 0, self.o_w_in[i], 0, 8, 1024, 3072, wk)
        Wo = self.load_w_rows(slot, 24576, self.o_w_out[i], 512, 8, 0, 1024, wk)
        wsT = self.wa[slot][:, 32768:33792].rearrange("p (h t) -> p h t", t=128)
        self.dma(wsT, self.o_wsT[i].rearrange("p (h t) -> p h t", t=128), [], [wk], wk, eng="pool")
        return (Win, Wo, wsT, wk)

    def pass_ob1(self, L, W, hin, hin_key, hmid, hmid_key):
        i = L // 2
        A = self.A
        self.set_ring(range(8))
        Win, Wo, wsT, wk = W
        self.S.add("pool", lambda e: e.affine_select(out=wsT, in_=wsT, pattern=[[0, 8], [1, 128]], compare_op=ALU.is_ge, fill=0.0,
                                                    base=0, channel_multiplier=-1), reads=[wk], writes=["wsTm"])
        pk_ = "par_ob1"
        lng = A.f32(1024)
        lnb = A.f32(1024)
        bsb_f = A.f32(1024)
        bsb = bsb_f.rearrange("p (h t) -> p h t", t=128)
        self.dma(lng, self.o_lng[i].partition_broadcast(128), [], [pk_], pk_)
        self.dma(lnb, self.o_lnb[i].partition_broadcast(128), [], [pk_], pk_)
        self.dma(bsb_f, self.o_bs[i].partition_broadcast(128), [], [pk_], pk_)
        hnT2 = [A.bf16(8 * MT).rearrange("p (k t) -> p k t", t=MT) for _ in range(2)]
        vtmp = A.f32(1024)
        vnT2 = [A.bf16(NS * 1024).rearrange("p (j n) -> p j n", n=1024) for _ in range(2)]
        st4 = A.f32(16)
        junk = A.bf16(512)
        szt = [A.f32(MT) for _ in range(2)]
        t1 = [A.f32(MT) for _ in range(2)]
        yT2 = [A.bf16(8 * MT).rearrange("p (k t) -> p k t", t=MT) for _ in range(2)]
        hB = [A.f32(1024) for _ in range(NS)]
        for m in range(NMT):
            hnT = hnT2[m % 2]
            hk = "hnT%d" % (m % 2)
            yT = yT2[m % 2]
            yk = "yT%d" % (m % 2)
            vnT = vnT2[m % 2]
            vq = "vnT%d_" % (m % 2)
            self.load_hnT(m, hnT, hk)
            for j in range(NS):
                pbs = []
                for n2 in range(2):
                    pv_, pvk = self.bank()
                    for dk in range(8):
                        self.mm(pv_[:, :], hnT[:, dk, j * 128:(j + 1) * 128], Win[:, dk, 1024 + n2 * 512:1024 + (n2 + 1) * 512],
                                dk == 0, dk == 7, [hk, wk], [pvk])
                    self.actf(junk, pv_[:, :], AF.Identity, [pvk], ["junk", "st_s%d" % n2], accum=st4[:, n2:n2 + 1])
                    self.actf(junk, pv_[:, :], AF.Square, [pvk], ["junk", "st_q%d" % n2], accum=st4[:, 2 + n2:3 + n2])
                    pbs.append((pv_, pvk))
                self.tt(st4[:, 4:5], st4[:, 0:1], st4[:, 1:2], ALU.add, ["st_s0", "st_s1"], ["st_m"])
                self.ts(st4[:, 4:5], st4[:, 4:5], 1.0 / 1024.0, ALU.mult, ["st_m"], ["st_m"])
                self.tt(st4[:, 5:6], st4[:, 2:3], st4[:, 3:4], ALU.add, ["st_q0", "st_q1"], ["st_v"])
                self.tt(st4[:, 6:7], st4[:, 4:5], st4[:, 4:5], ALU.mult, ["st_m"], ["st_mm"])
                self.stt(st4[:, 5:6], st4[:, 5:6], 1.0 / 1024.0, st4[:, 6:7], ALU.mult, ALU.subtract, ["st_v", "st_mm"], ["st_v"])
                self.rsqrt_small(st4[:, 7:8], st4[:, 5:6], 1.0, ["st_v"], ["st_r"], "st_rt")
                self.stt(st4[:, 8:9], st4[:, 4:5], -1.0, st4[:, 7:8], ALU.mult, ALU.mult, ["st_m", "st_r"], ["st_n"])
                for n2 in range(2):
                    pv_, pvk = pbs[n2]
                    sl_ = slice(n2 * 512, (n2 + 1) * 512)
                    self.ts(vtmp[:, sl_], pv_[:, :], st4[:, 7:8], ALU.mult, [pvk, "st_r", "st_n"], ["vtmp%d" % n2], s2=st4[:, 8:9], op1=ALU.add)
                    self.tt(vtmp[:, sl_], vtmp[:, sl_], lng[:, sl_], ALU.mult, ["vtmp%d" % n2, pk_], ["vtmp%d" % n2], eng="pool")
                    self.tt(vnT[:, j, sl_], vtmp[:, sl_], lnb[:, sl_], ALU.add, ["vtmp%d" % n2, pk_], [vq + str(j)], eng="pool")
            for hd in range(8):
                b = hd % 2
                psv, psvk = self.bank()
                for j in range(NS):
                    self.mm(psv[:, j * 128:(j + 1) * 128], vnT[:, j, hd * 128:(hd + 1) * 128], wsT[:, hd, :], True, True,
                            [vq + str(j), "wsTm"], [psvk])
                pu, puk = self.bank()
                for dk in range(8):
                    self.mm(pu[:, 0:MT], Win[:, dk, hd * 128:(hd + 1) * 128], hnT[:, dk, :], dk == 0, dk == 7, [hk, wk], [puk])
                pz, pzk = self.bank()
                for dk in range(8):
                    self.mm(pz[:, 0:MT], Win[:, dk, 2048 + hd * 128:2048 + (hd + 1) * 128], hnT[:, dk, :], dk == 0, dk == 7, [hk, wk], [pzk])
                self.actf(szt[b], pz[:, 0:MT], AF.Silu, [pzk], ["szt%d" % b])
                self.tt(t1[b].rearrange("p (j t) -> p j t", t=128), psv[:, 0:MT].rearrange("p (j t) -> p j t", t=128),
                        bsb[:, hd, :].unsqueeze(1).to_broadcast([P, NS, 128]), ALU.add, [psvk, pk_], ["t1_%d" % b])
                self.tt(t1[b], pu[:, 0:MT], t1[b], ALU.mult, [puk, "t1_%d" % b], ["t1_%d" % b])
                self.tt(yT[:, hd, :], t1[b], szt[b], ALU.mult, ["t1_%d" % b, "szt%d" % b], [yk])
            self.outproj_residual(m, yT, 8, Wo, wk, hin, hin_key, hmid, hmid_key, hB, yk=yk)

    def w_ob2(self, L, slot):
        i = L // 2
        wk = ("w", slot)
        Wz = self.load_w_rows(slot, 0, self.o_w_in[i], 0, 8, 512, 512, wk)
        Wg = self.load_w_rows(slot, 4096, self.o_w_glu[i], 0, 4, 0, 512, wk)
        Wo = self.load_w_rows(slot, 6144, self.o_w_out[i], 0, 4, 0, 1024, wk)
        usm = self.wa[slot][:, 16384:32768].rearrange("p (c s n) -> p c s n", c=4, s=8)
        return (Wz, Wg, Wo, usm, wk)

    def pass_ob2(self, L, W, hmid, hmid_key, last):
        i = L // 2
        A = self.A
        self.set_ring(range(8))
        Wz, Wg, Wo, usm, wk = W
        pk_ = "par_ob2"
        bglu = A.f32(4)
        self.dma(bglu, self.o_bglu[i], [], [pk_], pk_)
        gfin = None
        if last:
            gfin = A.f32(1024)
            self.dma(gfin, self.final_g.partition_broadcast(128), [], ["gfin"], "par_gfin")
        hnT2 = [A.bf16(8 * MT).rearrange("p (k t) -> p k t", t=MT) for _ in range(2)]
        ge322 = [A.f32(4 * MT).rearrange("p (c t) -> p c t", t=MT) for _ in range(2)]
        gebf2 = [A.bf16(4 * MT).rearrange("p (c t) -> p c t", t=MT) for _ in range(2)]
        sgm = [A.f32(MT) for _ in range(2)]
        szt = [A.f32(MT) for _ in range(2)]
        yT2 = [A.bf16(4 * MT).rearrange("p (k t) -> p k t", t=MT) for _ in range(2)]
        hB = [A.f32(1024) for _ in range(NS)]
        junk = A.bf16(1024)
        ss = A.f32(8)
        for m in range(NMT):
            par = m % 2
            hnT = hnT2[par]
            hk = "hnT%d" % par
            yT = yT2[par]
            yk = "yT%d" % par
            ge32 = ge322[par]
            gebf = gebf2[par]
            self.load_hnT(m, hnT, hk)
            for ct in range(4):
                self.actf(ge32[:, ct, :].rearrange("p (c s) -> p s c", s=8), usm[:, ct, :, m * 32:(m + 1) * 32], AF.Gelu_apprx_tanh,
                          [("usm", ct, s_) for s_ in range(8)], [("ge32", par, ct)])
                self.cp(gebf[:, ct, :], ge32[:, ct, :], [("ge32", par, ct)], [("gebf", par, ct)], eng="dve")
            for mt in range(4):
                b = mt % 2
                pg, pgk = self.bank()
                for kt in range(4):
                    self.mm(pg[:, 0:MT], Wg[:, kt, mt * 128:(mt + 1) * 128], gebf[:, kt, :], kt == 0, kt == 3, [("gebf", par, kt), wk], [pgk])
                self.actf(sgm[b], pg[:, 0:MT], AF.Sigmoid, [pgk, pk_], ["sgm%d" % b], bias=bglu[:, mt:mt + 1])
                pz, pzk = self.bank()
                for dk in range(8):
                    self.mm(pz[:, 0:MT], Wz[:, dk, mt * 128:(mt + 1) * 128], hnT[:, dk, :], dk == 0, dk == 7, [hk, wk], [pzk])
                self.actf(szt[b], pz[:, 0:MT], AF.Silu, [pzk], ["szt%d" % b])
                self.tt(sgm[b], sgm[b], ge32[:, mt, :], ALU.mult, ["sgm%d" % b, ("ge32", par, mt)], ["sgm%d" % b])
                self.tt(yT[:, mt, :], sgm[b], szt[b], ALU.mult, ["sgm%d" % b, "szt%d" % b], [yk])
            if last:
                self.outproj_residual(m, yT, 4, Wo, wk, hmid, hmid_key, self.out, "out", hB, final_g=gfin, ss=ss, junk=junk, yk=yk)
            else:
                self.outproj_residual(m, yT, 4, Wo, wk, hmid, hmid_key, hmid, hmid_key, hB, yk=yk)

    def build(self):
        passes = []
        hin, hin_key = self.x, "x"
        for L in range(self.n_layers):
            hmid = self.hbuf[(L + 1) % 2]
            hmid_key = ("hb", (L + 1) % 2)
            if getattr(self, "first_pass", 0) > 0:
                hin, hin_key = self.x, "x"
            if L % 2 == 0:
                passes.append((lambda slot, L=L: self.w_e1(L, slot),
                               lambda W, L=L, a=hin, ak=hin_key, b=hmid, bk=hmid_key: self.pass_e1(L, W, a, ak, b, bk)))
                passes.append((lambda slot, L=L: self.w_e2(L, slot),
                               lambda W, L=L, b=hmid, bk=hmid_key: self.pass_e2(L, W, b, bk)))
            else:
                last = (L == self.n_layers - 1) and self.final_norm
                passes.append((lambda slot, L=L: self.w_oa(L, slot),
                               lambda W, L=L, a=hin, ak=hin_key: self.pass_oa(L, W, a, ak)))
                passes.append((lambda slot, L=L: self.w_ob1(L, slot),
                               lambda W, L=L, a=hin, ak=hin_key, b=hmid, bk=hmid_key: self.pass_ob1(L, W, a, ak, b, bk)))
                passes.append((lambda slot, L=L: self.w_ob2(L, slot),
                               lambda W, L=L, b=hmid, bk=hmid_key, last=last: self.pass_ob2(L, W, b, bk, last)))
                self.fused_out = last
            hin, hin_key = hmid, hmid_key
        if self.max_passes is not None:
            passes = passes[getattr(self, 'first_pass', 0):self.max_passes]
        slot = 0
        Wn = passes[0][0](slot)
        for k, (wl, run) in enumerate(passes):
            self.S.barrier()
            self.A.reset()
            Wcur = Wn
            slot ^= 1
            if k + 1 < len(passes):
                Wn = passes[k + 1][0](slot)
            run(Wcur)
        if getattr(self, "fused_out", False) and self.max_passes is None:
            self.S.emit()
            return
        if getattr(self, "first_pass", 0) > 0 and self.max_passes is not None and self.max_passes <= 3:
            hin, hin_key = self.x, "x"
        self.S.barrier()
        A = self.A
        A.reset()
        cpb = [A.f32(1024) for _ in range(2)]
        for t in range(0 if not self.debug else SEQ // 128, SEQ // 128):
            b = t % 2
            self.dma(cpb[b], hin[t * 128:(t + 1) * 128, :], [(hin_key, t // NS, t % NS)], ["cpb%d" % b], "ld_cp%d" % b)
            self.dma(self.out[t * 128:(t + 1) * 128, :], cpb[b], ["cpb%d" % b], [("out", t)], "st_cp%d" % b)
        self.S.emit()


def build_program(n_layers=4, final_norm=True, max_passes=None, first_pass=0):
    nc = bass.Bass("TRN2", target_bir_lowering=False)
    st = ExitStack()
    with st:
        b = Builder(nc, st, n_layers=n_layers, final_norm=final_norm, max_passes=max_passes)
        b.first_pass = first_pass
        b.build()
    return nc


def host_layout(inputs):
    f = lambda a: np.ascontiguousarray(np.asarray(a, dtype=np.float32))
    g = {}
    g["norm_g"] = f(inputs["norm_g"])
    g["final_g"] = f(inputs["final_g"]).reshape(1, D)
    g["e_w_in"] = f(inputs["e_w_in"])
    g["e_w_a2"] = f(inputs["e_w_a2"])
    g["e_b_a"] = f(inputs["e_b_a"]).reshape(2, 1, 512)
    g["e_gla_g"] = f(inputs["e_gla_g"]).reshape(2, 1, 1024)
    cw = f(inputs["e_conv_w"])
    g["e_cw"] = f(cw.reshape(2, 31, 8, 128).transpose(0, 3, 2, 1))
    cpl = lambda a: f(f(a).reshape(2, 8, 128).transpose(0, 2, 1))
    g["e_cb"] = cpl(inputs["e_conv_b"])
    g["e_lg"] = cpl(inputs["e_cln_g"])
    g["e_lb"] = cpl(inputs["e_cln_b"])
    g["e_w_out"] = f(inputs["e_w_out"])
    g["o_w_in"] = f(inputs["o_w_in"])
    gp_l = lambda a: f(f(a).reshape(2, 16, 2, 64).transpose(0, 2, 3, 1).reshape(2, 128, 16))
    g["o_lamre"] = gp_l(inputs["o_lam_re"])
    g["o_lamim"] = gp_l(inputs["o_lam_im"])
    ldt = f(inputs["o_log_dt"]).reshape(2, 16, 2)
    g["o_logdt"] = f(np.broadcast_to(ldt.transpose(0, 2, 1)[:, :, None, :], (2, 2, 64, 16)).reshape(2, 128, 16))
    b_l = lambda a: f(f(a).reshape(2, 16, 2, 64, 16).transpose(0, 2, 3, 1, 4).reshape(2, 128, 256))
    g["o_bre"] = b_l(inputs["o_b_re"])
    g["o_bim"] = b_l(inputs["o_b_im"])
    c_l = lambda a: f(f(a).reshape(2, 16, 2, 16, 64).transpose(0, 2, 4, 1, 3).reshape(2, 128, 256))
    g["o_cre"] = c_l(inputs["o_c_re"])
    g["o_cim"] = c_l(inputs["o_c_im"])
    dd = f(inputs["o_d"]).reshape(2, 32, 16)
    g["o_dcol"] = f(np.broadcast_to(dd.transpose(0, 2, 1)[:, None, :, :], (2, 8, 16, 32)).reshape(2, 128, 32))
    g["o_w_glu"] = f(inputs["o_w_glu"])
    g["o_bglu"] = f(f(inputs["o_b_glu"]).reshape(2, 4, 128).transpose(0, 2, 1))
    g["o_lng"] = f(inputs["o_sg_ln_g"]).reshape(2, 1, 1024)
    g["o_lnb"] = f(inputs["o_sg_ln_b"]).reshape(2, 1, 1024)
    ws = f(inputs["o_w_s"])
    g["o_wsT"] = f(ws.transpose(0, 3, 1, 2).reshape(2, 128, 1024))
    g["o_bs"] = f(inputs["o_b_s"]).reshape(2, 1, 1024)
    g["o_w_out"] = f(inputs["o_w_out"])
    return g


def kernel(**inputs):
    x = np.asarray(inputs["x"], dtype=np.float32)
    shared = host_layout(inputs)
    nc = build_program()
    in_maps = []
    for c in range(8):
        m = dict(shared)
        m["x"] = np.ascontiguousarray(x[c])
        in_maps.append(m)
    res = run_bass_kernel_spmd(nc, in_maps, core_ids=list(range(8)))
    return np.stack([np.asarray(r["out"], dtype=np.float32) for r in res.results], axis=0)
```

```python
import numpy as np
import concourse.bass as bass
import concourse.mybir as mybir
from concourse.bass_utils import run_bass_kernel_spmd
from contextlib import ExitStack

F32 = mybir.dt.float32
BF16 = mybir.dt.bfloat16
AF = mybir.ActivationFunctionType
ALU = mybir.AluOpType

P = 128
SEQ = 4096
D = 1024
MT = 256
NMT = SEQ // MT
NS = MT // 128
EPS = 1e-6
EVEN_IN = 6160
ODD_IN = 4096
WA_N = 33792
OA_STAGE = 3
PREP_CUT = 99


class _Op:
    __slots__ = ("eng", "fn", "deps", "edges", "signal", "dma_sem", "dma_val", "cnt", "cost", "seg", "idx", "fin", "placed")

    def __init__(self, eng, fn):
        self.eng = eng
        self.fn = fn
        self.deps = []
        self.edges = []
        self.signal = False
        self.dma_sem = None
        self.dma_val = 0
        self.cnt = 0
        self.cost = 200.0
        self.seg = 0
        self.idx = 0
        self.fin = 0.0
        self.placed = False


class Sched:
    ENGS = ("pe", "act", "dve", "pool", "sp")
    WINDOW = {"pe": 256, "act": 64, "dve": 64, "pool": 32, "sp": 32}

    def __init__(self, nc, stack):
        self.nc = nc
        self.stack = stack
        self.all_ops = []
        self.last_w = {}
        self.readers = {}
        self.dma_keys = {}
        self.last_dma = {}
        self.eng_sem = {}
        for e in ("pe", "act", "dve", "pool"):
            self.eng_sem[e] = stack.enter_context(nc.semaphore("es_" + e))
        self.seg = 0
        self.reorder = True

    def barrier(self):
        self.seg += 1

    def add(self, eng, fn, reads=(), writes=(), dma_key=None, cost=None):
        op = _Op(eng, fn)
        op.seg = self.seg
        op.idx = len(self.all_ops)
        is_dma = dma_key is not None
        if cost is not None:
            op.cost = float(cost)
        my_sem = None
        if is_dma:
            ent = self.dma_keys.get(dma_key)
            if ent is None:
                sem = self.stack.enter_context(self.nc.semaphore("ds_%d" % len(self.dma_keys)))
                ent = [sem, 0]
                self.dma_keys[dma_key] = ent
            my_sem = ent[0]
            prev = self.last_dma.get(dma_key)
            if prev is not None:
                op.edges.append(prev)
        cand = []
        for k in reads:
            w = self.last_w.get(k)
            if w is not None:
                cand.append(w)
        for k in writes:
            w = self.last_w.get(k)
            if w is not None:
                cand.append(w)
            for r in self.readers.get(k, ()):
                cand.append(r)
        seen = set(id(x) for x in op.edges)
        for d in cand:
            if id(d) in seen or d is op:
                continue
            seen.add(id(d))
            op.edges.append(d)
            d_is_dma = d.dma_sem is not None
            if d_is_dma:
                if is_dma and d.dma_sem is my_sem:
                    continue
            elif d.eng == eng and not is_dma and eng == "pe":
                continue
            op.deps.append(d)
        if is_dma:
            ent[1] += 16
            op.dma_sem = ent[0]
            op.dma_val = ent[1]
            self.last_dma[dma_key] = op
        for k in reads:
            self.readers.setdefault(k, []).append(op)
        for k in writes:
            self.last_w[k] = op
            self.readers[k] = []
        self.all_ops.append(op)
        return op

    def _schedule(self):
        order = {e: [] for e in self.ENGS}
        segs = {}
        for op in self.all_ops:
            segs.setdefault(op.seg, []).append(op)
        bar_points = {e: [] for e in self.ENGS}
        t_base = 0.0
        LAT = 120.0
        for sg in sorted(segs):
            ops = segs[sg]
            for e in self.ENGS:
                bar_points[e].append(len(order[e]))
            if not self.reorder:
                for op in ops:
                    order[op.eng].append(op)
                    op.placed = True
                continue
            queues = {e: [op for op in ops if op.eng == e] for e in self.ENGS}
            heads = {e: 0 for e in self.ENGS}
            free = {e: t_base for e in self.ENGS}
            remaining = len(ops)
            tmax = t_base
            while remaining:
                best = None
                best_key = None
                for e in self.ENGS:
                    q = queues[e]
                    h = heads[e]
                    while h < len(q) and q[h].placed:
                        h += 1
                    heads[e] = h
                    lim = min(len(q), h + self.WINDOW[e])
                    k = h
                    found = 0
                    while k < lim:
                        op = q[k]
                        k += 1
                        if op.placed:
                            continue
                        ok = True
                        rdy = free[e]
                        for d in op.edges:
                            if d.seg != sg:
                                continue
                            if not d.placed:
                                ok = False
                                break
                            f = d.fin + (LAT if d.eng != e or d.dma_sem is not None else 30.0)
                            if f > rdy:
                                rdy = f
                        if not ok:
                            continue
                        key = (rdy, op.idx)
                        if best_key is None or key < best_key:
                            best_key = key
                            best = op
                        found += 1
                        if found >= 6:
                            break
                op = best
                assert op is not None, "scheduler stuck (cyclic deps?)"
                st = best_key[0]
                op.placed = True
                if op.dma_sem is not None:
                    op.fin = st + op.cost
                    free[op.eng] = st + 60.0
                else:
                    op.fin = st + op.cost
                    free[op.eng] = op.fin
                if op.fin > tmax:
                    tmax = op.fin
                order[op.eng].append(op)
                remaining -= 1
            t_base = tmax
        self.est_ns = t_base
        return order, bar_points

    def emit(self):
        nc = self.nc
        order, bar_points = self._schedule()
        dma_by_seg = {}
        for op in self.all_ops:
            if op.dma_sem is not None:
                dma_by_seg.setdefault(op.seg, {})[id(op.dma_sem)] = op
        nseg = self.seg + 1
        for si in range(1, nseg):
            lasts = []
            for e in self.ENGS:
                pos = bar_points[e][si]
                for k in range(pos - 1, -1, -1):
                    if order[e][k].dma_sem is None:
                        lasts.append(order[e][k])
                        break
            dl = {}
            for sj in range(si):
                dl.update(dma_by_seg.get(sj, {}))
            lasts += list(dl.values())
            for e in self.ENGS:
                pos = bar_points[e][si]
                if pos < len(order[e]):
                    op = order[e][pos]
                    have = set(id(x) for x in op.deps)
                    for d in lasts:
                        if id(d) in have:
                            continue
                        if d.dma_sem is None and d.eng == e and op.dma_sem is None:
                            continue
                        op.deps.append(d)
        for e in self.ENGS:
            for op in order[e]:
                for d in op.deps:
                    if d.dma_sem is None:
                        d.signal = True
        for e in ("pe", "act", "dve", "pool"):
            for op in reversed(order[e]):
                if op.dma_sem is None:
                    op.signal = True
                    break
        for e in self.ENGS:
            c = 0
            for op in order[e]:
                if op.dma_sem is None and op.signal:
                    c += 1
                op.cnt = c
        sched = self

        def replay(ename, eng):
            waited = {}
            for op in order[ename]:
                for d in op.deps:
                    if d.dma_sem is not None:
                        sem, val = d.dma_sem, d.dma_val
                    else:
                        sem, val = sched.eng_sem[d.eng], d.cnt
                    key = id(sem)
                    if waited.get(key, 0) >= val:
                        continue
                    waited[key] = val
                    eng.wait_ge(sem, val)
                ins = op.fn(eng)
                if op.dma_sem is not None:
                    ins.then_inc(op.dma_sem, 16)
                elif op.signal:
                    ins.then_inc(sched.eng_sem[ename], 1)
            if ename == "sp":
                for e in ("pe", "act", "dve", "pool"):
                    last = None
                    for op in order[e]:
                        if op.dma_sem is None:
                            last = op
                    if last is not None:
                        eng.wait_ge(sched.eng_sem[e], last.cnt)
                for k, (sem, cnt) in sched.dma_keys.items():
                    eng.wait_ge(sem, cnt)

        with nc.Block() as block:
            @block.tensor
            def _(e):
                replay("pe", e)

            @block.scalar
            def _(e):
                replay("act", e)

            @block.vector
            def _(e):
                replay("dve", e)

            @block.gpsimd
            def _(e):
                replay("pool", e)

            @block.sync
            def _(e):
                replay("sp", e)


class Arena:
    def __init__(self, tile_ap, n, name):
        self.t = tile_ap
        self.n = n
        self.off = 0
        self.name = name
        self.gen = 0

    def reset(self):
        self.off = 0
        self.gen += 1

    def f32(self, n):
        a = self.t[:, self.off:self.off + n]
        self.off += n
        assert self.off <= self.n, (self.name, self.off, self.n)
        return a

    def bf16(self, n):
        w = (n + 1) // 2
        a = self.t[:, self.off:self.off + w].bitcast(BF16)
        self.off += w
        assert self.off <= self.n, (self.name, self.off, self.n)
        return a


class Builder:
    def __init__(self, nc, st, n_layers=4, final_norm=True, max_passes=None):
        self.max_passes = max_passes
        self.debug = max_passes == 1
        self.nc = nc
        self.st = st
        self.S = Sched(nc, st)
        self.n_layers = n_layers
        self.final_norm = final_norm
        self.uid = 0
        S = self.S
        dt_in = lambda name, shape: nc.dram_tensor(name, shape, F32, kind="ExternalInput").ap()
        self.x = dt_in("x", [SEQ, D])
        self.norm_g = dt_in("norm_g", [4, D])
        self.final_g = dt_in("final_g", [1, D])
        self.e_w_in = dt_in("e_w_in", [2, D, EVEN_IN])
        self.e_w_a2 = dt_in("e_w_a2", [2, 16, 512])
        self.e_b_a = dt_in("e_b_a", [2, 1, 512])
        self.e_gla_g = dt_in("e_gla_g", [2, 1, 1024])
        self.e_cw = dt_in("e_cw", [2, P, 8, 31])
        self.e_cb = dt_in("e_cb", [2, P, 8])
        self.e_lg = dt_in("e_lg", [2, P, 8])
        self.e_lb = dt_in("e_lb", [2, P, 8])
        self.e_w_out = dt_in("e_w_out", [2, 2048, D])
        self.o_w_in = dt_in("o_w_in", [2, D, ODD_IN])
        self.o_lamre = dt_in("o_lamre", [2, P, 16])
        self.o_lamim = dt_in("o_lamim", [2, P, 16])
        self.o_logdt = dt_in("o_logdt", [2, P, 16])
        self.o_bre = dt_in("o_bre", [2, P, 256])
        self.o_bim = dt_in("o_bim", [2, P, 256])
        self.o_cre = dt_in("o_cre", [2, P, 256])
        self.o_cim = dt_in("o_cim", [2, P, 256])
        self.o_dcol = dt_in("o_dcol", [2, P, 32])
        self.o_w_glu = dt_in("o_w_glu", [2, 512, 512])
        self.o_bglu = dt_in("o_bglu", [2, P, 4])
        self.o_lng = dt_in("o_lng", [2, 1, 1024])
        self.o_lnb = dt_in("o_lnb", [2, 1, 1024])
        self.o_wsT = dt_in("o_wsT", [2, P, 1024])
        self.o_bs = dt_in("o_bs", [2, 1, 1024])
        self.o_w_out = dt_in("o_w_out", [2, 1536, D])
        self.out = nc.dram_tensor("out", [SEQ, D], F32, kind="ExternalOutput").ap()
        self.hbuf = [nc.dram_tensor("hbuf%d" % i, [SEQ, D], F32, kind="Internal").ap() for i in range(2)]
        self.hnscr = nc.dram_tensor("hnscr", [NMT, P, 8 * MT], BF16, kind="Internal").ap()

        sb = lambda name, shape, dt: st.enter_context(nc.sbuf_tensor(name, shape, dt))
        self.wa = [sb("wa%d" % i, [P, WA_N], BF16) for i in range(2)]
        self.cst = sb("cst", [P, 4 * 128], BF16)
        self.ident = self.cst[:, 0:128]
        self.triInc = self.cst[:, 128:256]
        self.triRev = self.cst[:, 256:384]
        self.cmask = self.cst[:, 384:512]
        self.pst = sb("pst", [P, 1024], F32)
        rem = int(nc.sbuf_bytes_remaining) - 256
        self.act_n = rem // 4
        self.act_t = sb("actarena", [P, self.act_n], F32)
        self.A = Arena(self.act_t, self.act_n, "act")
        self.pb = [st.enter_context(nc.psum_tensor("pb%d" % i, [P, 512], F32)) for i in range(8)]
        self.ring = list(range(8))
        self.ring_i = 0
        self.slot = 0
        self._consts()

    def key(self, name):
        self.uid += 1
        return "%s#%d" % (name, self.uid)

    def bank(self):
        i = self.ring[self.ring_i % len(self.ring)]
        self.ring_i += 1
        return self.pb[i], ("pb", i)

    def set_ring(self, banks):
        self.ring = list(banks)
        self.ring_i = 0

    def add(self, *a, **k):
        return self.S.add(*a, **k)

    @staticmethod
    def _n(ap):
        n = 1
        for d in ap.shape[1:]:
            n *= int(d)
        return n

    def mm(self, out, lhsT, rhs, start, stop, reads, writes):
        c = max(self._n(out), 64) / 2.0 + 10.0
        if lhsT.dtype == F32:
            c *= 4.0
        self.S.add("pe", lambda e: e.matmul(out, lhsT=lhsT, rhs=rhs, start=start, stop=stop), reads=reads, writes=writes, cost=c)

    def tp(self, out, in_, ident, reads, writes):
        self.S.add("pe", lambda e: e.transpose(out=out, in_=in_, identity=ident), reads=reads, writes=writes, cost=80.0)

    def actf(self, out, in_, func, reads, writes, bias=None, scale=None, accum=None):
        kw = {}
        if bias is not None:
            kw["bias"] = bias
        if scale is not None:
            kw["scale"] = scale
        if accum is not None:
            kw["accum_out"] = accum
        c = self._n(out) / 1.4 + 230.0 + (100.0 if accum is not None else 0.0)
        self.S.add("act", lambda e: e.activation(out=out, in_=in_, func=func, **kw), reads=reads, writes=writes, cost=c)

    def tt(self, out, in0, in1, op, reads, writes, eng="dve"):
        c = self._n(out) / (0.96 if eng == "dve" else 0.45) + (130.0 if eng == "dve" else 300.0)
        self.S.add(eng, lambda e: e.tensor_tensor(out=out, in0=in0, in1=in1, op=op), reads=reads, writes=writes, cost=c)

    def ts(self, out, in0, s1, op0, reads, writes, s2=None, op1=None, eng="dve"):
        c = self._n(out) / (0.96 if eng == "dve" else 0.45) + (130.0 if eng == "dve" else 300.0)
        if op1 is None:
            self.S.add(eng, lambda e: e.tensor_scalar(out=out, in0=in0, scalar1=s1, scalar2=None, op0=op0), reads=reads, writes=writes, cost=c)
        else:
            self.S.add(eng, lambda e: e.tensor_scalar(out=out, in0=in0, scalar1=s1, scalar2=s2, op0=op0, op1=op1), reads=reads, writes=writes, cost=c)

    def stt(self, out, in0, scalar, in1, op0, op1, reads, writes):
        c = self._n(out) / 0.96 + 130.0
        self.S.add("dve", lambda e: e.scalar_tensor_tensor(out=out, in0=in0, scalar=scalar, in1=in1, op0=op0, op1=op1), reads=reads, writes=writes, cost=c)

    def cp(self, out, in_, reads, writes, eng="act"):
        n = self._n(out)
        if eng == "act":
            self.S.add("act", lambda e: e.copy(out=out, in_=in_), reads=reads, writes=writes, cost=n / 1.4 + 230.0)
        else:
            c = n / (0.96 if eng == "dve" else 0.45) + (130.0 if eng == "dve" else 300.0)
            self.S.add(eng, lambda e: e.tensor_copy(out=out, in_=in_), reads=reads, writes=writes, cost=c)

    def memset(self, ap, val, writes, eng="pool"):
        self.S.add(eng, lambda e: e.memset(ap, val), writes=writes, cost=self._n(ap) / 0.9 + 150.0)

    def dma(self, out, in_, reads, writes, key, eng="sp"):
        nbytes = self._n(out) * int(out.shape[0]) * 4
        self.S.add(eng, lambda e: e.dma_start(out=out, in_=in_), reads=reads, writes=writes, dma_key=key, cost=2500.0 + nbytes / 60.0)

    def rsqrt_small(self, out, in_, scale, reads, writes, tmpkey=None):
        self.ts(out, in_, scale, ALU.mult, reads, writes, s2=EPS, op1=ALU.add)
        self.actf(out, out, AF.Sqrt, writes, writes)
        self.S.add("dve", lambda e: e.reciprocal(out=out, in_=out), reads=writes, writes=writes)

    def dump(self, name, ap, reads, row0, col0=0):
        if not getattr(self, "debug", False):
            return
        n = ap.shape[-1] if len(ap.shape) == 2 else None
        self.dma(self.out[row0:row0 + ap.shape[0], col0:col0 + n], ap, reads, [("dbg", name)], "dbg", eng="pool")

    def _consts(self):
        A = self.A
        ones = A.bf16(128)
        k1 = "c_ones"
        self.memset(ones, 1.0, [k1])
        self.memset(self.cst[:, :], 0.0, ["cst"])
        self.S.add("pool", lambda e: e.affine_select(out=self.ident, in_=ones, pattern=[[-1, 128]], compare_op=ALU.is_equal,
                                                    fill=0.0, base=0, channel_multiplier=1), reads=[k1, "cst"], writes=["cst"])
        sc = A.bf16(128)
        tmpm = A.bf16(128)
        self.memset(sc, -1.0 / 16.0, ["c_sc"])

        def tri(dst, src, srckey, upper):
            if upper:
                self.S.add("pool", lambda e: e.affine_select(out=tmpm, in_=src, pattern=[[1, 128]], compare_op=ALU.is_ge,
                                                            fill=0.0, base=0, channel_multiplier=-1), reads=[srckey, "tmpm"], writes=["tmpm"])
                self.cp(dst[:, 0:64], tmpm[:, 0:64], ["tmpm"], ["cst"], eng="pool")
                self.S.add("pool", lambda e: e.affine_select(out=dst[:, 64:128], in_=tmpm[:, 64:128], pattern=[[0, 64]], compare_op=ALU.is_ge,
                                                            fill=0.0, base=-64, channel_multiplier=1), reads=["tmpm", "cst"], writes=["cst"])
            else:
                self.S.add("pool", lambda e: e.affine_select(out=tmpm, in_=src, pattern=[[-1, 128]], compare_op=ALU.is_ge,
                                                            fill=0.0, base=-1, channel_multiplier=1), reads=[srckey, "tmpm"], writes=["tmpm"])
                self.cp(dst[:, 64:128], tmpm[:, 64:128], ["tmpm"], ["cst"], eng="pool")
                self.S.add("pool", lambda e: e.affine_select(out=dst[:, 0:64], in_=tmpm[:, 0:64], pattern=[[0, 64]], compare_op=ALU.is_ge,
                                                            fill=0.0, base=63, channel_multiplier=-1), reads=["tmpm", "cst"], writes=["cst"])
        tri(self.triInc, sc, "c_sc", True)
        tri(self.triRev, sc, "c_sc", False)
        tri(self.cmask, ones, k1, True)

    def load_w_rows(self, slot, off, src, r0, nrows_k, c0, ncols, key):
        view = self.wa[slot][:, off:off + nrows_k * ncols].rearrange("p (k n) -> p k n", n=ncols)
        s = src[r0:r0 + 128 * nrows_k, c0:c0 + ncols].rearrange("(k p) n -> p k n", p=128)
        for k in range(nrows_k):
            self.dma(view[:, k, :], s[:, k, :], [], [key], key, eng="pool")
        return view

    def norm_tile(self, m, hsrc, hsrc_key, bufs, store_scr):
        hA, hnb, gtile, hnT, ss, gkey = bufs["hA"], bufs["hnb"], bufs["gn"], bufs["hnT"], bufs["ss"], bufs["gkey"]
        hk = bufs.get("hk", "hnT")
        for j in range(NS):
            t0 = m * MT + j * 128
            bi = j if len(hA) <= NS else (m % 2) * NS + j
            hb = hA[bi]
            kh = "hA%d" % bi
            self.dma(hb, hsrc[t0:t0 + 128, :], [(hsrc_key, m, j)], [kh], "ld_hA%d" % bi)
            self.actf(hnb.bitcast(F32) if False else bufs["junk"], hb, AF.Square, [kh], ["junk", "ss"], accum=ss[:, 0:1])
            self.rsqrt_small(ss[:, 1:2], ss[:, 0:1], 1.0 / D, ["ss"], ["rstd"], "ss_t")
            self.stt(hnb, hb, ss[:, 1:2], gtile, ALU.mult, ALU.mult, [kh, "rstd", gkey], ["hnb"])
            if m == 0 and j == 0:
                self.dump("hnb", hnb, ["hnb"], 0)
            pbk, pk = self.bank()
            pv = pbk[:, :].bitcast(BF16)
            for dk in range(8):
                self.tp(pv[:, dk * 128:(dk + 1) * 128], hnb[:, dk * 128:(dk + 1) * 128], self.ident, ["hnb", "cst"], [pk])
            self.cp(hnT[:, :, j * 128:(j + 1) * 128], pv[:, 0:1024].rearrange("p (k t) -> p k t", t=128), [pk], [hk])
        if store_scr:
            self.dma(self.hnscr[m].rearrange("p (k t) -> p k t", t=MT), hnT, [hk], [("hnscr", m)], "st_" + hk)

    def load_hnT(self, m, hnT, hk="hnT"):
        self.dma(hnT, self.hnscr[m].rearrange("p (k t) -> p k t", t=MT), [("hnscr", m)], [hk], "ld_" + hk)

    def outproj_residual(self, m, yT, nk, Wo, wkey, hres, hres_key, hdst, hdst_key, hB, final_g=None, ss=None, junk=None, hbk="hB", yk="yT"):
        for j in range(NS):
            t0 = m * MT + j * 128
            bi = j if len(hB) <= NS else (m % 2) * NS + j
            hb = hB[bi]
            kh = "%s%d" % (hbk, bi)
            self.dma(hb, hres[t0:t0 + 128, :], [(hres_key, m, j)], [kh], "ld_%s%d" % (hbk, bi))
            for n2 in range(2):
                pbk, pk = self.bank()
                for mk in range(nk):
                    self.mm(pbk[:, :], yT[:, mk, j * 128:(j + 1) * 128], Wo[:, mk, n2 * 512:(n2 + 1) * 512],
                            mk == 0, mk == nk - 1, [yk, wkey], [pk])
                self.tt(hb[:, n2 * 512:(n2 + 1) * 512], hb[:, n2 * 512:(n2 + 1) * 512], pbk[:, :], ALU.add, [kh, pk], [kh])
            if final_g is not None:
                self.actf(junk, hb, AF.Square, [kh], ["junk", "fss"], accum=ss[:, 2:3])
                self.rsqrt_small(ss[:, 3:4], ss[:, 2:3], 1.0 / D, ["fss"], ["frstd"], "fss_t")
                self.stt(hb, hb, ss[:, 3:4], final_g, ALU.mult, ALU.mult, [kh, "frstd", "gfin"], [kh])
            self.dma(hdst[t0:t0 + 128, :], hb, [kh], [(hdst_key, m, j)], "st_h%d" % bi)

    def w_e1(self, L, slot):
        i = L // 2
        wk = ("w", slot)
        Win = self.load_w_rows(slot, 0, self.e_w_in[i], 0, 8, 0, 3088, wk)
        Wo = self.load_w_rows(slot, 8 * 3088, self.e_w_out[i], 0, 8, 0, 1024, wk)
        return (Win, Wo, wk)

    def pass_e1(self, L, W, hin, hin_key, hmid, hmid_key):
        i = L // 2
        A = self.A
        self.set_ring(range(6))
        Win, Wo, wk = W
        gn = A.f32(1024)
        gg = A.f32(1024)
        pk_ = "par_e1"
        self.dma(gn, self.norm_g[L:L + 1, :].partition_broadcast(128), [], [pk_], pk_)
        self.dma(gg, self.e_gla_g[i].partition_broadcast(128), [], [pk_], pk_)
        wa2f = A.f32(512)
        wa2 = A.bf16(512)
        self.memset(wa2f[0:32, :], 0.0, ["wa2f"])
        self.dma(wa2f[0:16, :], self.e_w_a2[i], ["wa2f"], [pk_], pk_)
        self.dma(wa2f[16:17, :], self.e_b_a[i], ["wa2f"], [pk_], pk_)
        self.cp(wa2[0:32, :], wa2f[0:32, :], [pk_, "wa2f"], ["wa2"], eng="dve")
        hA = [A.f32(1024) for _ in range(NS)]
        hnb = A.bf16(1024)
        junk = A.bf16(1024)
        ss = A.f32(8)
        hnT2 = [A.bf16(8 * MT).rearrange("p (k t) -> p k t", t=MT) for _ in range(2)]
        nb = dict(hA=hA, hnb=hnb, gn=gn, hnT=None, ss=ss, gkey=pk_, junk=junk)
        alow = A.bf16(MT)
        self.memset(alow[0:32, :], 1.0, ["alow"])
        tmpf = [A.f32(512) for _ in range(2)]
        spT = A.bf16(NS * 512).rearrange("p (j n) -> p j n", n=512)
        Eb = A.f32(MT)
        Enb = A.f32(MT)
        dec = A.f32(16).rearrange("p (h c) -> p h c", c=4)
        qf = A.bf16(4 * MT).rearrange("p (h t) -> p h t", t=MT)
        kin = A.bf16(4 * MT).rearrange("p (h t) -> p h t", t=MT)
        kst = A.bf16(NS * 512).rearrange("p (j n) -> p j n", n=512)
        v = A.bf16(NS * 1024).rearrange("p (j n) -> p j n", n=1024)
        S32 = A.f32(1024).rearrange("p (h n) -> p h n", n=256)
        Sbf = [A.bf16(1024).rearrange("p (h n) -> p h n", n=256) for _ in range(2)]
        attT = A.bf16(512).rearrange("p (h n) -> p h n", n=128)
        sz = A.f32(1024)
        ss4 = A.f32(8)
        ygla = A.bf16(1024)
        yT = A.bf16(8 * MT).rearrange("p (k t) -> p k t", t=MT)
        self.memset(S32, 0.0, ["S32_%d" % h for h in range(4)])
        self.memset(Sbf[0], 0.0, ["Sbf0_%d" % h for h in range(4)])
        chunk_ctr = 0
        QSC = 128.0 ** -0.5
        for m in range(NMT):
            hnT = hnT2[m % 2]
            hk = "hnT%d" % (m % 2)
            nb["hnT"] = hnT
            nb["hk"] = hk
            self.norm_tile(m, hin, hin_key, nb, store_scr=True)
            pbk, pk = self.bank()
            for dk in range(8):
                self.mm(pbk[0:16, 0:MT], Win[:, dk, 3072:3088], hnT[:, dk, :], dk == 0, dk == 7, [hk, wk], [pk])
            self.cp(alow[0:16, :], pbk[0:16, 0:MT], [pk], ["alow"])
            for j in range(NS):
                pbk, pk = self.bank()
                self.mm(pbk[:, :], alow[0:32, j * 128:(j + 1) * 128], wa2[0:32, :], True, True, ["alow", "wa2"], [pk])
                self.actf(tmpf[j], pbk[:, :], AF.Exp, [pk], ["tmpf%d" % j], scale=-1.0)
            for j in range(NS):
                self.actf(spT[:, j, :], tmpf[j], AF.Ln, ["tmpf%d" % j], ["spT"], bias=1.0)
            if m == 0:
                self.dump("spT", spT[:, 0, :], ["spT"], 128)
                self.dump("hnT0", hnT[:, 0, :], ["hnT"], 128, 512)
                self.dump("hnT1", hnT[:, 1, :], ["hnT"], 128, 768)
            for hd in range(4):
                pbk, pk = self.bank()
                for j in range(NS):
                    self.mm(pbk[:, j * 128:(j + 1) * 128], spT[:, j, hd * 128:(hd + 1) * 128], self.triInc, True, True, ["spT", "cst"], [pk])
                self.actf(Eb, pbk[:, 0:MT], AF.Exp, [pk], ["Eb"])
                self.actf(Enb, pbk[:, 0:MT], AF.Exp, [pk], ["Enb"], scale=-1.0)
                self.cp(dec[:, hd, :], Eb.rearrange("p (c t) -> p c t", t=64)[:, :, 63], ["Eb"], ["dec%d" % hd], eng="dve")
                pq, pqk = self.bank()
                for dk in range(8):
                    self.mm(pq[:, 0:MT], Win[:, dk, hd * 128:(hd + 1) * 128], hnT[:, dk, :], dk == 0, dk == 7, [hk, wk], [pqk])
                self.stt(qf[:, hd, :], pq[:, 0:MT], QSC, Eb, ALU.mult, ALU.mult, [pqk, "Eb"], ["qf%d" % hd])
                pkk, pkkk = self.bank()
                for dk in range(8):
                    self.mm(pkk[:, 0:MT], Win[:, dk, 512 + hd * 128:512 + (hd + 1) * 128], hnT[:, dk, :], dk == 0, dk == 7, [hk, wk], [pkkk])
                self.tt(kin[:, hd, :], pkk[:, 0:MT], Enb, ALU.mult, [pkkk, "Enb"], ["kin%d" % hd])
                if m == 0 and hd == 0:
                    self.dump("Eb", Eb, ["Eb"], 256)
                    self.dump("qf", qf[:, 0, :], ["qf0"], 256, 256)
                    self.dump("kin", kin[:, 0, :], ["kin0"], 256, 512)
            for j in range(NS):
                pbk, pk = self.bank()
                self.mm(pbk[:, :], self.triRev, spT[:, j, :], True, True, ["spT", "cst"], [pk])
                self.actf(tmpf[j], pbk[:, :], AF.Exp, [pk], ["tmpf%d" % j])
                pk2, pk2k = self.bank()
                for dk in range(8):
                    self.mm(pk2[:, :], hnT[:, dk, j * 128:(j + 1) * 128], Win[:, dk, 512:1024], dk == 0, dk == 7, [hk, wk], [pk2k])
                self.tt(kst[:, j, :], pk2[:, :], tmpf[j], ALU.mult, [pk2k, "tmpf%d" % j], ["kst%d" % j])
                for n2 in range(2):
                    pv_, pvk = self.bank()
                    for dk in range(8):
                        self.mm(pv_[:, :], hnT[:, dk, j * 128:(j + 1) * 128], Win[:, dk, 1024 + n2 * 512:1024 + (n2 + 1) * 512],
                                dk == 0, dk == 7, [hk, wk], [pvk])
                    self.cp(v[:, j, n2 * 512:(n2 + 1) * 512], pv_[:, :], [pvk], ["v%d" % j])
            for j in range(NS):
                pat, patk = self.bank()
                for hd in range(4):
                    self.mm(pat[:, hd * 128:(hd + 1) * 128], kin[:, hd, j * 128:(j + 1) * 128], qf[:, hd, j * 128:(j + 1) * 128],
                            True, True, ["kin%d" % hd, "qf%d" % hd], [patk])
                self.tt(attT, pat[:, :].rearrange("p (h n) -> p h n", n=128), self.cmask.unsqueeze(1).to_broadcast([P, 4, 128]),
                        ALU.mult, [patk, "cst"], ["attT"])
                po = [(self.pb[6], ("pb", 6)), (self.pb[7], ("pb", 7))]
                for c2 in range(2):
                    cs = slice(64 * c2, 64 * c2 + 64)
                    cur = chunk_ctr % 2
                    nxt = 1 - cur
                    cidx = 2 * j + c2
                    for hd in range(4):
                        ob, obk = po[hd // 2]
                        osl = ob[cs, (hd % 2) * 256:(hd % 2) * 256 + 256]
                        okey = obk
                        self.mm(osl, attT[cs, hd, 64 * c2:64 * c2 + 64], v[cs, j, hd * 256:(hd + 1) * 256], True, False,
                                ["attT", "v%d" % j], [okey])
                        self.mm(osl, qf[:, hd, j * 128 + 64 * c2:j * 128 + 64 * c2 + 64], Sbf[cur][:, hd, :], False, True,
                                ["qf%d" % hd, "Sbf%d_%d" % (cur, hd)], [okey])
                    for hd in range(4):
                        pkv, pkvk = self.bank()
                        self.mm(pkv[:, 0:256], kst[cs, j, hd * 128:(hd + 1) * 128], v[cs, j, hd * 256:(hd + 1) * 256], True, True,
                                ["kst%d" % j, "v%d" % j], [pkvk])
                        self.stt(S32[:, hd, :], S32[:, hd, :], dec[:, hd, cidx:cidx + 1], pkv[:, 0:256], ALU.mult, ALU.add,
                                 ["S32_%d" % hd, "dec%d" % hd, pkvk], ["S32_%d" % hd])
                        self.cp(Sbf[nxt][:, hd, :], S32[:, hd, :], ["S32_%d" % hd], ["Sbf%d_%d" % (nxt, hd)])
                    chunk_ctr += 1
                for n2 in range(2):
                    pz, pzk = self.bank()
                    for dk in range(8):
                        self.mm(pz[:, :], hnT[:, dk, j * 128:(j + 1) * 128], Win[:, dk, 2048 + n2 * 512:2048 + (n2 + 1) * 512],
                                dk == 0, dk == 7, [hk, wk], [pzk])
                    self.actf(sz[:, n2 * 512:(n2 + 1) * 512], pz[:, :], AF.Silu, [pzk], ["sz%d" % n2])
                    self.tt(sz[:, n2 * 512:(n2 + 1) * 512], sz[:, n2 * 512:(n2 + 1) * 512], gg[:, n2 * 512:(n2 + 1) * 512], ALU.mult,
                            ["sz%d" % n2, pk_], ["sz%d" % n2], eng="pool")
                for hd in range(4):
                    ob, obk = po[hd // 2]
                    osl = ob[:, (hd % 2) * 256:(hd % 2) * 256 + 256]
                    okeys = [obk]
                    self.actf(junk[:, 0:256], osl, AF.Square, okeys, ["junk", "ss4_%d" % hd], accum=ss4[:, hd:hd + 1])
                self.rsqrt_small(ss4[:, 4:8], ss4[:, 0:4], 1.0 / 256.0, ["ss4_%d" % h for h in range(4)], ["rstd4"], "ss4_t")
                for hd in range(4):
                    ob, obk = po[hd // 2]
                    osl = ob[:, (hd % 2) * 256:(hd % 2) * 256 + 256]
                    okeys = [obk]
                    self.stt(ygla[:, hd * 256:(hd + 1) * 256], osl, ss4[:, 4 + hd:5 + hd], sz[:, hd * 256:(hd + 1) * 256],
                             ALU.mult, ALU.mult, okeys + ["rstd4", "sz%d" % (hd // 2)], ["ygla"])
                if m == 0 and j == 0:
                    self.dump("ygla", ygla, ["ygla"], 384)
                    self.dump("v", v[:, 0, :], ["v0"], 512)
                    self.dump("kst", kst[:, 0, :], ["kst0"], 640, 0)
                    self.dump("attT", attT[:, 0, :], ["attT"], 640, 512)
                    self.dump("sz", sz, ["sz0", "sz1"], 768)
                pbk, pk = self.bank()
                pvw = pbk[:, :].bitcast(BF16)
                for mk in range(8):
                    self.tp(pvw[:, mk * 128:(mk + 1) * 128], ygla[:, mk * 128:(mk + 1) * 128], self.ident, ["ygla", "cst"], [pk])
                self.cp(yT[:, :, j * 128:(j + 1) * 128], pvw[:, 0:1024].rearrange("p (k t) -> p k t", t=128), [pk], ["yT"])
            self.outproj_residual(m, yT, 8, Wo, wk, hin, hin_key, hmid, hmid_key, hA, hbk="hA")

    def w_e2(self, L, slot):
        i = L // 2
        wk = ("w", slot)
        Win = self.load_w_rows(slot, 0, self.e_w_in[i], 0, 8, 3088, 3072, wk)
        Wo = self.load_w_rows(slot, 8 * 3072, self.e_w_out[i], 1024, 8, 0, 1024, wk)
        return (Win, Wo, wk)

    def pass_e2(self, L, W, hmid, hmid_key):
        i = L // 2
        A = self.A
        self.set_ring(range(2, 8))
        Win, Wo, wk = W
        NPE = 29
        oslot = wk[1] ^ 1
        dg = self.wa[oslot][:, 4096:4096 + 8 * NPE * 128].rearrange("p (c t n) -> p c t n", c=8, t=NPE)
        pk_ = "par_e2"
        cw = A.f32(8 * 31).rearrange("p (c k) -> p c k", k=31)
        cb = A.f32(8)
        lg = A.f32(8)
        lb = A.f32(8)
        self.dma(cw, self.e_cw[i], [], [pk_], pk_)
        self.dma(cb, self.e_cb[i], [], [pk_], pk_)
        self.dma(lg, self.e_lg[i], [], [pk_], pk_)
        self.dma(lb, self.e_lb[i], [], [pk_], pk_)
        for ct in range(8):
            self.tt(dg[:, ct, :, :], self.ident.unsqueeze(1).to_broadcast([P, NPE, 128]),
                    cw[:, ct, 0:NPE].unsqueeze(2).to_broadcast([P, NPE, 128]), ALU.mult, ["cst", pk_], [("dg", ct)],
                    eng=("dve" if ct % 2 == 0 else "pool"))
        ones32 = A.f32(128)
        self.memset(ones32, 1.0, ["ones32"])
        hnT2 = [A.bf16(8 * MT).rearrange("p (k t) -> p k t", t=MT) for _ in range(2)]
        u = [A.bf16(MT + 32) for _ in range(2)]
        halo = A.bf16(8 * 32).rearrange("p (c k) -> p c k", k=32)
        self.memset(halo, 0.0, [("halo", c) for c in range(8)])
        sg = [A.f32(MT) for _ in range(2)]
        acc = [A.f32(MT) for _ in range(2)]
        xc = A.f32(8 * MT).rearrange("p (c t) -> p c t", t=MT)
        sq = [A.f32(MT) for _ in range(2)]
        mean = A.f32(MT)
        rstd = A.f32(MT)
        nmr = A.f32(MT)
        tn = [A.f32(MT) for _ in range(2)]
        sl = [A.f32(MT) for _ in range(2)]
        szc = [A.f32(MT) for _ in range(2)]
        yT2 = [A.bf16(8 * MT).rearrange("p (k t) -> p k t", t=MT) for _ in range(2)]
        hB = [A.f32(1024) for _ in range(2 * NS)]
        s1b, s1k = self.pb[0], ("pb", 0)
        s2b, s2k = self.pb[1], ("pb", 1)
        for m in range(NMT):
            hnT = hnT2[m % 2]
            hk = "hnT%d" % (m % 2)
            yT = yT2[m % 2]
            yk = "yT%d" % (m % 2)
            self.load_hnT(m, hnT, hk)
            for ct in range(8):
                b = ct % 2
                ub = u[b]
                uk = "u%d" % b
                pval, pvk = self.bank()
                for dk in range(8):
                    self.mm(pval[:, 0:MT], Win[:, dk, ct * 128:(ct + 1) * 128], hnT[:, dk, :], dk == 0, dk == 7, [hk, wk], [pvk])
                pg, pgk = self.bank()
                for dk in range(8):
                    self.mm(pg[:, 0:MT], Win[:, dk, 1024 + ct * 128:1024 + (ct + 1) * 128], hnT[:, dk, :], dk == 0, dk == 7, [hk, wk], [pgk])
                self.actf(sg[b], pg[:, 0:MT], AF.Sigmoid, [pgk], ["sg%d" % b])
                self.cp(ub[:, 0:30], halo[:, ct, 0:30], [("halo", ct)], [uk + "h"], eng="pool")
                self.tt(ub[:, 30:30 + MT], pval[:, 0:MT], sg[b], ALU.mult, [pvk, "sg%d" % b], [uk])
                self.cp(halo[:, ct, 0:30], ub[:, MT:MT + 30], [uk], [("halo", ct)], eng="pool")
                pc, pck = self.bank()
                for t in range(NPE):
                    self.mm(pc[:, 0:MT], dg[:, ct, t, :], ub[:, t:t + MT], t == 0, t == NPE - 1, [uk, uk + "h", ("dg", ct)], [pck])
                a_ = acc[b]
                ka = "acc%d" % b
                self.ts(a_, ub[:, NPE:NPE + MT], cw[:, ct, NPE:NPE + 1], ALU.mult, [uk, uk + "h", pk_], [ka])
                for t in range(NPE + 1, 31):
                    self.stt(a_, ub[:, t:t + MT], cw[:, ct, t:t + 1], a_, ALU.mult, ALU.add, [uk, uk + "h", ka], [ka])
                self.stt(xc[:, ct, :], pc[:, 0:MT], cb[:, ct:ct + 1], a_, ALU.add, ALU.add, [pck, ka, pk_], [("xc", ct)])
                self.actf(sq[b], xc[:, ct, :], AF.Square, [("xc", ct)], ["sq%d" % b])
                self.mm(s1b[:, 0:MT], ones32, xc[:, ct, :], ct == 0, ct == 7, [("xc", ct), "ones32"], [s1k])
                self.mm(s2b[:, 0:MT], ones32, sq[b], ct == 0, ct == 7, ["sq%d" % b, "ones32"], [s2k])
            self.ts(mean, s1b[:, 0:MT], 1.0 / 1024.0, ALU.mult, [s1k], ["mean"])
            self.tt(nmr, mean, mean, ALU.mult, ["mean"], ["nmr"])
            self.stt(rstd, s2b[:, 0:MT], 1.0 / 1024.0, nmr, ALU.mult, ALU.subtract, [s2k, "nmr"], ["rstd"])
            self.ts(rstd, rstd, EPS, ALU.add, ["rstd"], ["rstd"])
            self.actf(rstd, rstd, AF.Sqrt, ["rstd"], ["rstd"])
            self.S.add("dve", lambda e: e.reciprocal(out=rstd, in_=rstd), reads=["rstd"], writes=["rstd"])
            self.stt(nmr, mean, -1.0, rstd, ALU.mult, ALU.mult, ["mean", "rstd"], ["nmr"])
            for ct in range(8):
                b = ct % 2
                self.tt(tn[b], xc[:, ct, :], rstd, ALU.mult, [("xc", ct), "rstd"], ["tn%d" % b])
                self.tt(tn[b], tn[b], nmr, ALU.add, ["tn%d" % b, "nmr"], ["tn%d" % b])
                self.actf(sl[b], tn[b], AF.Silu, ["tn%d" % b, pk_], ["sl%d" % b], bias=lb[:, ct:ct + 1], scale=lg[:, ct:ct + 1])
                pz, pzk = self.bank()
                for dk in range(8):
                    self.mm(pz[:, 0:MT], Win[:, dk, 2048 + ct * 128:2048 + (ct + 1) * 128], hnT[:, dk, :], dk == 0, dk == 7, [hk, wk], [pzk])
                self.actf(szc[b], pz[:, 0:MT], AF.Silu, [pzk], ["szc%d" % b])
                self.tt(yT[:, ct, :], sl[b], szc[b], ALU.mult, ["sl%d" % b, "szc%d" % b], [yk])
            self.outproj_residual(m, yT, 8, Wo, wk, hmid, hmid_key, hmid, hmid_key, hB, yk=yk)

    def w_oa(self, L, slot):
        i = L // 2
        wk = ("w", slot)
        Wu = self.load_w_rows(slot, 0, self.o_w_in[i], 0, 8, 0, 512, wk)
        w = self.wa[slot]
        V = dict(Wu=Wu, wk=wk, slot=slot)
        V["Kblk"] = w[:, 4096:8192].rearrange("p (g n) -> p g n", n=128)
        V["Wa_re"] = w[:, 8192:10240].rearrange("p (g n) -> p g n", n=128)
        V["Wa_im"] = w[:, 10240:12288].rearrange("p (g n) -> p g n", n=128)
        V["CAre"] = w[:, 12288:14336].rearrange("p (g n) -> p g n", n=128)
        V["nCAim"] = w[:, 14336:16384].rearrange("p (g n) -> p g n", n=128)
        V["usm"] = w[:, 16384:32768].rearrange("p (c s n) -> p c s n", c=4, s=8)
        return V

    def cmul(self, o_re, o_im, a_re, a_im, b_re, b_im, tmp, reads, wkeys, neg_im=False):
        kre, kim, kt = wkeys
        self.tt(o_re, a_re, b_re, ALU.mult, reads, [kre])
        self.tt(tmp, a_im, b_im, ALU.mult, reads, [kt])
        self.tt(o_re, o_re, tmp, ALU.subtract, [kre, kt], [kre])
        self.tt(o_im, a_re, b_im, ALU.mult, reads, [kim])
        self.tt(tmp, a_im, b_re, ALU.mult, reads + [kre], [kt])
        if neg_im:
            self.stt(o_im, o_im, -1.0, tmp, ALU.mult, ALU.subtract, [kim, kt], [kim])
        else:
            self.tt(o_im, o_im, tmp, ALU.add, [kim, kt], [kim])

    def reduce_angle(self, x, t, key):
        TWO_PI = float(2.0 * np.pi)
        PI = float(np.pi)
        for _ in range(8):
            self.ts(t, x, PI, ALU.is_gt, [key], ["ra_t"], s2=TWO_PI, op1=ALU.mult)
            self.tt(x, x, t, ALU.subtract, [key, "ra_t"], [key])
        for _ in range(2):
            self.ts(t, x, -PI, ALU.is_lt, [key], ["ra_t"], s2=TWO_PI, op1=ALU.mult)
            self.tt(x, x, t, ALU.add, [key, "ra_t"], [key])

    def s5_prep(self, L, V):
        i = L // 2
        A = self.A
        pk_ = "par_s5"
        T = self.pst
        sm = lambda: A.f32(16)
        lr, li, ldt, dtt, mag, ang, sarg, carg, t1, are, aim, den, nr, cfr, cfi, u1, u2 = [sm() for _ in range(17)]
        self.dma(lr, self.o_lamre[i], [], [pk_], pk_)
        self.dma(li, self.o_lamim[i], [], [pk_], pk_)
        self.dma(ldt, self.o_logdt[i], [], [pk_], pk_)
        b3 = lambda: A.f32(256).rearrange("p (g h) -> p g h", h=16)
        bre, bim, cre, cim, Bre, Bim, tb = [b3() for _ in range(7)]
        self.dma(bre, self.o_bre[i].rearrange("p (g h) -> p g h", h=16), [], [pk_], pk_)
        self.dma(bim, self.o_bim[i].rearrange("p (g h) -> p g h", h=16), [], [pk_], pk_)
        self.dma(cre, self.o_cre[i].rearrange("p (g h) -> p g h", h=16), [], [pk_], pk_)
        self.dma(cim, self.o_cim[i].rearrange("p (g h) -> p g h", h=16), [], [pk_], pk_)
        dcol = A.f32(32)
        self.dma(dcol, self.o_dcol[i], [], [pk_], pk_)
        R = [pk_]
        self.actf(dtt, ldt, AF.Exp, R, ["dtt"])
        self.tt(u1, lr, dtt, ALU.mult, R + ["dtt"], ["u1"])
        self.actf(mag, u1, AF.Exp, ["u1"], ["mag"])
        self.tt(ang, li, dtt, ALU.mult, R + ["dtt"], ["ang"])
        self.cp(sarg, ang, ["ang"], ["sarg"], eng="dve")
        self.ts(carg, ang, float(np.pi / 2), ALU.add, ["ang"], ["carg"])
        self.reduce_angle(sarg, t1, "sarg")
        self.reduce_angle(carg, t1, "carg")
        self.actf(sarg, sarg, AF.Sin, ["sarg"], ["sarg"])
        self.actf(carg, carg, AF.Sin, ["carg"], ["carg"])
        self.tt(are, mag, carg, ALU.mult, ["mag", "carg"], ["are"])
        self.tt(aim, mag, sarg, ALU.mult, ["mag", "sarg"], ["aim"])
        self.tt(den, lr, lr, ALU.mult, R, ["den"])
        self.tt(u1, li, li, ALU.mult, R, ["u1"])
        self.tt(den, den, u1, ALU.add, ["den", "u1"], ["den"])
        self.S.add("dve", lambda e: e.reciprocal(out=den, in_=den), reads=["den"], writes=["den"])
        self.ts(nr, are, -1.0, ALU.add, ["are"], ["nr"])
        self.tt(cfr, nr, lr, ALU.mult, ["nr"] + R, ["cfr"])
        self.tt(u1, aim, li, ALU.mult, ["aim"] + R, ["u1"])
        self.tt(cfr, cfr, u1, ALU.add, ["cfr", "u1"], ["cfr"])
        self.tt(cfr, cfr, den, ALU.mult, ["cfr", "den"], ["cfr"])
        self.tt(cfi, aim, lr, ALU.mult, ["aim"] + R, ["cfi"])
        self.tt(u2, nr, li, ALU.mult, ["nr"] + R, ["u2"])
        self.tt(cfi, cfi, u2, ALU.subtract, ["cfi", "u2"], ["cfi"])
        self.tt(cfi, cfi, den, ALU.mult, ["cfi", "den"], ["cfi"])
        bc3 = lambda a: a.unsqueeze(2).to_broadcast([P, 16, 16])
        self.cmul(Bre, Bim, bc3(cfr), bc3(cfi), bre, bim, tb, ["cfr", "cfi"] + R, ["Bre", "Bim", "tb"])
        if PREP_CUT <= 1:
            return
        Pre = A.f32(144).rearrange("p (g j) -> p g j", j=9)
        Pim = A.f32(144).rearrange("p (g j) -> p g j", j=9)
        Qre = A.f32(128).rearrange("p (g j) -> p g j", j=8)
        Qim = A.f32(128).rearrange("p (g j) -> p g j", j=8)
        Vre = A.f32(128).rearrange("p (g j) -> p g j", j=8)
        Vim = A.f32(128).rearrange("p (g j) -> p g j", j=8)
        self.memset(Pre[:, :, 0], 1.0, ["Pre"], eng="dve")
        self.memset(Pim[:, :, 0], 0.0, ["Pim"], eng="dve")
        self.memset(Qre[:, :, 0], 1.0, ["Qre"], eng="dve")
        self.memset(Qim[:, :, 0], 0.0, ["Qim"], eng="dve")
        for j in range(1, 9):
            self.cmul(Pre[:, :, j], Pim[:, :, j], Pre[:, :, j - 1], Pim[:, :, j - 1], are, aim, u1, ["Pre", "Pim", "are", "aim"], ["Pre", "Pim", "u1"])
        ire, iim = sm(), sm()
        self.tt(u2, mag, mag, ALU.mult, ["mag"], ["u2"])
        self.S.add("dve", lambda e: e.reciprocal(out=u2, in_=u2), reads=["u2"], writes=["u2"])
        self.tt(ire, are, u2, ALU.mult, ["are", "u2"], ["ire"])
        self.stt(iim, aim, -1.0, u2, ALU.mult, ALU.mult, ["aim", "u2"], ["iim"])
        for j in range(1, 8):
            self.cmul(Qre[:, :, j], Qim[:, :, j], Qre[:, :, j - 1], Qim[:, :, j - 1], ire, iim, u1, ["Qre", "Qim", "ire", "iim"], ["Qre", "Qim", "u1"])
        for s_ in range(8):
            self.cp(Vre[:, :, s_], Pre[:, :, 7 - s_], ["Pre"], ["Vre"], eng="dve")
            self.cp(Vim[:, :, s_], Pim[:, :, 7 - s_], ["Pim"], ["Vim"], eng="dve")
        c8 = [T[:, k * 128:(k + 1) * 128].rearrange("p (g j) -> p g j", j=8) for k in range(3)]
        c64 = [T[:, 384 + k * 128:384 + (k + 1) * 128].rearrange("p (g j) -> p g j", j=8) for k in range(3)]
        c512 = [T[:, 768 + k * 16:768 + (k + 1) * 16] for k in range(3)]
        self.cp(c8[0][:, :, 0], Pre[:, :, 8], ["Pre"], ["c8"], eng="dve")
        self.cp(c8[1][:, :, 0], Pim[:, :, 8], ["Pim"], ["c8"], eng="dve")
        for j in range(1, 8):
            self.cmul(c8[0][:, :, j], c8[1][:, :, j], c8[0][:, :, j - 1], c8[1][:, :, j - 1], c8[0][:, :, 0], c8[1][:, :, 0], u1, ["c8"], ["c8", "c8", "u1"])
        self.cp(c64[0][:, :, 0], c8[0][:, :, 7], ["c8"], ["c64"], eng="dve")
        self.cp(c64[1][:, :, 0], c8[1][:, :, 7], ["c8"], ["c64"], eng="dve")
        for j in range(1, 8):
            self.cmul(c64[0][:, :, j], c64[1][:, :, j], c64[0][:, :, j - 1], c64[1][:, :, j - 1], c64[0][:, :, 0], c64[1][:, :, 0], u1, ["c64"], ["c64", "c64", "u1"])
        self.cp(c512[0], c64[0][:, :, 7], ["c64"], ["c512"], eng="dve")
        self.cp(c512[1], c64[1][:, :, 7], ["c64"], ["c512"], eng="dve")
        self.ts(c8[2], c8[1], -1.0, ALU.mult, ["c8"], ["c8n"])
        self.ts(c64[2], c64[1], -1.0, ALU.mult, ["c64"], ["c64n"])
        self.ts(c512[2], c512[1], -1.0, ALU.mult, ["c512"], ["c512n"])
        if PREP_CUT <= 2:
            return
        big = lambda: A.f32(2048).rearrange("p (g s h) -> p g s h", s=8, h=16)
        Lre, Lim, Rre, nRim, tbig = [big() for _ in range(5)]
        bp = lambda a: a.unsqueeze(3).to_broadcast([P, 16, 8, 16])
        bb = lambda a: a.unsqueeze(2).to_broadcast([P, 16, 8, 16])
        self.cmul(Lre, Lim, bp(Qre), bp(Qim), bb(Bre), bb(Bim), tbig, ["Qre", "Qim", "Bre", "Bim"], ["Lre", "Lim", "tbig"])
        self.cmul(Rre, nRim, bp(Pre[:, :, 0:8]), bp(Pim[:, :, 0:8]), bb(cre), bb(cim), tbig, ["Pre", "Pim"] + R, ["Rre", "nRim", "tbig"], neg_im=True)
        if PREP_CUT <= 3:
            return
        ident32 = A.f32(128)
        ones32 = A.f32(128)
        maskST = A.f32(128)
        tK = [A.f32(128) for _ in range(2)]
        self.memset(ones32, 1.0, ["ones32"])
        self.S.add("pool", lambda e: e.affine_select(out=ident32, in_=ones32, pattern=[[-1, 128]], compare_op=ALU.is_equal,
                                                    fill=0.0, base=0, channel_multiplier=1), reads=["ones32"], writes=["ident32"])
        self.S.add("pool", lambda e: e.affine_select(out=maskST.rearrange("p (t h) -> p t h", h=16), in_=ones32.rearrange("p (t h) -> p t h", h=16),
                                                    pattern=[[16, 8], [0, 16]], compare_op=ALU.is_ge, fill=0.0, base=15, channel_multiplier=-1),
                   reads=["ones32"], writes=["maskST"])
        bigb = lambda: A.bf16(2048).rearrange("p (g n) -> p g n", n=128)
        Lre_b, Lim_b, Rre_b, nRim_b = [bigb() for _ in range(4)]
        f3 = lambda a: a.rearrange("p g s h -> p g (s h)")
        self.cp(Lre_b, f3(Lre), ["Lre"], ["Lre_b"], eng="dve")
        self.cp(Lim_b, f3(Lim), ["Lim"], ["Lim_b"], eng="act")
        self.cp(Rre_b, f3(Rre), ["Rre"], ["Rre_b"], eng="dve")
        self.cp(nRim_b, f3(nRim), ["nRim"], ["nRim_b"], eng="act")
        Kblk = V["Kblk"]
        if PREP_CUT <= 3.2:
            return
        for g0 in range(0, 32, 8):
            bks = [self.bank(), self.bank()]
            for q in range(8):
                g = g0 + q
                gp, g2 = g // 2, g % 2
                rs = slice(64 * g2, 64 * g2 + 64)
                pbk, pk = bks[g2]
                c0 = (q // 2) * 128
                self.mm(pbk[:, c0:c0 + 128], Lre_b[rs, gp, :], Rre_b[rs, gp, :], True, False, ["Lre_b", "Rre_b"], [pk])
                self.mm(pbk[:, c0:c0 + 128], Lim_b[rs, gp, :], nRim_b[rs, gp, :], False, True, ["Lim_b", "nRim_b"], [pk])
            for q in range(8):
                g = g0 + q
                g2 = g % 2
                pbk, pk = bks[g2]
                c0 = (q // 2) * 128
                tk = tK[q % 2]
                self.tt(tk, pbk[:, c0:c0 + 128], maskST, ALU.mult, [pk, "maskST"], ["tK%d" % (q % 2)])
                self.stt(Kblk[:, g, :], ident32, dcol[:, g:g + 1], tk, ALU.mult, ALU.add, ["ident32", "tK%d" % (q % 2)] + R, ["Kblk"])
        if PREP_CUT <= 4:
            return
        Wre, Wim = Rre, nRim
        self.cmul(Wre, Wim, bp(Vre), bp(Vim), bb(Bre), bb(Bim), tbig, ["Vre", "Vim", "Bre", "Bim"], ["Rre", "nRim", "tbig"])
        self.cp(Rre_b, f3(Wre), ["Rre"], ["Rre_b"], eng="dve")
        self.cp(nRim_b, f3(Wim), ["nRim"], ["nRim_b"], eng="act")
        for src, dst, sk in ((Rre_b, V["Wa_re"], "Rre_b"), (nRim_b, V["Wa_im"], "nRim_b")):
            for g0 in range(0, 16, 4):
                pbk, pk = self.bank()
                pvw = pbk[:, :].bitcast(BF16)
                for q in range(4):
                    self.tp(pvw[:, q * 128:(q + 1) * 128], src[:, g0 + q, :], self.ident, [sk, "cst"], [pk])
                self.cp(dst[:, g0:g0 + 4, :], pvw[:, 0:512].rearrange("p (g n) -> p g n", n=128), [pk], ["Wa"])
        if PREP_CUT <= 5:
            return
        Cre32, Cim32 = Lre, Lim
        self.cmul(Cre32, Cim32, bp(Pre[:, :, 1:9]), bp(Pim[:, :, 1:9]), bb(cre), bb(cim), tbig, ["Pre", "Pim"] + R, ["Lre", "Lim", "tbig"], neg_im=True)
        self.cp(V["CAre"], Cre32.rearrange("p g s h -> p g (s h)"), ["Lre"], ["CA"], eng="dve")
        self.cp(V["nCAim"], Cim32.rearrange("p g s h -> p g (s h)"), ["Lim"], ["CA"], eng="dve")
        V["c8"], V["c64"], V["c512"] = c8, c64, c512

    def s5_core(self, V):
        A = self.A
        usm = V["usm"]
        c8, c64, c512 = V["c8"], V["c64"], V["c512"]
        Zs = A.bf16(8 * 240).rearrange("p (g x) -> p g x", x=240)
        onesb = A.bf16(128)
        self.memset(onesb, 1.0, ["onesb"])
        self.memset(Zs, 0.0, ["Zs"])
        self.S.add("pool", lambda e: e.affine_select(out=Zs[:, :, 112:128], in_=onesb.rearrange("p (a b) -> p a b", b=16),
                                                    pattern=[[-16, 8], [-1, 16]], compare_op=ALU.is_equal, fill=0.0, base=0, channel_multiplier=1),
                   reads=["onesb", "Zs"], writes=["Zs"])
        U8 = A.bf16(8 * 512).rearrange("p (g c) -> p g c", c=512)
        X2 = [A.f32(1024).rearrange("p (r c) -> p r c", r=2) for _ in range(4)]
        X = [[x2[:, 0, :], x2[:, 1, :]] for x2 in X2]
        Sp = [[A.bf16(512) for _ in range(2)] for _ in range(4)]
        for pp in range(4):
            for r in range(2):
                self.memset(Sp[pp][r][:, 0:1], 0.0, [("Sp", pp, r)])
        for ct in range(4):
            uk = [("usm", ct, s_) for s_ in range(8)]
            for gq in range(8):
                pbk, pk = self.bank()
                for s_ in range(8):
                    self.mm(pbk[:, :], Zs[:, gq, 112 - 16 * s_:240 - 16 * s_], usm[:, ct, s_, :], s_ == 0, s_ == 7, ["Zs", uk[s_]], [pk])
                self.cp(U8[:, gq, :], pbk[:, :], [pk], [("U8", gq)], eng=("act" if gq % 2 == 0 else "dve"))
            for pp in range(4):
                gp = 4 * ct + pp
                for r, Wn in ((0, "Wa_re"), (1, "Wa_im")):
                    pbk, pk = self.bank()
                    self.mm(pbk[0:64, :], V[Wn][:, gp, 0:64], U8[:, 2 * pp, :], True, True, ["Wa", ("U8", 2 * pp)], [pk])
                    self.mm(pbk[64:128, :], V[Wn][:, gp, 64:128], U8[:, 2 * pp + 1, :], True, True, ["Wa", ("U8", 2 * pp + 1)], [pk])
                    self.cp(X[pp][r], pbk[:, :], [pk], [("X", pp, r)])
            steps = []
            for pp in range(4):
                gp = 4 * ct + pp
                xb = X2[pp]
                kx = [("X", pp, 0), ("X", pp, 1)]
                ckeys = ["c8", "c64", "c512", "c8n", "c64n", "c512n"]
                lst = []

                def cmac(o_b, s_b, cr, ci, cni, lst=lst, kx=kx, ckeys=ckeys):
                    lst.append((o_b, s_b, cr, o_b, kx + ckeys, kx))
                    lst.append((o_b[:, 0], s_b[:, 1], cni, o_b[:, 0], kx + ckeys, kx))
                    lst.append((o_b[:, 1], s_b[:, 0], ci, o_b[:, 1], kx + ckeys, kx))
                v3 = xb.rearrange("p r (m j) -> p r m j", j=8)
                vz = xb.rearrange("p r (q j w) -> p r q j w", j=8, w=8)[:, :, :, :, 7]
                vw = xb.rearrange("p r (q w) -> p r q w", w=64)[:, :, :, 63]
                co = lambda c, j: (c[0][:, gp, j:j + 1], c[1][:, gp, j:j + 1], c[2][:, gp, j:j + 1])
                for j in range(1, 8):
                    cmac(v3[:, :, :, j], v3[:, :, :, j - 1], *co(c8, 0))
                for j in range(1, 8):
                    cmac(vz[:, :, :, j], vz[:, :, :, j - 1], *co(c64, 0))
                c5 = (c512[0][:, gp:gp + 1], c512[1][:, gp:gp + 1], c512[2][:, gp:gp + 1])
                for q in range(1, 8):
                    cmac(vw[:, :, q:q + 1], vw[:, :, q - 1:q], *c5)
                for j in range(0, 7):
                    cmac(vz[:, :, 1:8, j], vw[:, :, 0:7], *co(c64, j))
                for j in range(0, 7):
                    cmac(v3[:, :, 1:64, j], v3[:, :, 0:63, 7], *co(c8, j))
                steps.append(lst)
            for k in range(len(steps[0])):
                for pp in range(4):
                    o_, s_in, c_, a_, rd, wr = steps[pp][k]
                    self.stt(o_, s_in, c_, a_, ALU.mult, ALU.add, rd, wr)
            for pp in range(4):
                for r in range(2):
                    self.cp(Sp[pp][r][:, 1:512], X[pp][r][:, 0:511], [("X", pp, r)], [("Sp", pp, r)], eng=("act" if r == 0 else "pool"))
            for gq in range(8):
                g = 8 * ct + gq
                gp, g2 = g // 2, g % 2
                pp = gq // 2
                rs = slice(64 * g2, 64 * g2 + 64)
                pbk, pk = self.bank()
                self.mm(pbk[:, :], V["Kblk"][:, g, :], U8[:, gq, :], True, False, ["Kblk", ("U8", gq)], [pk])
                self.mm(pbk[:, :], V["CAre"][rs, gp, :], Sp[pp][0][rs, :], False, False, ["CA", ("Sp", pp, 0)], [pk])
                self.mm(pbk[:, :], V["nCAim"][rs, gp, :], Sp[pp][1][rs, :], False, True, ["CA", ("Sp", pp, 1)], [pk])
                self.cp(U8[:, gq, :], pbk[:, :], [pk], [("U8", gq)], eng=("act" if gq % 2 == 0 else "dve"))
            for t_ in range(8):
                pbk, pk = self.bank()
                for gq in range(8):
                    self.mm(pbk[:, :], Zs[:, t_, 112 - 16 * gq:240 - 16 * gq], U8[:, gq, :], gq == 0, gq == 7, ["Zs", ("U8", gq)], [pk])
                self.cp(usm[:, ct, t_, :], pbk[:, :], [pk], [("usm", ct, t_)], eng=("act" if t_ % 2 == 0 else "dve"))

    def pass_oa(self, L, V, hin, hin_key):
        A = self.A
        self.set_ring(range(8))
        self.s5_prep(L, V)
        if OA_STAGE < 2:
            return
        self.S.barrier()
        A.reset()
        Wu, wk, usm = V["Wu"], V["wk"], V["usm"]
        gn = A.f32(1024)
        pk_ = "par_oa"
        self.dma(gn, self.norm_g[L:L + 1, :].partition_broadcast(128), [], [pk_], pk_)
        hA = [A.f32(1024) for _ in range(2 * NS)]
        hnb = A.bf16(1024)
        junk = A.bf16(1024)
        ss = A.f32(8)
        hnT2 = [A.bf16(8 * MT).rearrange("p (k t) -> p k t", t=MT) for _ in range(2)]
        nb = dict(hA=hA, hnb=hnb, gn=gn, hnT=None, ss=ss, gkey=pk_, junk=junk)
        for m in range(NMT):
            hnT = hnT2[m % 2]
            hk = "hnT%d" % (m % 2)
            nb["hnT"] = hnT
            nb["hk"] = hk
            self.norm_tile(m, hin, hin_key, nb, store_scr=True)
            for ct in range(4):
                pbk, pk = self.bank()
                for dk in range(8):
                    self.mm(pbk[:, 0:MT], Wu[:, dk, ct * 128:(ct + 1) * 128], hnT[:, dk, :], dk == 0, dk == 7, [hk, wk], [pk])
                self.cp(usm[:, ct, :, m * 32:(m + 1) * 32], pbk[:, 0:MT].rearrange("p (c s) -> p s c", s=8), [pk],
                        [("usm", ct, s_) for s_ in range(8)], eng=("act" if ct % 2 == 0 else "dve"))
        if OA_STAGE < 3:
            return
        self.S.barrier()
        A.reset()
        self.s5_core(V)

    def w_ob1(self, L, slot):
        i = L // 2
        wk = ("w", slot)
        Win = self.load_w_rows(slot, 0, self.o_w_in[i], 0, 8, 1024, 3072, wk)
        Wo = self.load_w_rows(slot, 24576, self.o_w_out[i], 512, 8, 0, 1024, wk)
        wsT = self.wa[slot][:, 32768:33792].rearrange("p (h t) -> p h t", t=128)
        self.dma(wsT, self.o_wsT[i].rearrange("p (h t) -> p h t", t=128), [], [wk], wk, eng="pool")
        return (Win, Wo, wsT, wk)

    def pass_ob1(self, L, W, hin, hin_key, hmid, hmid_key):
        i = L // 2
        A = self.A
        self.set_ring(range(8))
        Win, Wo, wsT, wk = W
        self.S.add("pool", lambda e: e.affine_select(out=wsT, in_=wsT, pattern=[[0, 8], [1, 128]], compare_op=ALU.is_ge, fill=0.0,
                                                    base=0, channel_multiplier=-1), reads=[wk], writes=["wsTm"])
        pk_ = "par_ob1"
        lng = A.f32(1024)
        lnb = A.f32(1024)
        bsb_f = A.f32(1024)
        bsb = bsb_f.rearrange("p (h t) -> p h t", t=128)
        self.dma(lng, self.o_lng[i].partition_broadcast(128), [], [pk_], pk_)
        self.dma(lnb, self.o_lnb[i].partition_broadcast(128), [], [pk_], pk_)
        self.dma(bsb_f, self.o_bs[i].partition_broadcast(128), [], [pk_], pk_)
        hnT2 = [A.bf16(8 * MT).rearrange("p (k t) -> p k t", t=MT) for _ in range(2)]
        vtmp = A.f32(1024)
        vnT2 = [A.bf16(NS * 1024).rearrange("p (j n) -> p j n", n=1024) for _ in range(2)]
        st4 = A.f32(16)
        junk = A.bf16(512)
        szt = [A.f32(MT) for _ in range(2)]
        t1 = [A.f32(MT) for _ in range(2)]
        yT2 = [A.bf16(8 * MT).rearrange("p (k t) -> p k t", t=MT) for _ in range(2)]
        hB = [A.f32(1024) for _ in range(2 * NS)]
        for m in range(NMT):
            hnT = hnT2[m % 2]
            hk = "hnT%d" % (m % 2)
            yT = yT2[m % 2]
            yk = "yT%d" % (m % 2)
            vnT = vnT2[m % 2]
            vq = "vnT%d_" % (m % 2)
            self.load_hnT(m, hnT, hk)
            for j in range(NS):
                pbs = []
                for n2 in range(2):
                    pv_, pvk = self.bank()
                    for dk in range(8):
                        self.mm(pv_[:, :], hnT[:, dk, j * 128:(j + 1) * 128], Win[:, dk, 1024 + n2 * 512:1024 + (n2 + 1) * 512],
                                dk == 0, dk == 7, [hk, wk], [pvk])
                    self.actf(junk, pv_[:, :], AF.Identity, [pvk], ["junk", "st_s%d" % n2], accum=st4[:, n2:n2 + 1])
                    self.actf(junk, pv_[:, :], AF.Square, [pvk], ["junk", "st_q%d" % n2], accum=st4[:, 2 + n2:3 + n2])
                    pbs.append((pv_, pvk))
                self.tt(st4[:, 4:5], st4[:, 0:1], st4[:, 1:2], ALU.add, ["st_s0", "st_s1"], ["st_m"])
                self.ts(st4[:, 4:5], st4[:, 4:5], 1.0 / 1024.0, ALU.mult, ["st_m"], ["st_m"])
                self.tt(st4[:, 5:6], st4[:, 2:3], st4[:, 3:4], ALU.add, ["st_q0", "st_q1"], ["st_v"])
                self.tt(st4[:, 6:7], st4[:, 4:5], st4[:, 4:5], ALU.mult, ["st_m"], ["st_mm"])
                self.stt(st4[:, 5:6], st4[:, 5:6], 1.0 / 1024.0, st4[:, 6:7], ALU.mult, ALU.subtract, ["st_v", "st_mm"], ["st_v"])
                self.rsqrt_small(st4[:, 7:8], st4[:, 5:6], 1.0, ["st_v"], ["st_r"], "st_rt")
                self.stt(st4[:, 8:9], st4[:, 4:5], -1.0, st4[:, 7:8], ALU.mult, ALU.mult, ["st_m", "st_r"], ["st_n"])
                for n2 in range(2):
                    pv_, pvk = pbs[n2]
                    sl_ = slice(n2 * 512, (n2 + 1) * 512)
                    self.ts(vtmp[:, sl_], pv_[:, :], st4[:, 7:8], ALU.mult, [pvk, "st_r", "st_n"], ["vtmp%d" % n2], s2=st4[:, 8:9], op1=ALU.add)
                    self.tt(vtmp[:, sl_], vtmp[:, sl_], lng[:, sl_], ALU.mult, ["vtmp%d" % n2, pk_], ["vtmp%d" % n2], eng="pool")
                    self.tt(vnT[:, j, sl_], vtmp[:, sl_], lnb[:, sl_], ALU.add, ["vtmp%d" % n2, pk_], [vq + str(j)], eng="pool")
            for hd in range(8):
                b = hd % 2
                psv, psvk = self.bank()
                for j in range(NS):
                    self.mm(psv[:, j * 128:(j + 1) * 128], vnT[:, j, hd * 128:(hd + 1) * 128], wsT[:, hd, :], True, True,
                            [vq + str(j), "wsTm"], [psvk])
                pu, puk = self.bank()
                for dk in range(8):
                    self.mm(pu[:, 0:MT], Win[:, dk, hd * 128:(hd + 1) * 128], hnT[:, dk, :], dk == 0, dk == 7, [hk, wk], [puk])
                pz, pzk = self.bank()
                for dk in range(8):
                    self.mm(pz[:, 0:MT], Win[:, dk, 2048 + hd * 128:2048 + (hd + 1) * 128], hnT[:, dk, :], dk == 0, dk == 7, [hk, wk], [pzk])
                self.actf(szt[b], pz[:, 0:MT], AF.Silu, [pzk], ["szt%d" % b])
                self.tt(t1[b].rearrange("p (j t) -> p j t", t=128), psv[:, 0:MT].rearrange("p (j t) -> p j t", t=128),
                        bsb[:, hd, :].unsqueeze(1).to_broadcast([P, NS, 128]), ALU.add, [psvk, pk_], ["t1_%d" % b])
                self.tt(t1[b], pu[:, 0:MT], t1[b], ALU.mult, [puk, "t1_%d" % b], ["t1_%d" % b])
                self.tt(yT[:, hd, :], t1[b], szt[b], ALU.mult, ["t1_%d" % b, "szt%d" % b], [yk])
            self.outproj_residual(m, yT, 8, Wo, wk, hin, hin_key, hmid, hmid_key, hB, yk=yk)

    def w_ob2(self, L, slot):
        i = L // 2
        wk = ("w", slot)
        Wz = self.load_w_rows(slot, 0, self.o_w_in[i], 0, 8, 512, 512, wk)
        Wg = self.load_w_rows(slot, 4096, self.o_w_glu[i], 0, 4, 0, 512, wk)
        Wo = self.load_w_rows(slot, 6144, self.o_w_out[i], 0, 4, 0, 1024, wk)
        usm = self.wa[slot][:, 16384:32768].rearrange("p (c s n) -> p c s n", c=4, s=8)
        return (Wz, Wg, Wo, usm, wk)

    def pass_ob2(self, L, W, hmid, hmid_key, last):
        i = L // 2
        A = self.A
        self.set_ring(range(8))
        Wz, Wg, Wo, usm, wk = W
        pk_ = "par_ob2"
        bglu = A.f32(4)
        self.dma(bglu, self.o_bglu[i], [], [pk_], pk_)
        gfin = None
        if last:
            gfin = A.f32(1024)
            self.dma(gfin, self.final_g.partition_broadcast(128), [], ["gfin"], "par_gfin")
        hnT2 = [A.bf16(8 * MT).rearrange("p (k t) -> p k t", t=MT) for _ in range(2)]
        ge322 = [A.f32(4 * MT).rearrange("p (c t) -> p c t", t=MT) for _ in range(2)]
        gebf2 = [A.bf16(4 * MT).rearrange("p (c t) -> p c t", t=MT) for _ in range(2)]
        sgm = [A.f32(MT) for _ in range(2)]
        szt = [A.f32(MT) for _ in range(2)]
        yT2 = [A.bf16(4 * MT).rearrange("p (k t) -> p k t", t=MT) for _ in range(2)]
        hB = [A.f32(1024) for _ in range(2 * NS)]
        junk = A.bf16(1024)
        ss = A.f32(8)
        for m in range(NMT):
            par = m % 2
            hnT = hnT2[par]
            hk = "hnT%d" % par
            yT = yT2[par]
            yk = "yT%d" % par
            ge32 = ge322[par]
            gebf = gebf2[par]
            self.load_hnT(m, hnT, hk)
            for ct in range(4):
                self.actf(ge32[:, ct, :].rearrange("p (c s) -> p s c", s=8), usm[:, ct, :, m * 32:(m + 1) * 32], AF.Gelu_apprx_tanh,
                          [("usm", ct, s_) for s_ in range(8)], [("ge32", par, ct)])
                self.cp(gebf[:, ct, :], ge32[:, ct, :], [("ge32", par, ct)], [("gebf", par, ct)], eng="dve")
            for mt in range(4):
                b = mt % 2
                pg, pgk = self.bank()
                for kt in range(4):
                    self.mm(pg[:, 0:MT], Wg[:, kt, mt * 128:(mt + 1) * 128], gebf[:, kt, :], kt == 0, kt == 3, [("gebf", par, kt), wk], [pgk])
                self.actf(sgm[b], pg[:, 0:MT], AF.Sigmoid, [pgk, pk_], ["sgm%d" % b], bias=bglu[:, mt:mt + 1])
                pz, pzk = self.bank()
                for dk in range(8):
                    self.mm(pz[:, 0:MT], Wz[:, dk, mt * 128:(mt + 1) * 128], hnT[:, dk, :], dk == 0, dk == 7, [hk, wk], [pzk])
                self.actf(szt[b], pz[:, 0:MT], AF.Silu, [pzk], ["szt%d" % b])
                self.tt(sgm[b], sgm[b], ge32[:, mt, :], ALU.mult, ["sgm%d" % b, ("ge32", par, mt)], ["sgm%d" % b])
                self.tt(yT[:, mt, :], sgm[b], szt[b], ALU.mult, ["sgm%d" % b, "szt%d" % b], [yk])
            if last:
                self.outproj_residual(m, yT, 4, Wo, wk, hmid, hmid_key, self.out, "out", hB, final_g=gfin, ss=ss, junk=junk, yk=yk)
            else:
                self.outproj_residual(m, yT, 4, Wo, wk, hmid, hmid_key, hmid, hmid_key, hB, yk=yk)

    def build(self):
        passes = []
        hin, hin_key = self.x, "x"
        for L in range(self.n_layers):
            hmid = self.hbuf[(L + 1) % 2]
            hmid_key = ("hb", (L + 1) % 2)
            if getattr(self, "first_pass", 0) > 0:
                hin, hin_key = self.x, "x"
            if L % 2 == 0:
                passes.append((lambda slot, L=L: self.w_e1(L, slot),
                               lambda W, L=L, a=hin, ak=hin_key, b=hmid, bk=hmid_key: self.pass_e1(L, W, a, ak, b, bk)))
                passes.append((lambda slot, L=L: self.w_e2(L, slot),
                               lambda W, L=L, b=hmid, bk=hmid_key: self.pass_e2(L, W, b, bk)))
            else:
                last = (L == self.n_layers - 1) and self.final_norm
                passes.append((lambda slot, L=L: self.w_oa(L, slot),
                               lambda W, L=L, a=hin, ak=hin_key: self.pass_oa(L, W, a, ak)))
                passes.append((lambda slot, L=L: self.w_ob1(L, slot),
                               lambda W, L=L, a=hin, ak=hin_key, b=hmid, bk=hmid_key: self.pass_ob1(L, W, a, ak, b, bk)))
                passes.append((lambda slot, L=L: self.w_ob2(L, slot),
                               lambda W, L=L, b=hmid, bk=hmid_key, last=last: self.pass_ob2(L, W, b, bk, last)))
                self.fused_out = last
            hin, hin_key = hmid, hmid_key
        if self.max_passes is not None:
            passes = passes[getattr(self, 'first_pass', 0):self.max_passes]
        slot = 0
        Wn = passes[0][0](slot)
        for k, (wl, run) in enumerate(passes):
            self.S.barrier()
            self.A.reset()
            Wcur = Wn
            slot ^= 1
            if k + 1 < len(passes):
                Wn = passes[k + 1][0](slot)
            run(Wcur)
        if getattr(self, "fused_out", False) and self.max_passes is None:
            self.S.emit()
            return
        if getattr(self, "first_pass", 0) > 0 and self.max_passes is not None and self.max_passes <= 3:
            hin, hin_key = self.x, "x"
        self.S.barrier()
        A = self.A
        A.reset()
        cpb = [A.f32(1024) for _ in range(2)]
        for t in range(0 if not self.debug else SEQ // 128, SEQ // 128):
            b = t % 2
            self.dma(cpb[b], hin[t * 128:(t + 1) * 128, :], [(hin_key, t // NS, t % NS)], ["cpb%d" % b], "ld_cp%d" % b)
            self.dma(self.out[t * 128:(t + 1) * 128, :], cpb[b], ["cpb%d" % b], [("out", t)], "st_cp%d" % b)
        self.S.emit()


def build_program(n_layers=4, final_norm=True, max_passes=None, first_pass=0):
    nc = bass.Bass("TRN2", target_bir_lowering=False)
    st = ExitStack()
    with st:
        b = Builder(nc, st, n_layers=n_layers, final_norm=final_norm, max_passes=max_passes)
        b.first_pass = first_pass
        b.build()
    return nc


def host_layout(inputs):
    f = lambda a: np.ascontiguousarray(np.asarray(a, dtype=np.float32))
    g = {}
    g["norm_g"] = f(inputs["norm_g"])
    g["final_g"] = f(inputs["final_g"]).reshape(1, D)
    g["e_w_in"] = f(inputs["e_w_in"])
    g["e_w_a2"] = f(inputs["e_w_a2"])
    g["e_b_a"] = f(inputs["e_b_a"]).reshape(2, 1, 512)
    g["e_gla_g"] = f(inputs["e_gla_g"]).reshape(2, 1, 1024)
    cw = f(inputs["e_conv_w"])
    g["e_cw"] = f(cw.reshape(2, 31, 8, 128).transpose(0, 3, 2, 1))
    cpl = lambda a: f(f(a).reshape(2, 8, 128).transpose(0, 2, 1))
    g["e_cb"] = cpl(inputs["e_conv_b"])
    g["e_lg"] = cpl(inputs["e_cln_g"])
    g["e_lb"] = cpl(inputs["e_cln_b"])
    g["e_w_out"] = f(inputs["e_w_out"])
    g["o_w_in"] = f(inputs["o_w_in"])
    gp_l = lambda a: f(f(a).reshape(2, 16, 2, 64).transpose(0, 2, 3, 1).reshape(2, 128, 16))
    g["o_lamre"] = gp_l(inputs["o_lam_re"])
    g["o_lamim"] = gp_l(inputs["o_lam_im"])
    ldt = f(inputs["o_log_dt"]).reshape(2, 16, 2)
    g["o_logdt"] = f(np.broadcast_to(ldt.transpose(0, 2, 1)[:, :, None, :], (2, 2, 64, 16)).reshape(2, 128, 16))
    b_l = lambda a: f(f(a).reshape(2, 16, 2, 64, 16).transpose(0, 2, 3, 1, 4).reshape(2, 128, 256))
    g["o_bre"] = b_l(inputs["o_b_re"])
    g["o_bim"] = b_l(inputs["o_b_im"])
    c_l = lambda a: f(f(a).reshape(2, 16, 2, 16, 64).transpose(0, 2, 4, 1, 3).reshape(2, 128, 256))
    g["o_cre"] = c_l(inputs["o_c_re"])
    g["o_cim"] = c_l(inputs["o_c_im"])
    dd = f(inputs["o_d"]).reshape(2, 32, 16)
    g["o_dcol"] = f(np.broadcast_to(dd.transpose(0, 2, 1)[:, None, :, :], (2, 8, 16, 32)).reshape(2, 128, 32))
    g["o_w_glu"] = f(inputs["o_w_glu"])
    g["o_bglu"] = f(f(inputs["o_b_glu"]).reshape(2, 4, 128).transpose(0, 2, 1))
    g["o_lng"] = f(inputs["o_sg_ln_g"]).reshape(2, 1, 1024)
    g["o_lnb"] = f(inputs["o_sg_ln_b"]).reshape(2, 1, 1024)
    ws = f(inputs["o_w_s"])
    g["o_wsT"] = f(ws.transpose(0, 3, 1, 2).reshape(2, 128, 1024))
    g["o_bs"] = f(inputs["o_b_s"]).reshape(2, 1, 1024)
    g["o_w_out"] = f(inputs["o_w_out"])
    return g


def kernel(**inputs):
    x = np.asarray(inputs["x"], dtype=np.float32)
    shared = host_layout(inputs)
    nc = build_program()
    in_maps = []
    for c in range(8):
        m = dict(shared)
        m["x"] = np.ascontiguousarray(x[c])
        in_maps.append(m)
    res = run_bass_kernel_spmd(nc, in_maps, core_ids=list(range(8)))
    return np.stack([np.asarray(r["out"], dtype=np.float32) for r in res.results], axis=0)
```

```python
import numpy as np
import concourse.bass as bass
import concourse.mybir as mybir
from concourse.bass_utils import run_bass_kernel_spmd
from contextlib import ExitStack

F32 = mybir.dt.float32
BF16 = mybir.dt.bfloat16
AF = mybir.ActivationFunctionType
ALU = mybir.AluOpType

P = 128
SEQ = 4096
D = 1024
MT = 256
NMT = SEQ // MT
NS = MT // 128
EPS = 1e-6
EVEN_IN = 6160
ODD_IN = 4096
WA_N = 33792
OA_STAGE = 3
E1_HNT_BUFS = 1
PREP_CUT = 99


class _Op:
    __slots__ = ("eng", "fn", "deps", "edges", "signal", "dma_sem", "dma_val", "cnt", "cost", "seg", "idx", "fin", "placed")

    def __init__(self, eng, fn):
        self.eng = eng
        self.fn = fn
        self.deps = []
        self.edges = []
        self.signal = False
        self.dma_sem = None
        self.dma_val = 0
        self.cnt = 0
        self.cost = 200.0
        self.seg = 0
        self.idx = 0
        self.fin = 0.0
        self.placed = False


class Sched:
    ENGS = ("pe", "act", "dve", "pool", "sp")
    WINDOW = {"pe": 256, "act": 64, "dve": 64, "pool": 32, "sp": 32}

    def __init__(self, nc, stack):
        self.nc = nc
        self.stack = stack
        self.all_ops = []
        self.last_w = {}
        self.readers = {}
        self.dma_keys = {}
        self.last_dma = {}
        self.eng_sem = {}
        for e in ("pe", "act", "dve", "pool"):
            self.eng_sem[e] = stack.enter_context(nc.semaphore("es_" + e))
        self.seg = 0
        self.reorder = True

    def barrier(self):
        self.seg += 1

    def add(self, eng, fn, reads=(), writes=(), dma_key=None, cost=None):
        op = _Op(eng, fn)
        op.seg = self.seg
        op.idx = len(self.all_ops)
        is_dma = dma_key is not None
        if cost is not None:
            op.cost = float(cost)
        my_sem = None
        if is_dma:
            ent = self.dma_keys.get(dma_key)
            if ent is None:
                sem = self.stack.enter_context(self.nc.semaphore("ds_%d" % len(self.dma_keys)))
                ent = [sem, 0]
                self.dma_keys[dma_key] = ent
            my_sem = ent[0]
            prev = self.last_dma.get(dma_key)
            if prev is not None:
                op.edges.append(prev)
        cand = []
        for k in reads:
            w = self.last_w.get(k)
            if w is not None:
                cand.append(w)
        for k in writes:
            w = self.last_w.get(k)
            if w is not None:
                cand.append(w)
            for r in self.readers.get(k, ()):
                cand.append(r)
        seen = set(id(x) for x in op.edges)
        for d in cand:
            if id(d) in seen or d is op:
                continue
            seen.add(id(d))
            op.edges.append(d)
            d_is_dma = d.dma_sem is not None
            if d_is_dma:
                if is_dma and d.dma_sem is my_sem:
                    continue
            elif d.eng == eng and not is_dma and eng == "pe":
                continue
            op.deps.append(d)
        if is_dma:
            ent[1] += 16
            op.dma_sem = ent[0]
            op.dma_val = ent[1]
            self.last_dma[dma_key] = op
        for k in reads:
            self.readers.setdefault(k, []).append(op)
        for k in writes:
            self.last_w[k] = op
            self.readers[k] = []
        self.all_ops.append(op)
        return op

    def _schedule(self):
        order = {e: [] for e in self.ENGS}
        segs = {}
        for op in self.all_ops:
            segs.setdefault(op.seg, []).append(op)
        bar_points = {e: [] for e in self.ENGS}
        t_base = 0.0
        LAT = 120.0
        for sg in sorted(segs):
            ops = segs[sg]
            for e in self.ENGS:
                bar_points[e].append(len(order[e]))
            if not self.reorder:
                for op in ops:
                    order[op.eng].append(op)
                    op.placed = True
                continue
            queues = {e: [op for op in ops if op.eng == e] for e in self.ENGS}
            heads = {e: 0 for e in self.ENGS}
            free = {e: t_base for e in self.ENGS}
            remaining = len(ops)
            tmax = t_base
            while remaining:
                best = None
                best_key = None
                for e in self.ENGS:
                    q = queues[e]
                    h = heads[e]
                    while h < len(q) and q[h].placed:
                        h += 1
                    heads[e] = h
                    lim = min(len(q), h + self.WINDOW[e])
                    k = h
                    found = 0
                    while k < lim:
                        op = q[k]
                        k += 1
                        if op.placed:
                            continue
                        ok = True
                        rdy = free[e]
                        for d in op.edges:
                            if d.seg != sg:
                                continue
                            if not d.placed:
                                ok = False
                                break
                            f = d.fin + (LAT if d.eng != e or d.dma_sem is not None else 30.0)
                            if f > rdy:
                                rdy = f
                        if not ok:
                            continue
                        key = (rdy, op.idx)
                        if best_key is None or key < best_key:
                            best_key = key
                            best = op
                        found += 1
                        if found >= 6:
                            break
                op = best
                assert op is not None, "scheduler stuck (cyclic deps?)"
                st = best_key[0]
                op.placed = True
                if op.dma_sem is not None:
                    op.fin = st + op.cost
                    free[op.eng] = st + 60.0
                else:
                    op.fin = st + op.cost
                    free[op.eng] = op.fin
                if op.fin > tmax:
                    tmax = op.fin
                order[op.eng].append(op)
                remaining -= 1
            t_base = tmax
        self.est_ns = t_base
        return order, bar_points

    def emit(self):
        nc = self.nc
        order, bar_points = self._schedule()
        dma_by_seg = {}
        for op in self.all_ops:
            if op.dma_sem is not None:
                dma_by_seg.setdefault(op.seg, {})[id(op.dma_sem)] = op
        nseg = self.seg + 1
        for si in range(1, nseg):
            lasts = []
            for e in self.ENGS:
                pos = bar_points[e][si]
                for k in range(pos - 1, -1, -1):
                    if order[e][k].dma_sem is None:
                        lasts.append(order[e][k])
                        break
            dl = {}
            for sj in range(si):
                dl.update(dma_by_seg.get(sj, {}))
            lasts += list(dl.values())
            for e in self.ENGS:
                pos = bar_points[e][si]
                if pos < len(order[e]):
                    op = order[e][pos]
                    have = set(id(x) for x in op.deps)
                    for d in lasts:
                        if id(d) in have:
                            continue
                        if d.dma_sem is None and d.eng == e and op.dma_sem is None:
                            continue
                        op.deps.append(d)
        for e in self.ENGS:
            for op in order[e]:
                for d in op.deps:
                    if d.dma_sem is None:
                        d.signal = True
        for e in ("pe", "act", "dve", "pool"):
            for op in reversed(order[e]):
                if op.dma_sem is None:
                    op.signal = True
                    break
        for e in self.ENGS:
            c = 0
            for op in order[e]:
                if op.dma_sem is None and op.signal:
                    c += 1
                op.cnt = c
        sched = self

        def replay(ename, eng):
            waited = {}
            for op in order[ename]:
                for d in op.deps:
                    if d.dma_sem is not None:
                        sem, val = d.dma_sem, d.dma_val
                    else:
                        sem, val = sched.eng_sem[d.eng], d.cnt
                    key = id(sem)
                    if waited.get(key, 0) >= val:
                        continue
                    waited[key] = val
                    eng.wait_ge(sem, val)
                ins = op.fn(eng)
                if op.dma_sem is not None:
                    ins.then_inc(op.dma_sem, 16)
                elif op.signal:
                    ins.then_inc(sched.eng_sem[ename], 1)
            if ename == "sp":
                for e in ("pe", "act", "dve", "pool"):
                    last = None
                    for op in order[e]:
                        if op.dma_sem is None:
                            last = op
                    if last is not None:
                        eng.wait_ge(sched.eng_sem[e], last.cnt)
                for k, (sem, cnt) in sched.dma_keys.items():
                    eng.wait_ge(sem, cnt)

        with nc.Block() as block:
            @block.tensor
            def _(e):
                replay("pe", e)

            @block.scalar
            def _(e):
                replay("act", e)

            @block.vector
            def _(e):
                replay("dve", e)

            @block.gpsimd
            def _(e):
                replay("pool", e)

            @block.sync
            def _(e):
                replay("sp", e)


class Arena:
    def __init__(self, tile_ap, n, name):
        self.t = tile_ap
        self.n = n
        self.off = 0
        self.name = name
        self.gen = 0

    def reset(self):
        self.off = 0
        self.gen += 1

    def f32(self, n):
        a = self.t[:, self.off:self.off + n]
        self.off += n
        assert self.off <= self.n, (self.name, self.off, self.n)
        return a

    def bf16(self, n):
        w = (n + 1) // 2
        a = self.t[:, self.off:self.off + w].bitcast(BF16)
        self.off += w
        assert self.off <= self.n, (self.name, self.off, self.n)
        return a


class Builder:
    def __init__(self, nc, st, n_layers=4, final_norm=True, max_passes=None):
        self.max_passes = max_passes
        self.debug = max_passes == 1
        self.nc = nc
        self.st = st
        self.S = Sched(nc, st)
        self.n_layers = n_layers
        self.final_norm = final_norm
        self.uid = 0
        S = self.S
        dt_in = lambda name, shape: nc.dram_tensor(name, shape, F32, kind="ExternalInput").ap()
        self.x = dt_in("x", [SEQ, D])
        self.norm_g = dt_in("norm_g", [4, D])
        self.final_g = dt_in("final_g", [1, D])
        self.e_w_in = dt_in("e_w_in", [2, D, EVEN_IN])
        self.e_w_a2 = dt_in("e_w_a2", [2, 16, 512])
        self.e_b_a = dt_in("e_b_a", [2, 1, 512])
        self.e_gla_g = dt_in("e_gla_g", [2, 1, 1024])
        self.e_cw = dt_in("e_cw", [2, P, 8, 31])
        self.e_cb = dt_in("e_cb", [2, P, 8])
        self.e_lg = dt_in("e_lg", [2, P, 8])
        self.e_lb = dt_in("e_lb", [2, P, 8])
        self.e_w_out = dt_in("e_w_out", [2, 2048, D])
        self.o_w_in = dt_in("o_w_in", [2, D, ODD_IN])
        self.o_lamre = dt_in("o_lamre", [2, P, 16])
        self.o_lamim = dt_in("o_lamim", [2, P, 16])
        self.o_logdt = dt_in("o_logdt", [2, P, 16])
        self.o_bre = dt_in("o_bre", [2, P, 256])
        self.o_bim = dt_in("o_bim", [2, P, 256])
        self.o_cre = dt_in("o_cre", [2, P, 256])
        self.o_cim = dt_in("o_cim", [2, P, 256])
        self.o_dcol = dt_in("o_dcol", [2, P, 32])
        self.o_w_glu = dt_in("o_w_glu", [2, 512, 512])
        self.o_bglu = dt_in("o_bglu", [2, P, 4])
        self.o_lng = dt_in("o_lng", [2, 1, 1024])
        self.o_lnb = dt_in("o_lnb", [2, 1, 1024])
        self.o_wsT = dt_in("o_wsT", [2, P, 1024])
        self.o_bs = dt_in("o_bs", [2, 1, 1024])
        self.o_w_out = dt_in("o_w_out", [2, 1536, D])
        self.out = nc.dram_tensor("out", [SEQ, D], F32, kind="ExternalOutput").ap()
        self.hbuf = [nc.dram_tensor("hbuf%d" % i, [SEQ, D], F32, kind="Internal").ap() for i in range(2)]
        self.hnscr = nc.dram_tensor("hnscr", [NMT, P, 8 * MT], BF16, kind="Internal").ap()

        sb = lambda name, shape, dt: st.enter_context(nc.sbuf_tensor(name, shape, dt))
        self.wa = [sb("wa%d" % i, [P, WA_N], BF16) for i in range(2)]
        self.cst = sb("cst", [P, 4 * 128], BF16)
        self.ident = self.cst[:, 0:128]
        self.triInc = self.cst[:, 128:256]
        self.triRev = self.cst[:, 256:384]
        self.cmask = self.cst[:, 384:512]
        self.pst = sb("pst", [P, 1024], F32)
        rem = int(nc.sbuf_bytes_remaining) - 256
        self.act_n = rem // 4
        self.act_t = sb("actarena", [P, self.act_n], F32)
        self.A = Arena(self.act_t, self.act_n, "act")
        self.pb = [st.enter_context(nc.psum_tensor("pb%d" % i, [P, 512], F32)) for i in range(8)]
        self.ring = list(range(8))
        self.ring_i = 0
        self.slot = 0
        self._consts()

    def key(self, name):
        self.uid += 1
        return "%s#%d" % (name, self.uid)

    def bank(self):
        i = self.ring[self.ring_i % len(self.ring)]
        self.ring_i += 1
        return self.pb[i], ("pb", i)

    def set_ring(self, banks):
        self.ring = list(banks)
        self.ring_i = 0

    def add(self, *a, **k):
        return self.S.add(*a, **k)

    @staticmethod
    def _n(ap):
        n = 1
        for d in ap.shape[1:]:
            n *= int(d)
        return n

    def mm(self, out, lhsT, rhs, start, stop, reads, writes):
        c = max(self._n(out), 64) / 2.0 + 10.0
        if lhsT.dtype == F32:
            c *= 4.0
        self.S.add("pe", lambda e: e.matmul(out, lhsT=lhsT, rhs=rhs, start=start, stop=stop), reads=reads, writes=writes, cost=c)

    def tp(self, out, in_, ident, reads, writes):
        self.S.add("pe", lambda e: e.transpose(out=out, in_=in_, identity=ident), reads=reads, writes=writes, cost=80.0)

    def actf(self, out, in_, func, reads, writes, bias=None, scale=None, accum=None):
        kw = {}
        if bias is not None:
            kw["bias"] = bias
        if scale is not None:
            kw["scale"] = scale
        if accum is not None:
            kw["accum_out"] = accum
        c = self._n(out) / 1.4 + 230.0 + (100.0 if accum is not None else 0.0)
        self.S.add("act", lambda e: e.activation(out=out, in_=in_, func=func, **kw), reads=reads, writes=writes, cost=c)

    def tt(self, out, in0, in1, op, reads, writes, eng="dve"):
        c = self._n(out) / (0.96 if eng == "dve" else 0.45) + (130.0 if eng == "dve" else 300.0)
        self.S.add(eng, lambda e: e.tensor_tensor(out=out, in0=in0, in1=in1, op=op), reads=reads, writes=writes, cost=c)

    def ts(self, out, in0, s1, op0, reads, writes, s2=None, op1=None, eng="dve"):
        c = self._n(out) / (0.96 if eng == "dve" else 0.45) + (130.0 if eng == "dve" else 300.0)
        if op1 is None:
            self.S.add(eng, lambda e: e.tensor_scalar(out=out, in0=in0, scalar1=s1, scalar2=None, op0=op0), reads=reads, writes=writes, cost=c)
        else:
            self.S.add(eng, lambda e: e.tensor_scalar(out=out, in0=in0, scalar1=s1, scalar2=s2, op0=op0, op1=op1), reads=reads, writes=writes, cost=c)

    def stt(self, out, in0, scalar, in1, op0, op1, reads, writes):
        c = self._n(out) / 0.96 + 130.0
        self.S.add("dve", lambda e: e.scalar_tensor_tensor(out=out, in0=in0, scalar=scalar, in1=in1, op0=op0, op1=op1), reads=reads, writes=writes, cost=c)

    def cp(self, out, in_, reads, writes, eng="act"):
        n = self._n(out)
        if eng == "act":
            self.S.add("act", lambda e: e.copy(out=out, in_=in_), reads=reads, writes=writes, cost=n / 1.4 + 230.0)
        else:
            c = n / (0.96 if eng == "dve" else 0.45) + (130.0 if eng == "dve" else 300.0)
            self.S.add(eng, lambda e: e.tensor_copy(out=out, in_=in_), reads=reads, writes=writes, cost=c)

    def memset(self, ap, val, writes, eng="pool"):
        self.S.add(eng, lambda e: e.memset(ap, val), writes=writes, cost=self._n(ap) / 0.9 + 150.0)

    def dma(self, out, in_, reads, writes, key, eng="sp"):
        nbytes = self._n(out) * int(out.shape[0]) * 4
        self.S.add(eng, lambda e: e.dma_start(out=out, in_=in_), reads=reads, writes=writes, dma_key=key, cost=2500.0 + nbytes / 60.0)

    def rsqrt_small(self, out, in_, scale, reads, writes, tmpkey=None):
        self.ts(out, in_, scale, ALU.mult, reads, writes, s2=EPS, op1=ALU.add)
        self.actf(out, out, AF.Sqrt, writes, writes)
        self.S.add("dve", lambda e: e.reciprocal(out=out, in_=out), reads=writes, writes=writes)

    def dump(self, name, ap, reads, row0, col0=0):
        if not getattr(self, "debug", False):
            return
        n = ap.shape[-1] if len(ap.shape) == 2 else None
        self.dma(self.out[row0:row0 + ap.shape[0], col0:col0 + n], ap, reads, [("dbg", name)], "dbg", eng="pool")

    def _consts(self):
        A = self.A
        ones = A.bf16(128)
        k1 = "c_ones"
        self.memset(ones, 1.0, [k1])
        self.memset(self.cst[:, :], 0.0, ["cst"])
        self.S.add("pool", lambda e: e.affine_select(out=self.ident, in_=ones, pattern=[[-1, 128]], compare_op=ALU.is_equal,
                                                    fill=0.0, base=0, channel_multiplier=1), reads=[k1, "cst"], writes=["cst"])
        sc = A.bf16(128)
        tmpm = A.bf16(128)
        self.memset(sc, -1.0 / 16.0, ["c_sc"])

        def tri(dst, src, srckey, upper):
            if upper:
                self.S.add("pool", lambda e: e.affine_select(out=tmpm, in_=src, pattern=[[1, 128]], compare_op=ALU.is_ge,
                                                            fill=0.0, base=0, channel_multiplier=-1), reads=[srckey, "tmpm"], writes=["tmpm"])
                self.cp(dst[:, 0:64], tmpm[:, 0:64], ["tmpm"], ["cst"], eng="pool")
                self.S.add("pool", lambda e: e.affine_select(out=dst[:, 64:128], in_=tmpm[:, 64:128], pattern=[[0, 64]], compare_op=ALU.is_ge,
                                                            fill=0.0, base=-64, channel_multiplier=1), reads=["tmpm", "cst"], writes=["cst"])
            else:
                self.S.add("pool", lambda e: e.affine_select(out=tmpm, in_=src, pattern=[[-1, 128]], compare_op=ALU.is_ge,
                                                            fill=0.0, base=-1, channel_multiplier=1), reads=[srckey, "tmpm"], writes=["tmpm"])
                self.cp(dst[:, 64:128], tmpm[:, 64:128], ["tmpm"], ["cst"], eng="pool")
                self.S.add("pool", lambda e: e.affine_select(out=dst[:, 0:64], in_=tmpm[:, 0:64], pattern=[[0, 64]], compare_op=ALU.is_ge,
                                                            fill=0.0, base=63, channel_multiplier=-1), reads=["tmpm", "cst"], writes=["cst"])
        tri(self.triInc, sc, "c_sc", True)
        tri(self.triRev, sc, "c_sc", False)
        tri(self.cmask, ones, k1, True)

    def load_w_rows(self, slot, off, src, r0, nrows_k, c0, ncols, key):
        view = self.wa[slot][:, off:off + nrows_k * ncols].rearrange("p (k n) -> p k n", n=ncols)
        s = src[r0:r0 + 128 * nrows_k, c0:c0 + ncols].rearrange("(k p) n -> p k n", p=128)
        for k in range(nrows_k):
            self.dma(view[:, k, :], s[:, k, :], [], [key], key, eng="pool")
        return view

    def norm_tile(self, m, hsrc, hsrc_key, bufs, store_scr):
        hA, hnb, gtile, hnT, ss, gkey = bufs["hA"], bufs["hnb"], bufs["gn"], bufs["hnT"], bufs["ss"], bufs["gkey"]
        hk = bufs.get("hk", "hnT")
        for j in range(NS):
            t0 = m * MT + j * 128
            bi = j if len(hA) <= NS else (m % 2) * NS + j
            hb = hA[bi]
            kh = "hA%d" % bi
            self.dma(hb, hsrc[t0:t0 + 128, :], [(hsrc_key, m, j)], [kh], "ld_hA%d" % bi)
            self.actf(hnb.bitcast(F32) if False else bufs["junk"], hb, AF.Square, [kh], ["junk", "ss"], accum=ss[:, 0:1])
            self.rsqrt_small(ss[:, 1:2], ss[:, 0:1], 1.0 / D, ["ss"], ["rstd"], "ss_t")
            self.stt(hnb, hb, ss[:, 1:2], gtile, ALU.mult, ALU.mult, [kh, "rstd", gkey], ["hnb"])
            if m == 0 and j == 0:
                self.dump("hnb", hnb, ["hnb"], 0)
            pbk, pk = self.bank()
            pv = pbk[:, :].bitcast(BF16)
            for dk in range(8):
                self.tp(pv[:, dk * 128:(dk + 1) * 128], hnb[:, dk * 128:(dk + 1) * 128], self.ident, ["hnb", "cst"], [pk])
            self.cp(hnT[:, :, j * 128:(j + 1) * 128], pv[:, 0:1024].rearrange("p (k t) -> p k t", t=128), [pk], [hk])
        if store_scr:
            self.dma(self.hnscr[m].rearrange("p (k t) -> p k t", t=MT), hnT, [hk], [("hnscr", m)], "st_" + hk)

    def load_hnT(self, m, hnT, hk="hnT"):
        self.dma(hnT, self.hnscr[m].rearrange("p (k t) -> p k t", t=MT), [("hnscr", m)], [hk], "ld_" + hk)

    def outproj_residual(self, m, yT, nk, Wo, wkey, hres, hres_key, hdst, hdst_key, hB, final_g=None, ss=None, junk=None, hbk="hB", yk="yT"):
        for j in range(NS):
            t0 = m * MT + j * 128
            bi = j if len(hB) <= NS else (m % 2) * NS + j
            if len(hB) == 1:
                bi = 0
            hb = hB[bi]
            kh = "%s%d" % (hbk, bi)
            self.dma(hb, hres[t0:t0 + 128, :], [(hres_key, m, j)], [kh], "ld_%s%d" % (hbk, bi))
            for n2 in range(2):
                pbk, pk = self.bank()
                for mk in range(nk):
                    self.mm(pbk[:, :], yT[:, mk, j * 128:(j + 1) * 128], Wo[:, mk, n2 * 512:(n2 + 1) * 512],
                            mk == 0, mk == nk - 1, [yk, wkey], [pk])
                self.tt(hb[:, n2 * 512:(n2 + 1) * 512], hb[:, n2 * 512:(n2 + 1) * 512], pbk[:, :], ALU.add, [kh, pk], [kh])
            if final_g is not None:
                self.actf(junk, hb, AF.Square, [kh], ["junk", "fss"], accum=ss[:, 2:3])
                self.rsqrt_small(ss[:, 3:4], ss[:, 2:3], 1.0 / D, ["fss"], ["frstd"], "fss_t")
                self.stt(hb, hb, ss[:, 3:4], final_g, ALU.mult, ALU.mult, [kh, "frstd", "gfin"], [kh])
            self.dma(hdst[t0:t0 + 128, :], hb, [kh], [(hdst_key, m, j)], "st_h%d" % bi)

    def w_e1(self, L, slot):
        i = L // 2
        wk = ("w", slot)
        Win = self.load_w_rows(slot, 0, self.e_w_in[i], 0, 8, 0, 3088, wk)
        Wo = self.load_w_rows(slot, 8 * 3088, self.e_w_out[i], 0, 8, 0, 1024, wk)
        return (Win, Wo, wk)

    def pass_e1(self, L, W, hin, hin_key, hmid, hmid_key):
        i = L // 2
        A = self.A
        self.set_ring(range(6))
        Win, Wo, wk = W
        gn = A.f32(1024)
        gg = A.f32(1024)
        pk_ = "par_e1"
        self.dma(gn, self.norm_g[L:L + 1, :].partition_broadcast(128), [], [pk_], pk_)
        self.dma(gg, self.e_gla_g[i].partition_broadcast(128), [], [pk_], pk_)
        tmpf = [A.f32(512) for _ in range(2)]
        wa2f = tmpf[0]
        wa2 = A.bf16(512)
        self.memset(wa2f[0:32, :], 0.0, ["tmpf0"])
        self.dma(wa2f[0:16, :], self.e_w_a2[i], ["tmpf0"], ["tmpf0"], "par_e1b")
        self.dma(wa2f[16:17, :], self.e_b_a[i], ["tmpf0"], ["tmpf0"], "par_e1b")
        self.cp(wa2[0:32, :], wa2f[0:32, :], ["tmpf0"], ["wa2"], eng="dve")
        hA = [A.f32(1024) for _ in range(NS)]
        hnb = A.bf16(1024)
        junk = A.bf16(1024)
        ss = A.f32(8)
        hnT2 = [A.bf16(8 * MT).rearrange("p (k t) -> p k t", t=MT) for _ in range(E1_HNT_BUFS)]
        nb = dict(hA=hA, hnb=hnb, gn=gn, hnT=None, ss=ss, gkey=pk_, junk=junk)
        alow = self.pst[:, 0:MT // 2].bitcast(BF16)
        self.memset(alow[0:32, :], 1.0, ["alow"])
        hBe = [A.f32(1024) for _ in range(NS)]
        spT = A.bf16(NS * 512).rearrange("p (j n) -> p j n", n=512)
        Eb = A.f32(MT)
        Enb = A.f32(MT)
        dec = A.f32(16).rearrange("p (h c) -> p h c", c=4)
        qf = A.bf16(4 * MT).rearrange("p (h t) -> p h t", t=MT)
        kin = A.bf16(4 * MT).rearrange("p (h t) -> p h t", t=MT)
        kst = A.bf16(NS * 512).rearrange("p (j n) -> p j n", n=512)
        v = A.bf16(NS * 1024).rearrange("p (j n) -> p j n", n=1024)
        S32 = A.f32(1024).rearrange("p (h n) -> p h n", n=256)
        Sbf = [A.bf16(1024).rearrange("p (h n) -> p h n", n=256) for _ in range(2)]
        attT = A.bf16(512).rearrange("p (h n) -> p h n", n=128)
        sz = A.f32(1024)
        ss4 = A.f32(8)
        ygla = A.bf16(1024)
        yT = A.bf16(8 * MT).rearrange("p (k t) -> p k t", t=MT)
        self.memset(S32, 0.0, ["S32_%d" % h for h in range(4)])
        self.memset(Sbf[0], 0.0, ["Sbf0_%d" % h for h in range(4)])
        chunk_ctr = 0
        QSC = 128.0 ** -0.5
        for m in range(NMT):
            hnT = hnT2[m % E1_HNT_BUFS]
            hk = "hnT%d" % (m % E1_HNT_BUFS)
            nb["hnT"] = hnT
            nb["hk"] = hk
            self.norm_tile(m, hin, hin_key, nb, store_scr=True)
            pbk, pk = self.bank()
            for dk in range(8):
                self.mm(pbk[0:16, 0:MT], Win[:, dk, 3072:3088], hnT[:, dk, :], dk == 0, dk == 7, [hk, wk], [pk])
            self.cp(alow[0:16, :], pbk[0:16, 0:MT], [pk], ["alow"])
            for j in range(NS):
                pbk, pk = self.bank()
                self.mm(pbk[:, :], alow[0:32, j * 128:(j + 1) * 128], wa2[0:32, :], True, True, ["alow", "wa2"], [pk])
                self.actf(tmpf[j], pbk[:, :], AF.Exp, [pk], ["tmpf%d" % j], scale=-1.0)
            for j in range(NS):
                self.actf(spT[:, j, :], tmpf[j], AF.Ln, ["tmpf%d" % j], ["spT"], bias=1.0)
            if m == 0:
                self.dump("spT", spT[:, 0, :], ["spT"], 128)
                self.dump("hnT0", hnT[:, 0, :], ["hnT"], 128, 512)
                self.dump("hnT1", hnT[:, 1, :], ["hnT"], 128, 768)
            for hd in range(4):
                pbk, pk = self.bank()
                for j in range(NS):
                    self.mm(pbk[:, j * 128:(j + 1) * 128], spT[:, j, hd * 128:(hd + 1) * 128], self.triInc, True, True, ["spT", "cst"], [pk])
                self.actf(Eb, pbk[:, 0:MT], AF.Exp, [pk], ["Eb"])
                self.actf(Enb, pbk[:, 0:MT], AF.Exp, [pk], ["Enb"], scale=-1.0)
                self.cp(dec[:, hd, :], Eb.rearrange("p (c t) -> p c t", t=64)[:, :, 63], ["Eb"], ["dec%d" % hd], eng="dve")
                pq, pqk = self.bank()
                for dk in range(8):
                    self.mm(pq[:, 0:MT], Win[:, dk, hd * 128:(hd + 1) * 128], hnT[:, dk, :], dk == 0, dk == 7, [hk, wk], [pqk])
                self.stt(qf[:, hd, :], pq[:, 0:MT], QSC, Eb, ALU.mult, ALU.mult, [pqk, "Eb"], ["qf%d" % hd])
                pkk, pkkk = self.bank()
                for dk in range(8):
                    self.mm(pkk[:, 0:MT], Win[:, dk, 512 + hd * 128:512 + (hd + 1) * 128], hnT[:, dk, :], dk == 0, dk == 7, [hk, wk], [pkkk])
                self.tt(kin[:, hd, :], pkk[:, 0:MT], Enb, ALU.mult, [pkkk, "Enb"], ["kin%d" % hd])
                if m == 0 and hd == 0:
                    self.dump("Eb", Eb, ["Eb"], 256)
                    self.dump("qf", qf[:, 0, :], ["qf0"], 256, 256)
                    self.dump("kin", kin[:, 0, :], ["kin0"], 256, 512)
            for j in range(NS):
                pbk, pk = self.bank()
                self.mm(pbk[:, :], self.triRev, spT[:, j, :], True, True, ["spT", "cst"], [pk])
                self.actf(tmpf[j], pbk[:, :], AF.Exp, [pk], ["tmpf%d" % j])
                pk2, pk2k = self.bank()
                for dk in range(8):
                    self.mm(pk2[:, :], hnT[:, dk, j * 128:(j + 1) * 128], Win[:, dk, 512:1024], dk == 0, dk == 7, [hk, wk], [pk2k])
                self.tt(kst[:, j, :], pk2[:, :], tmpf[j], ALU.mult, [pk2k, "tmpf%d" % j], ["kst%d" % j])
                for n2 in range(2):
                    pv_, pvk = self.bank()
                    for dk in range(8):
                        self.mm(pv_[:, :], hnT[:, dk, j * 128:(j + 1) * 128], Win[:, dk, 1024 + n2 * 512:1024 + (n2 + 1) * 512],
                                dk == 0, dk == 7, [hk, wk], [pvk])
                    self.cp(v[:, j, n2 * 512:(n2 + 1) * 512], pv_[:, :], [pvk], ["v%d" % j])
            for j in range(NS):
                pat, patk = self.bank()
                for hd in range(4):
                    self.mm(pat[:, hd * 128:(hd + 1) * 128], kin[:, hd, j * 128:(j + 1) * 128], qf[:, hd, j * 128:(j + 1) * 128],
                            True, True, ["kin%d" % hd, "qf%d" % hd], [patk])
                self.tt(attT, pat[:, :].rearrange("p (h n) -> p h n", n=128), self.cmask.unsqueeze(1).to_broadcast([P, 4, 128]),
                        ALU.mult, [patk, "cst"], ["attT"])
                po = [(self.pb[6], ("pb", 6)), (self.pb[7], ("pb", 7))]
                for c2 in range(2):
                    cs = slice(64 * c2, 64 * c2 + 64)
                    cur = chunk_ctr % 2
                    nxt = 1 - cur
                    cidx = 2 * j + c2
                    for hd in range(4):
                        ob, obk = po[hd // 2]
                        osl = ob[cs, (hd % 2) * 256:(hd % 2) * 256 + 256]
                        okey = obk
                        self.mm(osl, attT[cs, hd, 64 * c2:64 * c2 + 64], v[cs, j, hd * 256:(hd + 1) * 256], True, False,
                                ["attT", "v%d" % j], [okey])
                        self.mm(osl, qf[:, hd, j * 128 + 64 * c2:j * 128 + 64 * c2 + 64], Sbf[cur][:, hd, :], False, True,
                                ["qf%d" % hd, "Sbf%d_%d" % (cur, hd)], [okey])
                    for hd in range(4):
                        pkv, pkvk = self.bank()
                        self.mm(pkv[:, 0:256], kst[cs, j, hd * 128:(hd + 1) * 128], v[cs, j, hd * 256:(hd + 1) * 256], True, True,
                                ["kst%d" % j, "v%d" % j], [pkvk])
                        self.stt(S32[:, hd, :], S32[:, hd, :], dec[:, hd, cidx:cidx + 1], pkv[:, 0:256], ALU.mult, ALU.add,
                                 ["S32_%d" % hd, "dec%d" % hd, pkvk], ["S32_%d" % hd])
                        self.cp(Sbf[nxt][:, hd, :], S32[:, hd, :], ["S32_%d" % hd], ["Sbf%d_%d" % (nxt, hd)])
                    chunk_ctr += 1
                for n2 in range(2):
                    pz, pzk = self.bank()
                    for dk in range(8):
                        self.mm(pz[:, :], hnT[:, dk, j * 128:(j + 1) * 128], Win[:, dk, 2048 + n2 * 512:2048 + (n2 + 1) * 512],
                                dk == 0, dk == 7, [hk, wk], [pzk])
                    self.actf(sz[:, n2 * 512:(n2 + 1) * 512], pz[:, :], AF.Silu, [pzk], ["sz%d" % n2])
                    self.tt(sz[:, n2 * 512:(n2 + 1) * 512], sz[:, n2 * 512:(n2 + 1) * 512], gg[:, n2 * 512:(n2 + 1) * 512], ALU.mult,
                            ["sz%d" % n2, pk_], ["sz%d" % n2], eng="pool")
                for hd in range(4):
                    ob, obk = po[hd // 2]
                    osl = ob[:, (hd % 2) * 256:(hd % 2) * 256 + 256]
                    okeys = [obk]
                    self.actf(junk[:, 0:256], osl, AF.Square, okeys, ["junk", "ss4_%d" % hd], accum=ss4[:, hd:hd + 1])
                self.rsqrt_small(ss4[:, 4:8], ss4[:, 0:4], 1.0 / 256.0, ["ss4_%d" % h for h in range(4)], ["rstd4"], "ss4_t")
                for hd in range(4):
                    ob, obk = po[hd // 2]
                    osl = ob[:, (hd % 2) * 256:(hd % 2) * 256 + 256]
                    okeys = [obk]
                    self.stt(ygla[:, hd * 256:(hd + 1) * 256], osl, ss4[:, 4 + hd:5 + hd], sz[:, hd * 256:(hd + 1) * 256],
                             ALU.mult, ALU.mult, okeys + ["rstd4", "sz%d" % (hd // 2)], ["ygla"])
                if m == 0 and j == 0:
                    self.dump("ygla", ygla, ["ygla"], 384)
                    self.dump("v", v[:, 0, :], ["v0"], 512)
                    self.dump("kst", kst[:, 0, :], ["kst0"], 640, 0)
                    self.dump("attT", attT[:, 0, :], ["attT"], 640, 512)
                    self.dump("sz", sz, ["sz0", "sz1"], 768)
                pbk, pk = self.bank()
                pvw = pbk[:, :].bitcast(BF16)
                for mk in range(8):
                    self.tp(pvw[:, mk * 128:(mk + 1) * 128], ygla[:, mk * 128:(mk + 1) * 128], self.ident, ["ygla", "cst"], [pk])
                self.cp(yT[:, :, j * 128:(j + 1) * 128], pvw[:, 0:1024].rearrange("p (k t) -> p k t", t=128), [pk], ["yT"])
            self.outproj_residual(m, yT, 8, Wo, wk, hin, hin_key, hmid, hmid_key, hBe)

    def w_e2(self, L, slot):
        i = L // 2
        wk = ("w", slot)
        Win = self.load_w_rows(slot, 0, self.e_w_in[i], 0, 8, 3088, 3072, wk)
        Wo = self.load_w_rows(slot, 8 * 3072, self.e_w_out[i], 1024, 8, 0, 1024, wk)
        return (Win, Wo, wk)

    def pass_e2(self, L, W, hmid, hmid_key):
        i = L // 2
        A = self.A
        self.set_ring(range(2, 8))
        Win, Wo, wk = W
        NPE = 29
        oslot = wk[1] ^ 1
        dg = self.wa[oslot][:, 4096:4096 + 8 * NPE * 128].rearrange("p (c t n) -> p c t n", c=8, t=NPE)
        pk_ = "par_e2"
        cw = A.f32(8 * 31).rearrange("p (c k) -> p c k", k=31)
        cb = A.f32(8)
        lg = A.f32(8)
        lb = A.f32(8)
        self.dma(cw, self.e_cw[i], [], [pk_], pk_)
        self.dma(cb, self.e_cb[i], [], [pk_], pk_)
        self.dma(lg, self.e_lg[i], [], [pk_], pk_)
        self.dma(lb, self.e_lb[i], [], [pk_], pk_)
        for ct in range(8):
            self.tt(dg[:, ct, :, :], self.ident.unsqueeze(1).to_broadcast([P, NPE, 128]),
                    cw[:, ct, 0:NPE].unsqueeze(2).to_broadcast([P, NPE, 128]), ALU.mult, ["cst", pk_], [("dg", ct)],
                    eng=("dve" if ct % 2 == 0 else "pool"))
        ones32 = A.f32(128)
        self.memset(ones32, 1.0, ["ones32"])
        hnT2 = [A.bf16(8 * MT).rearrange("p (k t) -> p k t", t=MT) for _ in range(2)]
        u = [A.bf16(MT + 32) for _ in range(2)]
        halo = A.bf16(8 * 32).rearrange("p (c k) -> p c k", k=32)
        self.memset(halo, 0.0, [("halo", c) for c in range(8)])
        sg = [A.f32(MT) for _ in range(2)]
        acc = [A.f32(MT) for _ in range(2)]
        xc = A.f32(8 * MT).rearrange("p (c t) -> p c t", t=MT)
        sq = [A.f32(MT) for _ in range(2)]
        mean = A.f32(MT)
        rstd = A.f32(MT)
        nmr = A.f32(MT)
        tn = [A.f32(MT) for _ in range(2)]
        sl = [A.f32(MT) for _ in range(2)]
        szc = [A.f32(MT) for _ in range(2)]
        yT2 = [A.bf16(8 * MT).rearrange("p (k t) -> p k t", t=MT) for _ in range(2)]
        hB = [A.f32(1024) for _ in range(2 * NS)]
        s1b, s1k = self.pb[0], ("pb", 0)
        s2b, s2k = self.pb[1], ("pb", 1)
        for m in range(NMT):
            hnT = hnT2[m % 2]
            hk = "hnT%d" % (m % 2)
            yT = yT2[m % 2]
            yk = "yT%d" % (m % 2)
            self.load_hnT(m, hnT, hk)
            for ct in range(8):
                b = ct % 2
                ub = u[b]
                uk = "u%d" % b
                pval, pvk = self.bank()
                for dk in range(8):
                    self.mm(pval[:, 0:MT], Win[:, dk, ct * 128:(ct + 1) * 128], hnT[:, dk, :], dk == 0, dk == 7, [hk, wk], [pvk])
                pg, pgk = self.bank()
                for dk in range(8):
                    self.mm(pg[:, 0:MT], Win[:, dk, 1024 + ct * 128:1024 + (ct + 1) * 128], hnT[:, dk, :], dk == 0, dk == 7, [hk, wk], [pgk])
                self.actf(sg[b], pg[:, 0:MT], AF.Sigmoid, [pgk], ["sg%d" % b])
                self.cp(ub[:, 0:30], halo[:, ct, 0:30], [("halo", ct)], [uk + "h"], eng="pool")
                self.tt(ub[:, 30:30 + MT], pval[:, 0:MT], sg[b], ALU.mult, [pvk, "sg%d" % b], [uk])
                self.cp(halo[:, ct, 0:30], ub[:, MT:MT + 30], [uk], [("halo", ct)], eng="pool")
                pc, pck = self.bank()
                for t in range(NPE):
                    self.mm(pc[:, 0:MT], dg[:, ct, t, :], ub[:, t:t + MT], t == 0, t == NPE - 1, [uk, uk + "h", ("dg", ct)], [pck])
                a_ = acc[b]
                ka = "acc%d" % b
                self.ts(a_, ub[:, NPE:NPE + MT], cw[:, ct, NPE:NPE + 1], ALU.mult, [uk, uk + "h", pk_], [ka])
                for t in range(NPE + 1, 31):
                    self.stt(a_, ub[:, t:t + MT], cw[:, ct, t:t + 1], a_, ALU.mult, ALU.add, [uk, uk + "h", ka], [ka])
                self.stt(xc[:, ct, :], pc[:, 0:MT], cb[:, ct:ct + 1], a_, ALU.add, ALU.add, [pck, ka, pk_], [("xc", ct)])
                self.actf(sq[b], xc[:, ct, :], AF.Square, [("xc", ct)], ["sq%d" % b])
                self.mm(s1b[:, 0:MT], ones32, xc[:, ct, :], ct == 0, ct == 7, [("xc", ct), "ones32"], [s1k])
                self.mm(s2b[:, 0:MT], ones32, sq[b], ct == 0, ct == 7, ["sq%d" % b, "ones32"], [s2k])
            self.ts(mean, s1b[:, 0:MT], 1.0 / 1024.0, ALU.mult, [s1k], ["mean"])
            self.tt(nmr, mean, mean, ALU.mult, ["mean"], ["nmr"])
            self.stt(rstd, s2b[:, 0:MT], 1.0 / 1024.0, nmr, ALU.mult, ALU.subtract, [s2k, "nmr"], ["rstd"])
            self.ts(rstd, rstd, EPS, ALU.add, ["rstd"], ["rstd"])
            self.actf(rstd, rstd, AF.Sqrt, ["rstd"], ["rstd"])
            self.S.add("dve", lambda e: e.reciprocal(out=rstd, in_=rstd), reads=["rstd"], writes=["rstd"])
            self.stt(nmr, mean, -1.0, rstd, ALU.mult, ALU.mult, ["mean", "rstd"], ["nmr"])
            for ct in range(8):
                b = ct % 2
                self.tt(tn[b], xc[:, ct, :], rstd, ALU.mult, [("xc", ct), "rstd"], ["tn%d" % b])
                self.tt(tn[b], tn[b], nmr, ALU.add, ["tn%d" % b, "nmr"], ["tn%d" % b])
                self.actf(sl[b], tn[b], AF.Silu, ["tn%d" % b, pk_], ["sl%d" % b], bias=lb[:, ct:ct + 1], scale=lg[:, ct:ct + 1])
                pz, pzk = self.bank()
                for dk in range(8):
                    self.mm(pz[:, 0:MT], Win[:, dk, 2048 + ct * 128:2048 + (ct + 1) * 128], hnT[:, dk, :], dk == 0, dk == 7, [hk, wk], [pzk])
                self.actf(szc[b], pz[:, 0:MT], AF.Silu, [pzk], ["szc%d" % b])
                self.tt(yT[:, ct, :], sl[b], szc[b], ALU.mult, ["sl%d" % b, "szc%d" % b], [yk])
            self.outproj_residual(m, yT, 8, Wo, wk, hmid, hmid_key, hmid, hmid_key, hB, yk=yk)

    def w_oa(self, L, slot):
        i = L // 2
        wk = ("w", slot)
        Wu = self.load_w_rows(slot, 0, self.o_w_in[i], 0, 8, 0, 512, wk)
        w = self.wa[slot]
        V = dict(Wu=Wu, wk=wk, slot=slot)
        V["Kblk"] = w[:, 4096:8192].rearrange("p (g n) -> p g n", n=128)
        V["Wa_re"] = w[:, 8192:10240].rearrange("p (g n) -> p g n", n=128)
        V["Wa_im"] = w[:, 10240:12288].rearrange("p (g n) -> p g n", n=128)
        V["CAre"] = w[:, 12288:14336].rearrange("p (g n) -> p g n", n=128)
        V["nCAim"] = w[:, 14336:16384].rearrange("p (g n) -> p g n", n=128)
        V["usm"] = w[:, 16384:32768].rearrange("p (c s n) -> p c s n", c=4, s=8)
        return V

    def cmul(self, o_re, o_im, a_re, a_im, b_re, b_im, tmp, reads, wkeys, neg_im=False):
        kre, kim, kt = wkeys
        self.tt(o_re, a_re, b_re, ALU.mult, reads, [kre])
        self.tt(tmp, a_im, b_im, ALU.mult, reads, [kt])
        self.tt(o_re, o_re, tmp, ALU.subtract, [kre, kt], [kre])
        self.tt(o_im, a_re, b_im, ALU.mult, reads, [kim])
        self.tt(tmp, a_im, b_re, ALU.mult, reads + [kre], [kt])
        if neg_im:
            self.stt(o_im, o_im, -1.0, tmp, ALU.mult, ALU.subtract, [kim, kt], [kim])
        else:
            self.tt(o_im, o_im, tmp, ALU.add, [kim, kt], [kim])

    def reduce_angle(self, x, t, key):
        TWO_PI = float(2.0 * np.pi)
        PI = float(np.pi)
        for _ in range(8):
            self.ts(t, x, PI, ALU.is_gt, [key], ["ra_t"], s2=TWO_PI, op1=ALU.mult)
            self.tt(x, x, t, ALU.subtract, [key, "ra_t"], [key])
        for _ in range(2):
            self.ts(t, x, -PI, ALU.is_lt, [key], ["ra_t"], s2=TWO_PI, op1=ALU.mult)
            self.tt(x, x, t, ALU.add, [key, "ra_t"], [key])

    def s5_prep(self, L, V):
        i = L // 2
        A = self.A
        pk_ = "par_s5"
        T = self.pst
        sm = lambda: A.f32(16)
        lr, li, ldt, dtt, mag, ang, sarg, carg, t1, are, aim, den, nr, cfr, cfi, u1, u2 = [sm() for _ in range(17)]
        self.dma(lr, self.o_lamre[i], [], [pk_], pk_)
        self.dma(li, self.o_lamim[i], [], [pk_], pk_)
        self.dma(ldt, self.o_logdt[i], [], [pk_], pk_)
        b3 = lambda: A.f32(256).rearrange("p (g h) -> p g h", h=16)
        bre, bim, cre, cim, Bre, Bim, tb = [b3() for _ in range(7)]
        self.dma(bre, self.o_bre[i].rearrange("p (g h) -> p g h", h=16), [], [pk_], pk_)
        self.dma(bim, self.o_bim[i].rearrange("p (g h) -> p g h", h=16), [], [pk_], pk_)
        self.dma(cre, self.o_cre[i].rearrange("p (g h) -> p g h", h=16), [], [pk_], pk_)
        self.dma(cim, self.o_cim[i].rearrange("p (g h) -> p g h", h=16), [], [pk_], pk_)
        dcol = A.f32(32)
        self.dma(dcol, self.o_dcol[i], [], [pk_], pk_)
        R = [pk_]
        self.actf(dtt, ldt, AF.Exp, R, ["dtt"])
        self.tt(u1, lr, dtt, ALU.mult, R + ["dtt"], ["u1"])
        self.actf(mag, u1, AF.Exp, ["u1"], ["mag"])
        self.tt(ang, li, dtt, ALU.mult, R + ["dtt"], ["ang"])
        self.cp(sarg, ang, ["ang"], ["sarg"], eng="dve")
        self.ts(carg, ang, float(np.pi / 2), ALU.add, ["ang"], ["carg"])
        self.reduce_angle(sarg, t1, "sarg")
        self.reduce_angle(carg, t1, "carg")
        self.actf(sarg, sarg, AF.Sin, ["sarg"], ["sarg"])
        self.actf(carg, carg, AF.Sin, ["carg"], ["carg"])
        self.tt(are, mag, carg, ALU.mult, ["mag", "carg"], ["are"])
        self.tt(aim, mag, sarg, ALU.mult, ["mag", "sarg"], ["aim"])
        self.tt(den, lr, lr, ALU.mult, R, ["den"])
        self.tt(u1, li, li, ALU.mult, R, ["u1"])
        self.tt(den, den, u1, ALU.add, ["den", "u1"], ["den"])
        self.S.add("dve", lambda e: e.reciprocal(out=den, in_=den), reads=["den"], writes=["den"])
        self.ts(nr, are, -1.0, ALU.add, ["are"], ["nr"])
        self.tt(cfr, nr, lr, ALU.mult, ["nr"] + R, ["cfr"])
        self.tt(u1, aim, li, ALU.mult, ["aim"] + R, ["u1"])
        self.tt(cfr, cfr, u1, ALU.add, ["cfr", "u1"], ["cfr"])
        self.tt(cfr, cfr, den, ALU.mult, ["cfr", "den"], ["cfr"])
        self.tt(cfi, aim, lr, ALU.mult, ["aim"] + R, ["cfi"])
        self.tt(u2, nr, li, ALU.mult, ["nr"] + R, ["u2"])
        self.tt(cfi, cfi, u2, ALU.subtract, ["cfi", "u2"], ["cfi"])
        self.tt(cfi, cfi, den, ALU.mult, ["cfi", "den"], ["cfi"])
        bc3 = lambda a: a.unsqueeze(2).to_broadcast([P, 16, 16])
        self.cmul(Bre, Bim, bc3(cfr), bc3(cfi), bre, bim, tb, ["cfr", "cfi"] + R, ["Bre", "Bim", "tb"])
        if PREP_CUT <= 1:
            return
        Pre = A.f32(144).rearrange("p (g j) -> p g j", j=9)
        Pim = A.f32(144).rearrange("p (g j) -> p g j", j=9)
        Qre = A.f32(128).rearrange("p (g j) -> p g j", j=8)
        Qim = A.f32(128).rearrange("p (g j) -> p g j", j=8)
        Vre = A.f32(128).rearrange("p (g j) -> p g j", j=8)
        Vim = A.f32(128).rearrange("p (g j) -> p g j", j=8)
        self.memset(Pre[:, :, 0], 1.0, ["Pre"], eng="dve")
        self.memset(Pim[:, :, 0], 0.0, ["Pim"], eng="dve")
        self.memset(Qre[:, :, 0], 1.0, ["Qre"], eng="dve")
        self.memset(Qim[:, :, 0], 0.0, ["Qim"], eng="dve")
        for j in range(1, 9):
            self.cmul(Pre[:, :, j], Pim[:, :, j], Pre[:, :, j - 1], Pim[:, :, j - 1], are, aim, u1, ["Pre", "Pim", "are", "aim"], ["Pre", "Pim", "u1"])
        ire, iim = sm(), sm()
        self.tt(u2, mag, mag, ALU.mult, ["mag"], ["u2"])
        self.S.add("dve", lambda e: e.reciprocal(out=u2, in_=u2), reads=["u2"], writes=["u2"])
        self.tt(ire, are, u2, ALU.mult, ["are", "u2"], ["ire"])
        self.stt(iim, aim, -1.0, u2, ALU.mult, ALU.mult, ["aim", "u2"], ["iim"])
        for j in range(1, 8):
            self.cmul(Qre[:, :, j], Qim[:, :, j], Qre[:, :, j - 1], Qim[:, :, j - 1], ire, iim, u1, ["Qre", "Qim", "ire", "iim"], ["Qre", "Qim", "u1"])
        for s_ in range(8):
            self.cp(Vre[:, :, s_], Pre[:, :, 7 - s_], ["Pre"], ["Vre"], eng="dve")
            self.cp(Vim[:, :, s_], Pim[:, :, 7 - s_], ["Pim"], ["Vim"], eng="dve")
        c8 = [T[:, k * 128:(k + 1) * 128].rearrange("p (g j) -> p g j", j=8) for k in range(3)]
        c64 = [T[:, 384 + k * 128:384 + (k + 1) * 128].rearrange("p (g j) -> p g j", j=8) for k in range(3)]
        c512 = [T[:, 768 + k * 16:768 + (k + 1) * 16] for k in range(3)]
        self.cp(c8[0][:, :, 0], Pre[:, :, 8], ["Pre"], ["c8"], eng="dve")
        self.cp(c8[1][:, :, 0], Pim[:, :, 8], ["Pim"], ["c8"], eng="dve")
        for j in range(1, 8):
            self.cmul(c8[0][:, :, j], c8[1][:, :, j], c8[0][:, :, j - 1], c8[1][:, :, j - 1], c8[0][:, :, 0], c8[1][:, :, 0], u1, ["c8"], ["c8", "c8", "u1"])
        self.cp(c64[0][:, :, 0], c8[0][:, :, 7], ["c8"], ["c64"], eng="dve")
        self.cp(c64[1][:, :, 0], c8[1][:, :, 7], ["c8"], ["c64"], eng="dve")
        for j in range(1, 8):
            self.cmul(c64[0][:, :, j], c64[1][:, :, j], c64[0][:, :, j - 1], c64[1][:, :, j - 1], c64[0][:, :, 0], c64[1][:, :, 0], u1, ["c64"], ["c64", "c64", "u1"])
        self.cp(c512[0], c64[0][:, :, 7], ["c64"], ["c512"], eng="dve")
        self.cp(c512[1], c64[1][:, :, 7], ["c64"], ["c512"], eng="dve")
        self.ts(c8[2], c8[1], -1.0, ALU.mult, ["c8"], ["c8n"])
        self.ts(c64[2], c64[1], -1.0, ALU.mult, ["c64"], ["c64n"])
        self.ts(c512[2], c512[1], -1.0, ALU.mult, ["c512"], ["c512n"])
        if PREP_CUT <= 2:
            return
        big = lambda: A.f32(2048).rearrange("p (g s h) -> p g s h", s=8, h=16)
        Lre, Lim, Rre, nRim, tbig = [big() for _ in range(5)]
        bp = lambda a: a.unsqueeze(3).to_broadcast([P, 16, 8, 16])
        bb = lambda a: a.unsqueeze(2).to_broadcast([P, 16, 8, 16])
        self.cmul(Lre, Lim, bp(Qre), bp(Qim), bb(Bre), bb(Bim), tbig, ["Qre", "Qim", "Bre", "Bim"], ["Lre", "Lim", "tbig"])
        self.cmul(Rre, nRim, bp(Pre[:, :, 0:8]), bp(Pim[:, :, 0:8]), bb(cre), bb(cim), tbig, ["Pre", "Pim"] + R, ["Rre", "nRim", "tbig"], neg_im=True)
        if PREP_CUT <= 3:
            return
        ident32 = A.f32(128)
        ones32 = A.f32(128)
        maskST = A.f32(128)
        tK = [A.f32(128) for _ in range(2)]
        self.memset(ones32, 1.0, ["ones32"])
        self.S.add("pool", lambda e: e.affine_select(out=ident32, in_=ones32, pattern=[[-1, 128]], compare_op=ALU.is_equal,
                                                    fill=0.0, base=0, channel_multiplier=1), reads=["ones32"], writes=["ident32"])
        self.S.add("pool", lambda e: e.affine_select(out=maskST.rearrange("p (t h) -> p t h", h=16), in_=ones32.rearrange("p (t h) -> p t h", h=16),
                                                    pattern=[[16, 8], [0, 16]], compare_op=ALU.is_ge, fill=0.0, base=15, channel_multiplier=-1),
                   reads=["ones32"], writes=["maskST"])
        bigb = lambda: A.bf16(2048).rearrange("p (g n) -> p g n", n=128)
        Lre_b, Lim_b, Rre_b, nRim_b = [bigb() for _ in range(4)]
        f3 = lambda a: a.rearrange("p g s h -> p g (s h)")
        self.cp(Lre_b, f3(Lre), ["Lre"], ["Lre_b"], eng="dve")
        self.cp(Lim_b, f3(Lim), ["Lim"], ["Lim_b"], eng="act")
        self.cp(Rre_b, f3(Rre), ["Rre"], ["Rre_b"], eng="dve")
        self.cp(nRim_b, f3(nRim), ["nRim"], ["nRim_b"], eng="act")
        Kblk = V["Kblk"]
        if PREP_CUT <= 3.2:
            return
        for g0 in range(0, 32, 8):
            bks = [self.bank(), self.bank()]
            for q in range(8):
                g = g0 + q
                gp, g2 = g // 2, g % 2
                rs = slice(64 * g2, 64 * g2 + 64)
                pbk, pk = bks[g2]
                c0 = (q // 2) * 128
                self.mm(pbk[:, c0:c0 + 128], Lre_b[rs, gp, :], Rre_b[rs, gp, :], True, False, ["Lre_b", "Rre_b"], [pk])
                self.mm(pbk[:, c0:c0 + 128], Lim_b[rs, gp, :], nRim_b[rs, gp, :], False, True, ["Lim_b", "nRim_b"], [pk])
            for q in range(8):
                g = g0 + q
                g2 = g % 2
                pbk, pk = bks[g2]
                c0 = (q // 2) * 128
                tk = tK[q % 2]
                self.tt(tk, pbk[:, c0:c0 + 128], maskST, ALU.mult, [pk, "maskST"], ["tK%d" % (q % 2)])
                self.stt(Kblk[:, g, :], ident32, dcol[:, g:g + 1], tk, ALU.mult, ALU.add, ["ident32", "tK%d" % (q % 2)] + R, ["Kblk"])
        if PREP_CUT <= 4:
            return
        Wre, Wim = Rre, nRim
        self.cmul(Wre, Wim, bp(Vre), bp(Vim), bb(Bre), bb(Bim), tbig, ["Vre", "Vim", "Bre", "Bim"], ["Rre", "nRim", "tbig"])
        self.cp(Rre_b, f3(Wre), ["Rre"], ["Rre_b"], eng="dve")
        self.cp(nRim_b, f3(Wim), ["nRim"], ["nRim_b"], eng="act")
        for src, dst, sk in ((Rre_b, V["Wa_re"], "Rre_b"), (nRim_b, V["Wa_im"], "nRim_b")):
            for g0 in range(0, 16, 4):
                pbk, pk = self.bank()
                pvw = pbk[:, :].bitcast(BF16)
                for q in range(4):
                    self.tp(pvw[:, q * 128:(q + 1) * 128], src[:, g0 + q, :], self.ident, [sk, "cst"], [pk])
                self.cp(dst[:, g0:g0 + 4, :], pvw[:, 0:512].rearrange("p (g n) -> p g n", n=128), [pk], ["Wa"])
        if PREP_CUT <= 5:
            return
        Cre32, Cim32 = Lre, Lim
        self.cmul(Cre32, Cim32, bp(Pre[:, :, 1:9]), bp(Pim[:, :, 1:9]), bb(cre), bb(cim), tbig, ["Pre", "Pim"] + R, ["Lre", "Lim", "tbig"], neg_im=True)
        self.cp(V["CAre"], Cre32.rearrange("p g s h -> p g (s h)"), ["Lre"], ["CA"], eng="dve")
        self.cp(V["nCAim"], Cim32.rearrange("p g s h -> p g (s h)"), ["Lim"], ["CA"], eng="dve")
        V["c8"], V["c64"], V["c512"] = c8, c64, c512

    def s5_core(self, V):
        A = self.A
        usm = V["usm"]
        c8, c64, c512 = V["c8"], V["c64"], V["c512"]
        Zs = A.bf16(8 * 240).rearrange("p (g x) -> p g x", x=240)
        onesb = A.bf16(128)
        self.memset(onesb, 1.0, ["onesb"])
        self.memset(Zs, 0.0, ["Zs"])
        self.S.add("pool", lambda e: e.affine_select(out=Zs[:, :, 112:128], in_=onesb.rearrange("p (a b) -> p a b", b=16),
                                                    pattern=[[-16, 8], [-1, 16]], compare_op=ALU.is_equal, fill=0.0, base=0, channel_multiplier=1),
                   reads=["onesb", "Zs"], writes=["Zs"])
        U8 = A.bf16(8 * 512).rearrange("p (g c) -> p g c", c=512)
        X2 = [A.f32(1024).rearrange("p (r c) -> p r c", r=2) for _ in range(4)]
        X = [[x2[:, 0, :], x2[:, 1, :]] for x2 in X2]
        Sp = [[A.bf16(512) for _ in range(2)] for _ in range(4)]
        for pp in range(4):
            for r in range(2):
                self.memset(Sp[pp][r][:, 0:1], 0.0, [("Sp", pp, r)])
        for ct in range(4):
            uk = [("usm", ct, s_) for s_ in range(8)]
            for gq in range(8):
                pbk, pk = self.bank()
                for s_ in range(8):
                    self.mm(pbk[:, :], Zs[:, gq, 112 - 16 * s_:240 - 16 * s_], usm[:, ct, s_, :], s_ == 0, s_ == 7, ["Zs", uk[s_]], [pk])
                self.cp(U8[:, gq, :], pbk[:, :], [pk], [("U8", gq)], eng=("act" if gq % 2 == 0 else "dve"))
            for pp in range(4):
                gp = 4 * ct + pp
                for r, Wn in ((0, "Wa_re"), (1, "Wa_im")):
                    pbk, pk = self.bank()
                    self.mm(pbk[0:64, :], V[Wn][:, gp, 0:64], U8[:, 2 * pp, :], True, True, ["Wa", ("U8", 2 * pp)], [pk])
                    self.mm(pbk[64:128, :], V[Wn][:, gp, 64:128], U8[:, 2 * pp + 1, :], True, True, ["Wa", ("U8", 2 * pp + 1)], [pk])
                    self.cp(X[pp][r], pbk[:, :], [pk], [("X", pp, r)])
            steps = []
            for pp in range(4):
                gp = 4 * ct + pp
                xb = X2[pp]
                kx = [("X", pp, 0), ("X", pp, 1)]
                ckeys = ["c8", "c64", "c512", "c8n", "c64n", "c512n"]
                lst = []

                def cmac(o_b, s_b, cr, ci, cni, lst=lst, kx=kx, ckeys=ckeys):
                    lst.append((o_b, s_b, cr, o_b, kx + ckeys, kx))
                    lst.append((o_b[:, 0], s_b[:, 1], cni, o_b[:, 0], kx + ckeys, kx))
                    lst.append((o_b[:, 1], s_b[:, 0], ci, o_b[:, 1], kx + ckeys, kx))
                v3 = xb.rearrange("p r (m j) -> p r m j", j=8)
                vz = xb.rearrange("p r (q j w) -> p r q j w", j=8, w=8)[:, :, :, :, 7]
                vw = xb.rearrange("p r (q w) -> p r q w", w=64)[:, :, :, 63]
                co = lambda c, j: (c[0][:, gp, j:j + 1], c[1][:, gp, j:j + 1], c[2][:, gp, j:j + 1])
                for j in range(1, 8):
                    cmac(v3[:, :, :, j], v3[:, :, :, j - 1], *co(c8, 0))
                for j in range(1, 8):
                    cmac(vz[:, :, :, j], vz[:, :, :, j - 1], *co(c64, 0))
                c5 = (c512[0][:, gp:gp + 1], c512[1][:, gp:gp + 1], c512[2][:, gp:gp + 1])
                for q in range(1, 8):
                    cmac(vw[:, :, q:q + 1], vw[:, :, q - 1:q], *c5)
                for j in range(0, 7):
                    cmac(vz[:, :, 1:8, j], vw[:, :, 0:7], *co(c64, j))
                for j in range(0, 7):
                    cmac(v3[:, :, 1:64, j], v3[:, :, 0:63, 7], *co(c8, j))
                steps.append(lst)
            for k in range(len(steps[0])):
                for pp in range(4):
                    o_, s_in, c_, a_, rd, wr = steps[pp][k]
                    self.stt(o_, s_in, c_, a_, ALU.mult, ALU.add, rd, wr)
            for pp in range(4):
                for r in range(2):
                    self.cp(Sp[pp][r][:, 1:512], X[pp][r][:, 0:511], [("X", pp, r)], [("Sp", pp, r)], eng=("act" if r == 0 else "pool"))
            for gq in range(8):
                g = 8 * ct + gq
                gp, g2 = g // 2, g % 2
                pp = gq // 2
                rs = slice(64 * g2, 64 * g2 + 64)
                pbk, pk = self.bank()
                self.mm(pbk[:, :], V["Kblk"][:, g, :], U8[:, gq, :], True, False, ["Kblk", ("U8", gq)], [pk])
                self.mm(pbk[:, :], V["CAre"][rs, gp, :], Sp[pp][0][rs, :], False, False, ["CA", ("Sp", pp, 0)], [pk])
                self.mm(pbk[:, :], V["nCAim"][rs, gp, :], Sp[pp][1][rs, :], False, True, ["CA", ("Sp", pp, 1)], [pk])
                self.cp(U8[:, gq, :], pbk[:, :], [pk], [("U8", gq)], eng=("act" if gq % 2 == 0 else "dve"))
            for t_ in range(8):
                pbk, pk = self.bank()
                for gq in range(8):
                    self.mm(pbk[:, :], Zs[:, t_, 112 - 16 * gq:240 - 16 * gq], U8[:, gq, :], gq == 0, gq == 7, ["Zs", ("U8", gq)], [pk])
                self.cp(usm[:, ct, t_, :], pbk[:, :], [pk], [("usm", ct, t_)], eng=("act" if t_ % 2 == 0 else "dve"))

    def pass_oa(self, L, V, hin, hin_key):
        A = self.A
        self.set_ring(range(8))
        self.s5_prep(L, V)
        if OA_STAGE < 2:
            return
        self.S.barrier()
        A.reset()
        Wu, wk, usm = V["Wu"], V["wk"], V["usm"]
        gn = A.f32(1024)
        pk_ = "par_oa"
        self.dma(gn, self.norm_g[L:L + 1, :].partition_broadcast(128), [], [pk_], pk_)
        hA = [A.f32(1024) for _ in range(2 * NS)]
        hnb = A.bf16(1024)
        junk = A.bf16(1024)
        ss = A.f32(8)
        hnT2 = [A.bf16(8 * MT).rearrange("p (k t) -> p k t", t=MT) for _ in range(2)]
        nb = dict(hA=hA, hnb=hnb, gn=gn, hnT=None, ss=ss, gkey=pk_, junk=junk)
        for m in range(NMT):
            hnT = hnT2[m % 2]
            hk = "hnT%d" % (m % 2)
            nb["hnT"] = hnT
            nb["hk"] = hk
            self.norm_tile(m, hin, hin_key, nb, store_scr=True)
            for ct in range(4):
                pbk, pk = self.bank()
                for dk in range(8):
                    self.mm(pbk[:, 0:MT], Wu[:, dk, ct * 128:(ct + 1) * 128], hnT[:, dk, :], dk == 0, dk == 7, [hk, wk], [pk])
                self.cp(usm[:, ct, :, m * 32:(m + 1) * 32], pbk[:, 0:MT].rearrange("p (c s) -> p s c", s=8), [pk],
                        [("usm", ct, s_) for s_ in range(8)], eng=("act" if ct % 2 == 0 else "dve"))
        if OA_STAGE < 3:
            return
        self.S.barrier()
        A.reset()
        self.s5_core(V)

    def w_ob1(self, L, slot):
        i = L // 2
        wk = ("w", slot)
        Win = self.load_w_rows(slot, 0, self.o_w_in[i], 0, 8, 1024, 3072, wk)
        Wo = self.load_w_rows(slot, 24576, self.o_w_out[i], 512, 8, 0, 1024, wk)
        wsT = self.wa[slot][:, 32768:33792].rearrange("p (h t) -> p h t", t=128)
        self.dma(wsT, self.o_wsT[i].rearrange("p (h t) -> p h t", t=128), [], [wk], wk, eng="pool")
        return (Win, Wo, wsT, wk)

    def pass_ob1(self, L, W, hin, hin_key, hmid, hmid_key):
        i = L // 2
        A = self.A
        self.set_ring(range(8))
        Win, Wo, wsT, wk = W
        self.S.add("pool", lambda e: e.affine_select(out=wsT, in_=wsT, pattern=[[0, 8], [1, 128]], compare_op=ALU.is_ge, fill=0.0,
                                                    base=0, channel_multiplier=-1), reads=[wk], writes=["wsTm"])
        pk_ = "par_ob1"
        lng = A.f32(1024)
        lnb = A.f32(1024)
        bsb_f = A.f32(1024)
        bsb = bsb_f.rearrange("p (h t) -> p h t", t=128)
        self.dma(lng, self.o_lng[i].partition_broadcast(128), [], [pk_], pk_)
        self.dma(lnb, self.o_lnb[i].partition_broadcast(128), [], [pk_], pk_)
        self.dma(bsb_f, self.o_bs[i].partition_broadcast(128), [], [pk_], pk_)
        hnT2 = [A.bf16(8 * MT).rearrange("p (k t) -> p k t", t=MT) for _ in range(2)]
        vtmp = A.f32(1024)
        vnT2 = [A.bf16(NS * 1024).rearrange("p (j n) -> p j n", n=1024) for _ in range(2)]
        st4 = A.f32(16)
        junk = A.bf16(512)
        szt = [A.f32(MT) for _ in range(2)]
        t1 = [A.f32(MT) for _ in range(2)]
        yT2 = [A.bf16(8 * MT).rearrange("p (k t) -> p k t", t=MT) for _ in range(2)]
        hB = [A.f32(1024) for _ in range(2 * NS)]
        for m in range(NMT):
            hnT = hnT2[m % 2]
            hk = "hnT%d" % (m % 2)
            yT = yT2[m % 2]
            yk = "yT%d" % (m % 2)
            vnT = vnT2[m % 2]
            vq = "vnT%d_" % (m % 2)
            self.load_hnT(m, hnT, hk)
            for j in range(NS):
                pbs = []
                for n2 in range(2):
                    pv_, pvk = self.bank()
                    for dk in range(8):
                        self.mm(pv_[:, :], hnT[:, dk, j * 128:(j + 1) * 128], Win[:, dk, 1024 + n2 * 512:1024 + (n2 + 1) * 512],
                                dk == 0, dk == 7, [hk, wk], [pvk])
                    self.actf(junk, pv_[:, :], AF.Identity, [pvk], ["junk", "st_s%d" % n2], accum=st4[:, n2:n2 + 1])
                    self.actf(junk, pv_[:, :], AF.Square, [pvk], ["junk", "st_q%d" % n2], accum=st4[:, 2 + n2:3 + n2])
                    pbs.append((pv_, pvk))
                self.tt(st4[:, 4:5], st4[:, 0:1], st4[:, 1:2], ALU.add, ["st_s0", "st_s1"], ["st_m"])
                self.ts(st4[:, 4:5], st4[:, 4:5], 1.0 / 1024.0, ALU.mult, ["st_m"], ["st_m"])
                self.tt(st4[:, 5:6], st4[:, 2:3], st4[:, 3:4], ALU.add, ["st_q0", "st_q1"], ["st_v"])
                self.tt(st4[:, 6:7], st4[:, 4:5], st4[:, 4:5], ALU.mult, ["st_m"], ["st_mm"])
                self.stt(st4[:, 5:6], st4[:, 5:6], 1.0 / 1024.0, st4[:, 6:7], ALU.mult, ALU.subtract, ["st_v", "st_mm"], ["st_v"])
                self.rsqrt_small(st4[:, 7:8], st4[:, 5:6], 1.0, ["st_v"], ["st_r"], "st_rt")
                self.stt(st4[:, 8:9], st4[:, 4:5], -1.0, st4[:, 7:8], ALU.mult, ALU.mult, ["st_m", "st_r"], ["st_n"])
                for n2 in range(2):
                    pv_, pvk = pbs[n2]
                    sl_ = slice(n2 * 512, (n2 + 1) * 512)
                    self.ts(vtmp[:, sl_], pv_[:, :], st4[:, 7:8], ALU.mult, [pvk, "st_r", "st_n"], ["vtmp%d" % n2], s2=st4[:, 8:9], op1=ALU.add)
                    self.tt(vtmp[:, sl_], vtmp[:, sl_], lng[:, sl_], ALU.mult, ["vtmp%d" % n2, pk_], ["vtmp%d" % n2], eng="pool")
                    self.tt(vnT[:, j, sl_], vtmp[:, sl_], lnb[:, sl_], ALU.add, ["vtmp%d" % n2, pk_], [vq + str(j)], eng="pool")
            for hd in range(8):
                b = hd % 2
                psv, psvk = self.bank()
                for j in range(NS):
                    self.mm(psv[:, j * 128:(j + 1) * 128], vnT[:, j, hd * 128:(hd + 1) * 128], wsT[:, hd, :], True, True,
                            [vq + str(j), "wsTm"], [psvk])
                pu, puk = self.bank()
                for dk in range(8):
                    self.mm(pu[:, 0:MT], Win[:, dk, hd * 128:(hd + 1) * 128], hnT[:, dk, :], dk == 0, dk == 7, [hk, wk], [puk])
                pz, pzk = self.bank()
                for dk in range(8):
                    self.mm(pz[:, 0:MT], Win[:, dk, 2048 + hd * 128:2048 + (hd + 1) * 128], hnT[:, dk, :], dk == 0, dk == 7, [hk, wk], [pzk])
                self.actf(szt[b], pz[:, 0:MT], AF.Silu, [pzk], ["szt%d" % b])
                self.tt(t1[b].rearrange("p (j t) -> p j t", t=128), psv[:, 0:MT].rearrange("p (j t) -> p j t", t=128),
                        bsb[:, hd, :].unsqueeze(1).to_broadcast([P, NS, 128]), ALU.add, [psvk, pk_], ["t1_%d" % b])
                self.tt(t1[b], pu[:, 0:MT], t1[b], ALU.mult, [puk, "t1_%d" % b], ["t1_%d" % b])
                self.tt(yT[:, hd, :], t1[b], szt[b], ALU.mult, ["t1_%d" % b, "szt%d" % b], [yk])
            self.outproj_residual(m, yT, 8, Wo, wk, hin, hin_key, hmid, hmid_key, hB, yk=yk)

    def w_ob2(self, L, slot):
        i = L // 2
        wk = ("w", slot)
        Wz = self.load_w_rows(slot, 0, self.o_w_in[i], 0, 8, 512, 512, wk)
        Wg = self.load_w_rows(slot, 4096, self.o_w_glu[i], 0, 4, 0, 512, wk)
        Wo = self.load_w_rows(slot, 6144, self.o_w_out[i], 0, 4, 0, 1024, wk)
        usm = self.wa[slot][:, 16384:32768].rearrange("p (c s n) -> p c s n", c=4, s=8)
        return (Wz, Wg, Wo, usm, wk)

    def pass_ob2(self, L, W, hmid, hmid_key, last):
        i = L // 2
        A = self.A
        self.set_ring(range(8))
        Wz, Wg, Wo, usm, wk = W
        pk_ = "par_ob2"
        bglu = A.f32(4)
        self.dma(bglu, self.o_bglu[i], [], [pk_], pk_)
        gfin = None
        if last:
            gfin = A.f32(1024)
            self.dma(gfin, self.final_g.partition_broadcast(128), [], ["gfin"], "par_gfin")
        hnT2 = [A.bf16(8 * MT).rearrange("p (k t) -> p k t", t=MT) for _ in range(2)]
        ge322 = [A.f32(4 * MT).rearrange("p (c t) -> p c t", t=MT) for _ in range(2)]
        gebf2 = [A.bf16(4 * MT).rearrange("p (c t) -> p c t", t=MT) for _ in range(2)]
        sgm = [A.f32(MT) for _ in range(2)]
        szt = [A.f32(MT) for _ in range(2)]
        yT2 = [A.bf16(4 * MT).rearrange("p (k t) -> p k t", t=MT) for _ in range(2)]
        hB = [A.f32(1024) for _ in range(2 * NS)]
        junk = A.bf16(1024)
        ss = A.f32(8)
        for m in range(NMT):
            par = m % 2
            hnT = hnT2[par]
            hk = "hnT%d" % par
            yT = yT2[par]
            yk = "yT%d" % par
            ge32 = ge322[par]
            gebf = gebf2[par]
            self.load_hnT(m, hnT, hk)
            for ct in range(4):
                self.actf(ge32[:, ct, :].rearrange("p (c s) -> p s c", s=8), usm[:, ct, :, m * 32:(m + 1) * 32], AF.Gelu_apprx_tanh,
                          [("usm", ct, s_) for s_ in range(8)], [("ge32", par, ct)])
                self.cp(gebf[:, ct, :], ge32[:, ct, :], [("ge32", par, ct)], [("gebf", par, ct)], eng="dve")
            for mt in range(4):
                b = mt % 2
                pg, pgk = self.bank()
                for kt in range(4):
                    self.mm(pg[:, 0:MT], Wg[:, kt, mt * 128:(mt + 1) * 128], gebf[:, kt, :], kt == 0, kt == 3, [("gebf", par, kt), wk], [pgk])
                self.actf(sgm[b], pg[:, 0:MT], AF.Sigmoid, [pgk, pk_], ["sgm%d" % b], bias=bglu[:, mt:mt + 1])
                pz, pzk = self.bank()
                for dk in range(8):
                    self.mm(pz[:, 0:MT], Wz[:, dk, mt * 128:(mt + 1) * 128], hnT[:, dk, :], dk == 0, dk == 7, [hk, wk], [pzk])
                self.actf(szt[b], pz[:, 0:MT], AF.Silu, [pzk], ["szt%d" % b])
                self.tt(sgm[b], sgm[b], ge32[:, mt, :], ALU.mult, ["sgm%d" % b, ("ge32", par, mt)], ["sgm%d" % b])
                self.tt(yT[:, mt, :], sgm[b], szt[b], ALU.mult, ["sgm%d" % b, "szt%d" % b], [yk])
            if last:
                self.outproj_residual(m, yT, 4, Wo, wk, hmid, hmid_key, self.out, "out", hB, final_g=gfin, ss=ss, junk=junk, yk=yk)
            else:
                self.outproj_residual(m, yT, 4, Wo, wk, hmid, hmid_key, hmid, hmid_key, hB, yk=yk)

    def build(self):
        passes = []
        hin, hin_key = self.x, "x"
        for L in range(self.n_layers):
            hmid = self.hbuf[(L + 1) % 2]
            hmid_key = ("hb", (L + 1) % 2)
            if getattr(self, "first_pass", 0) > 0:
                hin, hin_key = self.x, "x"
            if L % 2 == 0:
                passes.append((lambda slot, L=L: self.w_e1(L, slot),
                               lambda W, L=L, a=hin, ak=hin_key, b=hmid, bk=hmid_key: self.pass_e1(L, W, a, ak, b, bk)))
                passes.append((lambda slot, L=L: self.w_e2(L, slot),
                               lambda W, L=L, b=hmid, bk=hmid_key: self.pass_e2(L, W, b, bk)))
            else:
                last = (L == self.n_layers - 1) and self.final_norm
                passes.append((lambda slot, L=L: self.w_oa(L, slot),
                               lambda W, L=L, a=hin, ak=hin_key: self.pass_oa(L, W, a, ak)))
                passes.append((lambda slot, L=L: self.w_ob1(L, slot),
                               lambda W, L=L, a=hin, ak=hin_key, b=hmid, bk=hmid_key: self.pass_ob1(L, W, a, ak, b, bk)))
                passes.append((lambda slot, L=L: self.w_ob2(L, slot),
                               lambda W, L=L, b=hmid, bk=hmid_key, last=last: self.pass_ob2(L, W, b, bk, last)))
                self.fused_out = last
            hin, hin_key = hmid, hmid_key
        if self.max_passes is not None:
            passes = passes[getattr(self, 'first_pass', 0):self.max_passes]
        slot = 0
        Wn = passes[0][0](slot)
        for k, (wl, run) in enumerate(passes):
            self.S.barrier()
            self.A.reset()
            Wcur = Wn
            slot ^= 1
            if k + 1 < len(passes):
                Wn = passes[k + 1][0](slot)
            run(Wcur)
        if getattr(self, "fused_out", False) and self.max_passes is None:
            self.S.emit()
            return
        if getattr(self, "first_pass", 0) > 0 and self.max_passes is not None and self.max_passes <= 3:
            hin, hin_key = self.x, "x"
        self.S.barrier()
        A = self.A
        A.reset()
        cpb = [A.f32(1024) for _ in range(2)]
        for t in range(0 if not self.debug else SEQ // 128, SEQ // 128):
            b = t % 2
            self.dma(cpb[b], hin[t * 128:(t + 1) * 128, :], [(hin_key, t // NS, t % NS)], ["cpb%d" % b], "ld_cp%d" % b)
            self.dma(self.out[t * 128:(t + 1) * 128, :], cpb[b], ["cpb%d" % b], [("out", t)], "st_cp%d" % b)
        self.S.emit()


def build_program(n_layers=4, final_norm=True, max_passes=None, first_pass=0):
    nc = bass.Bass("TRN2", target_bir_lowering=False)
    st = ExitStack()
    with st:
        b = Builder(nc, st, n_layers=n_layers, final_norm=final_norm, max_passes=max_passes)
        b.first_pass = first_pass
        b.build()
    return nc


def host_layout(inputs):
    f = lambda a: np.ascontiguousarray(np.asarray(a, dtype=np.float32))
    g = {}
    g["norm_g"] = f(inputs["norm_g"])
    g["final_g"] = f(inputs["final_g"]).reshape(1, D)
    g["e_w_in"] = f(inputs["e_w_in"])
    g["e_w_a2"] = f(inputs["e_w_a2"])
    g["e_b_a"] = f(inputs["e_b_a"]).reshape(2, 1, 512)
    g["e_gla_g"] = f(inputs["e_gla_g"]).reshape(2, 1, 1024)
    cw = f(inputs["e_conv_w"])
    g["e_cw"] = f(cw.reshape(2, 31, 8, 128).transpose(0, 3, 2, 1))
    cpl = lambda a: f(f(a).reshape(2, 8, 128).transpose(0, 2, 1))
    g["e_cb"] = cpl(inputs["e_conv_b"])
    g["e_lg"] = cpl(inputs["e_cln_g"])
    g["e_lb"] = cpl(inputs["e_cln_b"])
    g["e_w_out"] = f(inputs["e_w_out"])
    g["o_w_in"] = f(inputs["o_w_in"])
    gp_l = lambda a: f(f(a).reshape(2, 16, 2, 64).transpose(0, 2, 3, 1).reshape(2, 128, 16))
    g["o_lamre"] = gp_l(inputs["o_lam_re"])
    g["o_lamim"] = gp_l(inputs["o_lam_im"])
    ldt = f(inputs["o_log_dt"]).reshape(2, 16, 2)
    g["o_logdt"] = f(np.broadcast_to(ldt.transpose(0, 2, 1)[:, :, None, :], (2, 2, 64, 16)).reshape(2, 128, 16))
    b_l = lambda a: f(f(a).reshape(2, 16, 2, 64, 16).transpose(0, 2, 3, 1, 4).reshape(2, 128, 256))
    g["o_bre"] = b_l(inputs["o_b_re"])
    g["o_bim"] = b_l(inputs["o_b_im"])
    c_l = lambda a: f(f(a).reshape(2, 16, 2, 16, 64).transpose(0, 2, 4, 1, 3).reshape(2, 128, 256))
    g["o_cre"] = c_l(inputs["o_c_re"])
    g["o_cim"] = c_l(inputs["o_c_im"])
    dd = f(inputs["o_d"]).reshape(2, 32, 16)
    g["o_dcol"] = f(np.broadcast_to(dd.transpose(0, 2, 1)[:, None, :, :], (2, 8, 16, 32)).reshape(2, 128, 32))
    g["o_w_glu"] = f(inputs["o_w_glu"])
    g["o_bglu"] = f(f(inputs["o_b_glu"]).reshape(2, 4, 128).transpose(0, 2, 1))
    g["o_lng"] = f(inputs["o_sg_ln_g"]).reshape(2, 1, 1024)
    g["o_lnb"] = f(inputs["o_sg_ln_b"]).reshape(2, 1, 1024)
    ws = f(inputs["o_w_s"])
    g["o_wsT"] = f(ws.transpose(0, 3, 1, 2).reshape(2, 128, 1024))
    g["o_bs"] = f(inputs["o_b_s"]).reshape(2, 1, 1024)
    g["o_w_out"] = f(inputs["o_w_out"])
    return g


def kernel(**inputs):
    x = np.asarray(inputs["x"], dtype=np.float32)
    shared = host_layout(inputs)
    nc = build_program()
    in_maps = []
    for c in range(8):
        m = dict(shared)
        m["x"] = np.ascontiguousarray(x[c])
        in_maps.append(m)
    res = run_bass_kernel_spmd(nc, in_maps, core_ids=list(range(8)))
    return np.stack([np.asarray(r["out"], dtype=np.float32) for r in res.results], axis=0)
```

```python
import numpy as np
import concourse.bass as bass
import concourse.mybir as mybir
from concourse.bass_utils import run_bass_kernel_spmd
from contextlib import ExitStack

F32 = mybir.dt.float32
BF16 = mybir.dt.bfloat16
AF = mybir.ActivationFunctionType
ALU = mybir.AluOpType

P = 128
SEQ = 4096
D = 1024
MT = 256
NMT = SEQ // MT
NS = MT // 128
EPS = 1e-6
EVEN_IN = 6160
ODD_IN = 4096
WA_N = 33792
OA_STAGE = 3
E1_HNT_BUFS = 1
PREP_CUT = 99


class _Op:
    __slots__ = ("eng", "fn", "deps", "edges", "signal", "dma_sem", "dma_val", "cnt", "cost", "seg", "idx", "fin", "placed")

    def __init__(self, eng, fn):
        self.eng = eng
        self.fn = fn
        self.deps = []
        self.edges = []
        self.signal = False
        self.dma_sem = None
        self.dma_val = 0
        self.cnt = 0
        self.cost = 200.0
        self.seg = 0
        self.idx = 0
        self.fin = 0.0
        self.placed = False


class Sched:
    ENGS = ("pe", "act", "dve", "pool", "sp")
    WINDOW = {"pe": 256, "act": 64, "dve": 64, "pool": 32, "sp": 32}

    def __init__(self, nc, stack):
        self.nc = nc
        self.stack = stack
        self.all_ops = []
        self.last_w = {}
        self.readers = {}
        self.dma_keys = {}
        self.last_dma = {}
        self.eng_sem = {}
        for e in ("pe", "act", "dve", "pool"):
            self.eng_sem[e] = stack.enter_context(nc.semaphore("es_" + e))
        self.seg = 0
        self.reorder = True
        self.wide_segs = set()

    def barrier(self):
        self.seg += 1

    def add(self, eng, fn, reads=(), writes=(), dma_key=None, cost=None):
        op = _Op(eng, fn)
        op.seg = self.seg
        op.idx = len(self.all_ops)
        is_dma = dma_key is not None
        if cost is not None:
            op.cost = float(cost)
        my_sem = None
        if is_dma:
            ent = self.dma_keys.get(dma_key)
            if ent is None:
                sem = self.stack.enter_context(self.nc.semaphore("ds_%d" % len(self.dma_keys)))
                ent = [sem, 0]
                self.dma_keys[dma_key] = ent
            my_sem = ent[0]
            prev = self.last_dma.get(dma_key)
            if prev is not None:
                op.edges.append(prev)
        cand = []
        for k in reads:
            w = self.last_w.get(k)
            if w is not None:
                cand.append(w)
        for k in writes:
            w = self.last_w.get(k)
            if w is not None:
                cand.append(w)
            for r in self.readers.get(k, ()):
                cand.append(r)
        seen = set(id(x) for x in op.edges)
        for d in cand:
            if id(d) in seen or d is op:
                continue
            seen.add(id(d))
            op.edges.append(d)
            d_is_dma = d.dma_sem is not None
            if d_is_dma:
                if is_dma and d.dma_sem is my_sem:
                    continue
            elif d.eng == eng and not is_dma and eng == "pe":
                continue
            op.deps.append(d)
        if is_dma:
            ent[1] += 16
            op.dma_sem = ent[0]
            op.dma_val = ent[1]
            self.last_dma[dma_key] = op
        for k in reads:
            self.readers.setdefault(k, []).append(op)
        for k in writes:
            self.last_w[k] = op
            self.readers[k] = []
        self.all_ops.append(op)
        return op

    def _schedule(self):
        order = {e: [] for e in self.ENGS}
        segs = {}
        for op in self.all_ops:
            segs.setdefault(op.seg, []).append(op)
        bar_points = {e: [] for e in self.ENGS}
        t_base = 0.0
        LAT = 120.0
        for sg in sorted(segs):
            ops = segs[sg]
            for e in self.ENGS:
                bar_points[e].append(len(order[e]))
            if not self.reorder:
                for op in ops:
                    order[op.eng].append(op)
                    op.placed = True
                continue
            queues = {e: [op for op in ops if op.eng == e] for e in self.ENGS}
            heads = {e: 0 for e in self.ENGS}
            free = {e: t_base for e in self.ENGS}
            remaining = len(ops)
            tmax = t_base
            while remaining:
                best = None
                best_key = None
                for e in self.ENGS:
                    q = queues[e]
                    h = heads[e]
                    while h < len(q) and q[h].placed:
                        h += 1
                    heads[e] = h
                    lim = min(len(q), h + (self.WINDOW[e] if sg not in self.wide_segs else 4000))
                    k = h
                    found = 0
                    while k < lim:
                        op = q[k]
                        k += 1
                        if op.placed:
                            continue
                        ok = True
                        rdy = free[e]
                        for d in op.edges:
                            if d.seg != sg:
                                continue
                            if not d.placed:
                                ok = False
                                break
                            f = d.fin + (LAT if d.eng != e or d.dma_sem is not None else 30.0)
                            if f > rdy:
                                rdy = f
                        if not ok:
                            continue
                        key = (rdy, op.idx)
                        if best_key is None or key < best_key:
                            best_key = key
                            best = op
                        found += 1
                        if found >= (6 if sg not in self.wide_segs else 24):
                            break
                op = best
                assert op is not None, "scheduler stuck (cyclic deps?)"
                st = best_key[0]
                op.placed = True
                if op.dma_sem is not None:
                    op.fin = st + op.cost
                    free[op.eng] = st + 60.0
                else:
                    op.fin = st + op.cost
                    free[op.eng] = op.fin
                if op.fin > tmax:
                    tmax = op.fin
                order[op.eng].append(op)
                remaining -= 1
            t_base = tmax
        self.est_ns = t_base
        return order, bar_points

    def emit(self):
        nc = self.nc
        order, bar_points = self._schedule()
        dma_by_seg = {}
        for op in self.all_ops:
            if op.dma_sem is not None:
                dma_by_seg.setdefault(op.seg, {})[id(op.dma_sem)] = op
        nseg = self.seg + 1
        for si in range(1, nseg):
            lasts = []
            for e in self.ENGS:
                pos = bar_points[e][si]
                for k in range(pos - 1, -1, -1):
                    if order[e][k].dma_sem is None:
                        lasts.append(order[e][k])
                        break
            dl = {}
            for sj in range(si):
                dl.update(dma_by_seg.get(sj, {}))
            lasts += list(dl.values())
            for e in self.ENGS:
                pos = bar_points[e][si]
                if pos < len(order[e]):
                    op = order[e][pos]
                    have = set(id(x) for x in op.deps)
                    for d in lasts:
                        if id(d) in have:
                            continue
                        if d.dma_sem is None and d.eng == e and op.dma_sem is None:
                            continue
                        op.deps.append(d)
        for e in self.ENGS:
            for op in order[e]:
                for d in op.deps:
                    if d.dma_sem is None:
                        d.signal = True
        for e in ("pe", "act", "dve", "pool"):
            for op in reversed(order[e]):
                if op.dma_sem is None:
                    op.signal = True
                    break
        for e in self.ENGS:
            c = 0
            for op in order[e]:
                if op.dma_sem is None and op.signal:
                    c += 1
                op.cnt = c
        sched = self

        def replay(ename, eng):
            waited = {}
            for op in order[ename]:
                for d in op.deps:
                    if d.dma_sem is not None:
                        sem, val = d.dma_sem, d.dma_val
                    else:
                        sem, val = sched.eng_sem[d.eng], d.cnt
                    key = id(sem)
                    if waited.get(key, 0) >= val:
                        continue
                    waited[key] = val
                    eng.wait_ge(sem, val)
                ins = op.fn(eng)
                if op.dma_sem is not None:
                    ins.then_inc(op.dma_sem, 16)
                elif op.signal:
                    ins.then_inc(sched.eng_sem[ename], 1)
            if ename == "sp":
                for e in ("pe", "act", "dve", "pool"):
                    last = None
                    for op in order[e]:
                        if op.dma_sem is None:
                            last = op
                    if last is not None:
                        eng.wait_ge(sched.eng_sem[e], last.cnt)
                for k, (sem, cnt) in sched.dma_keys.items():
                    eng.wait_ge(sem, cnt)

        with nc.Block() as block:
            @block.tensor
            def _(e):
                replay("pe", e)

            @block.scalar
            def _(e):
                replay("act", e)

            @block.vector
            def _(e):
                replay("dve", e)

            @block.gpsimd
            def _(e):
                replay("pool", e)

            @block.sync
            def _(e):
                replay("sp", e)


class Arena:
    def __init__(self, tile_ap, n, name):
        self.t = tile_ap
        self.n = n
        self.off = 0
        self.name = name
        self.gen = 0

    def reset(self):
        self.off = 0
        self.gen += 1

    def f32(self, n):
        a = self.t[:, self.off:self.off + n]
        self.off += n
        assert self.off <= self.n, (self.name, self.off, self.n)
        return a

    def bf16(self, n):
        w = (n + 1) // 2
        a = self.t[:, self.off:self.off + w].bitcast(BF16)
        self.off += w
        assert self.off <= self.n, (self.name, self.off, self.n)
        return a


class Builder:
    def __init__(self, nc, st, n_layers=4, final_norm=True, max_passes=None):
        self.max_passes = max_passes
        self.debug = max_passes == 1
        self.nc = nc
        self.st = st
        self.S = Sched(nc, st)
        self.n_layers = n_layers
        self.final_norm = final_norm
        self.uid = 0
        S = self.S
        dt_in = lambda name, shape: nc.dram_tensor(name, shape, F32, kind="ExternalInput").ap()
        self.x = dt_in("x", [SEQ, D])
        self.norm_g = dt_in("norm_g", [4, D])
        self.final_g = dt_in("final_g", [1, D])
        self.e_w_in = dt_in("e_w_in", [2, D, EVEN_IN])
        self.e_w_a2 = dt_in("e_w_a2", [2, 16, 512])
        self.e_b_a = dt_in("e_b_a", [2, 1, 512])
        self.e_gla_g = dt_in("e_gla_g", [2, 1, 1024])
        self.e_cw = dt_in("e_cw", [2, P, 8, 31])
        self.e_cb = dt_in("e_cb", [2, P, 8])
        self.e_lg = dt_in("e_lg", [2, P, 8])
        self.e_lb = dt_in("e_lb", [2, P, 8])
        self.e_w_out = dt_in("e_w_out", [2, 2048, D])
        self.o_w_in = dt_in("o_w_in", [2, D, ODD_IN])
        self.o_lamre = dt_in("o_lamre", [2, P, 16])
        self.o_lamim = dt_in("o_lamim", [2, P, 16])
        self.o_logdt = dt_in("o_logdt", [2, P, 16])
        self.o_bre = dt_in("o_bre", [2, P, 256])
        self.o_bim = dt_in("o_bim", [2, P, 256])
        self.o_cre = dt_in("o_cre", [2, P, 256])
        self.o_cim = dt_in("o_cim", [2, P, 256])
        self.o_dcol = dt_in("o_dcol", [2, P, 32])
        self.o_w_glu = dt_in("o_w_glu", [2, 512, 512])
        self.o_bglu = dt_in("o_bglu", [2, P, 4])
        self.o_lng = dt_in("o_lng", [2, 1, 1024])
        self.o_lnb = dt_in("o_lnb", [2, 1, 1024])
        self.o_wsT = dt_in("o_wsT", [2, P, 1024])
        self.o_bs = dt_in("o_bs", [2, 1, 1024])
        self.o_w_out = dt_in("o_w_out", [2, 1536, D])
        self.out = nc.dram_tensor("out", [SEQ, D], F32, kind="ExternalOutput").ap()
        self.hbuf = [nc.dram_tensor("hbuf%d" % i, [SEQ, D], F32, kind="Internal").ap() for i in range(2)]
        self.hnscr = nc.dram_tensor("hnscr", [NMT, P, 8 * MT], BF16, kind="Internal").ap()

        sb = lambda name, shape, dt: st.enter_context(nc.sbuf_tensor(name, shape, dt))
        self.wa = [sb("wa%d" % i, [P, WA_N], BF16) for i in range(2)]
        self.cst = sb("cst", [P, 4 * 128], BF16)
        self.ident = self.cst[:, 0:128]
        self.triInc = self.cst[:, 128:256]
        self.triRev = self.cst[:, 256:384]
        self.cmask = self.cst[:, 384:512]
        self.pst = sb("pst", [P, 1024], F32)
        rem = int(nc.sbuf_bytes_remaining) - 256
        self.act_n = rem // 4
        self.act_t = sb("actarena", [P, self.act_n], F32)
        self.A = Arena(self.act_t, self.act_n, "act")
        self.pb = [st.enter_context(nc.psum_tensor("pb%d" % i, [P, 512], F32)) for i in range(8)]
        self.ring = list(range(8))
        self.ring_i = 0
        self.slot = 0
        self._consts()

    def key(self, name):
        self.uid += 1
        return "%s#%d" % (name, self.uid)

    def bank(self):
        i = self.ring[self.ring_i % len(self.ring)]
        self.ring_i += 1
        return self.pb[i], ("pb", i)

    def set_ring(self, banks):
        self.ring = list(banks)
        self.ring_i = 0

    def add(self, *a, **k):
        return self.S.add(*a, **k)

    @staticmethod
    def _n(ap):
        n = 1
        for d in ap.shape[1:]:
            n *= int(d)
        return n

    def mm(self, out, lhsT, rhs, start, stop, reads, writes):
        c = max(self._n(out), 64) / 2.0 + 10.0
        if lhsT.dtype == F32:
            c *= 4.0
        self.S.add("pe", lambda e: e.matmul(out, lhsT=lhsT, rhs=rhs, start=start, stop=stop), reads=reads, writes=writes, cost=c)

    def tp(self, out, in_, ident, reads, writes):
        self.S.add("pe", lambda e: e.transpose(out=out, in_=in_, identity=ident), reads=reads, writes=writes, cost=80.0)

    def actf(self, out, in_, func, reads, writes, bias=None, scale=None, accum=None):
        kw = {}
        if bias is not None:
            kw["bias"] = bias
        if scale is not None:
            kw["scale"] = scale
        if accum is not None:
            kw["accum_out"] = accum
        c = self._n(out) / 1.4 + 230.0 + (100.0 if accum is not None else 0.0)
        self.S.add("act", lambda e: e.activation(out=out, in_=in_, func=func, **kw), reads=reads, writes=writes, cost=c)

    def tt(self, out, in0, in1, op, reads, writes, eng="dve"):
        c = self._n(out) / (0.96 if eng == "dve" else 0.45) + (130.0 if eng == "dve" else 300.0)
        self.S.add(eng, lambda e: e.tensor_tensor(out=out, in0=in0, in1=in1, op=op), reads=reads, writes=writes, cost=c)

    def ts(self, out, in0, s1, op0, reads, writes, s2=None, op1=None, eng="dve"):
        c = self._n(out) / (0.96 if eng == "dve" else 0.45) + (130.0 if eng == "dve" else 300.0)
        if op1 is None:
            self.S.add(eng, lambda e: e.tensor_scalar(out=out, in0=in0, scalar1=s1, scalar2=None, op0=op0), reads=reads, writes=writes, cost=c)
        else:
            self.S.add(eng, lambda e: e.tensor_scalar(out=out, in0=in0, scalar1=s1, scalar2=s2, op0=op0, op1=op1), reads=reads, writes=writes, cost=c)

    def stt(self, out, in0, scalar, in1, op0, op1, reads, writes):
        c = self._n(out) / 0.96 + 130.0
        self.S.add("dve", lambda e: e.scalar_tensor_tensor(out=out, in0=in0, scalar=scalar, in1=in1, op0=op0, op1=op1), reads=reads, writes=writes, cost=c)

    def cp(self, out, in_, reads, writes, eng="act"):
        n = self._n(out)
        if eng == "act":
            self.S.add("act", lambda e: e.copy(out=out, in_=in_), reads=reads, writes=writes, cost=n / 1.4 + 230.0)
        else:
            c = n / (0.96 if eng == "dve" else 0.45) + (130.0 if eng == "dve" else 300.0)
            self.S.add(eng, lambda e: e.tensor_copy(out=out, in_=in_), reads=reads, writes=writes, cost=c)

    def memset(self, ap, val, writes, eng="pool"):
        self.S.add(eng, lambda e: e.memset(ap, val), writes=writes, cost=self._n(ap) / 0.9 + 150.0)

    def dma(self, out, in_, reads, writes, key, eng="sp"):
        nbytes = self._n(out) * int(out.shape[0]) * 4
        self.S.add(eng, lambda e: e.dma_start(out=out, in_=in_), reads=reads, writes=writes, dma_key=key, cost=2500.0 + nbytes / 60.0)

    def rsqrt_small(self, out, in_, scale, reads, writes, tmpkey=None):
        self.ts(out, in_, scale, ALU.mult, reads, writes, s2=EPS, op1=ALU.add)
        self.actf(out, out, AF.Sqrt, writes, writes)
        self.S.add("dve", lambda e: e.reciprocal(out=out, in_=out), reads=writes, writes=writes)

    def dump(self, name, ap, reads, row0, col0=0):
        if not getattr(self, "debug", False):
            return
        n = ap.shape[-1] if len(ap.shape) == 2 else None
        self.dma(self.out[row0:row0 + ap.shape[0], col0:col0 + n], ap, reads, [("dbg", name)], "dbg", eng="pool")

    def _consts(self):
        A = self.A
        ones = A.bf16(128)
        k1 = "c_ones"
        self.memset(ones, 1.0, [k1])
        self.memset(self.cst[:, :], 0.0, ["cst"])
        self.S.add("pool", lambda e: e.affine_select(out=self.ident, in_=ones, pattern=[[-1, 128]], compare_op=ALU.is_equal,
                                                    fill=0.0, base=0, channel_multiplier=1), reads=[k1, "cst"], writes=["cst"])
        sc = A.bf16(128)
        tmpm = A.bf16(128)
        self.memset(sc, -1.0 / 16.0, ["c_sc"])

        def tri(dst, src, srckey, upper):
            if upper:
                self.S.add("pool", lambda e: e.affine_select(out=tmpm, in_=src, pattern=[[1, 128]], compare_op=ALU.is_ge,
                                                            fill=0.0, base=0, channel_multiplier=-1), reads=[srckey, "tmpm"], writes=["tmpm"])
                self.cp(dst[:, 0:64], tmpm[:, 0:64], ["tmpm"], ["cst"], eng="pool")
                self.S.add("pool", lambda e: e.affine_select(out=dst[:, 64:128], in_=tmpm[:, 64:128], pattern=[[0, 64]], compare_op=ALU.is_ge,
                                                            fill=0.0, base=-64, channel_multiplier=1), reads=["tmpm", "cst"], writes=["cst"])
            else:
                self.S.add("pool", lambda e: e.affine_select(out=tmpm, in_=src, pattern=[[-1, 128]], compare_op=ALU.is_ge,
                                                            fill=0.0, base=-1, channel_multiplier=1), reads=[srckey, "tmpm"], writes=["tmpm"])
                self.cp(dst[:, 64:128], tmpm[:, 64:128], ["tmpm"], ["cst"], eng="pool")
                self.S.add("pool", lambda e: e.affine_select(out=dst[:, 0:64], in_=tmpm[:, 0:64], pattern=[[0, 64]], compare_op=ALU.is_ge,
                                                            fill=0.0, base=63, channel_multiplier=-1), reads=["tmpm", "cst"], writes=["cst"])
        tri(self.triInc, sc, "c_sc", True)
        tri(self.triRev, sc, "c_sc", False)
        tri(self.cmask, ones, k1, True)

    def load_w_rows(self, slot, off, src, r0, nrows_k, c0, ncols, key):
        view = self.wa[slot][:, off:off + nrows_k * ncols].rearrange("p (k n) -> p k n", n=ncols)
        s = src[r0:r0 + 128 * nrows_k, c0:c0 + ncols].rearrange("(k p) n -> p k n", p=128)
        for k in range(nrows_k):
            self.dma(view[:, k, :], s[:, k, :], [], [key], key, eng="pool")
        return view

    def norm_tile(self, m, hsrc, hsrc_key, bufs, store_scr):
        hA, hnb, gtile, hnT, ss, gkey = bufs["hA"], bufs["hnb"], bufs["gn"], bufs["hnT"], bufs["ss"], bufs["gkey"]
        hk = bufs.get("hk", "hnT")
        for j in range(NS):
            t0 = m * MT + j * 128
            bi = j if len(hA) <= NS else (m % 2) * NS + j
            hb = hA[bi]
            kh = "hA%d" % bi
            self.dma(hb, hsrc[t0:t0 + 128, :], [(hsrc_key, m, j)], [kh], "ld_hA%d" % bi)
            self.actf(hnb.bitcast(F32) if False else bufs["junk"], hb, AF.Square, [kh], ["junk", "ss"], accum=ss[:, 0:1])
            self.rsqrt_small(ss[:, 1:2], ss[:, 0:1], 1.0 / D, ["ss"], ["rstd"], "ss_t")
            self.stt(hnb, hb, ss[:, 1:2], gtile, ALU.mult, ALU.mult, [kh, "rstd", gkey], ["hnb"])
            if m == 0 and j == 0:
                self.dump("hnb", hnb, ["hnb"], 0)
            pbk, pk = self.bank()
            pv = pbk[:, :].bitcast(BF16)
            for dk in range(8):
                self.tp(pv[:, dk * 128:(dk + 1) * 128], hnb[:, dk * 128:(dk + 1) * 128], self.ident, ["hnb", "cst"], [pk])
            self.cp(hnT[:, :, j * 128:(j + 1) * 128], pv[:, 0:1024].rearrange("p (k t) -> p k t", t=128), [pk], [hk])
        if store_scr:
            self.dma(self.hnscr[m].rearrange("p (k t) -> p k t", t=MT), hnT, [hk], [("hnscr", m)], "st_" + hk)

    def load_hnT(self, m, hnT, hk="hnT"):
        self.dma(hnT, self.hnscr[m].rearrange("p (k t) -> p k t", t=MT), [("hnscr", m)], [hk], "ld_" + hk)

    def outproj_residual(self, m, yT, nk, Wo, wkey, hres, hres_key, hdst, hdst_key, hB, final_g=None, ss=None, junk=None, hbk="hB", yk="yT"):
        for j in range(NS):
            t0 = m * MT + j * 128
            bi = j if len(hB) <= NS else (m % 2) * NS + j
            if len(hB) == 1:
                bi = 0
            hb = hB[bi]
            kh = "%s%d" % (hbk, bi)
            self.dma(hb, hres[t0:t0 + 128, :], [(hres_key, m, j)], [kh], "ld_%s%d" % (hbk, bi))
            for n2 in range(2):
                pbk, pk = self.bank()
                for mk in range(nk):
                    self.mm(pbk[:, :], yT[:, mk, j * 128:(j + 1) * 128], Wo[:, mk, n2 * 512:(n2 + 1) * 512],
                            mk == 0, mk == nk - 1, [yk, wkey], [pk])
                self.tt(hb[:, n2 * 512:(n2 + 1) * 512], hb[:, n2 * 512:(n2 + 1) * 512], pbk[:, :], ALU.add, [kh, pk], [kh])
            if final_g is not None:
                self.actf(junk, hb, AF.Square, [kh], ["junk", "fss"], accum=ss[:, 2:3])
                self.rsqrt_small(ss[:, 3:4], ss[:, 2:3], 1.0 / D, ["fss"], ["frstd"], "fss_t")
                self.stt(hb, hb, ss[:, 3:4], final_g, ALU.mult, ALU.mult, [kh, "frstd", "gfin"], [kh])
            self.dma(hdst[t0:t0 + 128, :], hb, [kh], [(hdst_key, m, j)], "st_h%d" % bi)

    def w_e1(self, L, slot):
        i = L // 2
        wk = ("w", slot)
        Win = self.load_w_rows(slot, 0, self.e_w_in[i], 0, 8, 0, 3088, wk)
        Wo = self.load_w_rows(slot, 8 * 3088, self.e_w_out[i], 0, 8, 0, 1024, wk)
        return (Win, Wo, wk)

    def pass_e1(self, L, W, hin, hin_key, hmid, hmid_key):
        i = L // 2
        A = self.A
        self.set_ring(range(6))
        Win, Wo, wk = W
        gn = A.f32(1024)
        gg = A.f32(1024)
        pk_ = "par_e1"
        self.dma(gn, self.norm_g[L:L + 1, :].partition_broadcast(128), [], [pk_], pk_)
        self.dma(gg, self.e_gla_g[i].partition_broadcast(128), [], [pk_], pk_)
        tmpf = [A.f32(512) for _ in range(2)]
        wa2f = tmpf[0]
        wa2 = A.bf16(512)
        self.memset(wa2f[0:32, :], 0.0, ["tmpf0"])
        self.dma(wa2f[0:16, :], self.e_w_a2[i], ["tmpf0"], ["tmpf0"], "par_e1b")
        self.dma(wa2f[16:17, :], self.e_b_a[i], ["tmpf0"], ["tmpf0"], "par_e1b")
        self.cp(wa2[0:32, :], wa2f[0:32, :], ["tmpf0"], ["wa2"], eng="dve")
        hA = [A.f32(1024) for _ in range(NS)]
        hnb = A.bf16(1024)
        junk = A.bf16(1024)
        ss = A.f32(8)
        hnT2 = [A.bf16(8 * MT).rearrange("p (k t) -> p k t", t=MT) for _ in range(E1_HNT_BUFS)]
        nb = dict(hA=hA, hnb=hnb, gn=gn, hnT=None, ss=ss, gkey=pk_, junk=junk)
        alow = self.pst[:, 0:MT // 2].bitcast(BF16)
        self.memset(alow[0:32, :], 1.0, ["alow"])
        hBe = [A.f32(1024) for _ in range(NS)]
        spT = A.bf16(NS * 512).rearrange("p (j n) -> p j n", n=512)
        Eb = A.f32(MT)
        Enb = A.f32(MT)
        dec = A.f32(16).rearrange("p (h c) -> p h c", c=4)
        qf = A.bf16(4 * MT).rearrange("p (h t) -> p h t", t=MT)
        kin = A.bf16(4 * MT).rearrange("p (h t) -> p h t", t=MT)
        kst = A.bf16(NS * 512).rearrange("p (j n) -> p j n", n=512)
        v = A.bf16(NS * 1024).rearrange("p (j n) -> p j n", n=1024)
        S32 = A.f32(1024).rearrange("p (h n) -> p h n", n=256)
        Sbf = [A.bf16(1024).rearrange("p (h n) -> p h n", n=256) for _ in range(2)]
        attT = A.bf16(512).rearrange("p (h n) -> p h n", n=128)
        sz = A.f32(1024)
        ss4 = A.f32(8)
        ygla = A.bf16(1024)
        yT = A.bf16(8 * MT).rearrange("p (k t) -> p k t", t=MT)
        self.memset(S32, 0.0, ["S32_%d" % h for h in range(4)])
        self.memset(Sbf[0], 0.0, ["Sbf0_%d" % h for h in range(4)])
        chunk_ctr = 0
        QSC = 128.0 ** -0.5
        for m in range(NMT):
            hnT = hnT2[m % E1_HNT_BUFS]
            hk = "hnT%d" % (m % E1_HNT_BUFS)
            nb["hnT"] = hnT
            nb["hk"] = hk
            self.norm_tile(m, hin, hin_key, nb, store_scr=True)
            pbk, pk = self.bank()
            for dk in range(8):
                self.mm(pbk[0:16, 0:MT], Win[:, dk, 3072:3088], hnT[:, dk, :], dk == 0, dk == 7, [hk, wk], [pk])
            self.cp(alow[0:16, :], pbk[0:16, 0:MT], [pk], ["alow"])
            for j in range(NS):
                pbk, pk = self.bank()
                self.mm(pbk[:, :], alow[0:32, j * 128:(j + 1) * 128], wa2[0:32, :], True, True, ["alow", "wa2"], [pk])
                self.actf(tmpf[j], pbk[:, :], AF.Exp, [pk], ["tmpf%d" % j], scale=-1.0)
            for j in range(NS):
                self.actf(spT[:, j, :], tmpf[j], AF.Ln, ["tmpf%d" % j], ["spT"], bias=1.0)
            if m == 0:
                self.dump("spT", spT[:, 0, :], ["spT"], 128)
                self.dump("hnT0", hnT[:, 0, :], ["hnT"], 128, 512)
                self.dump("hnT1", hnT[:, 1, :], ["hnT"], 128, 768)
            for hd in range(4):
                pbk, pk = self.bank()
                for j in range(NS):
                    self.mm(pbk[:, j * 128:(j + 1) * 128], spT[:, j, hd * 128:(hd + 1) * 128], self.triInc, True, True, ["spT", "cst"], [pk])
                self.actf(Eb, pbk[:, 0:MT], AF.Exp, [pk], ["Eb"])
                self.actf(Enb, pbk[:, 0:MT], AF.Exp, [pk], ["Enb"], scale=-1.0)
                self.cp(dec[:, hd, :], Eb.rearrange("p (c t) -> p c t", t=64)[:, :, 63], ["Eb"], ["dec%d" % hd], eng="dve")
                pq, pqk = self.bank()
                for dk in range(8):
                    self.mm(pq[:, 0:MT], Win[:, dk, hd * 128:(hd + 1) * 128], hnT[:, dk, :], dk == 0, dk == 7, [hk, wk], [pqk])
                self.stt(qf[:, hd, :], pq[:, 0:MT], QSC, Eb, ALU.mult, ALU.mult, [pqk, "Eb"], ["qf%d" % hd])
                pkk, pkkk = self.bank()
                for dk in range(8):
                    self.mm(pkk[:, 0:MT], Win[:, dk, 512 + hd * 128:512 + (hd + 1) * 128], hnT[:, dk, :], dk == 0, dk == 7, [hk, wk], [pkkk])
                self.tt(kin[:, hd, :], pkk[:, 0:MT], Enb, ALU.mult, [pkkk, "Enb"], ["kin%d" % hd])
                if m == 0 and hd == 0:
                    self.dump("Eb", Eb, ["Eb"], 256)
                    self.dump("qf", qf[:, 0, :], ["qf0"], 256, 256)
                    self.dump("kin", kin[:, 0, :], ["kin0"], 256, 512)
            for j in range(NS):
                pbk, pk = self.bank()
                self.mm(pbk[:, :], self.triRev, spT[:, j, :], True, True, ["spT", "cst"], [pk])
                self.actf(tmpf[j], pbk[:, :], AF.Exp, [pk], ["tmpf%d" % j])
                pk2, pk2k = self.bank()
                for dk in range(8):
                    self.mm(pk2[:, :], hnT[:, dk, j * 128:(j + 1) * 128], Win[:, dk, 512:1024], dk == 0, dk == 7, [hk, wk], [pk2k])
                self.tt(kst[:, j, :], pk2[:, :], tmpf[j], ALU.mult, [pk2k, "tmpf%d" % j], ["kst%d" % j])
                for n2 in range(2):
                    pv_, pvk = self.bank()
                    for dk in range(8):
                        self.mm(pv_[:, :], hnT[:, dk, j * 128:(j + 1) * 128], Win[:, dk, 1024 + n2 * 512:1024 + (n2 + 1) * 512],
                                dk == 0, dk == 7, [hk, wk], [pvk])
                    self.cp(v[:, j, n2 * 512:(n2 + 1) * 512], pv_[:, :], [pvk], ["v%d" % j])
            for j in range(NS):
                pat, patk = self.bank()
                for hd in range(4):
                    self.mm(pat[:, hd * 128:(hd + 1) * 128], kin[:, hd, j * 128:(j + 1) * 128], qf[:, hd, j * 128:(j + 1) * 128],
                            True, True, ["kin%d" % hd, "qf%d" % hd], [patk])
                self.tt(attT, pat[:, :].rearrange("p (h n) -> p h n", n=128), self.cmask.unsqueeze(1).to_broadcast([P, 4, 128]),
                        ALU.mult, [patk, "cst"], ["attT"])
                po = [(self.pb[6], ("pb", 6)), (self.pb[7], ("pb", 7))]
                for c2 in range(2):
                    cs = slice(64 * c2, 64 * c2 + 64)
                    cur = chunk_ctr % 2
                    nxt = 1 - cur
                    cidx = 2 * j + c2
                    for hd in range(4):
                        ob, obk = po[hd // 2]
                        osl = ob[cs, (hd % 2) * 256:(hd % 2) * 256 + 256]
                        okey = obk
                        self.mm(osl, attT[cs, hd, 64 * c2:64 * c2 + 64], v[cs, j, hd * 256:(hd + 1) * 256], True, False,
                                ["attT", "v%d" % j], [okey])
                        self.mm(osl, qf[:, hd, j * 128 + 64 * c2:j * 128 + 64 * c2 + 64], Sbf[cur][:, hd, :], False, True,
                                ["qf%d" % hd, "Sbf%d_%d" % (cur, hd)], [okey])
                    for hd in range(4):
                        pkv, pkvk = self.bank()
                        self.mm(pkv[:, 0:256], kst[cs, j, hd * 128:(hd + 1) * 128], v[cs, j, hd * 256:(hd + 1) * 256], True, True,
                                ["kst%d" % j, "v%d" % j], [pkvk])
                        self.stt(S32[:, hd, :], S32[:, hd, :], dec[:, hd, cidx:cidx + 1], pkv[:, 0:256], ALU.mult, ALU.add,
                                 ["S32_%d" % hd, "dec%d" % hd, pkvk], ["S32_%d" % hd])
                        self.cp(Sbf[nxt][:, hd, :], S32[:, hd, :], ["S32_%d" % hd], ["Sbf%d_%d" % (nxt, hd)])
                    chunk_ctr += 1
                for n2 in range(2):
                    pz, pzk = self.bank()
                    for dk in range(8):
                        self.mm(pz[:, :], hnT[:, dk, j * 128:(j + 1) * 128], Win[:, dk, 2048 + n2 * 512:2048 + (n2 + 1) * 512],
                                dk == 0, dk == 7, [hk, wk], [pzk])
                    self.actf(sz[:, n2 * 512:(n2 + 1) * 512], pz[:, :], AF.Silu, [pzk], ["sz%d" % n2])
                    self.tt(sz[:, n2 * 512:(n2 + 1) * 512], sz[:, n2 * 512:(n2 + 1) * 512], gg[:, n2 * 512:(n2 + 1) * 512], ALU.mult,
                            ["sz%d" % n2, pk_], ["sz%d" % n2], eng="pool")
                for hd in range(4):
                    ob, obk = po[hd // 2]
                    osl = ob[:, (hd % 2) * 256:(hd % 2) * 256 + 256]
                    okeys = [obk]
                    self.actf(junk[:, 0:256], osl, AF.Square, okeys, ["junk", "ss4_%d" % hd], accum=ss4[:, hd:hd + 1])
                self.rsqrt_small(ss4[:, 4:8], ss4[:, 0:4], 1.0 / 256.0, ["ss4_%d" % h for h in range(4)], ["rstd4"], "ss4_t")
                for hd in range(4):
                    ob, obk = po[hd // 2]
                    osl = ob[:, (hd % 2) * 256:(hd % 2) * 256 + 256]
                    okeys = [obk]
                    self.stt(ygla[:, hd * 256:(hd + 1) * 256], osl, ss4[:, 4 + hd:5 + hd], sz[:, hd * 256:(hd + 1) * 256],
                             ALU.mult, ALU.mult, okeys + ["rstd4", "sz%d" % (hd // 2)], ["ygla"])
                if m == 0 and j == 0:
                    self.dump("ygla", ygla, ["ygla"], 384)
                    self.dump("v", v[:, 0, :], ["v0"], 512)
                    self.dump("kst", kst[:, 0, :], ["kst0"], 640, 0)
                    self.dump("attT", attT[:, 0, :], ["attT"], 640, 512)
                    self.dump("sz", sz, ["sz0", "sz1"], 768)
                pbk, pk = self.bank()
                pvw = pbk[:, :].bitcast(BF16)
                for mk in range(8):
                    self.tp(pvw[:, mk * 128:(mk + 1) * 128], ygla[:, mk * 128:(mk + 1) * 128], self.ident, ["ygla", "cst"], [pk])
                self.cp(yT[:, :, j * 128:(j + 1) * 128], pvw[:, 0:1024].rearrange("p (k t) -> p k t", t=128), [pk], ["yT"])
            self.outproj_residual(m, yT, 8, Wo, wk, hin, hin_key, hmid, hmid_key, hBe)

    def w_e2(self, L, slot):
        i = L // 2
        wk = ("w", slot)
        Win = self.load_w_rows(slot, 0, self.e_w_in[i], 0, 8, 3088, 3072, wk)
        Wo = self.load_w_rows(slot, 8 * 3072, self.e_w_out[i], 1024, 8, 0, 1024, wk)
        return (Win, Wo, wk)

    def pass_e2(self, L, W, hmid, hmid_key):
        i = L // 2
        A = self.A
        self.set_ring(range(2, 8))
        Win, Wo, wk = W
        NPE = 29
        oslot = wk[1] ^ 1
        dg = self.wa[oslot][:, 4096:4096 + 8 * NPE * 128].rearrange("p (c t n) -> p c t n", c=8, t=NPE)
        pk_ = "par_e2"
        cw = A.f32(8 * 31).rearrange("p (c k) -> p c k", k=31)
        cb = A.f32(8)
        lg = A.f32(8)
        lb = A.f32(8)
        self.dma(cw, self.e_cw[i], [], [pk_], pk_)
        self.dma(cb, self.e_cb[i], [], [pk_], pk_)
        self.dma(lg, self.e_lg[i], [], [pk_], pk_)
        self.dma(lb, self.e_lb[i], [], [pk_], pk_)
        for ct in range(8):
            self.tt(dg[:, ct, :, :], self.ident.unsqueeze(1).to_broadcast([P, NPE, 128]),
                    cw[:, ct, 0:NPE].unsqueeze(2).to_broadcast([P, NPE, 128]), ALU.mult, ["cst", pk_], [("dg", ct)],
                    eng=("dve" if ct % 2 == 0 else "pool"))
        ones32 = A.f32(128)
        self.memset(ones32, 1.0, ["ones32"])
        hnT2 = [A.bf16(8 * MT).rearrange("p (k t) -> p k t", t=MT) for _ in range(2)]
        u = [A.bf16(MT + 32) for _ in range(2)]
        halo = A.bf16(8 * 32).rearrange("p (c k) -> p c k", k=32)
        self.memset(halo, 0.0, [("halo", c) for c in range(8)])
        sg = [A.f32(MT) for _ in range(2)]
        acc = [A.f32(MT) for _ in range(2)]
        xc = A.f32(8 * MT).rearrange("p (c t) -> p c t", t=MT)
        sq = [A.f32(MT) for _ in range(2)]
        mean = A.f32(MT)
        rstd = A.f32(MT)
        nmr = A.f32(MT)
        tn = [A.f32(MT) for _ in range(2)]
        sl = [A.f32(MT) for _ in range(2)]
        szc = [A.f32(MT) for _ in range(2)]
        yT2 = [A.bf16(8 * MT).rearrange("p (k t) -> p k t", t=MT) for _ in range(2)]
        hB = [A.f32(1024) for _ in range(2 * NS)]
        s1b, s1k = self.pb[0], ("pb", 0)
        s2b, s2k = self.pb[1], ("pb", 1)
        for m in range(NMT):
            hnT = hnT2[m % 2]
            hk = "hnT%d" % (m % 2)
            yT = yT2[m % 2]
            yk = "yT%d" % (m % 2)
            self.load_hnT(m, hnT, hk)
            for ct in range(8):
                b = ct % 2
                ub = u[b]
                uk = "u%d" % b
                pval, pvk = self.bank()
                for dk in range(8):
                    self.mm(pval[:, 0:MT], Win[:, dk, ct * 128:(ct + 1) * 128], hnT[:, dk, :], dk == 0, dk == 7, [hk, wk], [pvk])
                pg, pgk = self.bank()
                for dk in range(8):
                    self.mm(pg[:, 0:MT], Win[:, dk, 1024 + ct * 128:1024 + (ct + 1) * 128], hnT[:, dk, :], dk == 0, dk == 7, [hk, wk], [pgk])
                self.actf(sg[b], pg[:, 0:MT], AF.Sigmoid, [pgk], ["sg%d" % b])
                self.cp(ub[:, 0:30], halo[:, ct, 0:30], [("halo", ct)], [uk + "h"], eng="pool")
                self.tt(ub[:, 30:30 + MT], pval[:, 0:MT], sg[b], ALU.mult, [pvk, "sg%d" % b], [uk])
                self.cp(halo[:, ct, 0:30], ub[:, MT:MT + 30], [uk], [("halo", ct)], eng="pool")
                pc, pck = self.bank()
                for t in range(NPE):
                    self.mm(pc[:, 0:MT], dg[:, ct, t, :], ub[:, t:t + MT], t == 0, t == NPE - 1, [uk, uk + "h", ("dg", ct)], [pck])
                a_ = acc[b]
                ka = "acc%d" % b
                self.ts(a_, ub[:, NPE:NPE + MT], cw[:, ct, NPE:NPE + 1], ALU.mult, [uk, uk + "h", pk_], [ka])
                for t in range(NPE + 1, 31):
                    self.stt(a_, ub[:, t:t + MT], cw[:, ct, t:t + 1], a_, ALU.mult, ALU.add, [uk, uk + "h", ka], [ka])
                self.stt(xc[:, ct, :], pc[:, 0:MT], cb[:, ct:ct + 1], a_, ALU.add, ALU.add, [pck, ka, pk_], [("xc", ct)])
                self.actf(sq[b], xc[:, ct, :], AF.Square, [("xc", ct)], ["sq%d" % b])
                self.mm(s1b[:, 0:MT], ones32, xc[:, ct, :], ct == 0, ct == 7, [("xc", ct), "ones32"], [s1k])
                self.mm(s2b[:, 0:MT], ones32, sq[b], ct == 0, ct == 7, ["sq%d" % b, "ones32"], [s2k])
            self.ts(mean, s1b[:, 0:MT], 1.0 / 1024.0, ALU.mult, [s1k], ["mean"])
            self.tt(nmr, mean, mean, ALU.mult, ["mean"], ["nmr"])
            self.stt(rstd, s2b[:, 0:MT], 1.0 / 1024.0, nmr, ALU.mult, ALU.subtract, [s2k, "nmr"], ["rstd"])
            self.ts(rstd, rstd, EPS, ALU.add, ["rstd"], ["rstd"])
            self.actf(rstd, rstd, AF.Sqrt, ["rstd"], ["rstd"])
            self.S.add("dve", lambda e: e.reciprocal(out=rstd, in_=rstd), reads=["rstd"], writes=["rstd"])
            self.stt(nmr, mean, -1.0, rstd, ALU.mult, ALU.mult, ["mean", "rstd"], ["nmr"])
            for ct in range(8):
                b = ct % 2
                self.tt(tn[b], xc[:, ct, :], rstd, ALU.mult, [("xc", ct), "rstd"], ["tn%d" % b])
                self.tt(tn[b], tn[b], nmr, ALU.add, ["tn%d" % b, "nmr"], ["tn%d" % b])
                self.actf(sl[b], tn[b], AF.Silu, ["tn%d" % b, pk_], ["sl%d" % b], bias=lb[:, ct:ct + 1], scale=lg[:, ct:ct + 1])
                pz, pzk = self.bank()
                for dk in range(8):
                    self.mm(pz[:, 0:MT], Win[:, dk, 2048 + ct * 128:2048 + (ct + 1) * 128], hnT[:, dk, :], dk == 0, dk == 7, [hk, wk], [pzk])
                self.actf(szc[b], pz[:, 0:MT], AF.Silu, [pzk], ["szc%d" % b])
                self.tt(yT[:, ct, :], sl[b], szc[b], ALU.mult, ["sl%d" % b, "szc%d" % b], [yk])
            self.outproj_residual(m, yT, 8, Wo, wk, hmid, hmid_key, hmid, hmid_key, hB, yk=yk)

    def w_oa(self, L, slot):
        i = L // 2
        wk = ("w", slot)
        Wu = self.load_w_rows(slot, 0, self.o_w_in[i], 0, 8, 0, 512, wk)
        w = self.wa[slot]
        V = dict(Wu=Wu, wk=wk, slot=slot)
        V["Kblk"] = w[:, 4096:8192].rearrange("p (g n) -> p g n", n=128)
        V["Wa_re"] = w[:, 8192:10240].rearrange("p (g n) -> p g n", n=128)
        V["Wa_im"] = w[:, 10240:12288].rearrange("p (g n) -> p g n", n=128)
        V["CAre"] = w[:, 12288:14336].rearrange("p (g n) -> p g n", n=128)
        V["nCAim"] = w[:, 14336:16384].rearrange("p (g n) -> p g n", n=128)
        V["usm"] = w[:, 16384:32768].rearrange("p (c s n) -> p c s n", c=4, s=8)
        return V

    def cmul(self, o_re, o_im, a_re, a_im, b_re, b_im, tmp, reads, wkeys, neg_im=False):
        kre, kim, kt = wkeys
        self.tt(o_re, a_re, b_re, ALU.mult, reads, [kre])
        self.tt(tmp, a_im, b_im, ALU.mult, reads, [kt])
        self.tt(o_re, o_re, tmp, ALU.subtract, [kre, kt], [kre])
        self.tt(o_im, a_re, b_im, ALU.mult, reads, [kim])
        self.tt(tmp, a_im, b_re, ALU.mult, reads + [kre], [kt])
        if neg_im:
            self.stt(o_im, o_im, -1.0, tmp, ALU.mult, ALU.subtract, [kim, kt], [kim])
        else:
            self.tt(o_im, o_im, tmp, ALU.add, [kim, kt], [kim])

    def reduce_angle(self, x, t, key):
        TWO_PI = float(2.0 * np.pi)
        PI = float(np.pi)
        for _ in range(8):
            self.ts(t, x, PI, ALU.is_gt, [key], ["ra_t"], s2=TWO_PI, op1=ALU.mult)
            self.tt(x, x, t, ALU.subtract, [key, "ra_t"], [key])
        for _ in range(2):
            self.ts(t, x, -PI, ALU.is_lt, [key], ["ra_t"], s2=TWO_PI, op1=ALU.mult)
            self.tt(x, x, t, ALU.add, [key, "ra_t"], [key])

    def s5_prep(self, L, V):
        i = L // 2
        A = self.A
        pk_ = "par_s5"
        T = self.pst
        sm = lambda: A.f32(16)
        lr, li, ldt, dtt, mag, ang, sarg, carg, t1, are, aim, den, nr, cfr, cfi, u1, u2 = [sm() for _ in range(17)]
        self.dma(lr, self.o_lamre[i], [], [pk_], pk_)
        self.dma(li, self.o_lamim[i], [], [pk_], pk_)
        self.dma(ldt, self.o_logdt[i], [], [pk_], pk_)
        b3 = lambda: A.f32(256).rearrange("p (g h) -> p g h", h=16)
        bre, bim, cre, cim, Bre, Bim, tb = [b3() for _ in range(7)]
        self.dma(bre, self.o_bre[i].rearrange("p (g h) -> p g h", h=16), [], [pk_], pk_)
        self.dma(bim, self.o_bim[i].rearrange("p (g h) -> p g h", h=16), [], [pk_], pk_)
        self.dma(cre, self.o_cre[i].rearrange("p (g h) -> p g h", h=16), [], [pk_], pk_)
        self.dma(cim, self.o_cim[i].rearrange("p (g h) -> p g h", h=16), [], [pk_], pk_)
        dcol = A.f32(32)
        self.dma(dcol, self.o_dcol[i], [], [pk_], pk_)
        R = [pk_]
        self.actf(dtt, ldt, AF.Exp, R, ["dtt"])
        self.tt(u1, lr, dtt, ALU.mult, R + ["dtt"], ["u1"])
        self.actf(mag, u1, AF.Exp, ["u1"], ["mag"])
        self.tt(ang, li, dtt, ALU.mult, R + ["dtt"], ["ang"])
        self.cp(sarg, ang, ["ang"], ["sarg"], eng="dve")
        self.ts(carg, ang, float(np.pi / 2), ALU.add, ["ang"], ["carg"])
        self.reduce_angle(sarg, t1, "sarg")
        self.reduce_angle(carg, t1, "carg")
        self.actf(sarg, sarg, AF.Sin, ["sarg"], ["sarg"])
        self.actf(carg, carg, AF.Sin, ["carg"], ["carg"])
        self.tt(are, mag, carg, ALU.mult, ["mag", "carg"], ["are"])
        self.tt(aim, mag, sarg, ALU.mult, ["mag", "sarg"], ["aim"])
        self.tt(den, lr, lr, ALU.mult, R, ["den"])
        self.tt(u1, li, li, ALU.mult, R, ["u1"])
        self.tt(den, den, u1, ALU.add, ["den", "u1"], ["den"])
        self.S.add("dve", lambda e: e.reciprocal(out=den, in_=den), reads=["den"], writes=["den"])
        self.ts(nr, are, -1.0, ALU.add, ["are"], ["nr"])
        self.tt(cfr, nr, lr, ALU.mult, ["nr"] + R, ["cfr"])
        self.tt(u1, aim, li, ALU.mult, ["aim"] + R, ["u1"])
        self.tt(cfr, cfr, u1, ALU.add, ["cfr", "u1"], ["cfr"])
        self.tt(cfr, cfr, den, ALU.mult, ["cfr", "den"], ["cfr"])
        self.tt(cfi, aim, lr, ALU.mult, ["aim"] + R, ["cfi"])
        self.tt(u2, nr, li, ALU.mult, ["nr"] + R, ["u2"])
        self.tt(cfi, cfi, u2, ALU.subtract, ["cfi", "u2"], ["cfi"])
        self.tt(cfi, cfi, den, ALU.mult, ["cfi", "den"], ["cfi"])
        bc3 = lambda a: a.unsqueeze(2).to_broadcast([P, 16, 16])
        self.cmul(Bre, Bim, bc3(cfr), bc3(cfi), bre, bim, tb, ["cfr", "cfi"] + R, ["Bre", "Bim", "tb"])
        if PREP_CUT <= 1:
            return
        Pre = A.f32(144).rearrange("p (g j) -> p g j", j=9)
        Pim = A.f32(144).rearrange("p (g j) -> p g j", j=9)
        Qre = A.f32(128).rearrange("p (g j) -> p g j", j=8)
        Qim = A.f32(128).rearrange("p (g j) -> p g j", j=8)
        Vre = A.f32(128).rearrange("p (g j) -> p g j", j=8)
        Vim = A.f32(128).rearrange("p (g j) -> p g j", j=8)
        self.memset(Pre[:, :, 0], 1.0, ["Pre"], eng="dve")
        self.memset(Pim[:, :, 0], 0.0, ["Pim"], eng="dve")
        self.memset(Qre[:, :, 0], 1.0, ["Qre"], eng="dve")
        self.memset(Qim[:, :, 0], 0.0, ["Qim"], eng="dve")
        for j in range(1, 9):
            self.cmul(Pre[:, :, j], Pim[:, :, j], Pre[:, :, j - 1], Pim[:, :, j - 1], are, aim, u1, ["Pre", "Pim", "are", "aim"], ["Pre", "Pim", "u1"])
        ire, iim = sm(), sm()
        self.tt(u2, mag, mag, ALU.mult, ["mag"], ["u2"])
        self.S.add("dve", lambda e: e.reciprocal(out=u2, in_=u2), reads=["u2"], writes=["u2"])
        self.tt(ire, are, u2, ALU.mult, ["are", "u2"], ["ire"])
        self.stt(iim, aim, -1.0, u2, ALU.mult, ALU.mult, ["aim", "u2"], ["iim"])
        for j in range(1, 8):
            self.cmul(Qre[:, :, j], Qim[:, :, j], Qre[:, :, j - 1], Qim[:, :, j - 1], ire, iim, u1, ["Qre", "Qim", "ire", "iim"], ["Qre", "Qim", "u1"])
        for s_ in range(8):
            self.cp(Vre[:, :, s_], Pre[:, :, 7 - s_], ["Pre"], ["Vre"], eng="dve")
            self.cp(Vim[:, :, s_], Pim[:, :, 7 - s_], ["Pim"], ["Vim"], eng="dve")
        c8 = [T[:, k * 128:(k + 1) * 128].rearrange("p (g j) -> p g j", j=8) for k in range(3)]
        c64 = [T[:, 384 + k * 128:384 + (k + 1) * 128].rearrange("p (g j) -> p g j", j=8) for k in range(3)]
        c512 = [T[:, 768 + k * 16:768 + (k + 1) * 16] for k in range(3)]
        self.cp(c8[0][:, :, 0], Pre[:, :, 8], ["Pre"], ["c8"], eng="dve")
        self.cp(c8[1][:, :, 0], Pim[:, :, 8], ["Pim"], ["c8"], eng="dve")
        for j in range(1, 8):
            self.cmul(c8[0][:, :, j], c8[1][:, :, j], c8[0][:, :, j - 1], c8[1][:, :, j - 1], c8[0][:, :, 0], c8[1][:, :, 0], u1, ["c8"], ["c8", "c8", "u1"])
        self.cp(c64[0][:, :, 0], c8[0][:, :, 7], ["c8"], ["c64"], eng="dve")
        self.cp(c64[1][:, :, 0], c8[1][:, :, 7], ["c8"], ["c64"], eng="dve")
        for j in range(1, 8):
            self.cmul(c64[0][:, :, j], c64[1][:, :, j], c64[0][:, :, j - 1], c64[1][:, :, j - 1], c64[0][:, :, 0], c64[1][:, :, 0], u1, ["c64"], ["c64", "c64", "u1"])
        self.cp(c512[0], c64[0][:, :, 7], ["c64"], ["c512"], eng="dve")
        self.cp(c512[1], c64[1][:, :, 7], ["c64"], ["c512"], eng="dve")
        self.ts(c8[2], c8[1], -1.0, ALU.mult, ["c8"], ["c8n"])
        self.ts(c64[2], c64[1], -1.0, ALU.mult, ["c64"], ["c64n"])
        self.ts(c512[2], c512[1], -1.0, ALU.mult, ["c512"], ["c512n"])
        if PREP_CUT <= 2:
            return
        HG = 4
        big = lambda: A.f32(HG * 128).rearrange("p (g s h) -> p g s h", s=8, h=16)
        Lre, Lim, Rre, nRim, tbig = [big() for _ in range(5)]
        bigb = lambda: A.bf16(HG * 128).rearrange("p (g n) -> p g n", n=128)
        Lre_b, Lim_b, Rre_b, nRim_b = [bigb() for _ in range(4)]
        ident32 = A.f32(128)
        ones32 = A.f32(128)
        maskST = A.f32(128)
        tK = [A.f32(128) for _ in range(2)]
        self.memset(ones32, 1.0, ["ones32"])
        self.S.add("pool", lambda e: e.affine_select(out=ident32, in_=ones32, pattern=[[-1, 128]], compare_op=ALU.is_equal,
                                                    fill=0.0, base=0, channel_multiplier=1), reads=["ones32"], writes=["ident32"])
        self.S.add("pool", lambda e: e.affine_select(out=maskST.rearrange("p (t h) -> p t h", h=16), in_=ones32.rearrange("p (t h) -> p t h", h=16),
                                                    pattern=[[16, 8], [0, 16]], compare_op=ALU.is_ge, fill=0.0, base=15, channel_multiplier=-1),
                   reads=["ones32"], writes=["maskST"])
        f3 = lambda a: a.rearrange("p g s h -> p g (s h)")
        Kblk = V["Kblk"]
        for hf in range(16 // HG):
            gs = slice(HG * hf, HG * hf + HG)
            bp = lambda a: a[:, gs].unsqueeze(3).to_broadcast([P, HG, 8, 16])
            bb = lambda a: a[:, gs].unsqueeze(2).to_broadcast([P, HG, 8, 16])
            self.cmul(Lre, Lim, bp(Qre), bp(Qim), bb(Bre), bb(Bim), tbig, ["Qre", "Qim", "Bre", "Bim"], ["Lre", "Lim", "tbig"])
            self.cmul(Rre, nRim, bp(Pre[:, :, 0:8]), bp(Pim[:, :, 0:8]), bb(cre), bb(cim), tbig, ["Pre", "Pim"] + R, ["Rre", "nRim", "tbig"], neg_im=True)
            self.cp(Lre_b, f3(Lre), ["Lre"], ["Lre_b"], eng="dve")
            self.cp(Lim_b, f3(Lim), ["Lim"], ["Lim_b"], eng="act")
            self.cp(Rre_b, f3(Rre), ["Rre"], ["Rre_b"], eng="dve")
            self.cp(nRim_b, f3(nRim), ["nRim"], ["nRim_b"], eng="act")
            for g0 in range(2 * HG * hf, 2 * HG * hf + 2 * HG, 8):
                bks = [self.bank(), self.bank()]
                for q in range(8):
                    g = g0 + q
                    gp, g2 = g // 2, g % 2
                    gl = gp - HG * hf
                    rs = slice(64 * g2, 64 * g2 + 64)
                    pbk, pk = bks[g2]
                    c0 = (q // 2) * 128
                    self.mm(pbk[:, c0:c0 + 128], Lre_b[rs, gl, :], Rre_b[rs, gl, :], True, False, ["Lre_b", "Rre_b"], [pk])
                    self.mm(pbk[:, c0:c0 + 128], Lim_b[rs, gl, :], nRim_b[rs, gl, :], False, True, ["Lim_b", "nRim_b"], [pk])
                for q in range(8):
                    g = g0 + q
                    g2 = g % 2
                    pbk, pk = bks[g2]
                    c0 = (q // 2) * 128
                    tk = tK[q % 2]
                    self.tt(tk, pbk[:, c0:c0 + 128], maskST, ALU.mult, [pk, "maskST"], ["tK%d" % (q % 2)])
                    self.stt(Kblk[:, g, :], ident32, dcol[:, g:g + 1], tk, ALU.mult, ALU.add, ["ident32", "tK%d" % (q % 2)] + R, ["Kblk"])
            self.cmul(Rre, nRim, bp(Vre), bp(Vim), bb(Bre), bb(Bim), tbig, ["Vre", "Vim", "Bre", "Bim"], ["Rre", "nRim", "tbig"])
            self.cp(Rre_b, f3(Rre), ["Rre"], ["Rre_b"], eng="dve")
            self.cp(nRim_b, f3(nRim), ["nRim"], ["nRim_b"], eng="act")
            for src, dst, sk in ((Rre_b, V["Wa_re"], "Rre_b"), (nRim_b, V["Wa_im"], "nRim_b")):
                for g0 in range(0, HG, 4):
                    pbk, pk = self.bank()
                    pvw = pbk[:, :].bitcast(BF16)
                    for q in range(4):
                        self.tp(pvw[:, q * 128:(q + 1) * 128], src[:, g0 + q, :], self.ident, [sk, "cst"], [pk])
                    self.cp(dst[:, HG * hf + g0:HG * hf + g0 + 4, :], pvw[:, 0:512].rearrange("p (g n) -> p g n", n=128), [pk], ["Wa"])
            self.cmul(Lre, Lim, bp(Pre[:, :, 1:9]), bp(Pim[:, :, 1:9]), bb(cre), bb(cim), tbig, ["Pre", "Pim"] + R, ["Lre", "Lim", "tbig"], neg_im=True)
            self.cp(V["CAre"][:, gs, :], f3(Lre), ["Lre"], ["CA"], eng="dve")
            self.cp(V["nCAim"][:, gs, :], f3(Lim), ["Lim"], ["CA"], eng="dve")
        V["c8"], V["c64"], V["c512"] = c8, c64, c512

    def s5_core(self, V):
        A = self.A
        usm = V["usm"]
        c8, c64, c512 = V["c8"], V["c64"], V["c512"]
        Zs = A.bf16(8 * 240).rearrange("p (g x) -> p g x", x=240)
        onesb = A.bf16(128)
        self.memset(onesb, 1.0, ["onesb"])
        self.memset(Zs, 0.0, ["Zs"])
        self.S.add("pool", lambda e: e.affine_select(out=Zs[:, :, 112:128], in_=onesb.rearrange("p (a b) -> p a b", b=16),
                                                    pattern=[[-16, 8], [-1, 16]], compare_op=ALU.is_equal, fill=0.0, base=0, channel_multiplier=1),
                   reads=["onesb", "Zs"], writes=["Zs"])
        U8 = A.bf16(8 * 512).rearrange("p (g c) -> p g c", c=512)
        X2 = [A.f32(1024).rearrange("p (r c) -> p r c", r=2) for _ in range(4)]
        X = [[x2[:, 0, :], x2[:, 1, :]] for x2 in X2]
        Sp = [[A.bf16(512) for _ in range(2)] for _ in range(4)]
        for pp in range(4):
            for r in range(2):
                self.memset(Sp[pp][r][:, 0:1], 0.0, [("Sp", pp, r)])
        for ct in range(4):
            uk = [("usm", ct, s_) for s_ in range(8)]
            for gq in range(8):
                pbk, pk = self.bank()
                for s_ in range(8):
                    self.mm(pbk[:, :], Zs[:, gq, 112 - 16 * s_:240 - 16 * s_], usm[:, ct, s_, :], s_ == 0, s_ == 7, ["Zs", uk[s_]], [pk])
                self.cp(U8[:, gq, :], pbk[:, :], [pk], [("U8", gq)], eng=("act" if gq % 2 == 0 else "dve"))
            for pp in range(4):
                gp = 4 * ct + pp
                for r, Wn in ((0, "Wa_re"), (1, "Wa_im")):
                    pbk, pk = self.bank()
                    self.mm(pbk[0:64, :], V[Wn][:, gp, 0:64], U8[:, 2 * pp, :], True, True, ["Wa", ("U8", 2 * pp)], [pk])
                    self.mm(pbk[64:128, :], V[Wn][:, gp, 64:128], U8[:, 2 * pp + 1, :], True, True, ["Wa", ("U8", 2 * pp + 1)], [pk])
                    self.cp(X[pp][r], pbk[:, :], [pk], [("X", pp, r)])
            steps = []
            for pp in range(4):
                gp = 4 * ct + pp
                xb = X2[pp]
                kx = [("X", pp, 0), ("X", pp, 1)]
                ckeys = ["c8", "c64", "c512", "c8n", "c64n", "c512n"]
                lst = []

                def cmac(o_b, s_b, cr, ci, cni, lst=lst, kx=kx, ckeys=ckeys):
                    lst.append((o_b, s_b, cr, o_b, kx + ckeys, kx))
                    lst.append((o_b[:, 0], s_b[:, 1], cni, o_b[:, 0], kx + ckeys, kx))
                    lst.append((o_b[:, 1], s_b[:, 0], ci, o_b[:, 1], kx + ckeys, kx))
                v3 = xb.rearrange("p r (m j) -> p r m j", j=8)
                vz = xb.rearrange("p r (q j w) -> p r q j w", j=8, w=8)[:, :, :, :, 7]
                vw = xb.rearrange("p r (q w) -> p r q w", w=64)[:, :, :, 63]
                co = lambda c, j: (c[0][:, gp, j:j + 1], c[1][:, gp, j:j + 1], c[2][:, gp, j:j + 1])
                for j in range(1, 8):
                    cmac(v3[:, :, :, j], v3[:, :, :, j - 1], *co(c8, 0))
                for j in range(1, 8):
                    cmac(vz[:, :, :, j], vz[:, :, :, j - 1], *co(c64, 0))
                c5 = (c512[0][:, gp:gp + 1], c512[1][:, gp:gp + 1], c512[2][:, gp:gp + 1])
                for q in range(1, 8):
                    cmac(vw[:, :, q:q + 1], vw[:, :, q - 1:q], *c5)
                for j in range(0, 7):
                    cmac(vz[:, :, 1:8, j], vw[:, :, 0:7], *co(c64, j))
                for j in range(0, 7):
                    cmac(v3[:, :, 1:64, j], v3[:, :, 0:63, 7], *co(c8, j))
                steps.append(lst)
            for k in range(len(steps[0])):
                for pp in range(4):
                    o_, s_in, c_, a_, rd, wr = steps[pp][k]
                    self.stt(o_, s_in, c_, a_, ALU.mult, ALU.add, rd, wr)
            for pp in range(4):
                for r in range(2):
                    self.cp(Sp[pp][r][:, 1:512], X[pp][r][:, 0:511], [("X", pp, r)], [("Sp", pp, r)], eng=("act" if r == 0 else "pool"))
            for gq in range(8):
                g = 8 * ct + gq
                gp, g2 = g // 2, g % 2
                pp = gq // 2
                rs = slice(64 * g2, 64 * g2 + 64)
                pbk, pk = self.bank()
                self.mm(pbk[:, :], V["Kblk"][:, g, :], U8[:, gq, :], True, False, ["Kblk", ("U8", gq)], [pk])
                self.mm(pbk[:, :], V["CAre"][rs, gp, :], Sp[pp][0][rs, :], False, False, ["CA", ("Sp", pp, 0)], [pk])
                self.mm(pbk[:, :], V["nCAim"][rs, gp, :], Sp[pp][1][rs, :], False, True, ["CA", ("Sp", pp, 1)], [pk])
                self.cp(U8[:, gq, :], pbk[:, :], [pk], [("U8", gq)], eng=("act" if gq % 2 == 0 else "dve"))
            for t_ in range(8):
                pbk, pk = self.bank()
                for gq in range(8):
                    self.mm(pbk[:, :], Zs[:, t_, 112 - 16 * gq:240 - 16 * gq], U8[:, gq, :], gq == 0, gq == 7, ["Zs", ("U8", gq)], [pk])
                self.cp(usm[:, ct, t_, :], pbk[:, :], [pk], [("usm", ct, t_)], eng=("act" if t_ % 2 == 0 else "dve"))

    def pass_oa(self, L, V, hin, hin_key):
        A = self.A
        self.set_ring(range(8))
        self.S.wide_segs.add(self.S.seg)
        self.s5_prep(L, V)
        if OA_STAGE < 2:
            return
        Wu, wk, usm = V["Wu"], V["wk"], V["usm"]
        gn = A.f32(1024)
        pk_ = "par_oa"
        self.dma(gn, self.norm_g[L:L + 1, :].partition_broadcast(128), [], [pk_], pk_)
        hA = [A.f32(1024) for _ in range(2 * NS)]
        hnb = A.bf16(1024)
        junk = A.bf16(1024)
        ss = A.f32(8)
        hnT2 = [A.bf16(8 * MT).rearrange("p (k t) -> p k t", t=MT) for _ in range(2)]
        nb = dict(hA=hA, hnb=hnb, gn=gn, hnT=None, ss=ss, gkey=pk_, junk=junk)
        for m in range(NMT):
            hnT = hnT2[m % 2]
            hk = "hnT%d" % (m % 2)
            nb["hnT"] = hnT
            nb["hk"] = hk
            self.norm_tile(m, hin, hin_key, nb, store_scr=True)
            for ct in range(4):
                pbk, pk = self.bank()
                for dk in range(8):
                    self.mm(pbk[:, 0:MT], Wu[:, dk, ct * 128:(ct + 1) * 128], hnT[:, dk, :], dk == 0, dk == 7, [hk, wk], [pk])
                self.cp(usm[:, ct, :, m * 32:(m + 1) * 32], pbk[:, 0:MT].rearrange("p (c s) -> p s c", s=8), [pk],
                        [("usm", ct, s_) for s_ in range(8)], eng=("act" if ct % 2 == 0 else "dve"))
        if OA_STAGE < 3:
            return
        self.S.barrier()
        A.reset()
        self.s5_core(V)

    def w_ob1(self, L, slot):
        i = L // 2
        wk = ("w", slot)
        Win = self.load_w_rows(slot, 0, self.o_w_in[i], 0, 8, 1024, 3072, wk)
        Wo = self.load_w_rows(slot, 24576, self.o_w_out[i], 512, 8, 0, 1024, wk)
        wsT = self.wa[slot][:, 32768:33792].rearrange("p (h t) -> p h t", t=128)
        self.dma(wsT, self.o_wsT[i].rearrange("p (h t) -> p h t", t=128), [], [wk], wk, eng="pool")
        return (Win, Wo, wsT, wk)

    def pass_ob1(self, L, W, hin, hin_key, hmid, hmid_key):
        i = L // 2
        A = self.A
        self.set_ring(range(8))
        Win, Wo, wsT, wk = W
        self.S.add("pool", lambda e: e.affine_select(out=wsT, in_=wsT, pattern=[[0, 8], [1, 128]], compare_op=ALU.is_ge, fill=0.0,
                                                    base=0, channel_multiplier=-1), reads=[wk], writes=["wsTm"])
        pk_ = "par_ob1"
        lng = A.f32(1024)
        lnb = A.f32(1024)
        bsb_f = A.f32(1024)
        bsb = bsb_f.rearrange("p (h t) -> p h t", t=128)
        self.dma(lng, self.o_lng[i].partition_broadcast(128), [], [pk_], pk_)
        self.dma(lnb, self.o_lnb[i].partition_broadcast(128), [], [pk_], pk_)
        self.dma(bsb_f, self.o_bs[i].partition_broadcast(128), [], [pk_], pk_)
        hnT2 = [A.bf16(8 * MT).rearrange("p (k t) -> p k t", t=MT) for _ in range(2)]
        vtmp = A.f32(1024)
        vnT2 = [A.bf16(NS * 1024).rearrange("p (j n) -> p j n", n=1024) for _ in range(2)]
        st4 = A.f32(16)
        junk = A.bf16(512)
        szt = [A.f32(MT) for _ in range(2)]
        t1 = [A.f32(MT) for _ in range(2)]
        yT2 = [A.bf16(8 * MT).rearrange("p (k t) -> p k t", t=MT) for _ in range(2)]
        hB = [A.f32(1024) for _ in range(2 * NS)]
        for m in range(NMT):
            hnT = hnT2[m % 2]
            hk = "hnT%d" % (m % 2)
            yT = yT2[m % 2]
            yk = "yT%d" % (m % 2)
            vnT = vnT2[m % 2]
            vq = "vnT%d_" % (m % 2)
            self.load_hnT(m, hnT, hk)
            for j in range(NS):
                pbs = []
                for n2 in range(2):
                    pv_, pvk = self.bank()
                    for dk in range(8):
                        self.mm(pv_[:, :], hnT[:, dk, j * 128:(j + 1) * 128], Win[:, dk, 1024 + n2 * 512:1024 + (n2 + 1) * 512],
                                dk == 0, dk == 7, [hk, wk], [pvk])
                    self.actf(junk, pv_[:, :], AF.Identity, [pvk], ["junk", "st_s%d" % n2], accum=st4[:, n2:n2 + 1])
                    self.actf(junk, pv_[:, :], AF.Square, [pvk], ["junk", "st_q%d" % n2], accum=st4[:, 2 + n2:3 + n2])
                    pbs.append((pv_, pvk))
                self.tt(st4[:, 4:5], st4[:, 0:1], st4[:, 1:2], ALU.add, ["st_s0", "st_s1"], ["st_m"])
                self.ts(st4[:, 4:5], st4[:, 4:5], 1.0 / 1024.0, ALU.mult, ["st_m"], ["st_m"])
                self.tt(st4[:, 5:6], st4[:, 2:3], st4[:, 3:4], ALU.add, ["st_q0", "st_q1"], ["st_v"])
                self.tt(st4[:, 6:7], st4[:, 4:5], st4[:, 4:5], ALU.mult, ["st_m"], ["st_mm"])
                self.stt(st4[:, 5:6], st4[:, 5:6], 1.0 / 1024.0, st4[:, 6:7], ALU.mult, ALU.subtract, ["st_v", "st_mm"], ["st_v"])
                self.rsqrt_small(st4[:, 7:8], st4[:, 5:6], 1.0, ["st_v"], ["st_r"], "st_rt")
                self.stt(st4[:, 8:9], st4[:, 4:5], -1.0, st4[:, 7:8], ALU.mult, ALU.mult, ["st_m", "st_r"], ["st_n"])
                for n2 in range(2):
                    pv_, pvk = pbs[n2]
                    sl_ = slice(n2 * 512, (n2 + 1) * 512)
                    self.ts(vtmp[:, sl_], pv_[:, :], st4[:, 7:8], ALU.mult, [pvk, "st_r", "st_n"], ["vtmp%d" % n2], s2=st4[:, 8:9], op1=ALU.add)
                    self.tt(vtmp[:, sl_], vtmp[:, sl_], lng[:, sl_], ALU.mult, ["vtmp%d" % n2, pk_], ["vtmp%d" % n2], eng="pool")
                    self.tt(vnT[:, j, sl_], vtmp[:, sl_], lnb[:, sl_], ALU.add, ["vtmp%d" % n2, pk_], [vq + str(j)], eng="pool")
            for hd in range(8):
                b = hd % 2
                psv, psvk = self.bank()
                for j in range(NS):
                    self.mm(psv[:, j * 128:(j + 1) * 128], vnT[:, j, hd * 128:(hd + 1) * 128], wsT[:, hd, :], True, True,
                            [vq + str(j), "wsTm"], [psvk])
                pu, puk = self.bank()
                for dk in range(8):
                    self.mm(pu[:, 0:MT], Win[:, dk, hd * 128:(hd + 1) * 128], hnT[:, dk, :], dk == 0, dk == 7, [hk, wk], [puk])
                pz, pzk = self.bank()
                for dk in range(8):
                    self.mm(pz[:, 0:MT], Win[:, dk, 2048 + hd * 128:2048 + (hd + 1) * 128], hnT[:, dk, :], dk == 0, dk == 7, [hk, wk], [pzk])
                self.actf(szt[b], pz[:, 0:MT], AF.Silu, [pzk], ["szt%d" % b])
                self.tt(t1[b].rearrange("p (j t) -> p j t", t=128), psv[:, 0:MT].rearrange("p (j t) -> p j t", t=128),
                        bsb[:, hd, :].unsqueeze(1).to_broadcast([P, NS, 128]), ALU.add, [psvk, pk_], ["t1_%d" % b])
                self.tt(t1[b], pu[:, 0:MT], t1[b], ALU.mult, [puk, "t1_%d" % b], ["t1_%d" % b])
                self.tt(yT[:, hd, :], t1[b], szt[b], ALU.mult, ["t1_%d" % b, "szt%d" % b], [yk])
            self.outproj_residual(m, yT, 8, Wo, wk, hin, hin_key, hmid, hmid_key, hB, yk=yk)

    def w_ob2(self, L, slot):
        i = L // 2
        wk = ("w", slot)
        Wz = self.load_w_rows(slot, 0, self.o_w_in[i], 0, 8, 512, 512, wk)
        Wg = self.load_w_rows(slot, 4096, self.o_w_glu[i], 0, 4, 0, 512, wk)
        Wo = self.load_w_rows(slot, 6144, self.o_w_out[i], 0, 4, 0, 1024, wk)
        usm = self.wa[slot][:, 16384:32768].rearrange("p (c s n) -> p c s n", c=4, s=8)
        return (Wz, Wg, Wo, usm, wk)

    def pass_ob2(self, L, W, hmid, hmid_key, last):
        i = L // 2
        A = self.A
        self.set_ring(range(8))
        Wz, Wg, Wo, usm, wk = W
        pk_ = "par_ob2"
        bglu = A.f32(4)
        self.dma(bglu, self.o_bglu[i], [], [pk_], pk_)
        gfin = None
        if last:
            gfin = A.f32(1024)
            self.dma(gfin, self.final_g.partition_broadcast(128), [], ["gfin"], "par_gfin")
        hnT2 = [A.bf16(8 * MT).rearrange("p (k t) -> p k t", t=MT) for _ in range(2)]
        ge322 = [A.f32(4 * MT).rearrange("p (c t) -> p c t", t=MT) for _ in range(2)]
        gebf2 = [A.bf16(4 * MT).rearrange("p (c t) -> p c t", t=MT) for _ in range(2)]
        sgm = [A.f32(MT) for _ in range(2)]
        szt = [A.f32(MT) for _ in range(2)]
        yT2 = [A.bf16(4 * MT).rearrange("p (k t) -> p k t", t=MT) for _ in range(2)]
        hB = [A.f32(1024) for _ in range(2 * NS)]
        junk = A.bf16(1024)
        ss = A.f32(8)
        for m in range(NMT):
            par = m % 2
            hnT = hnT2[par]
            hk = "hnT%d" % par
            yT = yT2[par]
            yk = "yT%d" % par
            ge32 = ge322[par]
            gebf = gebf2[par]
            self.load_hnT(m, hnT, hk)
            for ct in range(4):
                self.actf(ge32[:, ct, :].rearrange("p (c s) -> p s c", s=8), usm[:, ct, :, m * 32:(m + 1) * 32], AF.Gelu_apprx_tanh,
                          [("usm", ct, s_) for s_ in range(8)], [("ge32", par, ct)])
                self.cp(gebf[:, ct, :], ge32[:, ct, :], [("ge32", par, ct)], [("gebf", par, ct)], eng="dve")
            for mt in range(4):
                b = mt % 2
                pg, pgk = self.bank()
                for kt in range(4):
                    self.mm(pg[:, 0:MT], Wg[:, kt, mt * 128:(mt + 1) * 128], gebf[:, kt, :], kt == 0, kt == 3, [("gebf", par, kt), wk], [pgk])
                self.actf(sgm[b], pg[:, 0:MT], AF.Sigmoid, [pgk, pk_], ["sgm%d" % b], bias=bglu[:, mt:mt + 1])
                pz, pzk = self.bank()
                for dk in range(8):
                    self.mm(pz[:, 0:MT], Wz[:, dk, mt * 128:(mt + 1) * 128], hnT[:, dk, :], dk == 0, dk == 7, [hk, wk], [pzk])
                self.actf(szt[b], pz[:, 0:MT], AF.Silu, [pzk], ["szt%d" % b])
                self.tt(sgm[b], sgm[b], ge32[:, mt, :], ALU.mult, ["sgm%d" % b, ("ge32", par, mt)], ["sgm%d" % b])
                self.tt(yT[:, mt, :], sgm[b], szt[b], ALU.mult, ["sgm%d" % b, "szt%d" % b], [yk])
            if last:
                self.outproj_residual(m, yT, 4, Wo, wk, hmid, hmid_key, self.out, "out", hB, final_g=gfin, ss=ss, junk=junk, yk=yk)
            else:
                self.outproj_residual(m, yT, 4, Wo, wk, hmid, hmid_key, hmid, hmid_key, hB, yk=yk)

    def build(self):
        passes = []
        hin, hin_key = self.x, "x"
        for L in range(self.n_layers):
            hmid = self.hbuf[(L + 1) % 2]
            hmid_key = ("hb", (L + 1) % 2)
            if getattr(self, "first_pass", 0) > 0:
                hin, hin_key = self.x, "x"
            if L % 2 == 0:
                passes.append((lambda slot, L=L: self.w_e1(L, slot),
                               lambda W, L=L, a=hin, ak=hin_key, b=hmid, bk=hmid_key: self.pass_e1(L, W, a, ak, b, bk)))
                passes.append((lambda slot, L=L: self.w_e2(L, slot),
                               lambda W, L=L, b=hmid, bk=hmid_key: self.pass_e2(L, W, b, bk)))
            else:
                last = (L == self.n_layers - 1) and self.final_norm
                passes.append((lambda slot, L=L: self.w_oa(L, slot),
                               lambda W, L=L, a=hin, ak=hin_key: self.pass_oa(L, W, a, ak)))
                passes.append((lambda slot, L=L: self.w_ob1(L, slot),
                               lambda W, L=L, a=hin, ak=hin_key, b=hmid, bk=hmid_key: self.pass_ob1(L, W, a, ak, b, bk)))
                passes.append((lambda slot, L=L: self.w_ob2(L, slot),
                               lambda W, L=L, b=hmid, bk=hmid_key, last=last: self.pass_ob2(L, W, b, bk, last)))
                self.fused_out = last
            hin, hin_key = hmid, hmid_key
        if self.max_passes is not None:
            passes = passes[getattr(self, 'first_pass', 0):self.max_passes]
        slot = 0
        Wn = passes[0][0](slot)
        for k, (wl, run) in enumerate(passes):
            self.S.barrier()
            self.A.reset()
            Wcur = Wn
            slot ^= 1
            if k + 1 < len(passes):
                Wn = passes[k + 1][0](slot)
            run(Wcur)
        if getattr(self, "fused_out", False) and self.max_passes is None:
            self.S.emit()
            return
        if getattr(self, "first_pass", 0) > 0 and self.max_passes is not None and self.max_passes <= 3:
            hin, hin_key = self.x, "x"
        self.S.barrier()
        A = self.A
        A.reset()
        cpb = [A.f32(1024) for _ in range(2)]
        for t in range(0 if not self.debug else SEQ // 128, SEQ // 128):
            b = t % 2
            self.dma(cpb[b], hin[t * 128:(t + 1) * 128, :], [(hin_key, t // NS, t % NS)], ["cpb%d" % b], "ld_cp%d" % b)
            self.dma(self.out[t * 128:(t + 1) * 128, :], cpb[b], ["cpb%d" % b], [("out", t)], "st_cp%d" % b)
        self.S.emit()


def build_program(n_layers=4, final_norm=True, max_passes=None, first_pass=0):
    nc = bass.Bass("TRN2", target_bir_lowering=False)
    st = ExitStack()
    with st:
        b = Builder(nc, st, n_layers=n_layers, final_norm=final_norm, max_passes=max_passes)
        b.first_pass = first_pass
        b.build()
    return nc


def host_layout(inputs):
    f = lambda a: np.ascontiguousarray(np.asarray(a, dtype=np.float32))
    g = {}
    g["norm_g"] = f(inputs["norm_g"])
    g["final_g"] = f(inputs["final_g"]).reshape(1, D)
    g["e_w_in"] = f(inputs["e_w_in"])
    g["e_w_a2"] = f(inputs["e_w_a2"])
    g["e_b_a"] = f(inputs["e_b_a"]).reshape(2, 1, 512)
    g["e_gla_g"] = f(inputs["e_gla_g"]).reshape(2, 1, 1024)
    cw = f(inputs["e_conv_w"])
    g["e_cw"] = f(cw.reshape(2, 31, 8, 128).transpose(0, 3, 2, 1))
    cpl = lambda a: f(f(a).reshape(2, 8, 128).transpose(0, 2, 1))
    g["e_cb"] = cpl(inputs["e_conv_b"])
    g["e_lg"] = cpl(inputs["e_cln_g"])
    g["e_lb"] = cpl(inputs["e_cln_b"])
    g["e_w_out"] = f(inputs["e_w_out"])
    g["o_w_in"] = f(inputs["o_w_in"])
    gp_l = lambda a: f(f(a).reshape(2, 16, 2, 64).transpose(0, 2, 3, 1).reshape(2, 128, 16))
    g["o_lamre"] = gp_l(inputs["o_lam_re"])
    g["o_lamim"] = gp_l(inputs["o_lam_im"])
    ldt = f(inputs["o_log_dt"]).reshape(2, 16, 2)
    g["o_logdt"] = f(np.broadcast_to(ldt.transpose(0, 2, 1)[:, :, None, :], (2, 2, 64, 16)).reshape(2, 128, 16))
    b_l = lambda a: f(f(a).reshape(2, 16, 2, 64, 16).transpose(0, 2, 3, 1, 4).reshape(2, 128, 256))
    g["o_bre"] = b_l(inputs["o_b_re"])
    g["o_bim"] = b_l(inputs["o_b_im"])
    c_l = lambda a: f(f(a).reshape(2, 16, 2, 16, 64).transpose(0, 2, 4, 1, 3).reshape(2, 128, 256))
    g["o_cre"] = c_l(inputs["o_c_re"])
    g["o_cim"] = c_l(inputs["o_c_im"])
    dd = f(inputs["o_d"]).reshape(2, 32, 16)
    g["o_dcol"] = f(np.broadcast_to(dd.transpose(0, 2, 1)[:, None, :, :], (2, 8, 16, 32)).reshape(2, 128, 32))
    g["o_w_glu"] = f(inputs["o_w_glu"])
    g["o_bglu"] = f(f(inputs["o_b_glu"]).reshape(2, 4, 128).transpose(0, 2, 1))
    g["o_lng"] = f(inputs["o_sg_ln_g"]).reshape(2, 1, 1024)
    g["o_lnb"] = f(inputs["o_sg_ln_b"]).reshape(2, 1, 1024)
    ws = f(inputs["o_w_s"])
    g["o_wsT"] = f(ws.transpose(0, 3, 1, 2).reshape(2, 128, 1024))
    g["o_bs"] = f(inputs["o_b_s"]).reshape(2, 1, 1024)
    g["o_w_out"] = f(inputs["o_w_out"])
    return g


def kernel(**inputs):
    x = np.asarray(inputs["x"], dtype=np.float32)
    shared = host_layout(inputs)
    nc = build_program()
    in_maps = []
    for c in range(8):
        m = dict(shared)
        m["x"] = np.ascontiguousarray(x[c])
        in_maps.append(m)
    res = run_bass_kernel_spmd(nc, in_maps, core_ids=list(range(8)))
    return np.stack([np.asarray(r["out"], dtype=np.float32) for r in res.results], axis=0)
```

```python
import numpy as np
import concourse.bass as bass
import concourse.mybir as mybir
from concourse.bass_utils import run_bass_kernel_spmd
from contextlib import ExitStack

F32 = mybir.dt.float32
BF16 = mybir.dt.bfloat16
AF = mybir.ActivationFunctionType
ALU = mybir.AluOpType

P = 128
SEQ = 4096
D = 1024
MT = 256
NMT = SEQ // MT
NS = MT // 128
EPS = 1e-6
EVEN_IN = 6160
ODD_IN = 4096
WA_N = 33792
OA_STAGE = 3
E1_HNT_BUFS = 1
PREP_CUT = 99


class _Op:
    __slots__ = ("eng", "fn", "deps", "edges", "signal", "dma_sem", "dma_val", "cnt", "cost", "seg", "idx", "fin", "placed")

    def __init__(self, eng, fn):
        self.eng = eng
        self.fn = fn
        self.deps = []
        self.edges = []
        self.signal = False
        self.dma_sem = None
        self.dma_val = 0
        self.cnt = 0
        self.cost = 200.0
        self.seg = 0
        self.idx = 0
        self.fin = 0.0
        self.placed = False


class Sched:
    ENGS = ("pe", "act", "dve", "pool", "sp")
    WINDOW = {"pe": 256, "act": 64, "dve": 64, "pool": 32, "sp": 32}

    def __init__(self, nc, stack):
        self.nc = nc
        self.stack = stack
        self.all_ops = []
        self.last_w = {}
        self.readers = {}
        self.dma_keys = {}
        self.last_dma = {}
        self.eng_sem = {}
        for e in ("pe", "act", "dve", "pool"):
            self.eng_sem[e] = stack.enter_context(nc.semaphore("es_" + e))
        self.seg = 0
        self.reorder = True

    def barrier(self):
        self.seg += 1

    def add(self, eng, fn, reads=(), writes=(), dma_key=None, cost=None):
        op = _Op(eng, fn)
        op.seg = self.seg
        op.idx = len(self.all_ops)
        is_dma = dma_key is not None
        if cost is not None:
            op.cost = float(cost)
        my_sem = None
        if is_dma:
            ent = self.dma_keys.get(dma_key)
            if ent is None:
                sem = self.stack.enter_context(self.nc.semaphore("ds_%d" % len(self.dma_keys)))
                ent = [sem, 0]
                self.dma_keys[dma_key] = ent
            my_sem = ent[0]
            prev = self.last_dma.get(dma_key)
            if prev is not None:
                op.edges.append(prev)
        cand = []
        for k in reads:
            w = self.last_w.get(k)
            if w is not None:
                cand.append(w)
        for k in writes:
            w = self.last_w.get(k)
            if w is not None:
                cand.append(w)
            for r in self.readers.get(k, ()):
                cand.append(r)
        seen = set(id(x) for x in op.edges)
        for d in cand:
            if id(d) in seen or d is op:
                continue
            seen.add(id(d))
            op.edges.append(d)
            d_is_dma = d.dma_sem is not None
            if d_is_dma:
                if is_dma and d.dma_sem is my_sem:
                    continue
            elif d.eng == eng and not is_dma and eng == "pe":
                continue
            op.deps.append(d)
        if is_dma:
            ent[1] += 16
            op.dma_sem = ent[0]
            op.dma_val = ent[1]
            self.last_dma[dma_key] = op
        for k in reads:
            self.readers.setdefault(k, []).append(op)
        for k in writes:
            self.last_w[k] = op
            self.readers[k] = []
        self.all_ops.append(op)
        return op

    def _schedule(self):
        order = {e: [] for e in self.ENGS}
        segs = {}
        for op in self.all_ops:
            segs.setdefault(op.seg, []).append(op)
        bar_points = {e: [] for e in self.ENGS}
        t_base = 0.0
        LAT = 120.0
        for sg in sorted(segs):
            ops = segs[sg]
            for e in self.ENGS:
                bar_points[e].append(len(order[e]))
            if not self.reorder:
                for op in ops:
                    order[op.eng].append(op)
                    op.placed = True
                continue
            queues = {e: [op for op in ops if op.eng == e] for e in self.ENGS}
            heads = {e: 0 for e in self.ENGS}
            free = {e: t_base for e in self.ENGS}
            remaining = len(ops)
            tmax = t_base
            while remaining:
                best = None
                best_key = None
                for e in self.ENGS:
                    q = queues[e]
                    h = heads[e]
                    while h < len(q) and q[h].placed:
                        h += 1
                    heads[e] = h
                    lim = min(len(q), h + self.WINDOW[e])
                    k = h
                    found = 0
                    while k < lim:
                        op = q[k]
                        k += 1
                        if op.placed:
                            continue
                        ok = True
                        rdy = free[e]
                        for d in op.edges:
                            if d.seg != sg:
                                continue
                            if not d.placed:
                                ok = False
                                break
                            f = d.fin + (LAT if d.eng != e or d.dma_sem is not None else 30.0)
                            if f > rdy:
                                rdy = f
                        if not ok:
                            continue
                        key = (rdy, op.idx)
                        if best_key is None or key < best_key:
                            best_key = key
                            best = op
                        found += 1
                        if found >= 6:
                            break
                op = best
                assert op is not None, "scheduler stuck (cyclic deps?)"
                st = best_key[0]
                op.placed = True
                if op.dma_sem is not None:
                    op.fin = st + op.cost
                    free[op.eng] = st + 60.0
                else:
                    op.fin = st + op.cost
                    free[op.eng] = op.fin
                if op.fin > tmax:
                    tmax = op.fin
                order[op.eng].append(op)
                remaining -= 1
            t_base = tmax
        self.est_ns = t_base
        return order, bar_points

    def emit(self):
        nc = self.nc
        order, bar_points = self._schedule()
        dma_by_seg = {}
        for op in self.all_ops:
            if op.dma_sem is not None:
                dma_by_seg.setdefault(op.seg, {})[id(op.dma_sem)] = op
        nseg = self.seg + 1
        for si in range(1, nseg):
            lasts = []
            for e in self.ENGS:
                pos = bar_points[e][si]
                for k in range(pos - 1, -1, -1):
                    if order[e][k].dma_sem is None:
                        lasts.append(order[e][k])
                        break
            dl = {}
            for sj in range(si):
                dl.update(dma_by_seg.get(sj, {}))
            lasts += list(dl.values())
            for e in self.ENGS:
                pos = bar_points[e][si]
                if pos < len(order[e]):
                    op = order[e][pos]
                    have = set(id(x) for x in op.deps)
                    for d in lasts:
                        if id(d) in have:
                            continue
                        if d.dma_sem is None and d.eng == e and op.dma_sem is None:
                            continue
                        op.deps.append(d)
        for e in self.ENGS:
            for op in order[e]:
                for d in op.deps:
                    if d.dma_sem is None:
                        d.signal = True
        for e in ("pe", "act", "dve", "pool"):
            for op in reversed(order[e]):
                if op.dma_sem is None:
                    op.signal = True
                    break
        for e in self.ENGS:
            c = 0
            for op in order[e]:
                if op.dma_sem is None and op.signal:
                    c += 1
                op.cnt = c
        sched = self

        def replay(ename, eng):
            waited = {}
            for op in order[ename]:
                for d in op.deps:
                    if d.dma_sem is not None:
                        sem, val = d.dma_sem, d.dma_val
                    else:
                        sem, val = sched.eng_sem[d.eng], d.cnt
                    key = id(sem)
                    if waited.get(key, 0) >= val:
                        continue
                    waited[key] = val
                    eng.wait_ge(sem, val)
                ins = op.fn(eng)
                if op.dma_sem is not None:
                    ins.then_inc(op.dma_sem, 16)
                elif op.signal:
                    ins.then_inc(sched.eng_sem[ename], 1)
            if ename == "sp":
                for e in ("pe", "act", "dve", "pool"):
                    last = None
                    for op in order[e]:
                        if op.dma_sem is None:
                            last = op
                    if last is not None:
                        eng.wait_ge(sched.eng_sem[e], last.cnt)
                for k, (sem, cnt) in sched.dma_keys.items():
                    eng.wait_ge(sem, cnt)

        with nc.Block() as block:
            @block.tensor
            def _(e):
                replay("pe", e)

            @block.scalar
            def _(e):
                replay("act", e)

            @block.vector
            def _(e):
                replay("dve", e)

            @block.gpsimd
            def _(e):
                replay("pool", e)

            @block.sync
            def _(e):
                replay("sp", e)


class Arena:
    def __init__(self, tile_ap, n, name):
        self.t = tile_ap
        self.n = n
        self.off = 0
        self.name = name
        self.gen = 0

    def reset(self):
        self.off = 0
        self.gen += 1

    def f32(self, n):
        a = self.t[:, self.off:self.off + n]
        self.off += n
        assert self.off <= self.n, (self.name, self.off, self.n)
        return a

    def bf16(self, n):
        w = (n + 1) // 2
        a = self.t[:, self.off:self.off + w].bitcast(BF16)
        self.off += w
        assert self.off <= self.n, (self.name, self.off, self.n)
        return a


class Builder:
    def __init__(self, nc, st, n_layers=4, final_norm=True, max_passes=None):
        self.max_passes = max_passes
        self.debug = max_passes == 1
        self.nc = nc
        self.st = st
        self.S = Sched(nc, st)
        self.n_layers = n_layers
        self.final_norm = final_norm
        self.uid = 0
        S = self.S
        dt_in = lambda name, shape: nc.dram_tensor(name, shape, F32, kind="ExternalInput").ap()
        self.x = dt_in("x", [SEQ, D])
        self.norm_g = dt_in("norm_g", [4, D])
        self.final_g = dt_in("final_g", [1, D])
        self.e_w_in = dt_in("e_w_in", [2, D, EVEN_IN])
        self.e_w_a2 = dt_in("e_w_a2", [2, 16, 512])
        self.e_b_a = dt_in("e_b_a", [2, 1, 512])
        self.e_gla_g = dt_in("e_gla_g", [2, 1, 1024])
        self.e_cw = dt_in("e_cw", [2, P, 8, 31])
        self.e_cb = dt_in("e_cb", [2, P, 8])
        self.e_lg = dt_in("e_lg", [2, P, 8])
        self.e_lb = dt_in("e_lb", [2, P, 8])
        self.e_w_out = dt_in("e_w_out", [2, 2048, D])
        self.o_w_in = dt_in("o_w_in", [2, D, ODD_IN])
        self.o_lamre = dt_in("o_lamre", [2, P, 16])
        self.o_lamim = dt_in("o_lamim", [2, P, 16])
        self.o_logdt = dt_in("o_logdt", [2, P, 16])
        self.o_bre = dt_in("o_bre", [2, P, 256])
        self.o_bim = dt_in("o_bim", [2, P, 256])
        self.o_cre = dt_in("o_cre", [2, P, 256])
        self.o_cim = dt_in("o_cim", [2, P, 256])
        self.o_dcol = dt_in("o_dcol", [2, P, 32])
        self.o_w_glu = dt_in("o_w_glu", [2, 512, 512])
        self.o_bglu = dt_in("o_bglu", [2, P, 4])
        self.o_lng = dt_in("o_lng", [2, 1, 1024])
        self.o_lnb = dt_in("o_lnb", [2, 1, 1024])
        self.o_wsT = dt_in("o_wsT", [2, P, 1024])
        self.o_bs = dt_in("o_bs", [2, 1, 1024])
        self.o_w_out = dt_in("o_w_out", [2, 1536, D])
        self.out = nc.dram_tensor("out", [SEQ, D], F32, kind="ExternalOutput").ap()
        self.hbuf = [nc.dram_tensor("hbuf%d" % i, [SEQ, D], F32, kind="Internal").ap() for i in range(2)]
        self.hnscr = nc.dram_tensor("hnscr", [NMT, P, 8 * MT], BF16, kind="Internal").ap()

        sb = lambda name, shape, dt: st.enter_context(nc.sbuf_tensor(name, shape, dt))
        self.wa = [sb("wa%d" % i, [P, WA_N], BF16) for i in range(2)]
        self.cst = sb("cst", [P, 4 * 128], BF16)
        self.ident = self.cst[:, 0:128]
        self.triInc = self.cst[:, 128:256]
        self.triRev = self.cst[:, 256:384]
        self.cmask = self.cst[:, 384:512]
        self.pst = sb("pst", [P, 1024], F32)
        rem = int(nc.sbuf_bytes_remaining) - 256
        self.act_n = rem // 4
        self.act_t = sb("actarena", [P, self.act_n], F32)
        self.A = Arena(self.act_t, self.act_n, "act")
        self.pb = [st.enter_context(nc.psum_tensor("pb%d" % i, [P, 512], F32)) for i in range(8)]
        self.ring = list(range(8))
        self.ring_i = 0
        self.slot = 0
        self._consts()

    def key(self, name):
        self.uid += 1
        return "%s#%d" % (name, self.uid)

    def bank(self):
        i = self.ring[self.ring_i % len(self.ring)]
        self.ring_i += 1
        return self.pb[i], ("pb", i)

    def set_ring(self, banks):
        self.ring = list(banks)
        self.ring_i = 0

    def add(self, *a, **k):
        return self.S.add(*a, **k)

    @staticmethod
    def _n(ap):
        n = 1
        for d in ap.shape[1:]:
            n *= int(d)
        return n

    def mm(self, out, lhsT, rhs, start, stop, reads, writes):
        c = max(self._n(out), 64) / 2.0 + 10.0
        if lhsT.dtype == F32:
            c *= 4.0
        self.S.add("pe", lambda e: e.matmul(out, lhsT=lhsT, rhs=rhs, start=start, stop=stop), reads=reads, writes=writes, cost=c)

    def tp(self, out, in_, ident, reads, writes):
        self.S.add("pe", lambda e: e.transpose(out=out, in_=in_, identity=ident), reads=reads, writes=writes, cost=80.0)

    def actf(self, out, in_, func, reads, writes, bias=None, scale=None, accum=None):
        kw = {}
        if bias is not None:
            kw["bias"] = bias
        if scale is not None:
            kw["scale"] = scale
        if accum is not None:
            kw["accum_out"] = accum
        c = self._n(out) / 1.4 + 230.0 + (100.0 if accum is not None else 0.0)
        self.S.add("act", lambda e: e.activation(out=out, in_=in_, func=func, **kw), reads=reads, writes=writes, cost=c)

    def tt(self, out, in0, in1, op, reads, writes, eng="dve"):
        c = self._n(out) / (0.96 if eng == "dve" else 0.45) + (130.0 if eng == "dve" else 300.0)
        self.S.add(eng, lambda e: e.tensor_tensor(out=out, in0=in0, in1=in1, op=op), reads=reads, writes=writes, cost=c)

    def ts(self, out, in0, s1, op0, reads, writes, s2=None, op1=None, eng="dve"):
        c = self._n(out) / (0.96 if eng == "dve" else 0.45) + (130.0 if eng == "dve" else 300.0)
        if op1 is None:
            self.S.add(eng, lambda e: e.tensor_scalar(out=out, in0=in0, scalar1=s1, scalar2=None, op0=op0), reads=reads, writes=writes, cost=c)
        else:
            self.S.add(eng, lambda e: e.tensor_scalar(out=out, in0=in0, scalar1=s1, scalar2=s2, op0=op0, op1=op1), reads=reads, writes=writes, cost=c)

    def stt(self, out, in0, scalar, in1, op0, op1, reads, writes):
        c = self._n(out) / 0.96 + 130.0
        self.S.add("dve", lambda e: e.scalar_tensor_tensor(out=out, in0=in0, scalar=scalar, in1=in1, op0=op0, op1=op1), reads=reads, writes=writes, cost=c)

    def cp(self, out, in_, reads, writes, eng="act"):
        n = self._n(out)
        if eng == "act":
            self.S.add("act", lambda e: e.copy(out=out, in_=in_), reads=reads, writes=writes, cost=n / 1.4 + 230.0)
        else:
            c = n / (0.96 if eng == "dve" else 0.45) + (130.0 if eng == "dve" else 300.0)
            self.S.add(eng, lambda e: e.tensor_copy(out=out, in_=in_), reads=reads, writes=writes, cost=c)

    def memset(self, ap, val, writes, eng="pool"):
        self.S.add(eng, lambda e: e.memset(ap, val), writes=writes, cost=self._n(ap) / 0.9 + 150.0)

    def dma(self, out, in_, reads, writes, key, eng="sp"):
        nbytes = self._n(out) * int(out.shape[0]) * 4
        self.S.add(eng, lambda e: e.dma_start(out=out, in_=in_), reads=reads, writes=writes, dma_key=key, cost=2500.0 + nbytes / 60.0)

    def rsqrt_small(self, out, in_, scale, reads, writes, tmpkey=None):
        self.ts(out, in_, scale, ALU.mult, reads, writes, s2=EPS, op1=ALU.add)
        self.actf(out, out, AF.Sqrt, writes, writes)
        self.S.add("dve", lambda e: e.reciprocal(out=out, in_=out), reads=writes, writes=writes)

    def dump(self, name, ap, reads, row0, col0=0):
        if not getattr(self, "debug", False):
            return
        n = ap.shape[-1] if len(ap.shape) == 2 else None
        self.dma(self.out[row0:row0 + ap.shape[0], col0:col0 + n], ap, reads, [("dbg", name)], "dbg", eng="pool")

    def _consts(self):
        A = self.A
        ones = A.bf16(128)
        k1 = "c_ones"
        self.memset(ones, 1.0, [k1])
        self.memset(self.cst[:, :], 0.0, ["cst"])
        self.S.add("pool", lambda e: e.affine_select(out=self.ident, in_=ones, pattern=[[-1, 128]], compare_op=ALU.is_equal,
                                                    fill=0.0, base=0, channel_multiplier=1), reads=[k1, "cst"], writes=["cst"])
        sc = A.bf16(128)
        tmpm = A.bf16(128)
        self.memset(sc, -1.0 / 16.0, ["c_sc"])

        def tri(dst, src, srckey, upper):
            if upper:
                self.S.add("pool", lambda e: e.affine_select(out=tmpm, in_=src, pattern=[[1, 128]], compare_op=ALU.is_ge,
                                                            fill=0.0, base=0, channel_multiplier=-1), reads=[srckey, "tmpm"], writes=["tmpm"])
                self.cp(dst[:, 0:64], tmpm[:, 0:64], ["tmpm"], ["cst"], eng="pool")
                self.S.add("pool", lambda e: e.affine_select(out=dst[:, 64:128], in_=tmpm[:, 64:128], pattern=[[0, 64]], compare_op=ALU.is_ge,
                                                            fill=0.0, base=-64, channel_multiplier=1), reads=["tmpm", "cst"], writes=["cst"])
            else:
                self.S.add("pool", lambda e: e.affine_select(out=tmpm, in_=src, pattern=[[-1, 128]], compare_op=ALU.is_ge,
                                                            fill=0.0, base=-1, channel_multiplier=1), reads=[srckey, "tmpm"], writes=["tmpm"])
                self.cp(dst[:, 64:128], tmpm[:, 64:128], ["tmpm"], ["cst"], eng="pool")
                self.S.add("pool", lambda e: e.affine_select(out=dst[:, 0:64], in_=tmpm[:, 0:64], pattern=[[0, 64]], compare_op=ALU.is_ge,
                                                            fill=0.0, base=63, channel_multiplier=-1), reads=["tmpm", "cst"], writes=["cst"])
        tri(self.triInc, sc, "c_sc", True)
        tri(self.triRev, sc, "c_sc", False)
        tri(self.cmask, ones, k1, True)

    def load_w_rows(self, slot, off, src, r0, nrows_k, c0, ncols, key):
        view = self.wa[slot][:, off:off + nrows_k * ncols].rearrange("p (k n) -> p k n", n=ncols)
        s = src[r0:r0 + 128 * nrows_k, c0:c0 + ncols].rearrange("(k p) n -> p k n", p=128)
        for k in range(nrows_k):
            self.dma(view[:, k, :], s[:, k, :], [], [key], key, eng="pool")
        return view

    def norm_tile(self, m, hsrc, hsrc_key, bufs, store_scr):
        hA, hnb, gtile, hnT, ss, gkey = bufs["hA"], bufs["hnb"], bufs["gn"], bufs["hnT"], bufs["ss"], bufs["gkey"]
        hk = bufs.get("hk", "hnT")
        for j in range(NS):
            t0 = m * MT + j * 128
            bi = j if len(hA) <= NS else (m % 2) * NS + j
            hb = hA[bi]
            kh = "hA%d" % bi
            self.dma(hb, hsrc[t0:t0 + 128, :], [(hsrc_key, m, j)], [kh], "ld_hA%d" % bi)
            self.actf(hnb.bitcast(F32) if False else bufs["junk"], hb, AF.Square, [kh], ["junk", "ss"], accum=ss[:, 0:1])
            self.rsqrt_small(ss[:, 1:2], ss[:, 0:1], 1.0 / D, ["ss"], ["rstd"], "ss_t")
            self.stt(hnb, hb, ss[:, 1:2], gtile, ALU.mult, ALU.mult, [kh, "rstd", gkey], ["hnb"])
            if m == 0 and j == 0:
                self.dump("hnb", hnb, ["hnb"], 0)
            pbk, pk = self.bank()
            pv = pbk[:, :].bitcast(BF16)
            for dk in range(8):
                self.tp(pv[:, dk * 128:(dk + 1) * 128], hnb[:, dk * 128:(dk + 1) * 128], self.ident, ["hnb", "cst"], [pk])
            self.cp(hnT[:, :, j * 128:(j + 1) * 128], pv[:, 0:1024].rearrange("p (k t) -> p k t", t=128), [pk], [hk])
        if store_scr:
            self.dma(self.hnscr[m].rearrange("p (k t) -> p k t", t=MT), hnT, [hk], [("hnscr", m)], "st_" + hk)

    def load_hnT(self, m, hnT, hk="hnT"):
        self.dma(hnT, self.hnscr[m].rearrange("p (k t) -> p k t", t=MT), [("hnscr", m)], [hk], "ld_" + hk)

    def outproj_residual(self, m, yT, nk, Wo, wkey, hres, hres_key, hdst, hdst_key, hB, final_g=None, ss=None, junk=None, hbk="hB", yk="yT"):
        for j in range(NS):
            t0 = m * MT + j * 128
            bi = j if len(hB) <= NS else (m % 2) * NS + j
            if len(hB) == 1:
                bi = 0
            hb = hB[bi]
            kh = "%s%d" % (hbk, bi)
            self.dma(hb, hres[t0:t0 + 128, :], [(hres_key, m, j)], [kh], "ld_%s%d" % (hbk, bi))
            for n2 in range(2):
                pbk, pk = self.bank()
                for mk in range(nk):
                    self.mm(pbk[:, :], yT[:, mk, j * 128:(j + 1) * 128], Wo[:, mk, n2 * 512:(n2 + 1) * 512],
                            mk == 0, mk == nk - 1, [yk, wkey], [pk])
                self.tt(hb[:, n2 * 512:(n2 + 1) * 512], hb[:, n2 * 512:(n2 + 1) * 512], pbk[:, :], ALU.add, [kh, pk], [kh])
            if final_g is not None:
                self.actf(junk, hb, AF.Square, [kh], ["junk", "fss"], accum=ss[:, 2:3])
                self.rsqrt_small(ss[:, 3:4], ss[:, 2:3], 1.0 / D, ["fss"], ["frstd"], "fss_t")
                self.stt(hb, hb, ss[:, 3:4], final_g, ALU.mult, ALU.mult, [kh, "frstd", "gfin"], [kh])
            self.dma(hdst[t0:t0 + 128, :], hb, [kh], [(hdst_key, m, j)], "st_h%d" % bi)

    def w_e1(self, L, slot):
        i = L // 2
        wk = ("w", slot)
        Win = self.load_w_rows(slot, 0, self.e_w_in[i], 0, 8, 0, 3088, wk)
        Wo = self.load_w_rows(slot, 8 * 3088, self.e_w_out[i], 0, 8, 0, 1024, wk)
        return (Win, Wo, wk)

    def pass_e1(self, L, W, hin, hin_key, hmid, hmid_key):
        i = L // 2
        A = self.A
        self.set_ring(range(6))
        Win, Wo, wk = W
        gn = A.f32(1024)
        gg = A.f32(1024)
        pk_ = "par_e1"
        self.dma(gn, self.norm_g[L:L + 1, :].partition_broadcast(128), [], [pk_], pk_)
        self.dma(gg, self.e_gla_g[i].partition_broadcast(128), [], [pk_], pk_)
        tmpf = [A.f32(512) for _ in range(2)]
        wa2f = tmpf[0]
        wa2 = A.bf16(512)
        self.memset(wa2f[0:32, :], 0.0, ["tmpf0"])
        self.dma(wa2f[0:16, :], self.e_w_a2[i], ["tmpf0"], ["tmpf0"], "par_e1b")
        self.dma(wa2f[16:17, :], self.e_b_a[i], ["tmpf0"], ["tmpf0"], "par_e1b")
        self.cp(wa2[0:32, :], wa2f[0:32, :], ["tmpf0"], ["wa2"], eng="dve")
        hA = [A.f32(1024) for _ in range(NS)]
        hnb = A.bf16(1024)
        junk = A.bf16(1024)
        ss = A.f32(8)
        hnT2 = [A.bf16(8 * MT).rearrange("p (k t) -> p k t", t=MT) for _ in range(E1_HNT_BUFS)]
        nb = dict(hA=hA, hnb=hnb, gn=gn, hnT=None, ss=ss, gkey=pk_, junk=junk)
        alow = self.pst[:, 0:MT // 2].bitcast(BF16)
        self.memset(alow[0:32, :], 1.0, ["alow"])
        hBe = [A.f32(1024) for _ in range(NS)]
        spT = A.bf16(NS * 512).rearrange("p (j n) -> p j n", n=512)
        Eb = A.f32(MT)
        Enb = A.f32(MT)
        dec = A.f32(16).rearrange("p (h c) -> p h c", c=4)
        qf = A.bf16(4 * MT).rearrange("p (h t) -> p h t", t=MT)
        kin = A.bf16(4 * MT).rearrange("p (h t) -> p h t", t=MT)
        kst = A.bf16(NS * 512).rearrange("p (j n) -> p j n", n=512)
        v = A.bf16(NS * 1024).rearrange("p (j n) -> p j n", n=1024)
        S32 = A.f32(1024).rearrange("p (h n) -> p h n", n=256)
        Sbf = [A.bf16(1024).rearrange("p (h n) -> p h n", n=256) for _ in range(2)]
        attT = A.bf16(512).rearrange("p (h n) -> p h n", n=128)
        sz = A.f32(1024)
        ss4 = A.f32(8)
        ygla = A.bf16(1024)
        yT = A.bf16(8 * MT).rearrange("p (k t) -> p k t", t=MT)
        self.memset(S32, 0.0, ["S32_%d" % h for h in range(4)])
        self.memset(Sbf[0], 0.0, ["Sbf0_%d" % h for h in range(4)])
        chunk_ctr = 0
        QSC = 128.0 ** -0.5
        for m in range(NMT):
            hnT = hnT2[m % E1_HNT_BUFS]
            hk = "hnT%d" % (m % E1_HNT_BUFS)
            nb["hnT"] = hnT
            nb["hk"] = hk
            self.norm_tile(m, hin, hin_key, nb, store_scr=True)
            pbk, pk = self.bank()
            for dk in range(8):
                self.mm(pbk[0:16, 0:MT], Win[:, dk, 3072:3088], hnT[:, dk, :], dk == 0, dk == 7, [hk, wk], [pk])
            self.cp(alow[0:16, :], pbk[0:16, 0:MT], [pk], ["alow"])
            for j in range(NS):
                pbk, pk = self.bank()
                self.mm(pbk[:, :], alow[0:32, j * 128:(j + 1) * 128], wa2[0:32, :], True, True, ["alow", "wa2"], [pk])
                self.actf(tmpf[j], pbk[:, :], AF.Exp, [pk], ["tmpf%d" % j], scale=-1.0)
            for j in range(NS):
                self.actf(spT[:, j, :], tmpf[j], AF.Ln, ["tmpf%d" % j], ["spT"], bias=1.0)
            if m == 0:
                self.dump("spT", spT[:, 0, :], ["spT"], 128)
                self.dump("hnT0", hnT[:, 0, :], ["hnT"], 128, 512)
                self.dump("hnT1", hnT[:, 1, :], ["hnT"], 128, 768)
            for hd in range(4):
                pbk, pk = self.bank()
                for j in range(NS):
                    self.mm(pbk[:, j * 128:(j + 1) * 128], spT[:, j, hd * 128:(hd + 1) * 128], self.triInc, True, True, ["spT", "cst"], [pk])
                self.actf(Eb, pbk[:, 0:MT], AF.Exp, [pk], ["Eb"])
                self.actf(Enb, pbk[:, 0:MT], AF.Exp, [pk], ["Enb"], scale=-1.0)
                self.cp(dec[:, hd, :], Eb.rearrange("p (c t) -> p c t", t=64)[:, :, 63], ["Eb"], ["dec%d" % hd], eng="dve")
                pq, pqk = self.bank()
                for dk in range(8):
                    self.mm(pq[:, 0:MT], Win[:, dk, hd * 128:(hd + 1) * 128], hnT[:, dk, :], dk == 0, dk == 7, [hk, wk], [pqk])
                self.stt(qf[:, hd, :], pq[:, 0:MT], QSC, Eb, ALU.mult, ALU.mult, [pqk, "Eb"], ["qf%d" % hd])
                pkk, pkkk = self.bank()
                for dk in range(8):
                    self.mm(pkk[:, 0:MT], Win[:, dk, 512 + hd * 128:512 + (hd + 1) * 128], hnT[:, dk, :], dk == 0, dk == 7, [hk, wk], [pkkk])
                self.tt(kin[:, hd, :], pkk[:, 0:MT], Enb, ALU.mult, [pkkk, "Enb"], ["kin%d" % hd])
                if m == 0 and hd == 0:
                    self.dump("Eb", Eb, ["Eb"], 256)
                    self.dump("qf", qf[:, 0, :], ["qf0"], 256, 256)
                    self.dump("kin", kin[:, 0, :], ["kin0"], 256, 512)
            for j in range(NS):
                pbk, pk = self.bank()
                self.mm(pbk[:, :], self.triRev, spT[:, j, :], True, True, ["spT", "cst"], [pk])
                self.actf(tmpf[j], pbk[:, :], AF.Exp, [pk], ["tmpf%d" % j])
                pk2, pk2k = self.bank()
                for dk in range(8):
                    self.mm(pk2[:, :], hnT[:, dk, j * 128:(j + 1) * 128], Win[:, dk, 512:1024], dk == 0, dk == 7, [hk, wk], [pk2k])
                self.tt(kst[:, j, :], pk2[:, :], tmpf[j], ALU.mult, [pk2k, "tmpf%d" % j], ["kst%d" % j])
                for n2 in range(2):
                    pv_, pvk = self.bank()
                    for dk in range(8):
                        self.mm(pv_[:, :], hnT[:, dk, j * 128:(j + 1) * 128], Win[:, dk, 1024 + n2 * 512:1024 + (n2 + 1) * 512],
                                dk == 0, dk == 7, [hk, wk], [pvk])
                    self.cp(v[:, j, n2 * 512:(n2 + 1) * 512], pv_[:, :], [pvk], ["v%d" % j])
            for j in range(NS):
                pat, patk = self.bank()
                for hd in range(4):
                    self.mm(pat[:, hd * 128:(hd + 1) * 128], kin[:, hd, j * 128:(j + 1) * 128], qf[:, hd, j * 128:(j + 1) * 128],
                            True, True, ["kin%d" % hd, "qf%d" % hd], [patk])
                self.tt(attT, pat[:, :].rearrange("p (h n) -> p h n", n=128), self.cmask.unsqueeze(1).to_broadcast([P, 4, 128]),
                        ALU.mult, [patk, "cst"], ["attT"])
                po = [(self.pb[6], ("pb", 6)), (self.pb[7], ("pb", 7))]
                for c2 in range(2):
                    cs = slice(64 * c2, 64 * c2 + 64)
                    cur = chunk_ctr % 2
                    nxt = 1 - cur
                    cidx = 2 * j + c2
                    for hd in range(4):
                        ob, obk = po[hd // 2]
                        osl = ob[cs, (hd % 2) * 256:(hd % 2) * 256 + 256]
                        okey = obk
                        self.mm(osl, attT[cs, hd, 64 * c2:64 * c2 + 64], v[cs, j, hd * 256:(hd + 1) * 256], True, False,
                                ["attT", "v%d" % j], [okey])
                        self.mm(osl, qf[:, hd, j * 128 + 64 * c2:j * 128 + 64 * c2 + 64], Sbf[cur][:, hd, :], False, True,
                                ["qf%d" % hd, "Sbf%d_%d" % (cur, hd)], [okey])
                    for hd in range(4):
                        pkv, pkvk = self.bank()
                        self.mm(pkv[:, 0:256], kst[cs, j, hd * 128:(hd + 1) * 128], v[cs, j, hd * 256:(hd + 1) * 256], True, True,
                                ["kst%d" % j, "v%d" % j], [pkvk])
                        self.stt(S32[:, hd, :], S32[:, hd, :], dec[:, hd, cidx:cidx + 1], pkv[:, 0:256], ALU.mult, ALU.add,
                                 ["S32_%d" % hd, "dec%d" % hd, pkvk], ["S32_%d" % hd])
                        self.cp(Sbf[nxt][:, hd, :], S32[:, hd, :], ["S32_%d" % hd], ["Sbf%d_%d" % (nxt, hd)])
                    chunk_ctr += 1
                for n2 in range(2):
                    pz, pzk = self.bank()
                    for dk in range(8):
                        self.mm(pz[:, :], hnT[:, dk, j * 128:(j + 1) * 128], Win[:, dk, 2048 + n2 * 512:2048 + (n2 + 1) * 512],
                                dk == 0, dk == 7, [hk, wk], [pzk])
                    self.actf(sz[:, n2 * 512:(n2 + 1) * 512], pz[:, :], AF.Silu, [pzk], ["sz%d" % n2])
                    self.tt(sz[:, n2 * 512:(n2 + 1) * 512], sz[:, n2 * 512:(n2 + 1) * 512], gg[:, n2 * 512:(n2 + 1) * 512], ALU.mult,
                            ["sz%d" % n2, pk_], ["sz%d" % n2], eng="pool")
                for hd in range(4):
                    ob, obk = po[hd // 2]
                    osl = ob[:, (hd % 2) * 256:(hd % 2) * 256 + 256]
                    okeys = [obk]
                    self.actf(junk[:, 0:256], osl, AF.Square, okeys, ["junk", "ss4_%d" % hd], accum=ss4[:, hd:hd + 1])
                self.rsqrt_small(ss4[:, 4:8], ss4[:, 0:4], 1.0 / 256.0, ["ss4_%d" % h for h in range(4)], ["rstd4"], "ss4_t")
                for hd in range(4):
                    ob, obk = po[hd // 2]
                    osl = ob[:, (hd % 2) * 256:(hd % 2) * 256 + 256]
                    okeys = [obk]
                    self.stt(ygla[:, hd * 256:(hd + 1) * 256], osl, ss4[:, 4 + hd:5 + hd], sz[:, hd * 256:(hd + 1) * 256],
                             ALU.mult, ALU.mult, okeys + ["rstd4", "sz%d" % (hd // 2)], ["ygla"])
                if m == 0 and j == 0:
                    self.dump("ygla", ygla, ["ygla"], 384)
                    self.dump("v", v[:, 0, :], ["v0"], 512)
                    self.dump("kst", kst[:, 0, :], ["kst0"], 640, 0)
                    self.dump("attT", attT[:, 0, :], ["attT"], 640, 512)
                    self.dump("sz", sz, ["sz0", "sz1"], 768)
                pbk, pk = self.bank()
                pvw = pbk[:, :].bitcast(BF16)
                for mk in range(8):
                    self.tp(pvw[:, mk * 128:(mk + 1) * 128], ygla[:, mk * 128:(mk + 1) * 128], self.ident, ["ygla", "cst"], [pk])
                self.cp(yT[:, :, j * 128:(j + 1) * 128], pvw[:, 0:1024].rearrange("p (k t) -> p k t", t=128), [pk], ["yT"])
            self.outproj_residual(m, yT, 8, Wo, wk, hin, hin_key, hmid, hmid_key, hBe)

    def w_e2(self, L, slot):
        i = L // 2
        wk = ("w", slot)
        Win = self.load_w_rows(slot, 0, self.e_w_in[i], 0, 8, 3088, 3072, wk)
        Wo = self.load_w_rows(slot, 8 * 3072, self.e_w_out[i], 1024, 8, 0, 1024, wk)
        return (Win, Wo, wk)

    def pass_e2(self, L, W, hmid, hmid_key):
        i = L // 2
        A = self.A
        self.set_ring(range(2, 8))
        Win, Wo, wk = W
        NPE = 29
        oslot = wk[1] ^ 1
        dg = self.wa[oslot][:, 4096:4096 + 8 * NPE * 128].rearrange("p (c t n) -> p c t n", c=8, t=NPE)
        pk_ = "par_e2"
        cw = A.f32(8 * 31).rearrange("p (c k) -> p c k", k=31)
        cb = A.f32(8)
        lg = A.f32(8)
        lb = A.f32(8)
        self.dma(cw, self.e_cw[i], [], [pk_], pk_)
        self.dma(cb, self.e_cb[i], [], [pk_], pk_)
        self.dma(lg, self.e_lg[i], [], [pk_], pk_)
        self.dma(lb, self.e_lb[i], [], [pk_], pk_)
        for ct in range(8):
            self.tt(dg[:, ct, :, :], self.ident.unsqueeze(1).to_broadcast([P, NPE, 128]),
                    cw[:, ct, 0:NPE].unsqueeze(2).to_broadcast([P, NPE, 128]), ALU.mult, ["cst", pk_], [("dg", ct)],
                    eng=("dve" if ct % 2 == 0 else "pool"))
        ones32 = A.bf16(128)
        self.memset(ones32, 1.0, ["ones32"])
        xcb = [A.bf16(MT) for _ in range(2)]
        hnT2 = [A.bf16(8 * MT).rearrange("p (k t) -> p k t", t=MT) for _ in range(2)]
        u = [A.bf16(MT + 32) for _ in range(2)]
        halo = A.bf16(8 * 32).rearrange("p (c k) -> p c k", k=32)
        self.memset(halo, 0.0, [("halo", c) for c in range(8)])
        sg = [A.f32(MT) for _ in range(2)]
        acc = [A.f32(MT) for _ in range(2)]
        xc = A.f32(8 * MT).rearrange("p (c t) -> p c t", t=MT)
        sq = [A.bf16(MT) for _ in range(2)]
        mean = A.f32(MT)
        rstd = A.f32(MT)
        nmr = A.f32(MT)
        tn = [A.f32(MT) for _ in range(2)]
        sl = [A.f32(MT) for _ in range(2)]
        szc = [A.f32(MT) for _ in range(2)]
        yT2 = [A.bf16(8 * MT).rearrange("p (k t) -> p k t", t=MT) for _ in range(2)]
        hB = [A.f32(1024) for _ in range(2 * NS)]
        s1b, s1k = self.pb[0], ("pb", 0)
        s2b, s2k = self.pb[1], ("pb", 1)
        for m in range(NMT):
            hnT = hnT2[m % 2]
            hk = "hnT%d" % (m % 2)
            yT = yT2[m % 2]
            yk = "yT%d" % (m % 2)
            self.load_hnT(m, hnT, hk)
            for ct in range(8):
                b = ct % 2
                ub = u[b]
                uk = "u%d" % b
                pval, pvk = self.bank()
                for dk in range(8):
                    self.mm(pval[:, 0:MT], Win[:, dk, ct * 128:(ct + 1) * 128], hnT[:, dk, :], dk == 0, dk == 7, [hk, wk], [pvk])
                pg, pgk = self.bank()
                for dk in range(8):
                    self.mm(pg[:, 0:MT], Win[:, dk, 1024 + ct * 128:1024 + (ct + 1) * 128], hnT[:, dk, :], dk == 0, dk == 7, [hk, wk], [pgk])
                self.actf(sg[b], pg[:, 0:MT], AF.Sigmoid, [pgk], ["sg%d" % b])
                self.cp(ub[:, 0:30], halo[:, ct, 0:30], [("halo", ct)], [uk + "h"], eng="pool")
                self.tt(ub[:, 30:30 + MT], pval[:, 0:MT], sg[b], ALU.mult, [pvk, "sg%d" % b], [uk])
                self.cp(halo[:, ct, 0:30], ub[:, MT:MT + 30], [uk], [("halo", ct)], eng="pool")
                pc, pck = self.bank()
                for t in range(NPE):
                    self.mm(pc[:, 0:MT], dg[:, ct, t, :], ub[:, t:t + MT], t == 0, t == NPE - 1, [uk, uk + "h", ("dg", ct)], [pck])
                a_ = acc[b]
                ka = "acc%d" % b
                self.ts(a_, ub[:, NPE:NPE + MT], cw[:, ct, NPE:NPE + 1], ALU.mult, [uk, uk + "h", pk_], [ka])
                for t in range(NPE + 1, 31):
                    self.stt(a_, ub[:, t:t + MT], cw[:, ct, t:t + 1], a_, ALU.mult, ALU.add, [uk, uk + "h", ka], [ka])
                self.stt(xc[:, ct, :], pc[:, 0:MT], cb[:, ct:ct + 1], a_, ALU.add, ALU.add, [pck, ka, pk_], [("xc", ct)])
                self.actf(sq[b], xc[:, ct, :], AF.Square, [("xc", ct)], ["sq%d" % b])
                self.cp(xcb[b], xc[:, ct, :], [("xc", ct)], ["xcb%d" % b])
                self.mm(s1b[:, 0:MT], ones32, xcb[b], ct == 0, ct == 7, ["xcb%d" % b, "ones32"], [s1k])
                self.mm(s2b[:, 0:MT], ones32, sq[b], ct == 0, ct == 7, ["sq%d" % b, "ones32"], [s2k])
            self.ts(mean, s1b[:, 0:MT], 1.0 / 1024.0, ALU.mult, [s1k], ["mean"])
            self.tt(nmr, mean, mean, ALU.mult, ["mean"], ["nmr"])
            self.stt(rstd, s2b[:, 0:MT], 1.0 / 1024.0, nmr, ALU.mult, ALU.subtract, [s2k, "nmr"], ["rstd"])
            self.ts(rstd, rstd, EPS, ALU.add, ["rstd"], ["rstd"])
            self.actf(rstd, rstd, AF.Sqrt, ["rstd"], ["rstd"])
            self.S.add("dve", lambda e: e.reciprocal(out=rstd, in_=rstd), reads=["rstd"], writes=["rstd"])
            self.stt(nmr, mean, -1.0, rstd, ALU.mult, ALU.mult, ["mean", "rstd"], ["nmr"])
            for ct in range(8):
                b = ct % 2
                self.tt(tn[b], xc[:, ct, :], rstd, ALU.mult, [("xc", ct), "rstd"], ["tn%d" % b])
                self.tt(tn[b], tn[b], nmr, ALU.add, ["tn%d" % b, "nmr"], ["tn%d" % b])
                self.actf(sl[b], tn[b], AF.Silu, ["tn%d" % b, pk_], ["sl%d" % b], bias=lb[:, ct:ct + 1], scale=lg[:, ct:ct + 1])
                pz, pzk = self.bank()
                for dk in range(8):
                    self.mm(pz[:, 0:MT], Win[:, dk, 2048 + ct * 128:2048 + (ct + 1) * 128], hnT[:, dk, :], dk == 0, dk == 7, [hk, wk], [pzk])
                self.actf(szc[b], pz[:, 0:MT], AF.Silu, [pzk], ["szc%d" % b])
                self.tt(yT[:, ct, :], sl[b], szc[b], ALU.mult, ["sl%d" % b, "szc%d" % b], [yk])
            self.outproj_residual(m, yT, 8, Wo, wk, hmid, hmid_key, hmid, hmid_key, hB, yk=yk)

    def w_oa(self, L, slot):
        i = L // 2
        wk = ("w", slot)
        Wu = self.load_w_rows(slot, 0, self.o_w_in[i], 0, 8, 0, 512, wk)
        w = self.wa[slot]
        V = dict(Wu=Wu, wk=wk, slot=slot)
        V["Kblk"] = w[:, 4096:8192].rearrange("p (g n) -> p g n", n=128)
        V["Wa_re"] = w[:, 8192:10240].rearrange("p (g n) -> p g n", n=128)
        V["Wa_im"] = w[:, 10240:12288].rearrange("p (g n) -> p g n", n=128)
        V["CAre"] = w[:, 12288:14336].rearrange("p (g n) -> p g n", n=128)
        V["nCAim"] = w[:, 14336:16384].rearrange("p (g n) -> p g n", n=128)
        V["usm"] = w[:, 16384:32768].rearrange("p (c s n) -> p c s n", c=4, s=8)
        return V

    def cmul(self, o_re, o_im, a_re, a_im, b_re, b_im, tmp, reads, wkeys, neg_im=False):
        kre, kim, kt = wkeys
        self.tt(o_re, a_re, b_re, ALU.mult, reads, [kre])
        self.tt(tmp, a_im, b_im, ALU.mult, reads, [kt])
        self.tt(o_re, o_re, tmp, ALU.subtract, [kre, kt], [kre])
        self.tt(o_im, a_re, b_im, ALU.mult, reads, [kim])
        self.tt(tmp, a_im, b_re, ALU.mult, reads + [kre], [kt])
        if neg_im:
            self.stt(o_im, o_im, -1.0, tmp, ALU.mult, ALU.subtract, [kim, kt], [kim])
        else:
            self.tt(o_im, o_im, tmp, ALU.add, [kim, kt], [kim])

    def reduce_angle(self, x, t, key):
        TWO_PI = float(2.0 * np.pi)
        PI = float(np.pi)
        for _ in range(8):
            self.ts(t, x, PI, ALU.is_gt, [key], ["ra_t"], s2=TWO_PI, op1=ALU.mult)
            self.tt(x, x, t, ALU.subtract, [key, "ra_t"], [key])
        for _ in range(2):
            self.ts(t, x, -PI, ALU.is_lt, [key], ["ra_t"], s2=TWO_PI, op1=ALU.mult)
            self.tt(x, x, t, ALU.add, [key, "ra_t"], [key])

    def s5_prep(self, L, V):
        i = L // 2
        A = self.A
        pk_ = "par_s5"
        T = self.pst
        sm = lambda: A.f32(16)
        lr, li, ldt, dtt, mag, ang, sarg, carg, t1, are, aim, den, nr, cfr, cfi, u1, u2 = [sm() for _ in range(17)]
        self.dma(lr, self.o_lamre[i], [], [pk_], pk_)
        self.dma(li, self.o_lamim[i], [], [pk_], pk_)
        self.dma(ldt, self.o_logdt[i], [], [pk_], pk_)
        b3 = lambda: A.f32(256).rearrange("p (g h) -> p g h", h=16)
        bre, bim, cre, cim, Bre, Bim, tb = [b3() for _ in range(7)]
        self.dma(bre, self.o_bre[i].rearrange("p (g h) -> p g h", h=16), [], [pk_], pk_)
        self.dma(bim, self.o_bim[i].rearrange("p (g h) -> p g h", h=16), [], [pk_], pk_)
        self.dma(cre, self.o_cre[i].rearrange("p (g h) -> p g h", h=16), [], [pk_], pk_)
        self.dma(cim, self.o_cim[i].rearrange("p (g h) -> p g h", h=16), [], [pk_], pk_)
        dcol = A.f32(32)
        self.dma(dcol, self.o_dcol[i], [], [pk_], pk_)
        R = [pk_]
        self.actf(dtt, ldt, AF.Exp, R, ["dtt"])
        self.tt(u1, lr, dtt, ALU.mult, R + ["dtt"], ["u1"])
        self.actf(mag, u1, AF.Exp, ["u1"], ["mag"])
        self.tt(ang, li, dtt, ALU.mult, R + ["dtt"], ["ang"])
        self.cp(sarg, ang, ["ang"], ["sarg"], eng="dve")
        self.ts(carg, ang, float(np.pi / 2), ALU.add, ["ang"], ["carg"])
        self.reduce_angle(sarg, t1, "sarg")
        self.reduce_angle(carg, t1, "carg")
        self.actf(sarg, sarg, AF.Sin, ["sarg"], ["sarg"])
        self.actf(carg, carg, AF.Sin, ["carg"], ["carg"])
        self.tt(are, mag, carg, ALU.mult, ["mag", "carg"], ["are"])
        self.tt(aim, mag, sarg, ALU.mult, ["mag", "sarg"], ["aim"])
        self.tt(den, lr, lr, ALU.mult, R, ["den"])
        self.tt(u1, li, li, ALU.mult, R, ["u1"])
        self.tt(den, den, u1, ALU.add, ["den", "u1"], ["den"])
        self.S.add("dve", lambda e: e.reciprocal(out=den, in_=den), reads=["den"], writes=["den"])
        self.ts(nr, are, -1.0, ALU.add, ["are"], ["nr"])
        self.tt(cfr, nr, lr, ALU.mult, ["nr"] + R, ["cfr"])
        self.tt(u1, aim, li, ALU.mult, ["aim"] + R, ["u1"])
        self.tt(cfr, cfr, u1, ALU.add, ["cfr", "u1"], ["cfr"])
        self.tt(cfr, cfr, den, ALU.mult, ["cfr", "den"], ["cfr"])
        self.tt(cfi, aim, lr, ALU.mult, ["aim"] + R, ["cfi"])
        self.tt(u2, nr, li, ALU.mult, ["nr"] + R, ["u2"])
        self.tt(cfi, cfi, u2, ALU.subtract, ["cfi", "u2"], ["cfi"])
        self.tt(cfi, cfi, den, ALU.mult, ["cfi", "den"], ["cfi"])
        bc3 = lambda a: a.unsqueeze(2).to_broadcast([P, 16, 16])
        self.cmul(Bre, Bim, bc3(cfr), bc3(cfi), bre, bim, tb, ["cfr", "cfi"] + R, ["Bre", "Bim", "tb"])
        if PREP_CUT <= 1:
            return
        Pre = A.f32(144).rearrange("p (g j) -> p g j", j=9)
        Pim = A.f32(144).rearrange("p (g j) -> p g j", j=9)
        Qre = A.f32(128).rearrange("p (g j) -> p g j", j=8)
        Qim = A.f32(128).rearrange("p (g j) -> p g j", j=8)
        Vre = A.f32(128).rearrange("p (g j) -> p g j", j=8)
        Vim = A.f32(128).rearrange("p (g j) -> p g j", j=8)
        self.memset(Pre[:, :, 0], 1.0, ["Pre"], eng="dve")
        self.memset(Pim[:, :, 0], 0.0, ["Pim"], eng="dve")
        self.memset(Qre[:, :, 0], 1.0, ["Qre"], eng="dve")
        self.memset(Qim[:, :, 0], 0.0, ["Qim"], eng="dve")
        for j in range(1, 9):
            self.cmul(Pre[:, :, j], Pim[:, :, j], Pre[:, :, j - 1], Pim[:, :, j - 1], are, aim, u1, ["Pre", "Pim", "are", "aim"], ["Pre", "Pim", "u1"])
        ire, iim = sm(), sm()
        self.tt(u2, mag, mag, ALU.mult, ["mag"], ["u2"])
        self.S.add("dve", lambda e: e.reciprocal(out=u2, in_=u2), reads=["u2"], writes=["u2"])
        self.tt(ire, are, u2, ALU.mult, ["are", "u2"], ["ire"])
        self.stt(iim, aim, -1.0, u2, ALU.mult, ALU.mult, ["aim", "u2"], ["iim"])
        for j in range(1, 8):
            self.cmul(Qre[:, :, j], Qim[:, :, j], Qre[:, :, j - 1], Qim[:, :, j - 1], ire, iim, u1, ["Qre", "Qim", "ire", "iim"], ["Qre", "Qim", "u1"])
        for s_ in range(8):
            self.cp(Vre[:, :, s_], Pre[:, :, 7 - s_], ["Pre"], ["Vre"], eng="dve")
            self.cp(Vim[:, :, s_], Pim[:, :, 7 - s_], ["Pim"], ["Vim"], eng="dve")
        c8 = [T[:, k * 128:(k + 1) * 128].rearrange("p (g j) -> p g j", j=8) for k in range(3)]
        c64 = [T[:, 384 + k * 128:384 + (k + 1) * 128].rearrange("p (g j) -> p g j", j=8) for k in range(3)]
        c512 = [T[:, 768 + k * 16:768 + (k + 1) * 16] for k in range(3)]
        self.cp(c8[0][:, :, 0], Pre[:, :, 8], ["Pre"], ["c8"], eng="dve")
        self.cp(c8[1][:, :, 0], Pim[:, :, 8], ["Pim"], ["c8"], eng="dve")
        for j in range(1, 8):
            self.cmul(c8[0][:, :, j], c8[1][:, :, j], c8[0][:, :, j - 1], c8[1][:, :, j - 1], c8[0][:, :, 0], c8[1][:, :, 0], u1, ["c8"], ["c8", "c8", "u1"])
        self.cp(c64[0][:, :, 0], c8[0][:, :, 7], ["c8"], ["c64"], eng="dve")
        self.cp(c64[1][:, :, 0], c8[1][:, :, 7], ["c8"], ["c64"], eng="dve")
        for j in range(1, 8):
            self.cmul(c64[0][:, :, j], c64[1][:, :, j], c64[0][:, :, j - 1], c64[1][:, :, j - 1], c64[0][:, :, 0], c64[1][:, :, 0], u1, ["c64"], ["c64", "c64", "u1"])
        self.cp(c512[0], c64[0][:, :, 7], ["c64"], ["c512"], eng="dve")
        self.cp(c512[1], c64[1][:, :, 7], ["c64"], ["c512"], eng="dve")
        self.ts(c8[2], c8[1], -1.0, ALU.mult, ["c8"], ["c8n"])
        self.ts(c64[2], c64[1], -1.0, ALU.mult, ["c64"], ["c64n"])
        self.ts(c512[2], c512[1], -1.0, ALU.mult, ["c512"], ["c512n"])
        if PREP_CUT <= 2:
            return
        big = lambda: A.f32(2048).rearrange("p (g s h) -> p g s h", s=8, h=16)
        Lre, Lim, Rre, nRim, tbig = [big() for _ in range(5)]
        bp = lambda a: a.unsqueeze(3).to_broadcast([P, 16, 8, 16])
        bb = lambda a: a.unsqueeze(2).to_broadcast([P, 16, 8, 16])
        self.cmul(Lre, Lim, bp(Qre), bp(Qim), bb(Bre), bb(Bim), tbig, ["Qre", "Qim", "Bre", "Bim"], ["Lre", "Lim", "tbig"])
        self.cmul(Rre, nRim, bp(Pre[:, :, 0:8]), bp(Pim[:, :, 0:8]), bb(cre), bb(cim), tbig, ["Pre", "Pim"] + R, ["Rre", "nRim", "tbig"], neg_im=True)
        if PREP_CUT <= 3:
            return
        ident32 = A.f32(128)
        ones32 = A.f32(128)
        maskST = A.f32(128)
        tK = [A.f32(128) for _ in range(2)]
        self.memset(ones32, 1.0, ["ones32"])
        self.S.add("pool", lambda e: e.affine_select(out=ident32, in_=ones32, pattern=[[-1, 128]], compare_op=ALU.is_equal,
                                                    fill=0.0, base=0, channel_multiplier=1), reads=["ones32"], writes=["ident32"])
        self.S.add("pool", lambda e: e.affine_select(out=maskST.rearrange("p (t h) -> p t h", h=16), in_=ones32.rearrange("p (t h) -> p t h", h=16),
                                                    pattern=[[16, 8], [0, 16]], compare_op=ALU.is_ge, fill=0.0, base=15, channel_multiplier=-1),
                   reads=["ones32"], writes=["maskST"])
        bigb = lambda: A.bf16(2048).rearrange("p (g n) -> p g n", n=128)
        Lre_b, Lim_b, Rre_b, nRim_b = [bigb() for _ in range(4)]
        f3 = lambda a: a.rearrange("p g s h -> p g (s h)")
        self.cp(Lre_b, f3(Lre), ["Lre"], ["Lre_b"], eng="dve")
        self.cp(Lim_b, f3(Lim), ["Lim"], ["Lim_b"], eng="act")
        self.cp(Rre_b, f3(Rre), ["Rre"], ["Rre_b"], eng="dve")
        self.cp(nRim_b, f3(nRim), ["nRim"], ["nRim_b"], eng="act")
        Kblk = V["Kblk"]
        if PREP_CUT <= 3.2:
            return
        for g0 in range(0, 32, 8):
            bks = [self.bank(), self.bank()]
            for q in range(8):
                g = g0 + q
                gp, g2 = g // 2, g % 2
                rs = slice(64 * g2, 64 * g2 + 64)
                pbk, pk = bks[g2]
                c0 = (q // 2) * 128
                self.mm(pbk[:, c0:c0 + 128], Lre_b[rs, gp, :], Rre_b[rs, gp, :], True, False, ["Lre_b", "Rre_b"], [pk])
                self.mm(pbk[:, c0:c0 + 128], Lim_b[rs, gp, :], nRim_b[rs, gp, :], False, True, ["Lim_b", "nRim_b"], [pk])
            for q in range(8):
                g = g0 + q
                g2 = g % 2
                pbk, pk = bks[g2]
                c0 = (q // 2) * 128
                tk = tK[q % 2]
                self.tt(tk, pbk[:, c0:c0 + 128], maskST, ALU.mult, [pk, "maskST"], ["tK%d" % (q % 2)])
                self.stt(Kblk[:, g, :], ident32, dcol[:, g:g + 1], tk, ALU.mult, ALU.add, ["ident32", "tK%d" % (q % 2)] + R, ["Kblk"])
        if PREP_CUT <= 4:
            return
        Wre, Wim = Rre, nRim
        self.cmul(Wre, Wim, bp(Vre), bp(Vim), bb(Bre), bb(Bim), tbig, ["Vre", "Vim", "Bre", "Bim"], ["Rre", "nRim", "tbig"])
        self.cp(Rre_b, f3(Wre), ["Rre"], ["Rre_b"], eng="dve")
        self.cp(nRim_b, f3(Wim), ["nRim"], ["nRim_b"], eng="act")
        for src, dst, sk in ((Rre_b, V["Wa_re"], "Rre_b"), (nRim_b, V["Wa_im"], "nRim_b")):
            for g0 in range(0, 16, 4):
                pbk, pk = self.bank()
                pvw = pbk[:, :].bitcast(BF16)
                for q in range(4):
                    self.tp(pvw[:, q * 128:(q + 1) * 128], src[:, g0 + q, :], self.ident, [sk, "cst"], [pk])
                self.cp(dst[:, g0:g0 + 4, :], pvw[:, 0:512].rearrange("p (g n) -> p g n", n=128), [pk], ["Wa"])
        if PREP_CUT <= 5:
            return
        Cre32, Cim32 = Lre, Lim
        self.cmul(Cre32, Cim32, bp(Pre[:, :, 1:9]), bp(Pim[:, :, 1:9]), bb(cre), bb(cim), tbig, ["Pre", "Pim"] + R, ["Lre", "Lim", "tbig"], neg_im=True)
        self.cp(V["CAre"], Cre32.rearrange("p g s h -> p g (s h)"), ["Lre"], ["CA"], eng="dve")
        self.cp(V["nCAim"], Cim32.rearrange("p g s h -> p g (s h)"), ["Lim"], ["CA"], eng="dve")
        V["c8"], V["c64"], V["c512"] = c8, c64, c512

    def s5_core(self, V):
        A = self.A
        usm = V["usm"]
        c8, c64, c512 = V["c8"], V["c64"], V["c512"]
        Zs = A.bf16(8 * 240).rearrange("p (g x) -> p g x", x=240)
        onesb = A.bf16(128)
        self.memset(onesb, 1.0, ["onesb"])
        self.memset(Zs, 0.0, ["Zs"])
        self.S.add("pool", lambda e: e.affine_select(out=Zs[:, :, 112:128], in_=onesb.rearrange("p (a b) -> p a b", b=16),
                                                    pattern=[[-16, 8], [-1, 16]], compare_op=ALU.is_equal, fill=0.0, base=0, channel_multiplier=1),
                   reads=["onesb", "Zs"], writes=["Zs"])
        U8 = A.bf16(8 * 512).rearrange("p (g c) -> p g c", c=512)
        X2 = [A.f32(1024).rearrange("p (r c) -> p r c", r=2) for _ in range(4)]
        X = [[x2[:, 0, :], x2[:, 1, :]] for x2 in X2]
        Sp = [[A.bf16(512) for _ in range(2)] for _ in range(4)]
        for pp in range(4):
            for r in range(2):
                self.memset(Sp[pp][r][:, 0:1], 0.0, [("Sp", pp, r)])
        for ct in range(4):
            uk = [("usm", ct, s_) for s_ in range(8)]
            for gq in range(8):
                pbk, pk = self.bank()
                for s_ in range(8):
                    self.mm(pbk[:, :], Zs[:, gq, 112 - 16 * s_:240 - 16 * s_], usm[:, ct, s_, :], s_ == 0, s_ == 7, ["Zs", uk[s_]], [pk])
                self.cp(U8[:, gq, :], pbk[:, :], [pk], [("U8", gq)], eng=("act" if gq % 2 == 0 else "dve"))
            for pp in range(4):
                gp = 4 * ct + pp
                for r, Wn in ((0, "Wa_re"), (1, "Wa_im")):
                    pbk, pk = self.bank()
                    self.mm(pbk[0:64, :], V[Wn][:, gp, 0:64], U8[:, 2 * pp, :], True, True, ["Wa", ("U8", 2 * pp)], [pk])
                    self.mm(pbk[64:128, :], V[Wn][:, gp, 64:128], U8[:, 2 * pp + 1, :], True, True, ["Wa", ("U8", 2 * pp + 1)], [pk])
                    self.cp(X[pp][r], pbk[:, :], [pk], [("X", pp, r)])
            steps = []
            for pp in range(4):
                gp = 4 * ct + pp
                xb = X2[pp]
                kx = [("X", pp, 0), ("X", pp, 1)]
                ckeys = ["c8", "c64", "c512", "c8n", "c64n", "c512n"]
                lst = []

                def cmac(o_b, s_b, cr, ci, cni, lst=lst, kx=kx, ckeys=ckeys):
                    lst.append((o_b, s_b, cr, o_b, kx + ckeys, kx))
                    lst.append((o_b[:, 0], s_b[:, 1], cni, o_b[:, 0], kx + ckeys, kx))
                    lst.append((o_b[:, 1], s_b[:, 0], ci, o_b[:, 1], kx + ckeys, kx))
                v3 = xb.rearrange("p r (m j) -> p r m j", j=8)
                vz = xb.rearrange("p r (q j w) -> p r q j w", j=8, w=8)[:, :, :, :, 7]
                vw = xb.rearrange("p r (q w) -> p r q w", w=64)[:, :, :, 63]
                co = lambda c, j: (c[0][:, gp, j:j + 1], c[1][:, gp, j:j + 1], c[2][:, gp, j:j + 1])
                for j in range(1, 8):
                    cmac(v3[:, :, :, j], v3[:, :, :, j - 1], *co(c8, 0))
                for j in range(1, 8):
                    cmac(vz[:, :, :, j], vz[:, :, :, j - 1], *co(c64, 0))
                c5 = (c512[0][:, gp:gp + 1], c512[1][:, gp:gp + 1], c512[2][:, gp:gp + 1])
                for q in range(1, 8):
                    cmac(vw[:, :, q:q + 1], vw[:, :, q - 1:q], *c5)
                for j in range(0, 7):
                    cmac(vz[:, :, 1:8, j], vw[:, :, 0:7], *co(c64, j))
                for j in range(0, 7):
                    cmac(v3[:, :, 1:64, j], v3[:, :, 0:63, 7], *co(c8, j))
                steps.append(lst)
            for k in range(len(steps[0])):
                for pp in range(4):
                    o_, s_in, c_, a_, rd, wr = steps[pp][k]
                    self.stt(o_, s_in, c_, a_, ALU.mult, ALU.add, rd, wr)
            for pp in range(4):
                for r in range(2):
                    self.cp(Sp[pp][r][:, 1:512], X[pp][r][:, 0:511], [("X", pp, r)], [("Sp", pp, r)], eng=("act" if r == 0 else "pool"))
            for gq in range(8):
                g = 8 * ct + gq
                gp, g2 = g // 2, g % 2
                pp = gq // 2
                rs = slice(64 * g2, 64 * g2 + 64)
                pbk, pk = self.bank()
                self.mm(pbk[:, :], V["Kblk"][:, g, :], U8[:, gq, :], True, False, ["Kblk", ("U8", gq)], [pk])
                self.mm(pbk[:, :], V["CAre"][rs, gp, :], Sp[pp][0][rs, :], False, False, ["CA", ("Sp", pp, 0)], [pk])
                self.mm(pbk[:, :], V["nCAim"][rs, gp, :], Sp[pp][1][rs, :], False, True, ["CA", ("Sp", pp, 1)], [pk])
                self.cp(U8[:, gq, :], pbk[:, :], [pk], [("U8", gq)], eng=("act" if gq % 2 == 0 else "dve"))
            for t_ in range(8):
                pbk, pk = self.bank()
                for gq in range(8):
                    self.mm(pbk[:, :], Zs[:, t_, 112 - 16 * gq:240 - 16 * gq], U8[:, gq, :], gq == 0, gq == 7, ["Zs", ("U8", gq)], [pk])
                self.cp(usm[:, ct, t_, :], pbk[:, :], [pk], [("usm", ct, t_)], eng=("act" if t_ % 2 == 0 else "dve"))

    def pass_oa(self, L, V, hin, hin_key):
        A = self.A
        self.set_ring(range(8))
        self.s5_prep(L, V)
        if OA_STAGE < 2:
            return
        self.S.barrier()
        A.reset()
        Wu, wk, usm = V["Wu"], V["wk"], V["usm"]
        gn = A.f32(1024)
        pk_ = "par_oa"
        self.dma(gn, self.norm_g[L:L + 1, :].partition_broadcast(128), [], [pk_], pk_)
        hA = [A.f32(1024) for _ in range(2 * NS)]
        hnb = A.bf16(1024)
        junk = A.bf16(1024)
        ss = A.f32(8)
        hnT2 = [A.bf16(8 * MT).rearrange("p (k t) -> p k t", t=MT) for _ in range(2)]
        nb = dict(hA=hA, hnb=hnb, gn=gn, hnT=None, ss=ss, gkey=pk_, junk=junk)
        for m in range(NMT):
            hnT = hnT2[m % 2]
            hk = "hnT%d" % (m % 2)
            nb["hnT"] = hnT
            nb["hk"] = hk
            self.norm_tile(m, hin, hin_key, nb, store_scr=True)
            for ct in range(4):
                pbk, pk = self.bank()
                for dk in range(8):
                    self.mm(pbk[:, 0:MT], Wu[:, dk, ct * 128:(ct + 1) * 128], hnT[:, dk, :], dk == 0, dk == 7, [hk, wk], [pk])
                self.cp(usm[:, ct, :, m * 32:(m + 1) * 32], pbk[:, 0:MT].rearrange("p (c s) -> p s c", s=8), [pk],
                        [("usm", ct, s_) for s_ in range(8)], eng=("act" if ct % 2 == 0 else "dve"))
        if OA_STAGE < 3:
            return
        self.S.barrier()
        A.reset()
        self.s5_core(V)

    def w_ob1(self, L, slot):
        i = L // 2
        wk = ("w", slot)
        Win = self.load_w_rows(slot, 0, self.o_w_in[i], 0, 8, 1024, 3072, wk)
        Wo = self.load_w_rows(slot, 24576, self.o_w_out[i], 512, 8, 0, 1024, wk)
        wsT = self.wa[slot][:, 32768:33792].rearrange("p (h t) -> p h t", t=128)
        self.dma(wsT, self.o_wsT[i].rearrange("p (h t) -> p h t", t=128), [], [wk], wk, eng="pool")
        return (Win, Wo, wsT, wk)

    def pass_ob1(self, L, W, hin, hin_key, hmid, hmid_key):
        i = L // 2
        A = self.A
        self.set_ring(range(8))
        Win, Wo, wsT, wk = W
        self.S.add("pool", lambda e: e.affine_select(out=wsT, in_=wsT, pattern=[[0, 8], [1, 128]], compare_op=ALU.is_ge, fill=0.0,
                                                    base=0, channel_multiplier=-1), reads=[wk], writes=["wsTm"])
        pk_ = "par_ob1"
        lng = A.f32(1024)
        lnb = A.f32(1024)
        bsb_f = A.f32(1024)
        bsb = bsb_f.rearrange("p (h t) -> p h t", t=128)
        self.dma(lng, self.o_lng[i].partition_broadcast(128), [], [pk_], pk_)
        self.dma(lnb, self.o_lnb[i].partition_broadcast(128), [], [pk_], pk_)
        self.dma(bsb_f, self.o_bs[i].partition_broadcast(128), [], [pk_], pk_)
        hnT2 = [A.bf16(8 * MT).rearrange("p (k t) -> p k t", t=MT) for _ in range(2)]
        vtmp = A.f32(1024)
        vnT2 = [A.bf16(NS * 1024).rearrange("p (j n) -> p j n", n=1024) for _ in range(2)]
        st4 = A.f32(16)
        junk = A.bf16(512)
        szt = [A.f32(MT) for _ in range(2)]
        t1 = [A.f32(MT) for _ in range(2)]
        yT2 = [A.bf16(8 * MT).rearrange("p (k t) -> p k t", t=MT) for _ in range(2)]
        hB = [A.f32(1024) for _ in range(2 * NS)]
        for m in range(NMT):
            hnT = hnT2[m % 2]
            hk = "hnT%d" % (m % 2)
            yT = yT2[m % 2]
            yk = "yT%d" % (m % 2)
            vnT = vnT2[m % 2]
            vq = "vnT%d_" % (m % 2)
            self.load_hnT(m, hnT, hk)
            for j in range(NS):
                pbs = []
                for n2 in range(2):
                    pv_, pvk = self.bank()
                    for dk in range(8):
                        self.mm(pv_[:, :], hnT[:, dk, j * 128:(j + 1) * 128], Win[:, dk, 1024 + n2 * 512:1024 + (n2 + 1) * 512],
                                dk == 0, dk == 7, [hk, wk], [pvk])
                    self.actf(junk, pv_[:, :], AF.Identity, [pvk], ["junk", "st_s%d" % n2], accum=st4[:, n2:n2 + 1])
                    self.actf(junk, pv_[:, :], AF.Square, [pvk], ["junk", "st_q%d" % n2], accum=st4[:, 2 + n2:3 + n2])
                    pbs.append((pv_, pvk))
                self.tt(st4[:, 4:5], st4[:, 0:1], st4[:, 1:2], ALU.add, ["st_s0", "st_s1"], ["st_m"])
                self.ts(st4[:, 4:5], st4[:, 4:5], 1.0 / 1024.0, ALU.mult, ["st_m"], ["st_m"])
                self.tt(st4[:, 5:6], st4[:, 2:3], st4[:, 3:4], ALU.add, ["st_q0", "st_q1"], ["st_v"])
                self.tt(st4[:, 6:7], st4[:, 4:5], st4[:, 4:5], ALU.mult, ["st_m"], ["st_mm"])
                self.stt(st4[:, 5:6], st4[:, 5:6], 1.0 / 1024.0, st4[:, 6:7], ALU.mult, ALU.subtract, ["st_v", "st_mm"], ["st_v"])
                self.rsqrt_small(st4[:, 7:8], st4[:, 5:6], 1.0, ["st_v"], ["st_r"], "st_rt")
                self.stt(st4[:, 8:9], st4[:, 4:5], -1.0, st4[:, 7:8], ALU.mult, ALU.mult, ["st_m", "st_r"], ["st_n"])
                for n2 in range(2):
                    pv_, pvk = pbs[n2]
                    sl_ = slice(n2 * 512, (n2 + 1) * 512)
                    self.ts(vtmp[:, sl_], pv_[:, :], st4[:, 7:8], ALU.mult, [pvk, "st_r", "st_n"], ["vtmp%d" % n2], s2=st4[:, 8:9], op1=ALU.add)
                    self.tt(vtmp[:, sl_], vtmp[:, sl_], lng[:, sl_], ALU.mult, ["vtmp%d" % n2, pk_], ["vtmp%d" % n2], eng="pool")
                    self.tt(vnT[:, j, sl_], vtmp[:, sl_], lnb[:, sl_], ALU.add, ["vtmp%d" % n2, pk_], [vq + str(j)], eng="pool")
            for hd in range(8):
                b = hd % 2
                psv, psvk = self.bank()
                for j in range(NS):
                    self.mm(psv[:, j * 128:(j + 1) * 128], vnT[:, j, hd * 128:(hd + 1) * 128], wsT[:, hd, :], True, True,
                            [vq + str(j), "wsTm"], [psvk])
                pu, puk = self.bank()
                for dk in range(8):
                    self.mm(pu[:, 0:MT], Win[:, dk, hd * 128:(hd + 1) * 128], hnT[:, dk, :], dk == 0, dk == 7, [hk, wk], [puk])
                pz, pzk = self.bank()
                for dk in range(8):
                    self.mm(pz[:, 0:MT], Win[:, dk, 2048 + hd * 128:2048 + (hd + 1) * 128], hnT[:, dk, :], dk == 0, dk == 7, [hk, wk], [pzk])
                self.actf(szt[b], pz[:, 0:MT], AF.Silu, [pzk], ["szt%d" % b])
                self.tt(t1[b].rearrange("p (j t) -> p j t", t=128), psv[:, 0:MT].rearrange("p (j t) -> p j t", t=128),
                        bsb[:, hd, :].unsqueeze(1).to_broadcast([P, NS, 128]), ALU.add, [psvk, pk_], ["t1_%d" % b])
                self.tt(t1[b], pu[:, 0:MT], t1[b], ALU.mult, [puk, "t1_%d" % b], ["t1_%d" % b])
                self.tt(yT[:, hd, :], t1[b], szt[b], ALU.mult, ["t1_%d" % b, "szt%d" % b], [yk])
            self.outproj_residual(m, yT, 8, Wo, wk, hin, hin_key, hmid, hmid_key, hB, yk=yk)

    def w_ob2(self, L, slot):
        i = L // 2
        wk = ("w", slot)
        Wz = self.load_w_rows(slot, 0, self.o_w_in[i], 0, 8, 512, 512, wk)
        Wg = self.load_w_rows(slot, 4096, self.o_w_glu[i], 0, 4, 0, 512, wk)
        Wo = self.load_w_rows(slot, 6144, self.o_w_out[i], 0, 4, 0, 1024, wk)
        usm = self.wa[slot][:, 16384:32768].rearrange("p (c s n) -> p c s n", c=4, s=8)
        return (Wz, Wg, Wo, usm, wk)

    def pass_ob2(self, L, W, hmid, hmid_key, last):
        i = L // 2
        A = self.A
        self.set_ring(range(8))
        Wz, Wg, Wo, usm, wk = W
        pk_ = "par_ob2"
        bglu = A.f32(4)
        self.dma(bglu, self.o_bglu[i], [], [pk_], pk_)
        gfin = None
        if last:
            gfin = A.f32(1024)
            self.dma(gfin, self.final_g.partition_broadcast(128), [], ["gfin"], "par_gfin")
        hnT2 = [A.bf16(8 * MT).rearrange("p (k t) -> p k t", t=MT) for _ in range(2)]
        ge322 = [A.f32(4 * MT).rearrange("p (c t) -> p c t", t=MT) for _ in range(2)]
        gebf2 = [A.bf16(4 * MT).rearrange("p (c t) -> p c t", t=MT) for _ in range(2)]
        sgm = [A.f32(MT) for _ in range(2)]
        szt = [A.f32(MT) for _ in range(2)]
        yT2 = [A.bf16(4 * MT).rearrange("p (k t) -> p k t", t=MT) for _ in range(2)]
        hB = [A.f32(1024) for _ in range(2 * NS)]
        junk = A.bf16(1024)
        ss = A.f32(8)
        for m in range(NMT):
            par = m % 2
            hnT = hnT2[par]
            hk = "hnT%d" % par
            yT = yT2[par]
            yk = "yT%d" % par
            ge32 = ge322[par]
            gebf = gebf2[par]
            self.load_hnT(m, hnT, hk)
            for ct in range(4):
                self.actf(ge32[:, ct, :].rearrange("p (c s) -> p s c", s=8), usm[:, ct, :, m * 32:(m + 1) * 32], AF.Gelu_apprx_tanh,
                          [("usm", ct, s_) for s_ in range(8)], [("ge32", par, ct)])
                self.cp(gebf[:, ct, :], ge32[:, ct, :], [("ge32", par, ct)], [("gebf", par, ct)], eng="dve")
            for mt in range(4):
                b = mt % 2
                pg, pgk = self.bank()
                for kt in range(4):
                    self.mm(pg[:, 0:MT], Wg[:, kt, mt * 128:(mt + 1) * 128], gebf[:, kt, :], kt == 0, kt == 3, [("gebf", par, kt), wk], [pgk])
                self.actf(sgm[b], pg[:, 0:MT], AF.Sigmoid, [pgk, pk_], ["sgm%d" % b], bias=bglu[:, mt:mt + 1])
                pz, pzk = self.bank()
                for dk in range(8):
                    self.mm(pz[:, 0:MT], Wz[:, dk, mt * 128:(mt + 1) * 128], hnT[:, dk, :], dk == 0, dk == 7, [hk, wk], [pzk])
                self.actf(szt[b], pz[:, 0:MT], AF.Silu, [pzk], ["szt%d" % b])
                self.tt(sgm[b], sgm[b], ge32[:, mt, :], ALU.mult, ["sgm%d" % b, ("ge32", par, mt)], ["sgm%d" % b])
                self.tt(yT[:, mt, :], sgm[b], szt[b], ALU.mult, ["sgm%d" % b, "szt%d" % b], [yk])
            if last:
                self.outproj_residual(m, yT, 4, Wo, wk, hmid, hmid_key, self.out, "out", hB, final_g=gfin, ss=ss, junk=junk, yk=yk)
            else:
                self.outproj_residual(m, yT, 4, Wo, wk, hmid, hmid_key, hmid, hmid_key, hB, yk=yk)

    def build(self):
        passes = []
        hin, hin_key = self.x, "x"
        for L in range(self.n_layers):
            hmid = self.hbuf[(L + 1) % 2]
            hmid_key = ("hb", (L + 1) % 2)
            if getattr(self, "first_pass", 0) > 0:
                hin, hin_key = self.x, "x"
            if L % 2 == 0:
                passes.append((lambda slot, L=L: self.w_e1(L, slot),
                               lambda W, L=L, a=hin, ak=hin_key, b=hmid, bk=hmid_key: self.pass_e1(L, W, a, ak, b, bk)))
                passes.append((lambda slot, L=L: self.w_e2(L, slot),
                               lambda W, L=L, b=hmid, bk=hmid_key: self.pass_e2(L, W, b, bk)))
            else:
                last = (L == self.n_layers - 1) and self.final_norm
                passes.append((lambda slot, L=L: self.w_oa(L, slot),
                               lambda W, L=L, a=hin, ak=hin_key: self.pass_oa(L, W, a, ak)))
                passes.append((lambda slot, L=L: self.w_ob1(L, slot),
                               lambda W, L=L, a=hin, ak=hin_key, b=hmid, bk=hmid_key: self.pass_ob1(L, W, a, ak, b, bk)))
                passes.append((lambda slot, L=L: self.w_ob2(L, slot),
                               lambda W, L=L, b=hmid, bk=hmid_key, last=last: self.pass_ob2(L, W, b, bk, last)))
                self.fused_out = last
            hin, hin_key = hmid, hmid_key
        if self.max_passes is not None:
            passes = passes[getattr(self, 'first_pass', 0):self.max_passes]
        slot = 0
        Wn = passes[0][0](slot)
        for k, (wl, run) in enumerate(passes):
            self.S.barrier()
            self.A.reset()
            Wcur = Wn
            slot ^= 1
            if k + 1 < len(passes):
                Wn = passes[k + 1][0](slot)
            run(Wcur)
        if getattr(self, "fused_out", False) and self.max_passes is None:
            self.S.emit()
            return
        if getattr(self, "first_pass", 0) > 0 and self.max_passes is not None and self.max_passes <= 3:
            hin, hin_key = self.x, "x"
        self.S.barrier()
        A = self.A
        A.reset()
        cpb = [A.f32(1024) for _ in range(2)]
        for t in range(0 if not self.debug else SEQ // 128, SEQ // 128):
            b = t % 2
            self.dma(cpb[b], hin[t * 128:(t + 1) * 128, :], [(hin_key, t // NS, t % NS)], ["cpb%d" % b], "ld_cp%d" % b)
            self.dma(self.out[t * 128:(t + 1) * 128, :], cpb[b], ["cpb%d" % b], [("out", t)], "st_cp%d" % b)
        self.S.emit()


def build_program(n_layers=4, final_norm=True, max_passes=None, first_pass=0):
    nc = bass.Bass("TRN2", target_bir_lowering=False)
    st = ExitStack()
    with st:
        b = Builder(nc, st, n_layers=n_layers, final_norm=final_norm, max_passes=max_passes)
        b.first_pass = first_pass
        b.build()
    return nc


def host_layout(inputs):
    f = lambda a: np.ascontiguousarray(np.asarray(a, dtype=np.float32))
    g = {}
    g["norm_g"] = f(inputs["norm_g"])
    g["final_g"] = f(inputs["final_g"]).reshape(1, D)
    g["e_w_in"] = f(inputs["e_w_in"])
    g["e_w_a2"] = f(inputs["e_w_a2"])
    g["e_b_a"] = f(inputs["e_b_a"]).reshape(2, 1, 512)
    g["e_gla_g"] = f(inputs["e_gla_g"]).reshape(2, 1, 1024)
    cw = f(inputs["e_conv_w"])
    g["e_cw"] = f(cw.reshape(2, 31, 8, 128).transpose(0, 3, 2, 1))
    cpl = lambda a: f(f(a).reshape(2, 8, 128).transpose(0, 2, 1))
    g["e_cb"] = cpl(inputs["e_conv_b"])
    g["e_lg"] = cpl(inputs["e_cln_g"])
    g["e_lb"] = cpl(inputs["e_cln_b"])
    g["e_w_out"] = f(inputs["e_w_out"])
    g["o_w_in"] = f(inputs["o_w_in"])
    gp_l = lambda a: f(f(a).reshape(2, 16, 2, 64).transpose(0, 2, 3, 1).reshape(2, 128, 16))
    g["o_lamre"] = gp_l(inputs["o_lam_re"])
    g["o_lamim"] = gp_l(inputs["o_lam_im"])
    ldt = f(inputs["o_log_dt"]).reshape(2, 16, 2)
    g["o_logdt"] = f(np.broadcast_to(ldt.transpose(0, 2, 1)[:, :, None, :], (2, 2, 64, 16)).reshape(2, 128, 16))
    b_l = lambda a: f(f(a).reshape(2, 16, 2, 64, 16).transpose(0, 2, 3, 1, 4).reshape(2, 128, 256))
    g["o_bre"] = b_l(inputs["o_b_re"])
    g["o_bim"] = b_l(inputs["o_b_im"])
    c_l = lambda a: f(f(a).reshape(2, 16, 2, 16, 64).transpose(0, 2, 4, 1, 3).reshape(2, 128, 256))
    g["o_cre"] = c_l(inputs["o_c_re"])
    g["o_cim"] = c_l(inputs["o_c_im"])
    dd = f(inputs["o_d"]).reshape(2, 32, 16)
    g["o_dcol"] = f(np.broadcast_to(dd.transpose(0, 2, 1)[:, None, :, :], (2, 8, 16, 32)).reshape(2, 128, 32))
    g["o_w_glu"] = f(inputs["o_w_glu"])
    g["o_bglu"] = f(f(inputs["o_b_glu"]).reshape(2, 4, 128).transpose(0, 2, 1))
    g["o_lng"] = f(inputs["o_sg_ln_g"]).reshape(2, 1, 1024)
    g["o_lnb"] = f(inputs["o_sg_ln_b"]).reshape(2, 1, 1024)
    ws = f(inputs["o_w_s"])
    g["o_wsT"] = f(ws.transpose(0, 3, 1, 2).reshape(2, 128, 1024))
    g["o_bs"] = f(inputs["o_b_s"]).reshape(2, 1, 1024)
    g["o_w_out"] = f(inputs["o_w_out"])
    return g


def kernel(**inputs):
    x = np.asarray(inputs["x"], dtype=np.float32)
    shared = host_layout(inputs)
    nc = build_program()
    in_maps = []
    for c in range(8):
        m = dict(shared)
        m["x"] = np.ascontiguousarray(x[c])
        in_maps.append(m)
    res = run_bass_kernel_spmd(nc, in_maps, core_ids=list(range(8)))
    return np.stack([np.asarray(r["out"], dtype=np.float32) for r in res.results], axis=0)
```

```python
import numpy as np
import concourse.bass as bass
import concourse.mybir as mybir
from concourse.bass_utils import run_bass_kernel_spmd
from contextlib import ExitStack

F32 = mybir.dt.float32
BF16 = mybir.dt.bfloat16
AF = mybir.ActivationFunctionType
ALU = mybir.AluOpType

P = 128
SEQ = 4096
D = 1024
MT = 256
NMT = SEQ // MT
NS = MT // 128
EPS = 1e-6
EVEN_IN = 6160
ODD_IN = 4096
WA_N = 33792
OA_STAGE = 3
E1_HNT_BUFS = 1
PREP_CUT = 99


class _Op:
    __slots__ = ("eng", "fn", "deps", "edges", "signal", "dma_sem", "dma_val", "cnt", "cost", "seg", "idx", "fin", "placed")

    def __init__(self, eng, fn):
        self.eng = eng
        self.fn = fn
        self.deps = []
        self.edges = []
        self.signal = False
        self.dma_sem = None
        self.dma_val = 0
        self.cnt = 0
        self.cost = 200.0
        self.seg = 0
        self.idx = 0
        self.fin = 0.0
        self.placed = False


class Sched:
    ENGS = ("pe", "act", "dve", "pool", "sp")
    WINDOW = {"pe": 256, "act": 64, "dve": 64, "pool": 32, "sp": 32}

    def __init__(self, nc, stack):
        self.nc = nc
        self.stack = stack
        self.all_ops = []
        self.last_w = {}
        self.readers = {}
        self.dma_keys = {}
        self.last_dma = {}
        self.eng_sem = {}
        for e in ("pe", "act", "dve", "pool"):
            self.eng_sem[e] = stack.enter_context(nc.semaphore("es_" + e))
        self.seg = 0
        self.reorder = True

    def barrier(self):
        self.seg += 1

    def add(self, eng, fn, reads=(), writes=(), dma_key=None, cost=None):
        op = _Op(eng, fn)
        op.seg = self.seg
        op.idx = len(self.all_ops)
        is_dma = dma_key is not None
        if cost is not None:
            op.cost = float(cost)
        my_sem = None
        if is_dma:
            ent = self.dma_keys.get(dma_key)
            if ent is None:
                sem = self.stack.enter_context(self.nc.semaphore("ds_%d" % len(self.dma_keys)))
                ent = [sem, 0]
                self.dma_keys[dma_key] = ent
            my_sem = ent[0]
            prev = self.last_dma.get(dma_key)
            if prev is not None:
                op.edges.append(prev)
        cand = []
        for k in reads:
            w = self.last_w.get(k)
            if w is not None:
                cand.append(w)
        for k in writes:
            w = self.last_w.get(k)
            if w is not None:
                cand.append(w)
            for r in self.readers.get(k, ()):
                cand.append(r)
        seen = set(id(x) for x in op.edges)
        for d in cand:
            if id(d) in seen or d is op:
                continue
            seen.add(id(d))
            op.edges.append(d)
            d_is_dma = d.dma_sem is not None
            if d_is_dma:
                if is_dma and d.dma_sem is my_sem:
                    continue
            elif d.eng == eng and not is_dma and eng == "pe":
                continue
            op.deps.append(d)
        if is_dma:
            ent[1] += 16
            op.dma_sem = ent[0]
            op.dma_val = ent[1]
            self.last_dma[dma_key] = op
        for k in reads:
            self.readers.setdefault(k, []).append(op)
        for k in writes:
            self.last_w[k] = op
            self.readers[k] = []
        self.all_ops.append(op)
        return op

    def _schedule(self):
        order = {e: [] for e in self.ENGS}
        segs = {}
        for op in self.all_ops:
            segs.setdefault(op.seg, []).append(op)
        bar_points = {e: [] for e in self.ENGS}
        t_base = 0.0
        LAT = 120.0
        for sg in sorted(segs):
            ops = segs[sg]
            for e in self.ENGS:
                bar_points[e].append(len(order[e]))
            if not self.reorder:
                for op in ops:
                    order[op.eng].append(op)
                    op.placed = True
                continue
            queues = {e: [op for op in ops if op.eng == e] for e in self.ENGS}
            heads = {e: 0 for e in self.ENGS}
            free = {e: t_base for e in self.ENGS}
            remaining = len(ops)
            tmax = t_base
            while remaining:
                best = None
                best_key = None
                for e in self.ENGS:
                    q = queues[e]
                    h = heads[e]
                    while h < len(q) and q[h].placed:
                        h += 1
                    heads[e] = h
                    lim = min(len(q), h + self.WINDOW[e])
                    k = h
                    found = 0
                    while k < lim:
                        op = q[k]
                        k += 1
                        if op.placed:
                            continue
                        ok = True
                        rdy = free[e]
                        for d in op.edges:
                            if d.seg != sg:
                                continue
                            if not d.placed:
                                ok = False
                                break
                            f = d.fin + (LAT if d.eng != e or d.dma_sem is not None else 30.0)
                            if f > rdy:
                                rdy = f
                        if not ok:
                            continue
                        key = (rdy, op.idx)
                        if best_key is None or key < best_key:
                            best_key = key
                            best = op
                        found += 1
                        if found >= 6:
                            break
                op = best
                assert op is not None, "scheduler stuck (cyclic deps?)"
                st = best_key[0]
                op.placed = True
                if op.dma_sem is not None:
                    op.fin = st + op.cost
                    free[op.eng] = st + 60.0
                else:
                    op.fin = st + op.cost
                    free[op.eng] = op.fin
                if op.fin > tmax:
                    tmax = op.fin
                order[op.eng].append(op)
                remaining -= 1
            t_base = tmax
        self.est_ns = t_base
        return order, bar_points

    def emit(self):
        nc = self.nc
        order, bar_points = self._schedule()
        dma_by_seg = {}
        for op in self.all_ops:
            if op.dma_sem is not None:
                dma_by_seg.setdefault(op.seg, {})[id(op.dma_sem)] = op
        nseg = self.seg + 1
        for si in range(1, nseg):
            lasts = []
            for e in self.ENGS:
                pos = bar_points[e][si]
                for k in range(pos - 1, -1, -1):
                    if order[e][k].dma_sem is None:
                        lasts.append(order[e][k])
                        break
            dl = {}
            for sj in range(si):
                dl.update(dma_by_seg.get(sj, {}))
            lasts += list(dl.values())
            for e in self.ENGS:
                pos = bar_points[e][si]
                if pos < len(order[e]):
                    op = order[e][pos]
                    have = set(id(x) for x in op.deps)
                    for d in lasts:
                        if id(d) in have:
                            continue
                        if d.dma_sem is None and d.eng == e and op.dma_sem is None:
                            continue
                        op.deps.append(d)
        for e in self.ENGS:
            for op in order[e]:
                for d in op.deps:
                    if d.dma_sem is None:
                        d.signal = True
        for e in ("pe", "act", "dve", "pool"):
            for op in reversed(order[e]):
                if op.dma_sem is None:
                    op.signal = True
                    break
        for e in self.ENGS:
            c = 0
            for op in order[e]:
                if op.dma_sem is None and op.signal:
                    c += 1
                op.cnt = c
        sched = self

        def replay(ename, eng):
            waited = {}
            for op in order[ename]:
                for d in op.deps:
                    if d.dma_sem is not None:
                        sem, val = d.dma_sem, d.dma_val
                    else:
                        sem, val = sched.eng_sem[d.eng], d.cnt
                    key = id(sem)
                    if waited.get(key, 0) >= val:
                        continue
                    waited[key] = val
                    eng.wait_ge(sem, val)
                ins = op.fn(eng)
                if op.dma_sem is not None:
                    ins.then_inc(op.dma_sem, 16)
                elif op.signal:
                    ins.then_inc(sched.eng_sem[ename], 1)
            if ename == "sp":
                for e in ("pe", "act", "dve", "pool"):
                    last = None
                    for op in order[e]:
                        if op.dma_sem is None:
                            last = op
                    if last is not None:
                        eng.wait_ge(sched.eng_sem[e], last.cnt)
                for k, (sem, cnt) in sched.dma_keys.items():
                    eng.wait_ge(sem, cnt)

        with nc.Block() as block:
            @block.tensor
            def _(e):
                replay("pe", e)

            @block.scalar
            def _(e):
                replay("act", e)

            @block.vector
            def _(e):
                replay("dve", e)

            @block.gpsimd
            def _(e):
                replay("pool", e)

            @block.sync
            def _(e):
                replay("sp", e)


class Arena:
    def __init__(self, tile_ap, n, name):
        self.t = tile_ap
        self.n = n
        self.off = 0
        self.name = name
        self.gen = 0

    def reset(self):
        self.off = 0
        self.gen += 1

    def f32(self, n):
        a = self.t[:, self.off:self.off + n]
        self.off += n
        assert self.off <= self.n, (self.name, self.off, self.n)
        return a

    def bf16(self, n):
        w = (n + 1) // 2
        a = self.t[:, self.off:self.off + w].bitcast(BF16)
        self.off += w
        assert self.off <= self.n, (self.name, self.off, self.n)
        return a


class Builder:
    def __init__(self, nc, st, n_layers=4, final_norm=True, max_passes=None):
        self.max_passes = max_passes
        self.debug = max_passes == 1
        self.nc = nc
        self.st = st
        self.S = Sched(nc, st)
        self.n_layers = n_layers
        self.final_norm = final_norm
        self.uid = 0
        S = self.S
        dt_in = lambda name, shape: nc.dram_tensor(name, shape, F32, kind="ExternalInput").ap()
        self.x = dt_in("x", [SEQ, D])
        self.norm_g = dt_in("norm_g", [4, D])
        self.final_g = dt_in("final_g", [1, D])
        self.e_w_in = dt_in("e_w_in", [2, D, EVEN_IN])
        self.e_w_a2 = dt_in("e_w_a2", [2, 16, 512])
        self.e_b_a = dt_in("e_b_a", [2, 1, 512])
        self.e_gla_g = dt_in("e_gla_g", [2, 1, 1024])
        self.e_cw = dt_in("e_cw", [2, P, 8, 31])
        self.e_cb = dt_in("e_cb", [2, P, 8])
        self.e_lg = dt_in("e_lg", [2, P, 8])
        self.e_lb = dt_in("e_lb", [2, P, 8])
        self.e_w_out = dt_in("e_w_out", [2, 2048, D])
        self.o_w_in = dt_in("o_w_in", [2, D, ODD_IN])
        self.o_lamre = dt_in("o_lamre", [2, P, 16])
        self.o_lamim = dt_in("o_lamim", [2, P, 16])
        self.o_logdt = dt_in("o_logdt", [2, P, 16])
        self.o_bre = dt_in("o_bre", [2, P, 256])
        self.o_bim = dt_in("o_bim", [2, P, 256])
        self.o_cre = dt_in("o_cre", [2, P, 256])
        self.o_cim = dt_in("o_cim", [2, P, 256])
        self.o_dcol = dt_in("o_dcol", [2, P, 32])
        self.o_w_glu = dt_in("o_w_glu", [2, 512, 512])
        self.o_bglu = dt_in("o_bglu", [2, P, 4])
        self.o_lng = dt_in("o_lng", [2, 1, 1024])
        self.o_lnb = dt_in("o_lnb", [2, 1, 1024])
        self.o_wsT = dt_in("o_wsT", [2, P, 1024])
        self.o_bs = dt_in("o_bs", [2, 1, 1024])
        self.o_w_out = dt_in("o_w_out", [2, 1536, D])
        self.out = nc.dram_tensor("out", [SEQ, D], F32, kind="ExternalOutput").ap()
        self.hbuf = [nc.dram_tensor("hbuf%d" % i, [SEQ, D], F32, kind="Internal").ap() for i in range(2)]
        self.hnscr = nc.dram_tensor("hnscr", [NMT, P, 8 * MT], BF16, kind="Internal").ap()

        sb = lambda name, shape, dt: st.enter_context(nc.sbuf_tensor(name, shape, dt))
        self.wa = [sb("wa%d" % i, [P, WA_N], BF16) for i in range(2)]
        self.cst = sb("cst", [P, 4 * 128], BF16)
        self.ident = self.cst[:, 0:128]
        self.triInc = self.cst[:, 128:256]
        self.triRev = self.cst[:, 256:384]
        self.cmask = self.cst[:, 384:512]
        self.pst = sb("pst", [P, 1024], F32)
        rem = int(nc.sbuf_bytes_remaining) - 256
        self.act_n = rem // 4
        self.act_t = sb("actarena", [P, self.act_n], F32)
        self.A = Arena(self.act_t, self.act_n, "act")
        self.pb = [st.enter_context(nc.psum_tensor("pb%d" % i, [P, 512], F32)) for i in range(8)]
        self.ring = list(range(8))
        self.ring_i = 0
        self.slot = 0
        self._consts()

    def key(self, name):
        self.uid += 1
        return "%s#%d" % (name, self.uid)

    def bank(self):
        i = self.ring[self.ring_i % len(self.ring)]
        self.ring_i += 1
        return self.pb[i], ("pb", i)

    def set_ring(self, banks):
        self.ring = list(banks)
        self.ring_i = 0

    def add(self, *a, **k):
        return self.S.add(*a, **k)

    @staticmethod
    def _n(ap):
        n = 1
        for d in ap.shape[1:]:
            n *= int(d)
        return n

    def mm(self, out, lhsT, rhs, start, stop, reads, writes):
        c = max(self._n(out), 64) / 2.0 + 10.0
        if lhsT.dtype == F32:
            c *= 4.0
        self.S.add("pe", lambda e: e.matmul(out, lhsT=lhsT, rhs=rhs, start=start, stop=stop), reads=reads, writes=writes, cost=c)

    def tp(self, out, in_, ident, reads, writes):
        self.S.add("pe", lambda e: e.transpose(out=out, in_=in_, identity=ident), reads=reads, writes=writes, cost=80.0)

    def actf(self, out, in_, func, reads, writes, bias=None, scale=None, accum=None):
        kw = {}
        if bias is not None:
            kw["bias"] = bias
        if scale is not None:
            kw["scale"] = scale
        if accum is not None:
            kw["accum_out"] = accum
        c = self._n(out) / 1.4 + 230.0 + (100.0 if accum is not None else 0.0)
        self.S.add("act", lambda e: e.activation(out=out, in_=in_, func=func, **kw), reads=reads, writes=writes, cost=c)

    def tt(self, out, in0, in1, op, reads, writes, eng="dve"):
        c = self._n(out) / (0.96 if eng == "dve" else 0.45) + (130.0 if eng == "dve" else 300.0)
        self.S.add(eng, lambda e: e.tensor_tensor(out=out, in0=in0, in1=in1, op=op), reads=reads, writes=writes, cost=c)

    def ts(self, out, in0, s1, op0, reads, writes, s2=None, op1=None, eng="dve"):
        c = self._n(out) / (0.96 if eng == "dve" else 0.45) + (130.0 if eng == "dve" else 300.0)
        if op1 is None:
            self.S.add(eng, lambda e: e.tensor_scalar(out=out, in0=in0, scalar1=s1, scalar2=None, op0=op0), reads=reads, writes=writes, cost=c)
        else:
            self.S.add(eng, lambda e: e.tensor_scalar(out=out, in0=in0, scalar1=s1, scalar2=s2, op0=op0, op1=op1), reads=reads, writes=writes, cost=c)

    def stt(self, out, in0, scalar, in1, op0, op1, reads, writes):
        c = self._n(out) / 0.96 + 130.0
        self.S.add("dve", lambda e: e.scalar_tensor_tensor(out=out, in0=in0, scalar=scalar, in1=in1, op0=op0, op1=op1), reads=reads, writes=writes, cost=c)

    def cp(self, out, in_, reads, writes, eng="act"):
        n = self._n(out)
        if eng == "act":
            self.S.add("act", lambda e: e.copy(out=out, in_=in_), reads=reads, writes=writes, cost=n / 1.4 + 230.0)
        else:
            c = n / (0.96 if eng == "dve" else 0.45) + (130.0 if eng == "dve" else 300.0)
            self.S.add(eng, lambda e: e.tensor_copy(out=out, in_=in_), reads=reads, writes=writes, cost=c)

    def memset(self, ap, val, writes, eng="pool"):
        self.S.add(eng, lambda e: e.memset(ap, val), writes=writes, cost=self._n(ap) / 0.9 + 150.0)

    def dma(self, out, in_, reads, writes, key, eng="sp"):
        nbytes = self._n(out) * int(out.shape[0]) * 4
        self.S.add(eng, lambda e: e.dma_start(out=out, in_=in_), reads=reads, writes=writes, dma_key=key, cost=2500.0 + nbytes / 60.0)

    def rsqrt_small(self, out, in_, scale, reads, writes, tmpkey=None):
        self.ts(out, in_, scale, ALU.mult, reads, writes, s2=EPS, op1=ALU.add)
        self.actf(out, out, AF.Ln, writes, writes)
        self.actf(out, out, AF.Exp, writes, writes, scale=-0.5)

    def silu_from_psum(self, out, ps, pskey, okey, eng="dve"):
        self.actf(out, ps, AF.Sigmoid, [pskey], [okey])
        self.tt(out, out, ps, ALU.mult, [okey, pskey], [okey], eng=eng)

    def dump(self, name, ap, reads, row0, col0=0):
        if not getattr(self, "debug", False):
            return
        n = ap.shape[-1] if len(ap.shape) == 2 else None
        self.dma(self.out[row0:row0 + ap.shape[0], col0:col0 + n], ap, reads, [("dbg", name)], "dbg", eng="pool")

    def _consts(self):
        A = self.A
        ones = A.bf16(128)
        k1 = "c_ones"
        self.memset(ones, 1.0, [k1])
        self.memset(self.cst[:, :], 0.0, ["cst"])
        self.S.add("pool", lambda e: e.affine_select(out=self.ident, in_=ones, pattern=[[-1, 128]], compare_op=ALU.is_equal,
                                                    fill=0.0, base=0, channel_multiplier=1), reads=[k1, "cst"], writes=["cst"])
        sc = A.bf16(128)
        tmpm = A.bf16(128)
        self.memset(sc, -1.0 / 16.0, ["c_sc"])

        def tri(dst, src, srckey, upper):
            if upper:
                self.S.add("pool", lambda e: e.affine_select(out=tmpm, in_=src, pattern=[[1, 128]], compare_op=ALU.is_ge,
                                                            fill=0.0, base=0, channel_multiplier=-1), reads=[srckey, "tmpm"], writes=["tmpm"])
                self.cp(dst[:, 0:64], tmpm[:, 0:64], ["tmpm"], ["cst"], eng="pool")
                self.S.add("pool", lambda e: e.affine_select(out=dst[:, 64:128], in_=tmpm[:, 64:128], pattern=[[0, 64]], compare_op=ALU.is_ge,
                                                            fill=0.0, base=-64, channel_multiplier=1), reads=["tmpm", "cst"], writes=["cst"])
            else:
                self.S.add("pool", lambda e: e.affine_select(out=tmpm, in_=src, pattern=[[-1, 128]], compare_op=ALU.is_ge,
                                                            fill=0.0, base=-1, channel_multiplier=1), reads=[srckey, "tmpm"], writes=["tmpm"])
                self.cp(dst[:, 64:128], tmpm[:, 64:128], ["tmpm"], ["cst"], eng="pool")
                self.S.add("pool", lambda e: e.affine_select(out=dst[:, 0:64], in_=tmpm[:, 0:64], pattern=[[0, 64]], compare_op=ALU.is_ge,
                                                            fill=0.0, base=63, channel_multiplier=-1), reads=["tmpm", "cst"], writes=["cst"])
        tri(self.triInc, sc, "c_sc", True)
        tri(self.triRev, sc, "c_sc", False)
        tri(self.cmask, ones, k1, True)

    def load_w_rows(self, slot, off, src, r0, nrows_k, c0, ncols, key):
        view = self.wa[slot][:, off:off + nrows_k * ncols].rearrange("p (k n) -> p k n", n=ncols)
        s = src[r0:r0 + 128 * nrows_k, c0:c0 + ncols].rearrange("(k p) n -> p k n", p=128)
        for k in range(nrows_k):
            self.dma(view[:, k, :], s[:, k, :], [], [key], key, eng="pool")
        return view

    def norm_tile(self, m, hsrc, hsrc_key, bufs, store_scr):
        hA, hnb, gtile, hnT, ss, gkey = bufs["hA"], bufs["hnb"], bufs["gn"], bufs["hnT"], bufs["ss"], bufs["gkey"]
        hk = bufs.get("hk", "hnT")
        for j in range(NS):
            t0 = m * MT + j * 128
            bi = j if len(hA) <= NS else (m % 2) * NS + j
            hb = hA[bi]
            kh = "hA%d" % bi
            self.dma(hb, hsrc[t0:t0 + 128, :], [(hsrc_key, m, j)], [kh], "ld_hA%d" % bi)
            self.actf(hnb.bitcast(F32) if False else bufs["junk"], hb, AF.Square, [kh], ["junk", "ss"], accum=ss[:, 0:1])
            self.rsqrt_small(ss[:, 1:2], ss[:, 0:1], 1.0 / D, ["ss"], ["rstd"], "ss_t")
            self.stt(hnb, hb, ss[:, 1:2], gtile, ALU.mult, ALU.mult, [kh, "rstd", gkey], ["hnb"])
            if m == 0 and j == 0:
                self.dump("hnb", hnb, ["hnb"], 0)
            pbk, pk = self.bank()
            pv = pbk[:, :].bitcast(BF16)
            for dk in range(8):
                self.tp(pv[:, dk * 128:(dk + 1) * 128], hnb[:, dk * 128:(dk + 1) * 128], self.ident, ["hnb", "cst"], [pk])
            self.cp(hnT[:, :, j * 128:(j + 1) * 128], pv[:, 0:1024].rearrange("p (k t) -> p k t", t=128), [pk], [hk])
        if store_scr:
            self.dma(self.hnscr[m].rearrange("p (k t) -> p k t", t=MT), hnT, [hk], [("hnscr", m)], "st_" + hk)

    def load_hnT(self, m, hnT, hk="hnT"):
        self.dma(hnT, self.hnscr[m].rearrange("p (k t) -> p k t", t=MT), [("hnscr", m)], [hk], "ld_" + hk)

    def outproj_residual(self, m, yT, nk, Wo, wkey, hres, hres_key, hdst, hdst_key, hB, final_g=None, ss=None, junk=None, hbk="hB", yk="yT"):
        for j in range(NS):
            t0 = m * MT + j * 128
            bi = j if len(hB) <= NS else (m % 2) * NS + j
            if len(hB) == 1:
                bi = 0
            hb = hB[bi]
            kh = "%s%d" % (hbk, bi)
            self.dma(hb, hres[t0:t0 + 128, :], [(hres_key, m, j)], [kh], "ld_%s%d" % (hbk, bi))
            for n2 in range(2):
                pbk, pk = self.bank()
                for mk in range(nk):
                    self.mm(pbk[:, :], yT[:, mk, j * 128:(j + 1) * 128], Wo[:, mk, n2 * 512:(n2 + 1) * 512],
                            mk == 0, mk == nk - 1, [yk, wkey], [pk])
                self.tt(hb[:, n2 * 512:(n2 + 1) * 512], hb[:, n2 * 512:(n2 + 1) * 512], pbk[:, :], ALU.add, [kh, pk], [kh])
            if final_g is not None:
                self.actf(junk, hb, AF.Square, [kh], ["junk", "fss"], accum=ss[:, 2:3])
                self.rsqrt_small(ss[:, 3:4], ss[:, 2:3], 1.0 / D, ["fss"], ["frstd"], "fss_t")
                self.stt(hb, hb, ss[:, 3:4], final_g, ALU.mult, ALU.mult, [kh, "frstd", "gfin"], [kh])
            self.dma(hdst[t0:t0 + 128, :], hb, [kh], [(hdst_key, m, j)], "st_h%d" % bi)

    def w_e1(self, L, slot):
        i = L // 2
        wk = ("w", slot)
        Win = self.load_w_rows(slot, 0, self.e_w_in[i], 0, 8, 0, 3088, wk)
        Wo = self.load_w_rows(slot, 8 * 3088, self.e_w_out[i], 0, 8, 0, 1024, wk)
        return (Win, Wo, wk)

    def pass_e1(self, L, W, hin, hin_key, hmid, hmid_key):
        i = L // 2
        A = self.A
        self.set_ring(range(6))
        Win, Wo, wk = W
        gn = A.f32(1024)
        gg = A.f32(1024)
        pk_ = "par_e1"
        self.dma(gn, self.norm_g[L:L + 1, :].partition_broadcast(128), [], [pk_], pk_)
        self.dma(gg, self.e_gla_g[i].partition_broadcast(128), [], [pk_], pk_)
        tmpf = [A.f32(512) for _ in range(2)]
        wa2f = tmpf[0]
        wa2 = A.bf16(512)
        self.memset(wa2f[0:32, :], 0.0, ["tmpf0"])
        self.dma(wa2f[0:16, :], self.e_w_a2[i], ["tmpf0"], ["tmpf0"], "par_e1b")
        self.dma(wa2f[16:17, :], self.e_b_a[i], ["tmpf0"], ["tmpf0"], "par_e1b")
        self.cp(wa2[0:32, :], wa2f[0:32, :], ["tmpf0"], ["wa2"], eng="dve")
        hA = [A.f32(1024) for _ in range(NS)]
        hnb = A.bf16(1024)
        junk = A.bf16(1024)
        ss = A.f32(8)
        hnT2 = [A.bf16(8 * MT).rearrange("p (k t) -> p k t", t=MT) for _ in range(E1_HNT_BUFS)]
        nb = dict(hA=hA, hnb=hnb, gn=gn, hnT=None, ss=ss, gkey=pk_, junk=junk)
        alow = self.pst[:, 0:MT // 2].bitcast(BF16)
        self.memset(alow[0:32, :], 1.0, ["alow"])
        hBe = [A.f32(1024) for _ in range(NS)]
        spT = A.bf16(NS * 512).rearrange("p (j n) -> p j n", n=512)
        Eb = A.f32(MT)
        Enb = A.f32(MT)
        dec = A.f32(16).rearrange("p (h c) -> p h c", c=4)
        qf = A.bf16(4 * MT).rearrange("p (h t) -> p h t", t=MT)
        kin = A.bf16(4 * MT).rearrange("p (h t) -> p h t", t=MT)
        kst = A.bf16(NS * 512).rearrange("p (j n) -> p j n", n=512)
        v = A.bf16(NS * 1024).rearrange("p (j n) -> p j n", n=1024)
        S32 = A.f32(1024).rearrange("p (h n) -> p h n", n=256)
        Sbf = [A.bf16(1024).rearrange("p (h n) -> p h n", n=256) for _ in range(2)]
        attT = A.bf16(512).rearrange("p (h n) -> p h n", n=128)
        sz = A.f32(1024)
        ss4 = A.f32(8)
        ygla = A.bf16(1024)
        yT = A.bf16(8 * MT).rearrange("p (k t) -> p k t", t=MT)
        self.memset(S32, 0.0, ["S32_%d" % h for h in range(4)])
        self.memset(Sbf[0], 0.0, ["Sbf0_%d" % h for h in range(4)])
        chunk_ctr = 0
        QSC = 128.0 ** -0.5
        for m in range(NMT):
            hnT = hnT2[m % E1_HNT_BUFS]
            hk = "hnT%d" % (m % E1_HNT_BUFS)
            nb["hnT"] = hnT
            nb["hk"] = hk
            self.norm_tile(m, hin, hin_key, nb, store_scr=True)
            pbk, pk = self.bank()
            for dk in range(8):
                self.mm(pbk[0:16, 0:MT], Win[:, dk, 3072:3088], hnT[:, dk, :], dk == 0, dk == 7, [hk, wk], [pk])
            self.cp(alow[0:16, :], pbk[0:16, 0:MT], [pk], ["alow"])
            for j in range(NS):
                pbk, pk = self.bank()
                self.mm(pbk[:, :], alow[0:32, j * 128:(j + 1) * 128], wa2[0:32, :], True, True, ["alow", "wa2"], [pk])
                self.actf(tmpf[j], pbk[:, :], AF.Exp, [pk], ["tmpf%d" % j], scale=-1.0)
            for j in range(NS):
                self.actf(spT[:, j, :], tmpf[j], AF.Ln, ["tmpf%d" % j], ["spT"], bias=1.0)
            if m == 0:
                self.dump("spT", spT[:, 0, :], ["spT"], 128)
                self.dump("hnT0", hnT[:, 0, :], ["hnT"], 128, 512)
                self.dump("hnT1", hnT[:, 1, :], ["hnT"], 128, 768)
            for hd in range(4):
                pbk, pk = self.bank()
                for j in range(NS):
                    self.mm(pbk[:, j * 128:(j + 1) * 128], spT[:, j, hd * 128:(hd + 1) * 128], self.triInc, True, True, ["spT", "cst"], [pk])
                self.actf(Eb, pbk[:, 0:MT], AF.Exp, [pk], ["Eb"])
                self.actf(Enb, pbk[:, 0:MT], AF.Exp, [pk], ["Enb"], scale=-1.0)
                self.cp(dec[:, hd, :], Eb.rearrange("p (c t) -> p c t", t=64)[:, :, 63], ["Eb"], ["dec%d" % hd], eng="dve")
                pq, pqk = self.bank()
                for dk in range(8):
                    self.mm(pq[:, 0:MT], Win[:, dk, hd * 128:(hd + 1) * 128], hnT[:, dk, :], dk == 0, dk == 7, [hk, wk], [pqk])
                self.stt(qf[:, hd, :], pq[:, 0:MT], QSC, Eb, ALU.mult, ALU.mult, [pqk, "Eb"], ["qf%d" % hd])
                pkk, pkkk = self.bank()
                for dk in range(8):
                    self.mm(pkk[:, 0:MT], Win[:, dk, 512 + hd * 128:512 + (hd + 1) * 128], hnT[:, dk, :], dk == 0, dk == 7, [hk, wk], [pkkk])
                self.tt(kin[:, hd, :], pkk[:, 0:MT], Enb, ALU.mult, [pkkk, "Enb"], ["kin%d" % hd])
                if m == 0 and hd == 0:
                    self.dump("Eb", Eb, ["Eb"], 256)
                    self.dump("qf", qf[:, 0, :], ["qf0"], 256, 256)
                    self.dump("kin", kin[:, 0, :], ["kin0"], 256, 512)
            for j in range(NS):
                pbk, pk = self.bank()
                self.mm(pbk[:, :], self.triRev, spT[:, j, :], True, True, ["spT", "cst"], [pk])
                self.actf(tmpf[j], pbk[:, :], AF.Exp, [pk], ["tmpf%d" % j])
                pk2, pk2k = self.bank()
                for dk in range(8):
                    self.mm(pk2[:, :], hnT[:, dk, j * 128:(j + 1) * 128], Win[:, dk, 512:1024], dk == 0, dk == 7, [hk, wk], [pk2k])
                self.tt(kst[:, j, :], pk2[:, :], tmpf[j], ALU.mult, [pk2k, "tmpf%d" % j], ["kst%d" % j])
                for n2 in range(2):
                    pv_, pvk = self.bank()
                    for dk in range(8):
                        self.mm(pv_[:, :], hnT[:, dk, j * 128:(j + 1) * 128], Win[:, dk, 1024 + n2 * 512:1024 + (n2 + 1) * 512],
                                dk == 0, dk == 7, [hk, wk], [pvk])
                    self.cp(v[:, j, n2 * 512:(n2 + 1) * 512], pv_[:, :], [pvk], ["v%d" % j])
            for j in range(NS):
                pat, patk = self.bank()
                for hd in range(4):
                    self.mm(pat[:, hd * 128:(hd + 1) * 128], kin[:, hd, j * 128:(j + 1) * 128], qf[:, hd, j * 128:(j + 1) * 128],
                            True, True, ["kin%d" % hd, "qf%d" % hd], [patk])
                self.tt(attT, pat[:, :].rearrange("p (h n) -> p h n", n=128), self.cmask.unsqueeze(1).to_broadcast([P, 4, 128]),
                        ALU.mult, [patk, "cst"], ["attT"])
                po = [(self.pb[6], ("pb", 6)), (self.pb[7], ("pb", 7))]
                for c2 in range(2):
                    cs = slice(64 * c2, 64 * c2 + 64)
                    cur = chunk_ctr % 2
                    nxt = 1 - cur
                    cidx = 2 * j + c2
                    for hd in range(4):
                        ob, obk = po[hd // 2]
                        osl = ob[cs, (hd % 2) * 256:(hd % 2) * 256 + 256]
                        okey = obk
                        self.mm(osl, attT[cs, hd, 64 * c2:64 * c2 + 64], v[cs, j, hd * 256:(hd + 1) * 256], True, False,
                                ["attT", "v%d" % j], [okey])
                        self.mm(osl, qf[:, hd, j * 128 + 64 * c2:j * 128 + 64 * c2 + 64], Sbf[cur][:, hd, :], False, True,
                                ["qf%d" % hd, "Sbf%d_%d" % (cur, hd)], [okey])
                    for hd in range(4):
                        pkv, pkvk = self.bank()
                        self.mm(pkv[:, 0:256], kst[cs, j, hd * 128:(hd + 1) * 128], v[cs, j, hd * 256:(hd + 1) * 256], True, True,
                                ["kst%d" % j, "v%d" % j], [pkvk])
                        self.stt(S32[:, hd, :], S32[:, hd, :], dec[:, hd, cidx:cidx + 1], pkv[:, 0:256], ALU.mult, ALU.add,
                                 ["S32_%d" % hd, "dec%d" % hd, pkvk], ["S32_%d" % hd])
                        self.cp(Sbf[nxt][:, hd, :], S32[:, hd, :], ["S32_%d" % hd], ["Sbf%d_%d" % (nxt, hd)])
                    chunk_ctr += 1
                for n2 in range(2):
                    pz, pzk = self.bank()
                    for dk in range(8):
                        self.mm(pz[:, :], hnT[:, dk, j * 128:(j + 1) * 128], Win[:, dk, 2048 + n2 * 512:2048 + (n2 + 1) * 512],
                                dk == 0, dk == 7, [hk, wk], [pzk])
                    self.actf(sz[:, n2 * 512:(n2 + 1) * 512], pz[:, :], AF.Silu, [pzk], ["sz%d" % n2])
                    self.tt(sz[:, n2 * 512:(n2 + 1) * 512], sz[:, n2 * 512:(n2 + 1) * 512], gg[:, n2 * 512:(n2 + 1) * 512], ALU.mult,
                            ["sz%d" % n2, pk_], ["sz%d" % n2], eng="pool")
                for hd in range(4):
                    ob, obk = po[hd // 2]
                    osl = ob[:, (hd % 2) * 256:(hd % 2) * 256 + 256]
                    okeys = [obk]
                    self.actf(junk[:, 0:256], osl, AF.Square, okeys, ["junk", "ss4_%d" % hd], accum=ss4[:, hd:hd + 1])
                self.rsqrt_small(ss4[:, 4:8], ss4[:, 0:4], 1.0 / 256.0, ["ss4_%d" % h for h in range(4)], ["rstd4"], "ss4_t")
                for hd in range(4):
                    ob, obk = po[hd // 2]
                    osl = ob[:, (hd % 2) * 256:(hd % 2) * 256 + 256]
                    okeys = [obk]
                    self.stt(ygla[:, hd * 256:(hd + 1) * 256], osl, ss4[:, 4 + hd:5 + hd], sz[:, hd * 256:(hd + 1) * 256],
                             ALU.mult, ALU.mult, okeys + ["rstd4", "sz%d" % (hd // 2)], ["ygla"])
                if m == 0 and j == 0:
                    self.dump("ygla", ygla, ["ygla"], 384)
                    self.dump("v", v[:, 0, :], ["v0"], 512)
                    self.dump("kst", kst[:, 0, :], ["kst0"], 640, 0)
                    self.dump("attT", attT[:, 0, :], ["attT"], 640, 512)
                    self.dump("sz", sz, ["sz0", "sz1"], 768)
                pbk, pk = self.bank()
                pvw = pbk[:, :].bitcast(BF16)
                for mk in range(8):
                    self.tp(pvw[:, mk * 128:(mk + 1) * 128], ygla[:, mk * 128:(mk + 1) * 128], self.ident, ["ygla", "cst"], [pk])
                self.cp(yT[:, :, j * 128:(j + 1) * 128], pvw[:, 0:1024].rearrange("p (k t) -> p k t", t=128), [pk], ["yT"])
            self.outproj_residual(m, yT, 8, Wo, wk, hin, hin_key, hmid, hmid_key, hBe)

    def w_e2(self, L, slot):
        i = L // 2
        wk = ("w", slot)
        Win = self.load_w_rows(slot, 0, self.e_w_in[i], 0, 8, 3088, 3072, wk)
        Wo = self.load_w_rows(slot, 8 * 3072, self.e_w_out[i], 1024, 8, 0, 1024, wk)
        return (Win, Wo, wk)

    def pass_e2(self, L, W, hmid, hmid_key):
        i = L // 2
        A = self.A
        self.set_ring(range(2, 8))
        Win, Wo, wk = W
        NPE = 29
        oslot = wk[1] ^ 1
        dg = self.wa[oslot][:, 4096:4096 + 8 * NPE * 128].rearrange("p (c t n) -> p c t n", c=8, t=NPE)
        pk_ = "par_e2"
        cw = A.f32(8 * 31).rearrange("p (c k) -> p c k", k=31)
        cb = A.f32(8)
        lg = A.f32(8)
        lb = A.f32(8)
        self.dma(cw, self.e_cw[i], [], [pk_], pk_)
        self.dma(cb, self.e_cb[i], [], [pk_], pk_)
        self.dma(lg, self.e_lg[i], [], [pk_], pk_)
        self.dma(lb, self.e_lb[i], [], [pk_], pk_)
        for ct in range(8):
            self.tt(dg[:, ct, :, :], self.ident.unsqueeze(1).to_broadcast([P, NPE, 128]),
                    cw[:, ct, 0:NPE].unsqueeze(2).to_broadcast([P, NPE, 128]), ALU.mult, ["cst", pk_], [("dg", ct)],
                    eng=("dve" if ct % 2 == 0 else "pool"))
        ones32 = A.bf16(128)
        self.memset(ones32, 1.0, ["ones32"])
        xcb = [A.bf16(MT) for _ in range(2)]
        hnT2 = [A.bf16(8 * MT).rearrange("p (k t) -> p k t", t=MT) for _ in range(2)]
        u = [A.bf16(MT + 32) for _ in range(2)]
        halo = A.bf16(8 * 32).rearrange("p (c k) -> p c k", k=32)
        self.memset(halo, 0.0, [("halo", c) for c in range(8)])
        sg = [A.f32(MT) for _ in range(2)]
        acc = [A.f32(MT) for _ in range(2)]
        xc = A.f32(8 * MT).rearrange("p (c t) -> p c t", t=MT)
        sq = [A.bf16(MT) for _ in range(2)]
        mean = A.f32(MT)
        rstd = A.f32(MT)
        nmr = A.f32(MT)
        tn = [A.f32(MT) for _ in range(2)]
        sl = [A.f32(MT) for _ in range(2)]
        szc = [A.f32(MT) for _ in range(2)]
        yT2 = [A.bf16(8 * MT).rearrange("p (k t) -> p k t", t=MT) for _ in range(2)]
        hB = [A.f32(1024) for _ in range(2 * NS)]
        s1b, s1k = self.pb[0], ("pb", 0)
        s2b, s2k = self.pb[1], ("pb", 1)
        for m in range(NMT):
            hnT = hnT2[m % 2]
            hk = "hnT%d" % (m % 2)
            yT = yT2[m % 2]
            yk = "yT%d" % (m % 2)
            self.load_hnT(m, hnT, hk)
            for ct in range(8):
                b = ct % 2
                ub = u[b]
                uk = "u%d" % b
                pval, pvk = self.bank()
                for dk in range(8):
                    self.mm(pval[:, 0:MT], Win[:, dk, ct * 128:(ct + 1) * 128], hnT[:, dk, :], dk == 0, dk == 7, [hk, wk], [pvk])
                pg, pgk = self.bank()
                for dk in range(8):
                    self.mm(pg[:, 0:MT], Win[:, dk, 1024 + ct * 128:1024 + (ct + 1) * 128], hnT[:, dk, :], dk == 0, dk == 7, [hk, wk], [pgk])
                self.actf(sg[b], pg[:, 0:MT], AF.Sigmoid, [pgk], ["sg%d" % b])
                self.cp(ub[:, 0:30], halo[:, ct, 0:30], [("halo", ct)], [uk + "h"], eng="pool")
                self.tt(ub[:, 30:30 + MT], pval[:, 0:MT], sg[b], ALU.mult, [pvk, "sg%d" % b], [uk])
                self.cp(halo[:, ct, 0:30], ub[:, MT:MT + 30], [uk], [("halo", ct)], eng="pool")
                pc, pck = self.bank()
                for t in range(NPE):
                    self.mm(pc[:, 0:MT], dg[:, ct, t, :], ub[:, t:t + MT], t == 0, t == NPE - 1, [uk, uk + "h", ("dg", ct)], [pck])
                a_ = acc[b]
                ka = "acc%d" % b
                self.ts(a_, ub[:, NPE:NPE + MT], cw[:, ct, NPE:NPE + 1], ALU.mult, [uk, uk + "h", pk_], [ka])
                for t in range(NPE + 1, 31):
                    self.stt(a_, ub[:, t:t + MT], cw[:, ct, t:t + 1], a_, ALU.mult, ALU.add, [uk, uk + "h", ka], [ka])
                self.stt(xc[:, ct, :], pc[:, 0:MT], cb[:, ct:ct + 1], a_, ALU.add, ALU.add, [pck, ka, pk_], [("xc", ct)])
                self.actf(sq[b], xc[:, ct, :], AF.Square, [("xc", ct)], ["sq%d" % b])
                self.cp(xcb[b], xc[:, ct, :], [("xc", ct)], ["xcb%d" % b])
                self.mm(s1b[:, 0:MT], ones32, xcb[b], ct == 0, ct == 7, ["xcb%d" % b, "ones32"], [s1k])
                self.mm(s2b[:, 0:MT], ones32, sq[b], ct == 0, ct == 7, ["sq%d" % b, "ones32"], [s2k])
            self.ts(mean, s1b[:, 0:MT], 1.0 / 1024.0, ALU.mult, [s1k], ["mean"])
            self.tt(nmr, mean, mean, ALU.mult, ["mean"], ["nmr"])
            self.stt(rstd, s2b[:, 0:MT], 1.0 / 1024.0, nmr, ALU.mult, ALU.subtract, [s2k, "nmr"], ["rstd"])
            self.ts(rstd, rstd, EPS, ALU.add, ["rstd"], ["rstd"])
            self.actf(rstd, rstd, AF.Sqrt, ["rstd"], ["rstd"])
            self.S.add("dve", lambda e: e.reciprocal(out=rstd, in_=rstd), reads=["rstd"], writes=["rstd"])
            self.stt(nmr, mean, -1.0, rstd, ALU.mult, ALU.mult, ["mean", "rstd"], ["nmr"])
            for ct in range(8):
                b = ct % 2
                self.tt(tn[b], xc[:, ct, :], rstd, ALU.mult, [("xc", ct), "rstd"], ["tn%d" % b])
                self.tt(tn[b], tn[b], nmr, ALU.add, ["tn%d" % b, "nmr"], ["tn%d" % b])
                self.actf(sl[b], tn[b], AF.Silu, ["tn%d" % b, pk_], ["sl%d" % b], bias=lb[:, ct:ct + 1], scale=lg[:, ct:ct + 1])
                pz, pzk = self.bank()
                for dk in range(8):
                    self.mm(pz[:, 0:MT], Win[:, dk, 2048 + ct * 128:2048 + (ct + 1) * 128], hnT[:, dk, :], dk == 0, dk == 7, [hk, wk], [pzk])
                self.actf(szc[b], pz[:, 0:MT], AF.Silu, [pzk], ["szc%d" % b])
                self.tt(yT[:, ct, :], sl[b], szc[b], ALU.mult, ["sl%d" % b, "szc%d" % b], [yk])
            self.outproj_residual(m, yT, 8, Wo, wk, hmid, hmid_key, hmid, hmid_key, hB, yk=yk)

    def w_oa(self, L, slot):
        i = L // 2
        wk = ("w", slot)
        Wu = self.load_w_rows(slot, 0, self.o_w_in[i], 0, 8, 0, 512, wk)
        w = self.wa[slot]
        V = dict(Wu=Wu, wk=wk, slot=slot)
        V["Kblk"] = w[:, 4096:8192].rearrange("p (g n) -> p g n", n=128)
        V["Wa_re"] = w[:, 8192:10240].rearrange("p (g n) -> p g n", n=128)
        V["Wa_im"] = w[:, 10240:12288].rearrange("p (g n) -> p g n", n=128)
        V["CAre"] = w[:, 12288:14336].rearrange("p (g n) -> p g n", n=128)
        V["nCAim"] = w[:, 14336:16384].rearrange("p (g n) -> p g n", n=128)
        V["usm"] = w[:, 16384:32768].rearrange("p (c s n) -> p c s n", c=4, s=8)
        return V

    def cmul(self, o_re, o_im, a_re, a_im, b_re, b_im, tmp, reads, wkeys, neg_im=False):
        kre, kim, kt = wkeys
        self.tt(o_re, a_re, b_re, ALU.mult, reads, [kre])
        self.tt(tmp, a_im, b_im, ALU.mult, reads, [kt])
        self.tt(o_re, o_re, tmp, ALU.subtract, [kre, kt], [kre])
        self.tt(o_im, a_re, b_im, ALU.mult, reads, [kim])
        self.tt(tmp, a_im, b_re, ALU.mult, reads + [kre], [kt])
        if neg_im:
            self.stt(o_im, o_im, -1.0, tmp, ALU.mult, ALU.subtract, [kim, kt], [kim])
        else:
            self.tt(o_im, o_im, tmp, ALU.add, [kim, kt], [kim])

    def reduce_angle(self, x, t, key):
        TWO_PI = float(2.0 * np.pi)
        PI = float(np.pi)
        for _ in range(8):
            self.ts(t, x, PI, ALU.is_gt, [key], ["ra_t"], s2=TWO_PI, op1=ALU.mult)
            self.tt(x, x, t, ALU.subtract, [key, "ra_t"], [key])
        for _ in range(2):
            self.ts(t, x, -PI, ALU.is_lt, [key], ["ra_t"], s2=TWO_PI, op1=ALU.mult)
            self.tt(x, x, t, ALU.add, [key, "ra_t"], [key])

    def s5_prep(self, L, V):
        i = L // 2
        A = self.A
        pk_ = "par_s5"
        T = self.pst
        sm = lambda: A.f32(16)
        lr, li, ldt, dtt, mag, ang, sarg, carg, t1, are, aim, den, nr, cfr, cfi, u1, u2 = [sm() for _ in range(17)]
        self.dma(lr, self.o_lamre[i], [], [pk_], pk_)
        self.dma(li, self.o_lamim[i], [], [pk_], pk_)
        self.dma(ldt, self.o_logdt[i], [], [pk_], pk_)
        b3 = lambda: A.f32(256).rearrange("p (g h) -> p g h", h=16)
        bre, bim, cre, cim, Bre, Bim, tb = [b3() for _ in range(7)]
        self.dma(bre, self.o_bre[i].rearrange("p (g h) -> p g h", h=16), [], [pk_], pk_)
        self.dma(bim, self.o_bim[i].rearrange("p (g h) -> p g h", h=16), [], [pk_], pk_)
        self.dma(cre, self.o_cre[i].rearrange("p (g h) -> p g h", h=16), [], [pk_], pk_)
        self.dma(cim, self.o_cim[i].rearrange("p (g h) -> p g h", h=16), [], [pk_], pk_)
        dcol = A.f32(32)
        self.dma(dcol, self.o_dcol[i], [], [pk_], pk_)
        R = [pk_]
        self.actf(dtt, ldt, AF.Exp, R, ["dtt"])
        self.tt(u1, lr, dtt, ALU.mult, R + ["dtt"], ["u1"])
        self.actf(mag, u1, AF.Exp, ["u1"], ["mag"])
        self.tt(ang, li, dtt, ALU.mult, R + ["dtt"], ["ang"])
        self.cp(sarg, ang, ["ang"], ["sarg"], eng="dve")
        self.ts(carg, ang, float(np.pi / 2), ALU.add, ["ang"], ["carg"])
        self.reduce_angle(sarg, t1, "sarg")
        self.reduce_angle(carg, t1, "carg")
        self.actf(sarg, sarg, AF.Sin, ["sarg"], ["sarg"])
        self.actf(carg, carg, AF.Sin, ["carg"], ["carg"])
        self.tt(are, mag, carg, ALU.mult, ["mag", "carg"], ["are"])
        self.tt(aim, mag, sarg, ALU.mult, ["mag", "sarg"], ["aim"])
        self.tt(den, lr, lr, ALU.mult, R, ["den"])
        self.tt(u1, li, li, ALU.mult, R, ["u1"])
        self.tt(den, den, u1, ALU.add, ["den", "u1"], ["den"])
        self.S.add("dve", lambda e: e.reciprocal(out=den, in_=den), reads=["den"], writes=["den"])
        self.ts(nr, are, -1.0, ALU.add, ["are"], ["nr"])
        self.tt(cfr, nr, lr, ALU.mult, ["nr"] + R, ["cfr"])
        self.tt(u1, aim, li, ALU.mult, ["aim"] + R, ["u1"])
        self.tt(cfr, cfr, u1, ALU.add, ["cfr", "u1"], ["cfr"])
        self.tt(cfr, cfr, den, ALU.mult, ["cfr", "den"], ["cfr"])
        self.tt(cfi, aim, lr, ALU.mult, ["aim"] + R, ["cfi"])
        self.tt(u2, nr, li, ALU.mult, ["nr"] + R, ["u2"])
        self.tt(cfi, cfi, u2, ALU.subtract, ["cfi", "u2"], ["cfi"])
        self.tt(cfi, cfi, den, ALU.mult, ["cfi", "den"], ["cfi"])
        bc3 = lambda a: a.unsqueeze(2).to_broadcast([P, 16, 16])
        self.cmul(Bre, Bim, bc3(cfr), bc3(cfi), bre, bim, tb, ["cfr", "cfi"] + R, ["Bre", "Bim", "tb"])
        if PREP_CUT <= 1:
            return
        Pre = A.f32(144).rearrange("p (g j) -> p g j", j=9)
        Pim = A.f32(144).rearrange("p (g j) -> p g j", j=9)
        Qre = A.f32(128).rearrange("p (g j) -> p g j", j=8)
        Qim = A.f32(128).rearrange("p (g j) -> p g j", j=8)
        Vre = A.f32(128).rearrange("p (g j) -> p g j", j=8)
        Vim = A.f32(128).rearrange("p (g j) -> p g j", j=8)
        self.memset(Pre[:, :, 0], 1.0, ["Pre"], eng="dve")
        self.memset(Pim[:, :, 0], 0.0, ["Pim"], eng="dve")
        self.memset(Qre[:, :, 0], 1.0, ["Qre"], eng="dve")
        self.memset(Qim[:, :, 0], 0.0, ["Qim"], eng="dve")
        for j in range(1, 9):
            self.cmul(Pre[:, :, j], Pim[:, :, j], Pre[:, :, j - 1], Pim[:, :, j - 1], are, aim, u1, ["Pre", "Pim", "are", "aim"], ["Pre", "Pim", "u1"])
        ire, iim = sm(), sm()
        self.tt(u2, mag, mag, ALU.mult, ["mag"], ["u2"])
        self.S.add("dve", lambda e: e.reciprocal(out=u2, in_=u2), reads=["u2"], writes=["u2"])
        self.tt(ire, are, u2, ALU.mult, ["are", "u2"], ["ire"])
        self.stt(iim, aim, -1.0, u2, ALU.mult, ALU.mult, ["aim", "u2"], ["iim"])
        for j in range(1, 8):
            self.cmul(Qre[:, :, j], Qim[:, :, j], Qre[:, :, j - 1], Qim[:, :, j - 1], ire, iim, u1, ["Qre", "Qim", "ire", "iim"], ["Qre", "Qim", "u1"])
        for s_ in range(8):
            self.cp(Vre[:, :, s_], Pre[:, :, 7 - s_], ["Pre"], ["Vre"], eng="dve")
            self.cp(Vim[:, :, s_], Pim[:, :, 7 - s_], ["Pim"], ["Vim"], eng="dve")
        c8 = [T[:, k * 128:(k + 1) * 128].rearrange("p (g j) -> p g j", j=8) for k in range(3)]
        c64 = [T[:, 384 + k * 128:384 + (k + 1) * 128].rearrange("p (g j) -> p g j", j=8) for k in range(3)]
        c512 = [T[:, 768 + k * 16:768 + (k + 1) * 16] for k in range(3)]
        self.cp(c8[0][:, :, 0], Pre[:, :, 8], ["Pre"], ["c8"], eng="dve")
        self.cp(c8[1][:, :, 0], Pim[:, :, 8], ["Pim"], ["c8"], eng="dve")
        for j in range(1, 8):
            self.cmul(c8[0][:, :, j], c8[1][:, :, j], c8[0][:, :, j - 1], c8[1][:, :, j - 1], c8[0][:, :, 0], c8[1][:, :, 0], u1, ["c8"], ["c8", "c8", "u1"])
        self.cp(c64[0][:, :, 0], c8[0][:, :, 7], ["c8"], ["c64"], eng="dve")
        self.cp(c64[1][:, :, 0], c8[1][:, :, 7], ["c8"], ["c64"], eng="dve")
        for j in range(1, 8):
            self.cmul(c64[0][:, :, j], c64[1][:, :, j], c64[0][:, :, j - 1], c64[1][:, :, j - 1], c64[0][:, :, 0], c64[1][:, :, 0], u1, ["c64"], ["c64", "c64", "u1"])
        self.cp(c512[0], c64[0][:, :, 7], ["c64"], ["c512"], eng="dve")
        self.cp(c512[1], c64[1][:, :, 7], ["c64"], ["c512"], eng="dve")
        self.ts(c8[2], c8[1], -1.0, ALU.mult, ["c8"], ["c8n"])
        self.ts(c64[2], c64[1], -1.0, ALU.mult, ["c64"], ["c64n"])
        self.ts(c512[2], c512[1], -1.0, ALU.mult, ["c512"], ["c512n"])
        if PREP_CUT <= 2:
            return
        big = lambda: A.f32(2048).rearrange("p (g s h) -> p g s h", s=8, h=16)
        Lre, Lim, Rre, nRim, tbig = [big() for _ in range(5)]
        bp = lambda a: a.unsqueeze(3).to_broadcast([P, 16, 8, 16])
        bb = lambda a: a.unsqueeze(2).to_broadcast([P, 16, 8, 16])
        self.cmul(Lre, Lim, bp(Qre), bp(Qim), bb(Bre), bb(Bim), tbig, ["Qre", "Qim", "Bre", "Bim"], ["Lre", "Lim", "tbig"])
        self.cmul(Rre, nRim, bp(Pre[:, :, 0:8]), bp(Pim[:, :, 0:8]), bb(cre), bb(cim), tbig, ["Pre", "Pim"] + R, ["Rre", "nRim", "tbig"], neg_im=True)
        if PREP_CUT <= 3:
            return
        ident32 = A.f32(128)
        ones32 = A.f32(128)
        maskST = A.f32(128)
        tK = [A.f32(128) for _ in range(2)]
        self.memset(ones32, 1.0, ["ones32"])
        self.S.add("pool", lambda e: e.affine_select(out=ident32, in_=ones32, pattern=[[-1, 128]], compare_op=ALU.is_equal,
                                                    fill=0.0, base=0, channel_multiplier=1), reads=["ones32"], writes=["ident32"])
        self.S.add("pool", lambda e: e.affine_select(out=maskST.rearrange("p (t h) -> p t h", h=16), in_=ones32.rearrange("p (t h) -> p t h", h=16),
                                                    pattern=[[16, 8], [0, 16]], compare_op=ALU.is_ge, fill=0.0, base=15, channel_multiplier=-1),
                   reads=["ones32"], writes=["maskST"])
        bigb = lambda: A.bf16(2048).rearrange("p (g n) -> p g n", n=128)
        Lre_b, Lim_b, Rre_b, nRim_b = [bigb() for _ in range(4)]
        f3 = lambda a: a.rearrange("p g s h -> p g (s h)")
        self.cp(Lre_b, f3(Lre), ["Lre"], ["Lre_b"], eng="dve")
        self.cp(Lim_b, f3(Lim), ["Lim"], ["Lim_b"], eng="act")
        self.cp(Rre_b, f3(Rre), ["Rre"], ["Rre_b"], eng="dve")
        self.cp(nRim_b, f3(nRim), ["nRim"], ["nRim_b"], eng="act")
        Kblk = V["Kblk"]
        if PREP_CUT <= 3.2:
            return
        for g0 in range(0, 32, 8):
            bks = [self.bank(), self.bank()]
            for q in range(8):
                g = g0 + q
                gp, g2 = g // 2, g % 2
                rs = slice(64 * g2, 64 * g2 + 64)
                pbk, pk = bks[g2]
                c0 = (q // 2) * 128
                self.mm(pbk[:, c0:c0 + 128], Lre_b[rs, gp, :], Rre_b[rs, gp, :], True, False, ["Lre_b", "Rre_b"], [pk])
                self.mm(pbk[:, c0:c0 + 128], Lim_b[rs, gp, :], nRim_b[rs, gp, :], False, True, ["Lim_b", "nRim_b"], [pk])
            for q in range(8):
                g = g0 + q
                g2 = g % 2
                pbk, pk = bks[g2]
                c0 = (q // 2) * 128
                tk = tK[q % 2]
                self.tt(tk, pbk[:, c0:c0 + 128], maskST, ALU.mult, [pk, "maskST"], ["tK%d" % (q % 2)])
                self.stt(Kblk[:, g, :], ident32, dcol[:, g:g + 1], tk, ALU.mult, ALU.add, ["ident32", "tK%d" % (q % 2)] + R, ["Kblk"])
        if PREP_CUT <= 4:
            return
        Wre, Wim = Rre, nRim
        self.cmul(Wre, Wim, bp(Vre), bp(Vim), bb(Bre), bb(Bim), tbig, ["Vre", "Vim", "Bre", "Bim"], ["Rre", "nRim", "tbig"])
        self.cp(Rre_b, f3(Wre), ["Rre"], ["Rre_b"], eng="dve")
        self.cp(nRim_b, f3(Wim), ["nRim"], ["nRim_b"], eng="act")
        for src, dst, sk in ((Rre_b, V["Wa_re"], "Rre_b"), (nRim_b, V["Wa_im"], "nRim_b")):
            for g0 in range(0, 16, 4):
                pbk, pk = self.bank()
                pvw = pbk[:, :].bitcast(BF16)
                for q in range(4):
                    self.tp(pvw[:, q * 128:(q + 1) * 128], src[:, g0 + q, :], self.ident, [sk, "cst"], [pk])
                self.cp(dst[:, g0:g0 + 4, :], pvw[:, 0:512].rearrange("p (g n) -> p g n", n=128), [pk], ["Wa"])
        if PREP_CUT <= 5:
            return
        Cre32, Cim32 = Lre, Lim
        self.cmul(Cre32, Cim32, bp(Pre[:, :, 1:9]), bp(Pim[:, :, 1:9]), bb(cre), bb(cim), tbig, ["Pre", "Pim"] + R, ["Lre", "Lim", "tbig"], neg_im=True)
        self.cp(V["CAre"], Cre32.rearrange("p g s h -> p g (s h)"), ["Lre"], ["CA"], eng="dve")
        self.cp(V["nCAim"], Cim32.rearrange("p g s h -> p g (s h)"), ["Lim"], ["CA"], eng="dve")
        V["c8"], V["c64"], V["c512"] = c8, c64, c512

    def s5_core(self, V):
        A = self.A
        usm = V["usm"]
        c8, c64, c512 = V["c8"], V["c64"], V["c512"]
        Zs = A.bf16(8 * 240).rearrange("p (g x) -> p g x", x=240)
        onesb = A.bf16(128)
        self.memset(onesb, 1.0, ["onesb"])
        self.memset(Zs, 0.0, ["Zs"])
        self.S.add("pool", lambda e: e.affine_select(out=Zs[:, :, 112:128], in_=onesb.rearrange("p (a b) -> p a b", b=16),
                                                    pattern=[[-16, 8], [-1, 16]], compare_op=ALU.is_equal, fill=0.0, base=0, channel_multiplier=1),
                   reads=["onesb", "Zs"], writes=["Zs"])
        U8 = A.bf16(8 * 512).rearrange("p (g c) -> p g c", c=512)
        X2 = [A.f32(1024).rearrange("p (r c) -> p r c", r=2) for _ in range(4)]
        X = [[x2[:, 0, :], x2[:, 1, :]] for x2 in X2]
        Sp = [[A.bf16(512) for _ in range(2)] for _ in range(4)]
        for pp in range(4):
            for r in range(2):
                self.memset(Sp[pp][r][:, 0:1], 0.0, [("Sp", pp, r)])
        for ct in range(4):
            uk = [("usm", ct, s_) for s_ in range(8)]
            for gq in range(8):
                pbk, pk = self.bank()
                for s_ in range(8):
                    self.mm(pbk[:, :], Zs[:, gq, 112 - 16 * s_:240 - 16 * s_], usm[:, ct, s_, :], s_ == 0, s_ == 7, ["Zs", uk[s_]], [pk])
                self.cp(U8[:, gq, :], pbk[:, :], [pk], [("U8", gq)], eng=("act" if gq % 2 == 0 else "dve"))
            for pp in range(4):
                gp = 4 * ct + pp
                for r, Wn in ((0, "Wa_re"), (1, "Wa_im")):
                    pbk, pk = self.bank()
                    self.mm(pbk[0:64, :], V[Wn][:, gp, 0:64], U8[:, 2 * pp, :], True, True, ["Wa", ("U8", 2 * pp)], [pk])
                    self.mm(pbk[64:128, :], V[Wn][:, gp, 64:128], U8[:, 2 * pp + 1, :], True, True, ["Wa", ("U8", 2 * pp + 1)], [pk])
                    self.cp(X[pp][r], pbk[:, :], [pk], [("X", pp, r)])
            steps = []
            for pp in range(4):
                gp = 4 * ct + pp
                xb = X2[pp]
                kx = [("X", pp, 0), ("X", pp, 1)]
                ckeys = ["c8", "c64", "c512", "c8n", "c64n", "c512n"]
                lst = []

                def cmac(o_b, s_b, cr, ci, cni, lst=lst, kx=kx, ckeys=ckeys):
                    lst.append((o_b, s_b, cr, o_b, kx + ckeys, kx))
                    lst.append((o_b[:, 0], s_b[:, 1], cni, o_b[:, 0], kx + ckeys, kx))
                    lst.append((o_b[:, 1], s_b[:, 0], ci, o_b[:, 1], kx + ckeys, kx))
                v3 = xb.rearrange("p r (m j) -> p r m j", j=8)
                vz = xb.rearrange("p r (q j w) -> p r q j w", j=8, w=8)[:, :, :, :, 7]
                vw = xb.rearrange("p r (q w) -> p r q w", w=64)[:, :, :, 63]
                co = lambda c, j: (c[0][:, gp, j:j + 1], c[1][:, gp, j:j + 1], c[2][:, gp, j:j + 1])
                for j in range(1, 8):
                    cmac(v3[:, :, :, j], v3[:, :, :, j - 1], *co(c8, 0))
                for j in range(1, 8):
                    cmac(vz[:, :, :, j], vz[:, :, :, j - 1], *co(c64, 0))
                c5 = (c512[0][:, gp:gp + 1], c512[1][:, gp:gp + 1], c512[2][:, gp:gp + 1])
                for q in range(1, 8):
                    cmac(vw[:, :, q:q + 1], vw[:, :, q - 1:q], *c5)
                for j in range(0, 7):
                    cmac(vz[:, :, 1:8, j], vw[:, :, 0:7], *co(c64, j))
                for j in range(0, 7):
                    cmac(v3[:, :, 1:64, j], v3[:, :, 0:63, 7], *co(c8, j))
                steps.append(lst)
            for k in range(len(steps[0])):
                for pp in range(4):
                    o_, s_in, c_, a_, rd, wr = steps[pp][k]
                    self.stt(o_, s_in, c_, a_, ALU.mult, ALU.add, rd, wr)
            for pp in range(4):
                for r in range(2):
                    self.cp(Sp[pp][r][:, 1:512], X[pp][r][:, 0:511], [("X", pp, r)], [("Sp", pp, r)], eng=("act" if r == 0 else "pool"))
            for gq in range(8):
                g = 8 * ct + gq
                gp, g2 = g // 2, g % 2
                pp = gq // 2
                rs = slice(64 * g2, 64 * g2 + 64)
                pbk, pk = self.bank()
                self.mm(pbk[:, :], V["Kblk"][:, g, :], U8[:, gq, :], True, False, ["Kblk", ("U8", gq)], [pk])
                self.mm(pbk[:, :], V["CAre"][rs, gp, :], Sp[pp][0][rs, :], False, False, ["CA", ("Sp", pp, 0)], [pk])
                self.mm(pbk[:, :], V["nCAim"][rs, gp, :], Sp[pp][1][rs, :], False, True, ["CA", ("Sp", pp, 1)], [pk])
                self.cp(U8[:, gq, :], pbk[:, :], [pk], [("U8", gq)], eng=("act" if gq % 2 == 0 else "dve"))
            for t_ in range(8):
                pbk, pk = self.bank()
                for gq in range(8):
                    self.mm(pbk[:, :], Zs[:, t_, 112 - 16 * gq:240 - 16 * gq], U8[:, gq, :], gq == 0, gq == 7, ["Zs", ("U8", gq)], [pk])
                self.cp(usm[:, ct, t_, :], pbk[:, :], [pk], [("usm", ct, t_)], eng=("act" if t_ % 2 == 0 else "dve"))

    def pass_oa(self, L, V, hin, hin_key):
        A = self.A
        self.set_ring(range(8))
        self.s5_prep(L, V)
        if OA_STAGE < 2:
            return
        self.S.barrier()
        A.reset()
        Wu, wk, usm = V["Wu"], V["wk"], V["usm"]
        gn = A.f32(1024)
        pk_ = "par_oa"
        self.dma(gn, self.norm_g[L:L + 1, :].partition_broadcast(128), [], [pk_], pk_)
        hA = [A.f32(1024) for _ in range(2 * NS)]
        hnb = A.bf16(1024)
        junk = A.bf16(1024)
        ss = A.f32(8)
        hnT2 = [A.bf16(8 * MT).rearrange("p (k t) -> p k t", t=MT) for _ in range(2)]
        nb = dict(hA=hA, hnb=hnb, gn=gn, hnT=None, ss=ss, gkey=pk_, junk=junk)
        for m in range(NMT):
            hnT = hnT2[m % 2]
            hk = "hnT%d" % (m % 2)
            nb["hnT"] = hnT
            nb["hk"] = hk
            self.norm_tile(m, hin, hin_key, nb, store_scr=True)
            for ct in range(4):
                pbk, pk = self.bank()
                for dk in range(8):
                    self.mm(pbk[:, 0:MT], Wu[:, dk, ct * 128:(ct + 1) * 128], hnT[:, dk, :], dk == 0, dk == 7, [hk, wk], [pk])
                self.cp(usm[:, ct, :, m * 32:(m + 1) * 32], pbk[:, 0:MT].rearrange("p (c s) -> p s c", s=8), [pk],
                        [("usm", ct, s_) for s_ in range(8)], eng=("act" if ct % 2 == 0 else "dve"))
        if OA_STAGE < 3:
            return
        self.S.barrier()
        A.reset()
        self.s5_core(V)

    def w_ob1(self, L, slot):
        i = L // 2
        wk = ("w", slot)
        Win = self.load_w_rows(slot, 0, self.o_w_in[i], 0, 8, 1024, 3072, wk)
        Wo = self.load_w_rows(slot, 24576, self.o_w_out[i], 512, 8, 0, 1024, wk)
        wsT = self.wa[slot][:, 32768:33792].rearrange("p (h t) -> p h t", t=128)
        self.dma(wsT, self.o_wsT[i].rearrange("p (h t) -> p h t", t=128), [], [wk], wk, eng="pool")
        return (Win, Wo, wsT, wk)

    def pass_ob1(self, L, W, hin, hin_key, hmid, hmid_key):
        i = L // 2
        A = self.A
        self.set_ring(range(8))
        Win, Wo, wsT, wk = W
        self.S.add("pool", lambda e: e.affine_select(out=wsT, in_=wsT, pattern=[[0, 8], [1, 128]], compare_op=ALU.is_ge, fill=0.0,
                                                    base=0, channel_multiplier=-1), reads=[wk], writes=["wsTm"])
        pk_ = "par_ob1"
        lng = A.f32(1024)
        lnb = A.f32(1024)
        bsb_f = A.f32(1024)
        bsb = bsb_f.rearrange("p (h t) -> p h t", t=128)
        self.dma(lng, self.o_lng[i].partition_broadcast(128), [], [pk_], pk_)
        self.dma(lnb, self.o_lnb[i].partition_broadcast(128), [], [pk_], pk_)
        self.dma(bsb_f, self.o_bs[i].partition_broadcast(128), [], [pk_], pk_)
        hnT2 = [A.bf16(8 * MT).rearrange("p (k t) -> p k t", t=MT) for _ in range(2)]
        vtmp = A.f32(1024)
        vnT2 = [A.bf16(NS * 1024).rearrange("p (j n) -> p j n", n=1024) for _ in range(2)]
        st4 = A.f32(16)
        junk = A.bf16(512)
        szt = [A.f32(MT) for _ in range(2)]
        t1 = [A.f32(MT) for _ in range(2)]
        yT2 = [A.bf16(8 * MT).rearrange("p (k t) -> p k t", t=MT) for _ in range(2)]
        hB = [A.f32(1024) for _ in range(2 * NS)]
        for m in range(NMT):
            hnT = hnT2[m % 2]
            hk = "hnT%d" % (m % 2)
            yT = yT2[m % 2]
            yk = "yT%d" % (m % 2)
            vnT = vnT2[m % 2]
            vq = "vnT%d_" % (m % 2)
            self.load_hnT(m, hnT, hk)
            for j in range(NS):
                pbs = []
                for n2 in range(2):
                    pv_, pvk = self.bank()
                    for dk in range(8):
                        self.mm(pv_[:, :], hnT[:, dk, j * 128:(j + 1) * 128], Win[:, dk, 1024 + n2 * 512:1024 + (n2 + 1) * 512],
                                dk == 0, dk == 7, [hk, wk], [pvk])
                    self.actf(junk, pv_[:, :], AF.Identity, [pvk], ["junk", "st_s%d" % n2], accum=st4[:, n2:n2 + 1])
                    self.actf(junk, pv_[:, :], AF.Square, [pvk], ["junk", "st_q%d" % n2], accum=st4[:, 2 + n2:3 + n2])
                    pbs.append((pv_, pvk))
                self.tt(st4[:, 4:5], st4[:, 0:1], st4[:, 1:2], ALU.add, ["st_s0", "st_s1"], ["st_m"])
                self.ts(st4[:, 4:5], st4[:, 4:5], 1.0 / 1024.0, ALU.mult, ["st_m"], ["st_m"])
                self.tt(st4[:, 5:6], st4[:, 2:3], st4[:, 3:4], ALU.add, ["st_q0", "st_q1"], ["st_v"])
                self.tt(st4[:, 6:7], st4[:, 4:5], st4[:, 4:5], ALU.mult, ["st_m"], ["st_mm"])
                self.stt(st4[:, 5:6], st4[:, 5:6], 1.0 / 1024.0, st4[:, 6:7], ALU.mult, ALU.subtract, ["st_v", "st_mm"], ["st_v"])
                self.rsqrt_small(st4[:, 7:8], st4[:, 5:6], 1.0, ["st_v"], ["st_r"], "st_rt")
                self.stt(st4[:, 8:9], st4[:, 4:5], -1.0, st4[:, 7:8], ALU.mult, ALU.mult, ["st_m", "st_r"], ["st_n"])
                for n2 in range(2):
                    pv_, pvk = pbs[n2]
                    sl_ = slice(n2 * 512, (n2 + 1) * 512)
                    self.ts(vtmp[:, sl_], pv_[:, :], st4[:, 7:8], ALU.mult, [pvk, "st_r", "st_n"], ["vtmp%d" % n2], s2=st4[:, 8:9], op1=ALU.add)
                    self.tt(vtmp[:, sl_], vtmp[:, sl_], lng[:, sl_], ALU.mult, ["vtmp%d" % n2, pk_], ["vtmp%d" % n2], eng="pool")
                    self.tt(vnT[:, j, sl_], vtmp[:, sl_], lnb[:, sl_], ALU.add, ["vtmp%d" % n2, pk_], [vq + str(j)], eng="pool")
            for hd in range(8):
                b = hd % 2
                psv, psvk = self.bank()
                for j in range(NS):
                    self.mm(psv[:, j * 128:(j + 1) * 128], vnT[:, j, hd * 128:(hd + 1) * 128], wsT[:, hd, :], True, True,
                            [vq + str(j), "wsTm"], [psvk])
                pu, puk = self.bank()
                for dk in range(8):
                    self.mm(pu[:, 0:MT], Win[:, dk, hd * 128:(hd + 1) * 128], hnT[:, dk, :], dk == 0, dk == 7, [hk, wk], [puk])
                pz, pzk = self.bank()
                for dk in range(8):
                    self.mm(pz[:, 0:MT], Win[:, dk, 2048 + hd * 128:2048 + (hd + 1) * 128], hnT[:, dk, :], dk == 0, dk == 7, [hk, wk], [pzk])
                self.actf(szt[b], pz[:, 0:MT], AF.Silu, [pzk], ["szt%d" % b])
                self.tt(t1[b].rearrange("p (j t) -> p j t", t=128), psv[:, 0:MT].rearrange("p (j t) -> p j t", t=128),
                        bsb[:, hd, :].unsqueeze(1).to_broadcast([P, NS, 128]), ALU.add, [psvk, pk_], ["t1_%d" % b])
                self.tt(t1[b], pu[:, 0:MT], t1[b], ALU.mult, [puk, "t1_%d" % b], ["t1_%d" % b])
                self.tt(yT[:, hd, :], t1[b], szt[b], ALU.mult, ["t1_%d" % b, "szt%d" % b], [yk])
            self.outproj_residual(m, yT, 8, Wo, wk, hin, hin_key, hmid, hmid_key, hB, yk=yk)

    def w_ob2(self, L, slot):
        i = L // 2
        wk = ("w", slot)
        Wz = self.load_w_rows(slot, 0, self.o_w_in[i], 0, 8, 512, 512, wk)
        Wg = self.load_w_rows(slot, 4096, self.o_w_glu[i], 0, 4, 0, 512, wk)
        Wo = self.load_w_rows(slot, 6144, self.o_w_out[i], 0, 4, 0, 1024, wk)
        usm = self.wa[slot][:, 16384:32768].rearrange("p (c s n) -> p c s n", c=4, s=8)
        return (Wz, Wg, Wo, usm, wk)

    def pass_ob2(self, L, W, hmid, hmid_key, last):
        i = L // 2
        A = self.A
        self.set_ring(range(8))
        Wz, Wg, Wo, usm, wk = W
        pk_ = "par_ob2"
        bglu = A.f32(4)
        self.dma(bglu, self.o_bglu[i], [], [pk_], pk_)
        gfin = None
        if last:
            gfin = A.f32(1024)
            self.dma(gfin, self.final_g.partition_broadcast(128), [], ["gfin"], "par_gfin")
        hnT2 = [A.bf16(8 * MT).rearrange("p (k t) -> p k t", t=MT) for _ in range(2)]
        ge322 = [A.f32(4 * MT).rearrange("p (c t) -> p c t", t=MT) for _ in range(2)]
        gebf2 = [A.bf16(4 * MT).rearrange("p (c t) -> p c t", t=MT) for _ in range(2)]
        sgm = [A.f32(MT) for _ in range(2)]
        szt = [A.f32(MT) for _ in range(2)]
        yT2 = [A.bf16(4 * MT).rearrange("p (k t) -> p k t", t=MT) for _ in range(2)]
        hB = [A.f32(1024) for _ in range(2 * NS)]
        junk = A.bf16(1024)
        ss = A.f32(8)
        for m in range(NMT):
            par = m % 2
            hnT = hnT2[par]
            hk = "hnT%d" % par
            yT = yT2[par]
            yk = "yT%d" % par
            ge32 = ge322[par]
            gebf = gebf2[par]
            self.load_hnT(m, hnT, hk)
            for ct in range(4):
                self.actf(ge32[:, ct, :].rearrange("p (c s) -> p s c", s=8), usm[:, ct, :, m * 32:(m + 1) * 32], AF.Gelu_apprx_tanh,
                          [("usm", ct, s_) for s_ in range(8)], [("ge32", par, ct)])
                self.cp(gebf[:, ct, :], ge32[:, ct, :], [("ge32", par, ct)], [("gebf", par, ct)], eng="dve")
            for mt in range(4):
                b = mt % 2
                pg, pgk = self.bank()
                for kt in range(4):
                    self.mm(pg[:, 0:MT], Wg[:, kt, mt * 128:(mt + 1) * 128], gebf[:, kt, :], kt == 0, kt == 3, [("gebf", par, kt), wk], [pgk])
                self.actf(sgm[b], pg[:, 0:MT], AF.Sigmoid, [pgk, pk_], ["sgm%d" % b], bias=bglu[:, mt:mt + 1])
                pz, pzk = self.bank()
                for dk in range(8):
                    self.mm(pz[:, 0:MT], Wz[:, dk, mt * 128:(mt + 1) * 128], hnT[:, dk, :], dk == 0, dk == 7, [hk, wk], [pzk])
                self.silu_from_psum(szt[b], pz[:, 0:MT], pzk, "szt%d" % b)
                self.tt(sgm[b], sgm[b], ge32[:, mt, :], ALU.mult, ["sgm%d" % b, ("ge32", par, mt)], ["sgm%d" % b])
                self.tt(yT[:, mt, :], sgm[b], szt[b], ALU.mult, ["sgm%d" % b, "szt%d" % b], [yk])
            if last:
                self.outproj_residual(m, yT, 4, Wo, wk, hmid, hmid_key, self.out, "out", hB, final_g=gfin, ss=ss, junk=junk, yk=yk)
            else:
                self.outproj_residual(m, yT, 4, Wo, wk, hmid, hmid_key, hmid, hmid_key, hB, yk=yk)

    def build(self):
        passes = []
        hin, hin_key = self.x, "x"
        for L in range(self.n_layers):
            hmid = self.hbuf[(L + 1) % 2]
            hmid_key = ("hb", (L + 1) % 2)
            if getattr(self, "first_pass", 0) > 0:
                hin, hin_key = self.x, "x"
            if L % 2 == 0:
                passes.append((lambda slot, L=L: self.w_e1(L, slot),
                               lambda W, L=L, a=hin, ak=hin_key, b=hmid, bk=hmid_key: self.pass_e1(L, W, a, ak, b, bk)))
                passes.append((lambda slot, L=L: self.w_e2(L, slot),
                               lambda W, L=L, b=hmid, bk=hmid_key: self.pass_e2(L, W, b, bk)))
            else:
                last = (L == self.n_layers - 1) and self.final_norm
                passes.append((lambda slot, L=L: self.w_oa(L, slot),
                               lambda W, L=L, a=hin, ak=hin_key: self.pass_oa(L, W, a, ak)))
                passes.append((lambda slot, L=L: self.w_ob1(L, slot),
                               lambda W, L=L, a=hin, ak=hin_key, b=hmid, bk=hmid_key: self.pass_ob1(L, W, a, ak, b, bk)))
                passes.append((lambda slot, L=L: self.w_ob2(L, slot),
                               lambda W, L=L, b=hmid, bk=hmid_key, last=last: self.pass_ob2(L, W, b, bk, last)))
                self.fused_out = last
            hin, hin_key = hmid, hmid_key
        if self.max_passes is not None:
            passes = passes[getattr(self, 'first_pass', 0):self.max_passes]
        slot = 0
        Wn = passes[0][0](slot)
        for k, (wl, run) in enumerate(passes):
            self.S.barrier()
            self.A.reset()
            Wcur = Wn
            slot ^= 1
            if k + 1 < len(passes):
                Wn = passes[k + 1][0](slot)
            run(Wcur)
        if getattr(self, "fused_out", False) and self.max_passes is None:
            self.S.emit()
            return
        if getattr(self, "first_pass", 0) > 0 and self.max_passes is not None and self.max_passes <= 3:
            hin, hin_key = self.x, "x"
        self.S.barrier()
        A = self.A
        A.reset()
        cpb = [A.f32(1024) for _ in range(2)]
        for t in range(0 if not self.debug else SEQ // 128, SEQ // 128):
            b = t % 2
            self.dma(cpb[b], hin[t * 128:(t + 1) * 128, :], [(hin_key, t // NS, t % NS)], ["cpb%d" % b], "ld_cp%d" % b)
            self.dma(self.out[t * 128:(t + 1) * 128, :], cpb[b], ["cpb%d" % b], [("out", t)], "st_cp%d" % b)
        self.S.emit()


def build_program(n_layers=4, final_norm=True, max_passes=None, first_pass=0):
    nc = bass.Bass("TRN2", target_bir_lowering=False)
    st = ExitStack()
    with st:
        b = Builder(nc, st, n_layers=n_layers, final_norm=final_norm, max_passes=max_passes)
        b.first_pass = first_pass
        b.build()
    return nc


def host_layout(inputs):
    f = lambda a: np.ascontiguousarray(np.asarray(a, dtype=np.float32))
    g = {}
    g["norm_g"] = f(inputs["norm_g"])
    g["final_g"] = f(inputs["final_g"]).reshape(1, D)
    g["e_w_in"] = f(inputs["e_w_in"])
    g["e_w_a2"] = f(inputs["e_w_a2"])
    g["e_b_a"] = f(inputs["e_b_a"]).reshape(2, 1, 512)
    g["e_gla_g"] = f(inputs["e_gla_g"]).reshape(2, 1, 1024)
    cw = f(inputs["e_conv_w"])
    g["e_cw"] = f(cw.reshape(2, 31, 8, 128).transpose(0, 3, 2, 1))
    cpl = lambda a: f(f(a).reshape(2, 8, 128).transpose(0, 2, 1))
    g["e_cb"] = cpl(inputs["e_conv_b"])
    g["e_lg"] = cpl(inputs["e_cln_g"])
    g["e_lb"] = cpl(inputs["e_cln_b"])
    g["e_w_out"] = f(inputs["e_w_out"])
    g["o_w_in"] = f(inputs["o_w_in"])
    gp_l = lambda a: f(f(a).reshape(2, 16, 2, 64).transpose(0, 2, 3, 1).reshape(2, 128, 16))
    g["o_lamre"] = gp_l(inputs["o_lam_re"])
    g["o_lamim"] = gp_l(inputs["o_lam_im"])
    ldt = f(inputs["o_log_dt"]).reshape(2, 16, 2)
    g["o_logdt"] = f(np.broadcast_to(ldt.transpose(0, 2, 1)[:, :, None, :], (2, 2, 64, 16)).reshape(2, 128, 16))
    b_l = lambda a: f(f(a).reshape(2, 16, 2, 64, 16).transpose(0, 2, 3, 1, 4).reshape(2, 128, 256))
    g["o_bre"] = b_l(inputs["o_b_re"])
    g["o_bim"] = b_l(inputs["o_b_im"])
    c_l = lambda a: f(f(a).reshape(2, 16, 2, 16, 64).transpose(0, 2, 4, 1, 3).reshape(2, 128, 256))
    g["o_cre"] = c_l(inputs["o_c_re"])
    g["o_cim"] = c_l(inputs["o_c_im"])
    dd = f(inputs["o_d"]).reshape(2, 32, 16)
    g["o_dcol"] = f(np.broadcast_to(dd.transpose(0, 2, 1)[:, None, :, :], (2, 8, 16, 32)).reshape(2, 128, 32))
    g["o_w_glu"] = f(inputs["o_w_glu"])
    g["o_bglu"] = f(f(inputs["o_b_glu"]).reshape(2, 4, 128).transpose(0, 2, 1))
    g["o_lng"] = f(inputs["o_sg_ln_g"]).reshape(2, 1, 1024)
    g["o_lnb"] = f(inputs["o_sg_ln_b"]).reshape(2, 1, 1024)
    ws = f(inputs["o_w_s"])
    g["o_wsT"] = f(ws.transpose(0, 3, 1, 2).reshape(2, 128, 1024))
    g["o_bs"] = f(inputs["o_b_s"]).reshape(2, 1, 1024)
    g["o_w_out"] = f(inputs["o_w_out"])
    return g


def kernel(**inputs):
    x = np.asarray(inputs["x"], dtype=np.float32)
    shared = host_layout(inputs)
    nc = build_program()
    in_maps = []
    for c in range(8):
        m = dict(shared)
        m["x"] = np.ascontiguousarray(x[c])
        in_maps.append(m)
    res = run_bass_kernel_spmd(nc, in_maps, core_ids=list(range(8)))
    return np.stack([np.asarray(r["out"], dtype=np.float32) for r in res.results], axis=0)
```
